# Optimizing a Trainium2 kernel written in Bass

```python
import jax, jax.numpy as jnp
from jax import lax
import numpy as np

D_MODEL = 1024
BATCH = 8
SEQ = 2048
DEPTH = 1
DEC_BATCH = 128
DEC_SEQ = 4
PAST_LEN = 16384
PAGE_SIZE = 128

N_HEADS_A = 4
DK_A = D_MODEL // 16
DV_A = D_MODEL // 8
LOW_RANK = 16
GATE_TAU = 16.0
N_HEADS_R = 4
DK_R = D_MODEL // 16
DV_R = D_MODEL // 8
ROPE_BASE = 10000.0
QK_A = N_HEADS_A * DK_A
V_A = N_HEADS_A * DV_A
QK_R = N_HEADS_R * DK_R
V_R = N_HEADS_R * DV_R
D_FF = ((8 * D_MODEL // 3) + 127) // 128 * 128
D_PLE = 256
CHUNK = 64
EPS = 1e-6
IN_SPLITS = (QK_A, QK_A, V_A, V_A, QK_R, QK_R, V_R, V_R, LOW_RANK, D_MODEL, D_MODEL)
IN_COLS = sum(IN_SPLITS)

kernel_name = "hybrid_gla_retention_macaron_step"


def _split_points():
    return [int(c) for c in np.cumsum(IN_SPLITS)[:-1]]


def _ret_log_decay():
    h = jnp.arange(N_HEADS_R, dtype=jnp.float32)
    return jnp.log1p(-jnp.exp2(-5.0 - h))


def rmsnorm(x, w):
    xf = x.astype(jnp.float32)
    y = xf * lax.rsqrt(jnp.mean(xf * xf, axis=-1, keepdims=True) + EPS)
    return (y * w).astype(x.dtype)


def group_rmsnorm(o, w, dtype):
    B, T, H, dv = o.shape
    y = o * lax.rsqrt(jnp.mean(o * o, axis=-1, keepdims=True) + EPS)
    return (y * w.reshape(H, dv)).reshape(B, T, H * dv).astype(dtype)


def swiglu(x, w_in, w_out):
    a, b = jnp.split(x @ w_in, 2, axis=-1)
    return (jax.nn.silu(a) * b) @ w_out


def rotary(x, pos):
    d = x.shape[-1]
    half = d // 2
    freq = ROPE_BASE ** (-jnp.arange(half, dtype=jnp.float32) / half)
    ang = pos.astype(jnp.float32)[:, None] * freq[None, :]
    cos = jnp.cos(ang)[None, :, None, :]
    sin = jnp.sin(ang)[None, :, None, :]
    x1 = x[..., :half].astype(jnp.float32)
    x2 = x[..., half:].astype(jnp.float32)
    return jnp.concatenate([x1 * cos - x2 * sin, x1 * sin + x2 * cos], axis=-1).astype(x.dtype)


def gated_linear_scan(q, k, v, log_decay, s0, chunk):
    B, T, H, dk = q.shape
    dv = v.shape[-1]
    dg = log_decay.shape[-1]
    n = T // chunk

    def to_chunks(a):
        return a.astype(jnp.float32).reshape(B, n, chunk, H, a.shape[-1]).transpose(1, 0, 3, 2, 4)

    causal = jnp.tril(jnp.ones((chunk, chunk), dtype=bool))[:, :, None]

    def step(s, inp):
        qc, kc, vc, gc = inp
        b = jnp.cumsum(gc, axis=2)
        b_last = b[:, :, -1:, :]
        diff = b[:, :, :, None, :] - b[:, :, None, :, :]
        decay = jnp.exp(jnp.where(causal, diff, -jnp.inf))
        if dg == 1:
            scores = jnp.einsum('bhid,bhjd->bhij', qc, kc) * decay[..., 0]
        else:
            scores = jnp.einsum('bhid,bhjd,bhijd->bhij', qc, kc, decay)
        o = jnp.einsum('bhij,bhjv->bhiv', scores, vc) + jnp.einsum('bhid,bhdv->bhiv', qc * jnp.exp(b), s)
        s_new = jnp.exp(b_last)[:, :, 0, :, None] * s + jnp.einsum('bhjd,bhjv->bhdv', kc * jnp.exp(b_last - b), vc)
        return s_new, o

    s_final, o = lax.scan(step, s0.astype(jnp.float32),
                          (to_chunks(q), to_chunks(k), to_chunks(v), to_chunks(log_decay)))
    o = o.transpose(1, 0, 3, 2, 4).reshape(B, T, H, dv)
    return o, s_final


def trunk_layer(x, p, s_gla0, s_ret0, pos,
                norm_ffn1, w_ffn1_in, w_ffn1_out, norm_mix, w_in, w_alpha_up, b_alpha,
                gn_gla, gn_ret, w_out, norm_ffn2, w_ffn2_in, w_ffn2_out,
                norm_ple, w_ple_gate, w_ple_proj):
    B, T, _ = x.shape
    chunk = CHUNK if T % CHUNK == 0 else T
    h = x + 0.5 * swiglu(rmsnorm(x, norm_ffn1), w_ffn1_in, w_ffn1_out)
    u = rmsnorm(h, norm_mix)
    qa, ka, va, ra, qr, kr, vr, gr, a_low, gate_a, gate_r = jnp.split(u @ w_in, _split_points(), axis=-1)
    qa = qa.reshape(B, T, N_HEADS_A, DK_A) * (DK_A ** -0.5)
    ka = ka.reshape(B, T, N_HEADS_A, DK_A)
    va = va.reshape(B, T, N_HEADS_A, DV_A)
    log_alpha = jax.nn.log_sigmoid((a_low @ w_alpha_up + b_alpha).astype(jnp.float32)) / GATE_TAU
    log_alpha = log_alpha.reshape(B, T, N_HEADS_A, DK_A)
    oa, s_gla = gated_linear_scan(qa, ka, va, log_alpha, s_gla0, chunk)
    oa = group_rmsnorm(oa, gn_gla, x.dtype) * jax.nn.silu(ra)
    qr = rotary(qr.reshape(B, T, N_HEADS_R, DK_R), pos)
    kr = rotary(kr.reshape(B, T, N_HEADS_R, DK_R), pos) * (DK_R ** -0.5)
    vr = vr.reshape(B, T, N_HEADS_R, DV_R)
    log_gamma = jnp.broadcast_to(_ret_log_decay()[None, None, :, None], (B, T, N_HEADS_R, 1))
    orr, s_ret = gated_linear_scan(qr, kr, vr, log_gamma, s_ret0, chunk)
    orr = group_rmsnorm(orr, gn_ret, x.dtype) * jax.nn.silu(gr)
    mix = jax.nn.sigmoid(gate_a) * (oa @ w_out[:V_A]) + jax.nn.sigmoid(gate_r) * (orr @ w_out[V_A:])
    h = h + mix
    h = h + 0.5 * swiglu(rmsnorm(h, norm_ffn2), w_ffn2_in, w_ffn2_out)
    h = h + (p @ w_ple_proj) * jax.nn.sigmoid(rmsnorm(h, norm_ple) @ w_ple_gate)
    return h, s_gla, s_ret


def setup_inputs(seed: int = 0) -> dict:
    key = jax.random.key(seed)
    ks = jax.random.split(key, 24)
    f32 = jnp.float32
    nrm = lambda k, shape, s: jax.random.normal(k, shape, f32) * s
    gain = lambda k, n: 1.0 + 0.02 * jax.random.normal(k, (DEPTH, n), f32)
    return {
        "x_prompt": nrm(ks[0], (BATCH, SEQ, D_MODEL), 1.0),
        "x_sample": nrm(ks[1], (DEC_BATCH, DEC_SEQ, D_MODEL), 1.0),
        "state_gla": nrm(ks[2], (DEPTH, DEC_BATCH, N_HEADS_A, DK_A, DV_A), 0.5),
        "state_ret": nrm(ks[3], (DEPTH, DEC_BATCH, N_HEADS_R, DK_R, DV_R), 0.5),
        "p_prompt": nrm(ks[4], (DEPTH, BATCH, SEQ, D_PLE), 1.0),
        "p_sample": nrm(ks[5], (DEPTH, DEC_BATCH, DEC_SEQ, D_PLE), 1.0),
        "norm_ffn1": gain(ks[6], D_MODEL),
        "w_ffn1_in": nrm(ks[7], (DEPTH, D_MODEL, 2 * D_FF), D_MODEL ** -0.5),
        "w_ffn1_out": nrm(ks[8], (DEPTH, D_FF, D_MODEL), D_FF ** -0.5),
        "norm_mix": gain(ks[9], D_MODEL),
        "w_in": nrm(ks[10], (DEPTH, D_MODEL, IN_COLS), D_MODEL ** -0.5),
        "w_alpha_up": nrm(ks[11], (DEPTH, LOW_RANK, QK_A), LOW_RANK ** -0.5),
        "b_alpha": nrm(ks[12], (DEPTH, QK_A), 0.01),
        "gn_gla": gain(ks[13], V_A),
        "gn_ret": gain(ks[14], V_R),
        "w_out": nrm(ks[15], (DEPTH, V_A + V_R, D_MODEL), (V_A + V_R) ** -0.5),
        "norm_ffn2": gain(ks[16], D_MODEL),
        "w_ffn2_in": nrm(ks[17], (DEPTH, D_MODEL, 2 * D_FF), D_MODEL ** -0.5),
        "w_ffn2_out": nrm(ks[18], (DEPTH, D_FF, D_MODEL), D_FF ** -0.5),
        "norm_ple": gain(ks[19], D_MODEL),
        "w_ple_gate": nrm(ks[20], (DEPTH, D_MODEL, D_MODEL), D_MODEL ** -0.5),
        "w_ple_proj": nrm(ks[21], (DEPTH, D_PLE, D_MODEL), D_PLE ** -0.5),
        "norm_final": 1.0 + 0.02 * jax.random.normal(ks[22], (D_MODEL,), f32),
    }


def reference(x_prompt, x_sample, state_gla, state_ret, p_prompt, p_sample,
              norm_ffn1, w_ffn1_in, w_ffn1_out, norm_mix, w_in, w_alpha_up, b_alpha,
              gn_gla, gn_ret, w_out, norm_ffn2, w_ffn2_in, w_ffn2_out,
              norm_ple, w_ple_gate, w_ple_proj, norm_final):
    Bp, Tp, _ = x_prompt.shape
    Bs, Ts, _ = x_sample.shape
    pos_prompt = jnp.arange(Tp, dtype=jnp.int32)
    pos_sample = PAST_LEN + jnp.arange(Ts, dtype=jnp.int32)
    hp, hs = x_prompt, x_sample
    gla_p, ret_p, gla_s, ret_s = [], [], [], []
    for i in range(DEPTH):
        w = (norm_ffn1[i], w_ffn1_in[i], w_ffn1_out[i], norm_mix[i], w_in[i], w_alpha_up[i], b_alpha[i],
             gn_gla[i], gn_ret[i], w_out[i], norm_ffn2[i], w_ffn2_in[i], w_ffn2_out[i],
             norm_ple[i], w_ple_gate[i], w_ple_proj[i])
        zero_gla = jnp.zeros((Bp, N_HEADS_A, DK_A, DV_A), jnp.float32)
        zero_ret = jnp.zeros((Bp, N_HEADS_R, DK_R, DV_R), jnp.float32)
        hp, sa_p, sr_p = trunk_layer(hp, p_prompt[i], zero_gla, zero_ret, pos_prompt, *w)
        hs, sa_s, sr_s = trunk_layer(hs, p_sample[i], state_gla[i], state_ret[i], pos_sample, *w)
        gla_p.append(sa_p.astype(state_gla.dtype))
        ret_p.append(sr_p.astype(state_ret.dtype))
        gla_s.append(sa_s.astype(state_gla.dtype))
        ret_s.append(sr_s.astype(state_ret.dtype))
    y_prompt = rmsnorm(hp, norm_final)
    y_sample = rmsnorm(hs, norm_final)
    return (y_prompt, y_sample, jnp.stack(gla_p), jnp.stack(ret_p), jnp.stack(gla_s), jnp.stack(ret_s))
```

```python
import numpy as np
import ml_dtypes
import concourse.bass as bass
import concourse.mybir as mybir
from concourse.bass_utils import run_bass_kernel_spmd
from contextlib import ExitStack

F32 = mybir.dt.float32
BF16 = mybir.dt.bfloat16
AF = mybir.ActivationFunctionType
ALU = mybir.AluOpType

NCORES = 8
D = 1024
DFF = 2816
NFC = 22
TP = 2048
NSAMP = 64
TALL = TP + NSAMP
EPS = 1e-6
RING = 5
SLOT = 4096

ENGS = ("pe", "act", "dve", "pool", "sp")


class Op:
    __slots__ = ("eng", "fn", "reads", "writes", "dsem", "chan", "deps", "signaled", "count", "idx")

    def __init__(self, eng, fn, reads, writes, dsem):
        self.eng = eng
        self.fn = fn
        self.reads = tuple(reads)
        self.writes = tuple(writes)
        self.dsem = dsem
        self.chan = ("dma", dsem) if dsem is not None else ("eng", eng)
        self.deps = []
        self.signaled = dsem is not None
        self.count = 0


class Sched:
    def __init__(self):
        self.ops = []
        self.dma_sems = []
        self.barriers = []

    def op(self, eng, fn, reads=(), writes=(), dsem=None):
        o = Op(eng, fn, reads, writes, dsem)
        o.idx = len(self.ops)
        self.ops.append(o)
        if dsem is not None and dsem not in self.dma_sems:
            self.dma_sems.append(dsem)
        return o

    def barrier(self, exclude=()):
        self.barriers.append((len(self.ops), tuple(exclude)))

    def analyze(self):
        last_w, last_r = {}, {}
        waited = {e: {} for e in ENGS}
        chan_last = {}
        pending = {e: None for e in ENGS}
        bars = list(self.barriers)
        bi = 0
        ops = self.ops
        for o in ops:
            while bi < len(bars) and bars[bi][0] <= o.idx:
                snap = {ch: i for ch, i in chan_last.items()
                        if not (ch[0] == "dma" and isinstance(ch[1], tuple) and ch[1][0] == "w")}
                for e in ENGS:
                    if e in bars[bi][1]:
                        continue
                    if pending[e] is None:
                        pending[e] = snap
                    else:
                        m = dict(pending[e])
                        m.update(snap)
                        pending[e] = m
                bi += 1
            deps = {}

            def add(d):
                for ch, i in d.items():
                    if deps.get(ch, -1) < i:
                        deps[ch] = i

            if pending[o.eng] is not None:
                add(pending[o.eng])
                pending[o.eng] = None
            for k in o.reads:
                add(last_w.get(k, {}))
            for k in o.writes:
                add(last_w.get(k, {}))
                add(last_r.get(k, {}))
            w = waited[o.eng]
            for ch, i in deps.items():
                if ch == ("eng", "pe") and o.eng == "pe":
                    continue
                if w.get(ch, -1) >= i:
                    continue
                w[ch] = i
                o.deps.append(i)
                ops[i].signaled = True
            for k in o.reads:
                last_r.setdefault(k, {})[o.chan] = o.idx
            for k in o.writes:
                last_w[k] = {o.chan: o.idx}
                last_r[k] = {}
            chan_last[o.chan] = o.idx
        cnt = {}
        for o in ops:
            if o.signaled:
                inc = 16 if o.dsem is not None else 1
                cnt[o.chan] = cnt.get(o.chan, 0) + inc
                o.count = cnt[o.chan]
        self.final_counts = cnt

    def emit(self, nc, final_wait_eng="sp"):
        self.analyze()
        with ExitStack() as es:
            sems = {}
            for e in ENGS:
                sems[("eng", e)] = es.enter_context(nc.semaphore("sem_" + e))
            for i, d in enumerate(self.dma_sems):
                sems[("dma", d)] = es.enter_context(nc.semaphore("dsem_%d" % i))
            block = es.enter_context(nc.Block())
            ops = self.ops

            def run(engname):
                def body(eng):
                    for o in ops:
                        if o.eng != engname:
                            continue
                        for i in o.deps:
                            d = ops[i]
                            eng.wait_ge(sems[d.chan], d.count)
                        ins = o.fn(eng)
                        if o.signaled:
                            ins.then_inc(sems[o.chan], 16 if o.dsem is not None else 1)
                    if engname == final_wait_eng:
                        for ch, c in self.final_counts.items():
                            eng.wait_ge(sems[ch], c)
                return body

            block.tensor(run("pe"))
            block.scalar(run("act"))
            block.vector(run("dve"))
            block.gpsimd(run("pool"))
            block.sync(run("sp"))


def _tile_kc(W, cols):
    nk = W.shape[0] // 128
    sub = W[:, cols]
    return np.ascontiguousarray(sub.reshape(nk, 128, sub.shape[1]).transpose(1, 0, 2).reshape(128, -1))


def _swap_heads(idx):
    idx = np.asarray(idx).reshape(-1, 2, 32)
    return idx[:, ::-1, :].reshape(-1)


_QA, _KA, _VA, _RA = np.arange(0, 256), np.arange(256, 512), np.arange(512, 1024), np.arange(1024, 1536)
_QR, _KR, _VR, _GR = np.arange(1536, 1792), np.arange(1792, 2048), np.arange(2048, 2560), np.arange(2560, 3072)
_AL = np.arange(3072, 3088)
_GA, _GRT = np.arange(3088, 4112), np.arange(4112, 5136)


def _weight_tiles(inp):
    tiles = []

    def ffn(w_in, w_out):
        for g in range(6):
            wd = 512 if g < 5 else 256
            tiles.append(_tile_kc(w_in, np.arange(g * 512, g * 512 + wd)))
            tiles.append(_tile_kc(w_in, np.arange(DFF + g * 512, DFF + g * 512 + wd)))
        for dm in range(8):
            tiles.append(_tile_kc(w_out, np.arange(dm * 128, dm * 128 + 128)))

    ffn(inp["w_ffn1_in"][0], inp["w_ffn1_out"][0])
    wi = inp["w_in"][0]
    tiles.append(_tile_kc(wi, _AL))
    tiles.append(_tile_kc(wi, np.concatenate([_QA, _KA])))
    tiles.append(_tile_kc(wi, np.concatenate([_QR, _KR])))
    tiles.append(_tile_kc(wi, np.concatenate([_swap_heads(_QR), _swap_heads(_KR)])))
    tiles.append(_tile_kc(wi, np.concatenate([_KA, _KR])))
    tiles.append(_tile_kc(wi, _VA))
    tiles.append(_tile_kc(wi, _VR))
    tiles.append(_tile_kc(wi, _RA))
    tiles.append(_tile_kc(wi, _GR))
    wo = inp["w_out"][0]
    for dmg in range(2):
        c = np.arange(dmg * 512, dmg * 512 + 512)
        tiles.append(_tile_kc(wi, _GA[c]))
        tiles.append(_tile_kc(wi, _GRT[c]))
        tiles.append(_tile_kc(wo, c))
    ffn(inp["w_ffn2_in"][0], inp["w_ffn2_out"][0])
    for dmg in range(2):
        c = np.arange(dmg * 512, dmg * 512 + 512)
        tiles.append(_tile_kc(inp["w_ple_gate"][0], c))
        tiles.append(_tile_kc(inp["w_ple_proj"][0], c))
    sched = []
    off = 0
    for t in tiles:
        assert t.shape[1] <= SLOT
        sched.append((off, t.shape[1]))
        off += t.shape[1]
    return np.ascontiguousarray(np.concatenate(tiles, axis=1).astype(np.float32)), sched


def _weight_schedule():
    Ls = []

    def ffn():
        for g in range(6):
            wd = 512 if g < 5 else 256
            Ls.extend([8 * wd, 8 * wd])
        Ls.extend([NFC * 128] * 8)

    ffn()
    Ls.append(8 * 16)
    Ls.extend([8 * 512] * 8)
    Ls.extend([8 * 512] * 6)
    ffn()
    Ls.extend([8 * 512, 2 * 512] * 2)
    sched, off = [], 0
    for L in Ls:
        sched.append((off, L))
        off += L
    return sched, off


def _const_tables():
    f32 = np.float32
    half = 32
    freq = (f32(10000.0) ** (-(np.arange(half, dtype=f32) / f32(half)))).astype(f32)
    pos = np.concatenate([np.arange(TP), 16384 + (np.arange(NSAMP) % 4)]).astype(f32)
    il = np.concatenate([np.arange(TP) % 128, np.arange(NSAMP) % 4]).astype(np.float64)
    nchunk = np.concatenate([np.full(TP, 128.0), np.full(NSAMP, 4.0)])
    ang = (pos[:, None] * freq[None, :]).astype(f32).astype(np.float64)
    cos, sin = np.cos(ang), np.sin(ang)
    lg = np.log1p(-np.exp2(-5.0 - np.arange(4, dtype=np.float64)))
    d = np.arange(64)
    sgn = np.where(d < 32, -1.0, 1.0)
    CQ = np.zeros((2, 128, TALL)); SQ = np.zeros_like(CQ); CK = np.zeros_like(CQ); SK = np.zeros_like(CQ)
    for hp in range(2):
        for h2 in range(2):
            h = hp * 2 + h2
            up = np.exp((il + 1.0) * lg[h])[None, :]
            dn = np.exp(-(il + 1.0) * lg[h])[None, :] * 0.125
            c = cos[:, d % 32].T
            s = sin[:, d % 32].T * sgn[:, None]
            sl = slice(h2 * 64, h2 * 64 + 64)
            CQ[hp, sl] = c * up; SQ[hp, sl] = s * up; CK[hp, sl] = c * dn; SK[hp, sl] = s * dn
    rtab = np.zeros((5, 128, 8, 512), f32)
    for gtb in range(5):
        t0, n = (gtb * 512, 512) if gtb < 4 else (TP, NSAMP)
        for hp in range(2):
            for j, A in enumerate((CQ, SQ, CK, SK)):
                rtab[gtb, :, hp * 4 + j, :n] = A[hp, :, t0:t0 + n]
    ttab = np.zeros((17, 128, 2, 256), f32)
    for b in range(17):
        t0, n = (b * 128, 128) if b < 16 else (TP, NSAMP)
        for h in range(4):
            k = np.exp((nchunk[t0:t0 + n] - 1.0 - il[t0:t0 + n]) * lg[h])[:, None] * 0.125
            ttab[b, :n, 0, h * 64:(h + 1) * 64] = cos[t0:t0 + n][:, d % 32] * k
            ttab[b, :n, 1, h * 64:(h + 1) * 64] = sin[t0:t0 + n][:, d % 32] * sgn[None, :] * k
    decr = np.zeros((128, 4), f32)
    for hp in range(2):
        for h2 in range(2):
            h = hp * 2 + h2
            decr[h2 * 64:(h2 + 1) * 64, hp] = np.exp(128.0 * lg[h])
            decr[h2 * 64:(h2 + 1) * 64, 2 + hp] = np.exp(4.0 * lg[h])
    j = np.arange(128)[:, None]; i = np.arange(128)[None, :]
    cmat = np.zeros((128, 4, 128), f32)
    cmat[:, 0] = np.where(j <= i, -1.0 / 16, 0.0)
    cmat[:, 1] = np.where(j > i, -1.0 / 16, 0.0)
    same = (j // 4 == i // 4) & (j < 64) & (i < 64)
    cmat[:, 2] = np.where(same & (j <= i), -1.0 / 16, 0.0)
    cmat[:, 3] = np.where(same & (j > i), -1.0 / 16, 0.0)
    masks = np.zeros((128, 2, 128), f32)
    masks[:, 0] = (j <= i)
    masks[:, 1] = same & (j <= i)
    smask = (np.arange(64)[:, None] // 4 == np.arange(16)[None, :]).astype(f32)
    maskq = np.broadcast_to((np.arange(16)[:, None] == np.arange(64)[None, :] // 4).astype(f32).reshape(1, 1024),
                            (128, 1024)).copy()
    return dict(rtab=rtab.reshape(5, 128, 8 * 512), ttab=ttab.reshape(17, 128, 512), decr=decr,
                cmat=cmat.reshape(128, 512), masks=masks.reshape(128, 256), smask=smask, maskq=maskq)


def build_program(stage=5, nhalves=2, cut=99):
    nc = bass.Bass("TRN2", target_bir_lowering=False)
    S = Sched()
    wsched1, WTOT = _weight_schedule()

    def din(name, shape, dt=F32):
        return nc.dram_tensor(name, list(shape), dt, kind="ExternalInput").ap()

    def dout(name, shape, dt=F32):
        return nc.dram_tensor(name, list(shape), dt, kind="ExternalOutput").ap()

    xT = din("xT", [D, TALL]); pT = din("pT", [256, TALL])
    wflat = din("wflat", [128, WTOT])
    cpack = din("cpack", [128, 52])
    cmat_d = din("cmat", [128, 512]); masks_d = din("masks", [128, 256]); smask_d = din("smask", [64, 16])
    maskq_d = din("maskq", [128, 1024])
    wup_d = din("wup", [16, 256]); balpha_d = din("balpha", [1, 256])
    rtab_d = din("rtab", [5, 128, 8 * 512]); ttab_d = din("ttab", [17, 128, 512])
    sg_in = din("sg_in", [16, 4, 64, 128]); sr_in = din("sr_in", [16, 4, 64, 128])
    yT = dout("yT", [D, TALL])
    sgp = dout("sgp", [256, 128]); srp = dout("srp", [256, 128])
    sgs = dout("sgs", [16, 4, 64, 128]); srs = dout("srs", [16, 4, 64, 128])
    st_in = (sg_in, sr_in); st_out = (sgs, srs); sp_out = (sgp, srp)

    SB_BASE, SB_END = 16512, 229376
    ptr = [SB_BASE]

    def esz(dt):
        return 2 if dt == BF16 else 4

    def palloc(name, shape, dt):
        nbytes = int(np.prod(shape[1:])) * esz(dt)
        nbytes = (nbytes + 31) // 32 * 32
        t = nc.alloc_sbuf_tensor_at(name, list(shape), dt, offset=ptr[0])
        ptr[0] += nbytes
        return t

    TM = 1088
    h = palloc("h", [128, 8, TM], F32)
    u = palloc("u", [128, 8, TM], BF16)
    ring = [palloc("ring%d" % i, [128, SLOT], BF16) for i in range(RING)]
    gains = palloc("gains", [128, 52], F32)
    cmat = palloc("cmat_s", [128, 4, 128], BF16)
    masks = palloc("masks_s", [128, 2, 128], F32)
    smask = palloc("smask_s", [64, 16], F32)
    maskq = palloc("maskq_s", [128, 16, 64], BF16)
    wup = palloc("wup_s", [16, 256], BF16)
    balpha = palloc("balpha_s", [1, 256], BF16)
    ones_bf = palloc("ones_bf", [128, 128], BF16)
    ones_row = palloc("ones_row", [1, 128], BF16)
    Sst = palloc("Sst", [128, 4, 128], F32)
    Sbf = palloc("Sbf", [128, 3, 4, 128], BF16)
    decA = palloc("decA", [128, 2, 9], F32)
    decSA = palloc("decSA", [128, 2, 16], F32)
    sq = palloc("sq", [128, 2, 512], BF16)
    rstd = palloc("rstd", [128, 2, 512], F32)
    UB = ptr[0]
    USZ = SB_END - UB

    def ualloc(name, shape, dt, off):
        nbytes = int(np.prod(shape[1:])) * esz(dt)
        assert off % 32 == 0 and off + nbytes <= USZ, (name, off, nbytes, USZ)
        return nc.alloc_sbuf_tensor_at(name, list(shape), dt, offset=UB + off), off + (nbytes + 31) // 32 * 32

    gbuf, o = ualloc("gbuf", [128, NFC, TM], BF16, 0)
    stmp, o = ualloc("stmp", [128, 2, 512], F32, o)
    pb, o_pb = ualloc("pb", [128, 2, TM], BF16, o)
    stmp2, _ = ualloc("stmp2", [128, 2, 512], F32, o_pb)
    yout, _ = ualloc("yout", [128, 2, 8, 512], F32, o_pb + 4096)
    sq8f, _ = ualloc("sq8f", [128, 8, 512], BF16, o_pb + 4096 + 32768)
    sq8m, _ = ualloc("sq8m", [128, 8, 512], BF16, 17408)
    qk, o = ualloc("qk", [128, 8, TM], BF16, 0)
    vtm, o = ualloc("vtm", [128, 9, 1024], BF16, o)
    kh, o = ualloc("kh", [128, 9, 512], BF16, o)
    o_tr = o
    ed, o = ualloc("ed", [128, 9, 256], F32, o)
    alowT, o = ualloc("alowT", [16, TM], BF16, o)
    spb, o = ualloc("spb", [128, 2, 256], F32, o)
    sphi, o = ualloc("sphi", [128, 2, 256], BF16, o)
    splo, o = ualloc("splo", [128, 2, 256], BF16, o)
    etmp, o = ualloc("etmp", [128, 2, 256], F32, o)
    eb, o = ualloc("eb", [128, 2, 512], F32, o)
    enb, o = ualloc("enb", [128, 2, 512], F32, o)
    rtab, o = ualloc("rtab_s", [128, 8, 512], F32, o)
    ttab, o = ualloc("ttab_s", [128, 3, 512], F32, o)
    tA, o = ualloc("tA", [128, 512], F32, o)
    tB, o = ualloc("tB", [128, 512], F32, o)
    tC, o = ualloc("tC", [128, 512], F32, o)
    tD, o = ualloc("tD", [128, 512], F32, o)
    o = o_tr
    PT, o = ualloc("PT", [128, 8, 128], BF16, o)
    osb, o = ualloc("osb", [128, 2, 512], F32, o)
    sqb, o = ualloc("sqb", [128, 2, 512], BF16, o)
    rs2, o = ualloc("rs2", [128, 2, 512], F32, o)
    ee, o = ualloc("ee", [128, 2, 512], F32, o)
    tt1, o = ualloc("tt1", [128, 2, 512], F32, o)
    tt2, o = ualloc("tt2", [128, 2, 512], F32, o)
    S0bf, o = ualloc("S0bf", [128, 2, 16 * 128], BF16, o)
    S0f_buf, o = ualloc("S0f", [128, 16 * 128], F32, o)
    tmpS_buf, o = ualloc("tmpS", [128, 16 * 128], F32, o)
    Vblk, o = ualloc("Vblk", [64, 16 * 128], BF16, o)
    qm, o = ualloc("qm", [128, 16, 64], BF16, o)
    print("SBUF union size", USZ, "P2 end", o)

    PSA = nc.alloc_psum_tensor("PSA", [128, 2048], F32)
    PSB = nc.alloc_psum_tensor("PSB", [128, 2048], F32)

    def bank(b):
        t = PSA if b < 4 else PSB
        return t[:, (b % 4) * 512:(b % 4) * 512 + 512]

    def mm(out, lhsT, rhs, start, stop, reads, writes):
        S.op("pe", lambda e: e.matmul(out, lhsT=lhsT, rhs=rhs, start=start, stop=stop), reads, writes)

    def act(out, in_, func, reads, writes, scale=1.0, bias=0.0):
        S.op("act", lambda e: e.activation(out=out, in_=in_, func=func, bias=bias, scale=scale), reads, writes)

    def tt(out, in0, in1, op, reads, writes, eng="dve"):
        S.op(eng, lambda e: e.tensor_tensor(out=out, in0=in0, in1=in1, op=op), reads, writes)

    def stt(out, in0, scalar, in1, op0, op1, reads, writes):
        S.op("dve", lambda e: e.scalar_tensor_tensor(out=out, in0=in0, scalar=scalar, in1=in1, op0=op0, op1=op1),
             reads, writes)

    def tsmul(out, in0, scalar, reads, writes):
        S.op("dve", lambda e: e.tensor_scalar(out=out, in0=in0, scalar1=scalar, scalar2=None, op0=ALU.mult),
             reads, writes)

    def dma(eng, out, in_, reads, writes, dsem):
        S.op(eng, lambda e: e.dma_start(out=out, in_=in_), reads, writes, dsem=dsem)

    def memset(ap, val, writes):
        S.op("dve", lambda e: e.memset(ap, val), (), writes)

    full_sched = wsched1 + wsched1
    st = {"cur": 0}

    def issue(i):
        if i >= len(full_sched):
            return
        off, L = full_sched[i]
        slot = i % RING
        dma("pool", ring[slot][:, 0:L], wflat[:, off:off + L], (), [("ring", slot)], ("w", slot))

    def wnext(nk, ncols):
        i = st["cur"]
        st["cur"] += 1
        off, L = full_sched[i]
        assert L == nk * ncols, (i, L, nk, ncols)
        slot = i % RING
        return i, ring[slot][:, 0:L].rearrange("p (k c) -> p k c", k=nk), ("ring", slot)

    def wrelease(i):
        issue(i + RING)

    dma("sp", gains[:], cpack[:], (), ["gains"], "c0")

    def late_consts():
        dma("pool", cmat[:].rearrange("p a b -> p (a b)"), cmat_d[:], (), ["cmat"], "c1")
        dma("sp", masks[:].rearrange("p a b -> p (a b)"), masks_d[:], (), ["masks"], "c2")
        dma("sp", smask[:], smask_d[:], (), ["smask"], "c3")
        dma("pool", maskq[:].rearrange("p a b -> p (a b)"), maskq_d[:], (), ["maskq"], "c6")
        dma("pool", wup[:], wup_d[:], (), ["wup"], "c4")
        dma("pool", balpha[:], balpha_d[:], (), ["balpha"], "c5")
    memset(ones_bf[:], 1.0, ["ones_bf"])
    memset(ones_row[:], 1.0, ["ones_row"])
    memset(Sst[:].rearrange("p a b -> p (a b)"), 0.0, ["S0", "S1", "S2", "S3"])
    memset(Sbf[:].rearrange("p t a b -> p (t a b)"), 0.0, [("Sb", t_, b_) for t_ in range(3) for b_ in range(2)])
    for i in range(RING):
        issue(i)

    G_FFN1, G_MIX, G_FFN2, G_PLE, G_FIN, G_GNA, G_GNR, G_DECR = 0, 8, 16, 24, 32, 40, 44, 48

    def rmsnorm(tbs, gcol, dst_fn, dst_keys):
        for tbi, (off, n) in enumerate(tbs):
            pn = bank(6)
            for kc in range(8):
                b = kc % 2
                act(sq[:, b, 0:n], h[:, kc, off:off + n], AF.Square, [("h", kc, tbi)], [("sq", b)])
                mm(pn[:, 0:n], ones_bf[:], sq[:, b, 0:n], kc == 0, kc == 7, ["ones_bf", ("sq", b)], [("ps", 6)])
            rb = tbi % 2
            act(rstd[:, rb, 0:n], pn[:, 0:n], AF.Ln, [("ps", 6)], [("rstd", rb)], scale=1.0 / D, bias=EPS)
            act(rstd[:, rb, 0:n], rstd[:, rb, 0:n], AF.Exp, [("rstd", rb)], [("rstd", rb)], scale=-0.5)
            for kc in range(8):
                stt(dst_fn(kc, tbi, off, n), h[:, kc, off:off + n], gains[:, gcol + kc:gcol + kc + 1],
                    rstd[:, rb, 0:n], ALU.mult, ALU.mult,
                    [("h", kc, tbi), "gains", ("rstd", rb)], dst_keys(kc, tbi))

    def norm_to_u(tbs, gcol):
        rmsnorm(tbs, gcol, lambda kc, tbi, off, n: u[:, kc, off:off + n], lambda kc, tbi: [("u", kc, tbi)])

    def norm_u_tb(tbi, off, n, gcol):
        pn = bank(6)
        for kc in range(8):
            b = kc % 2
            act(sq[:, b, 0:n], h[:, kc, off:off + n], AF.Square, [("h", kc, tbi)], [("sq", b)])
            mm(pn[:, 0:n], ones_bf[:], sq[:, b, 0:n], kc == 0, kc == 7, ["ones_bf", ("sq", b)], [("ps", 6)])
        rb = tbi % 2
        act(rstd[:, rb, 0:n], pn[:, 0:n], AF.Ln, [("ps", 6)], [("rstd", rb)], scale=1.0 / D, bias=EPS)
        act(rstd[:, rb, 0:n], rstd[:, rb, 0:n], AF.Exp, [("rstd", rb)], [("rstd", rb)], scale=-0.5)
        for kc in range(8):
            stt(u[:, kc, off:off + n], h[:, kc, off:off + n], gains[:, gcol + kc:gcol + kc + 1],
                rstd[:, rb, 0:n], ALU.mult, ALU.mult,
                [("h", kc, tbi), "gains", ("rstd", rb)], [("u", kc, tbi)])

    def norm_part1(sq8, tbi, off, n):
        for kc in range(8):
            act(sq8[:, kc, 0:n], h[:, kc, off:off + n], AF.Square, [("h", kc, tbi)], [("sq8", kc)])

    def norm_part2(sq8, tbi, off, n, gcol):
        pn = bank(6)
        for kc in range(8):
            mm(pn[:, 0:n], ones_bf[:], sq8[:, kc, 0:n], kc == 0, kc == 7, ["ones_bf", ("sq8", kc)], [("ps", 6)])
        rb = tbi % 2
        act(rstd[:, rb, 0:n], pn[:, 0:n], AF.Ln, [("ps", 6)], [("rstd", rb)], scale=1.0 / D, bias=EPS)
        act(rstd[:, rb, 0:n], rstd[:, rb, 0:n], AF.Exp, [("rstd", rb)], [("rstd", rb)], scale=-0.5)
        for kc in range(8):
            stt(u[:, kc, off:off + n], h[:, kc, off:off + n], gains[:, gcol + kc:gcol + kc + 1],
                rstd[:, rb, 0:n], ALU.mult, ALU.mult,
                [("h", kc, tbi), "gains", ("rstd", rb)], [("u", kc, tbi)])

    def ffn(tbs, lazy, next_gcol=None):
        cnt = 0
        for g in range(6):
            nf = 4 if g < 5 else 2
            ia, wa, ka = wnext(8, nf * 128)
            ib, wb, kb = wnext(8, nf * 128)
            for tbi, (off, n) in enumerate(tbs):
                if g == 0 and tbi > 0:
                    lazy(tbi)
                for fl in range(nf):
                    fc = g * 4 + fl
                    pa, pbk = bank(cnt % 2), bank(2 + cnt % 2)
                    ka_, kb_ = ("ps", cnt % 2), ("ps", 2 + cnt % 2)
                    for kc in range(8):
                        mm(pa[:, 0:n], wa[:, kc, fl * 128:(fl + 1) * 128], u[:, kc, off:off + n], kc == 0, kc == 7,
                           [ka, ("u", kc, tbi)], [ka_])
                    for kc in range(8):
                        mm(pbk[:, 0:n], wb[:, kc, fl * 128:(fl + 1) * 128], u[:, kc, off:off + n], kc == 0, kc == 7,
                           [kb, ("u", kc, tbi)], [kb_])
                    sb_ = cnt % 2
                    act(stmp[:, sb_, 0:n], pa[:, 0:n], AF.Silu, [ka_], [("stmp", sb_)])
                    tt(gbuf[:, fc, off:off + n], stmp[:, sb_, 0:n], pbk[:, 0:n], ALU.mult,
                       [("stmp", sb_), kb_], [("g", fc, tbi)])
                    cnt += 1
            wrelease(ia)
            wrelease(ib)
        cnt = 0
        for dm in range(8):
            io, wo, ko = wnext(NFC, 128)
            for tbi, (off, n) in enumerate(tbs):
                po = bank(4 + cnt % 2)
                kp = ("ps", 4 + cnt % 2)
                for fc in range(NFC):
                    mm(po[:, 0:n], wo[:, fc, :], gbuf[:, fc, off:off + n], fc == 0, fc == NFC - 1,
                       [ko, ("g", fc, tbi)], [kp])
                stt(h[:, dm, off:off + n], po[:, 0:n], 0.5, h[:, dm, off:off + n], ALU.mult, ALU.add,
                    [kp, ("h", dm, tbi)], [("h", dm, tbi)])
                cnt += 1
                if next_gcol is not None and dm == 7:
                    if tbi == 0:
                        norm_part1(sq8f, 0, tbs[0][0], tbs[0][1])
                    elif tbi == 1:
                        norm_part2(sq8f, 0, tbs[0][0], tbs[0][1], next_gcol)
            wrelease(io)

    def mixing(hi, tok0, tbs, blocks, lazy, next_gcol=None):
        mix_start = st["cur"]

        def bail():
            skip_tiles(mix_start + 15 - st["cur"])

        T = sum(n for _, n in tbs)
        def rtab_load(tbi_, hp_):
            off_, n_ = tbs[tbi_]
            gtb = 4 if n_ == 64 else (tok0 + off_) // 512
            src = rtab_d[gtb].rearrange("p (a b) -> p a b", a=8)[:, hp_ * 4:(hp_ + 1) * 4, 0:n_]
            dma("sp", rtab[:, hp_ * 4:(hp_ + 1) * 4, 0:n_], src, (), [("rtab", hp_)], ("rt", hp_))

        def ttab_load(bi_):
            dma("sp", ttab[:, bi_ % 3, :], ttab_d[blocks[bi_][3]], (), [("ttab", bi_ % 3)], ("tt", bi_ % 3))

        rtab_load(0, 0)
        rtab_load(0, 1)
        for bi_ in range(min(3, len(blocks))):
            ttab_load(bi_)
        ial, wal, kal = wnext(8, 16)
        i0, w0, k0 = wnext(8, 512)
        for tbi, (off, n) in enumerate(tbs):
            if tbi > 0:
                lazy(tbi)
            pA = bank(7)
            for kc in range(8):
                mm(pA[0:16, 0:n], wal[:, kc, 0:16], u[:, kc, off:off + n], kc == 0, kc == 7,
                   [kal, ("u", kc, tbi)], [("ps", 7)])
            S.op("act", (lambda o_, i_: (lambda e: e.copy(out=o_, in_=i_)))(alowT[0:16, off:off + n], pA[0:16, 0:n]),
                 [("ps", 7)], [("alowT", tbi)])
            pend_b = []
            for bi, (boff, bn, smp, gblk) in enumerate(blocks):
                if not (off <= boff < off + n) or cut <= 0.2:
                    continue
                lo = boff - off
                dp = bi % 2
                px = bank(6 + dp)
                kpx = ("ps", 6 + dp)
                mm(px[0:bn, 0:256], alowT[0:16, boff:boff + bn], wup[:, :], True, False,
                   [("alowT", tbi), "wup"], [kpx])
                mm(px[0:bn, 0:256], ones_row[0:1, 0:bn], balpha[0:1, :], False, True,
                   ["ones_row", "balpha"], [kpx])
                act(etmp[0:bn, dp, :], px[0:bn, 0:256], AF.Exp, [kpx], [("etmp", dp)], scale=-1.0)
                act(spb[0:bn, dp, :], etmp[0:bn, dp, :], AF.Ln, [("etmp", dp)], [("spb", dp)], bias=1.0)
                if cut <= 0.4:
                    continue
                S.op("dve", (lambda o_, i_: (lambda e: e.tensor_copy(out=o_, in_=i_)))(sphi[0:bn, dp, :], spb[0:bn, dp, :]),
                     [("spb", dp)], [("sphi", dp)])
                tt(splo[0:bn, dp, :], spb[0:bn, dp, :], sphi[0:bn, dp, :], ALU.subtract,
                   [("spb", dp), ("sphi", dp)], [("splo", dp)])
                def part_b(bi=bi, boff=boff, bn=bn, smp=smp, lo=lo, dp=dp):
                    ci = 2 if smp else 0
                    b5 = 4 + bi % 2
                    p5 = bank(b5)
                    for hp in range(2):
                        for xi, (spx, kx) in enumerate(((sphi, ("sphi", dp)), (splo, ("splo", dp)))):
                            mm(p5[:, hp * 128:hp * 128 + bn], spx[0:bn, dp, hp * 128:(hp + 1) * 128],
                               cmat[0:bn, ci, 0:bn], xi == 0, xi == 1, [kx, "cmat"], [("ps", b5)])
                    for xi, (spx, kx) in enumerate(((sphi, ("sphi", dp)), (splo, ("splo", dp)))):
                        mm(p5[0:bn, 256:512], cmat[0:bn, ci + 1, 0:bn], spx[0:bn, dp, 0:256], xi == 0, xi == 1,
                           [kx, "cmat"], [("ps", b5)])
                    for hp in range(2):
                        src = p5[:, hp * 128:hp * 128 + bn]
                        act(eb[:, hp, lo:lo + bn], src, AF.Exp, [("ps", b5)], [("eb", hp)])
                        act(enb[:, hp, lo:lo + bn], src, AF.Exp, [("ps", b5)], [("enb", hp)], scale=-1.0)
                        if smp:
                            act(decSA[:, hp, :], p5[:, hp * 128 + 3:hp * 128 + 64:4], AF.Exp, [("ps", b5)], ["decSA"])
                        else:
                            act(decA[:, hp, bi:bi + 1], p5[:, hp * 128 + bn - 1:hp * 128 + bn], AF.Exp,
                                [("ps", b5)], [("decA", hp, bi)])
                    act(ed[0:bn, bi, :], p5[0:bn, 256:512], AF.Exp, [("ps", b5)], [("ed", bi)])

                for f_ in pend_b:
                    f_()
                pend_b = [part_b]
            for f_ in pend_b:
                f_()
            pend_b = []
            for ch in range(4):
                if cut <= 0.8:
                    continue
                pq = bank(ch)
                for kc in range(8):
                    mm(pq[:, 0:n], w0[:, kc, ch * 128:(ch + 1) * 128], u[:, kc, off:off + n], kc == 0, kc == 7,
                       [k0, ("u", kc, tbi)], [("ps", ch)])
                if cut <= 0.85:
                    continue
                if ch < 2:
                    stt(qk[:, ch, off:off + n], pq[:, 0:n], 0.125, eb[:, ch, 0:n], ALU.mult, ALU.mult,
                        [("ps", ch), ("eb", ch)], [("qk", 0, tbi, ch)])
                elif cut <= 0.9:
                    continue
                elif cut <= 0.95:
                    tt(qk[:, ch, off:off + n], pq[:, 0:n], eb[:, ch - 2, 0:n], ALU.mult,
                       [("ps", ch), ("eb", ch - 2)], [("qk", 0, tbi, ch)])
                elif cut <= 0.97:
                    stt(qk[:, ch, off:off + n], pq[:, 0:n], 1.0, enb[:, ch - 2, 0:n], ALU.mult, ALU.mult,
                        [("ps", ch), ("enb", ch - 2)], [("qk", 0, tbi, ch)])
                else:
                    tt(qk[:, ch, off:off + n], pq[:, 0:n], enb[:, ch - 2, 0:n], ALU.mult,
                       [("ps", ch), ("enb", ch - 2)], [("qk", 0, tbi, ch)])
        wrelease(ial)
        wrelease(i0)
        if cut <= 1:
            return bail()
        i1, w1, k1 = wnext(8, 512)
        i2, w2, k2 = wnext(8, 512)
        cnt = 0
        for tbi, (off, n) in enumerate(tbs):
            for hp in range(2):
                for isk in range(2):
                    ch = hp + 2 * isk
                    b0 = (cnt % 2) * 2
                    pa, pbk = bank(b0), bank(b0 + 1)
                    for kc in range(8):
                        mm(pa[:, 0:n], w1[:, kc, ch * 128:(ch + 1) * 128], u[:, kc, off:off + n], kc == 0, kc == 7,
                           [k1, ("u", kc, tbi)], [("ps", b0)])
                    for kc in range(8):
                        mm(pbk[:, 0:n], w2[:, kc, ch * 128:(ch + 1) * 128], u[:, kc, off:off + n], kc == 0, kc == 7,
                           [k2, ("u", kc, tbi)], [("ps", b0 + 1)])
                    tt(tA[:, 0:n], pa[:, 0:n], rtab[:, hp * 4 + 2 * isk, 0:n], ALU.mult, [("ps", b0), ("rtab", hp)], ["tA"])
                    tt(tB[:, 0:n], pbk[:, 0:n], rtab[:, hp * 4 + 2 * isk + 1, 0:n], ALU.mult,
                       [("ps", b0 + 1), ("rtab", hp)], ["tB"])
                    qi = 4 + 2 * isk + hp
                    tt(qk[:, qi, off:off + n], tA[:, 0:n], tB[:, 0:n], ALU.add, ["tA", "tB"], [("qk", 1, tbi, qi)])
                    cnt += 1
                if tbi + 1 < len(tbs):
                    rtab_load(tbi + 1, hp)
                else:
                    for bi_ in range(3 + 4 * hp, min(3 + 4 * hp + 4, len(blocks))):
                        dma("sp", rtab[:, bi_ - 3, :], ttab_d[blocks[bi_][3]], (), [("rtab", hp), ("ttx", bi_)],
                            ("ttx", bi_))
        wrelease(i1)
        wrelease(i2)
        if cut <= 2:
            return bail()
        i3, w3, k3 = wnext(8, 512)
        for bi, (boff, bn, smp, gblk) in enumerate(blocks):
            tbi = min(boff // 512, len(tbs) - 1)
            tb_ = bi % 3
            if bi < 3:
                tsrc = ttab[0:bn, tb_, :]
                ktt = [("ttab", tb_)]
            else:
                tsrc = rtab[0:bn, bi - 3, :]
                ktt = [("ttx", bi), ("rtab", 0 if bi - 3 < 4 else 1)]
            pk = bank(bi % 2)
            kp = ("ps", bi % 2)
            for kc in range(8):
                mm(pk[0:bn, :], u[:, kc, boff:boff + bn], w3[:, kc, :], kc == 0, kc == 7, [k3, ("u", kc, tbi)], [kp])
            tt(kh[0:bn, bi, 0:256], pk[0:bn, 0:256], ed[0:bn, bi, :], ALU.mult, [kp, ("ed", bi)], [("kh", bi)])
            pr = pk[0:bn, 256:512].rearrange("p (h s e) -> p h s e", h=4, s=2)
            Ct = tsrc[:, 0:256].rearrange("p (h s e) -> p h s e", h=4, s=2)
            St = tsrc[:, 256:512].rearrange("p (h s e) -> p h s e", h=4, s=2)
            kr = kh[0:bn, bi, 256:512].rearrange("p (h s e) -> p h s e", h=4, s=2)
            tA4 = tA[0:bn, 0:256].rearrange("p (h s e) -> p h s e", h=4, s=2)
            tB4 = tB[0:bn, 0:256].rearrange("p (h s e) -> p h s e", h=4, s=2)
            for s_ in range(2):
                tt(tA4[:, :, s_, :], pr[:, :, s_, :], Ct[:, :, s_, :], ALU.mult, [kp] + ktt, ["tA"])
                tt(tB4[:, :, s_, :], pr[:, :, 1 - s_, :], St[:, :, s_, :], ALU.mult, [kp] + ktt, ["tB"])
                tt(kr[:, :, s_, :], tA4[:, :, s_, :], tB4[:, :, s_, :], ALU.add, ["tA", "tB"], [("kh", bi)])
        wrelease(i3)
        if cut <= 3:
            return bail()
        for vi in range(2):
            iv, wv, kv = wnext(8, 512)
            for bi, (boff, bn, smp, gblk) in enumerate(blocks):
                tbi = min(boff // 512, len(tbs) - 1)
                pv = bank(2 + bi % 2)
                kp = ("ps", 2 + bi % 2)
                for kc in range(8):
                    mm(pv[0:bn, :], u[:, kc, boff:boff + bn], wv[:, kc, :], kc == 0, kc == 7,
                       [kv, ("u", kc, tbi)], [kp])
                S.op("act", (lambda o_, i_: (lambda e: e.copy(out=o_, in_=i_)))(
                    vtm[0:bn, bi, vi * 512:(vi + 1) * 512], pv[0:bn, :]), [kp], [("v", bi, vi)])
            wrelease(iv)
        if cut <= 4:
            return bail()
        S.barrier(exclude=("pe",))
        igt = []
        wg = []
        kg = []
        for br in range(2):
            i_, w_, k_ = wnext(8, 512)
            igt.append(i_); wg.append(w_); kg.append(k_)
        ecnt = [0]

        def epilogue(br, hl, po, off, n, tbi, pokey, ebanks=((2, 3), (0, 1))):
            ob = ecnt[0] % 2
            ecnt[0] += 1
            tbk, rbk = ebanks[ob]
            act(sqb[:, ob, 0:n], po, AF.Square, [pokey], [("sqb", ob)])
            pR = bank(rbk)
            for kc in range(8):
                mm(pR[:, 0:n], wg[br][:, kc, hl * 128:(hl + 1) * 128], u[:, kc, off:off + n], kc == 0, kc == 7,
                   [kg[br], ("u", kc, tbi)], [("ps", rbk)])
            pT_ = bank(tbk)
            mm(pT_[:, 0:n], ones_bf[:], sqb[:, ob, 0:n], True, True, ["ones_bf", ("sqb", ob)], [("ps", tbk)])
            act(rs2[:, ob, 0:n], pT_[:, 0:n], AF.Ln, [("ps", tbk)], [("rs2", ob)], scale=1.0 / 128, bias=EPS)
            act(ee[:, ob, 0:n], pR[:, 0:n], AF.Exp, [("ps", rbk)], [("ee", ob)], scale=-1.0)
            act(ee[:, ob, 0:n], ee[:, ob, 0:n], AF.Ln, [("ee", ob)], [("ee", ob)], bias=1.0)
            stt(tt2[:, ob, 0:n], rs2[:, ob, 0:n], -0.5, ee[:, ob, 0:n], ALU.mult, ALU.subtract,
                [("rs2", ob), ("ee", ob)], [("tt2", ob)])
            act(tt2[:, ob, 0:n], tt2[:, ob, 0:n], AF.Exp, [("tt2", ob)], [("tt2", ob)])
            gc = (G_GNA if br == 0 else G_GNR) + hl
            stt(tt1[:, ob, 0:n], po, gains[:, gc:gc + 1], tt2[:, ob, 0:n], ALU.mult, ALU.mult,
                [pokey, "gains", ("tt2", ob)], [("tt1", ob)])
            qi = br * 4 + hl
            wkeys = [("qk", br, tbi, br * 4 + j) for j in range(4)]
            tt(qk[:, qi, off:off + n], tt1[:, ob, 0:n], pR[:, 0:n], ALU.mult, [("tt1", ob), ("ps", rbk)], wkeys)

        scnt = [0]
        SX = (S0f_buf, tmpS_buf)

        def s0_load(sidx):
            br_, pair_ = sidx // 2, sidx % 2
            src = st_in[br_][:, pair_ * 2:pair_ * 2 + 2].rearrange("s h d v -> (h d) s v")
            sb_ = sidx % 2
            dma("pool", S0bf[:, sb_, :].rearrange("p (s v) -> p s v", s=16), src, (), [("S0bf", sb_)], ("s0b", sb_))
            x_ = SX[sidx % 2]
            kx = ("SX", sidx % 2)
            dma("sp", x_[:].rearrange("p (s v) -> p s v", s=16), src, (), [kx, kx + (0,), kx + (1,)], ("s0f", sidx % 2))

        if hi == 1:
            s0_load(0)
        for tbi, (off, n) in enumerate(tbs):
            smp_tb = (n == 64)
            rkeys_all = lambda br: [("qk", br, tbi, br * 4 + j) for j in range(4)]
            for br in range(2):
                qb, kb_i = br * 4, br * 4 + 2
                if not smp_tb:
                    tblocks = [(bi, b) for bi, b in enumerate(blocks) if off <= b[0] < off + n]
                    pend = []

                    def o_ops(c4, bi, boff, bn, gblk):
                        par = c4 % 2
                        for hl in range(4):
                            pair, h2 = hl // 2, hl % 2
                            sidx = br * 2 + pair
                            pr_ = slice(h2 * 64, h2 * 64 + 64)
                            po = bank(4 + hl)[:, c4 * 128:(c4 + 1) * 128]
                            vc = br * 512 + hl * 128
                            mm(po, vtm[:, bi, vc:vc + 128], PT[:, par * 4 + h2 * 2 + pair, :], True, False,
                               [("v", bi, br), ("PT", par)], [("ps", 4 + hl)])
                            mm(po, Sbf[pr_, (gblk - 1) % 3, sidx, :], qk[pr_, qb + pair, boff:boff + bn], False, True,
                               [("Sb", (gblk - 1) % 3, br)] + rkeys_all(br), [("ps", 4 + hl)])

                    for c4, (bi, (boff, bn, smp, gblk)) in enumerate(tblocks):
                        par = c4 % 2
                        ubk = (3, 1)[par]
                        pu = bank(ubk)
                        for pair in range(2):
                            kc0 = br * 256 + pair * 128
                            vc0 = br * 512 + pair * 256
                            mm(pu[:, pair * 256:(pair + 1) * 256], kh[:, bi, kc0:kc0 + 128], vtm[:, bi, vc0:vc0 + 256],
                               True, True, [("kh", bi), ("v", bi, br)], [("ps", ubk)])
                        for hl in (0, 2, 1, 3):
                            pair, h2 = hl // 2, hl % 2
                            pr_ = slice(h2 * 64, h2 * 64 + 64)
                            sb2 = (0, 2)[h2]
                            mm(bank(sb2)[:, pair * 128:(pair + 1) * 128], qk[pr_, kb_i + pair, boff:boff + bn],
                               qk[pr_, qb + pair, boff:boff + bn], True, True, rkeys_all(br), [("ps", sb2)])
                        for f_ in pend:
                            f_()
                        pend = []
                        for h2 in range(2):
                            sb2 = (0, 2)[h2]
                            tt(PT[:, par * 4 + h2 * 2:par * 4 + h2 * 2 + 2, :],
                               bank(sb2)[:, 0:256].rearrange("p (a b) -> p a b", a=2),
                               masks[:, 0, :].unsqueeze(1).to_broadcast([128, 2, 128]), ALU.mult,
                               [("ps", sb2), "masks"], [("PT", par)])
                        for pair in range(2):
                            sidx = br * 2 + pair
                            for h2 in range(2):
                                pr_ = slice(h2 * 64, h2 * 64 + 64)
                                if br == 0:
                                    dsc = decA[pr_, pair, bi:bi + 1]
                                    dk_ = [("decA", pair, bi)]
                                else:
                                    dsc = gains[pr_, G_DECR + pair:G_DECR + pair + 1]
                                    dk_ = ["gains"]
                                stt(Sst[pr_, sidx, :], Sst[pr_, sidx, :], dsc,
                                    pu[pr_, pair * 256 + h2 * 128:pair * 256 + (h2 + 1) * 128],
                                    ALU.mult, ALU.add, ["S%d" % sidx, ("ps", ubk)] + dk_, ["S%d" % sidx])
                        S.op("act", (lambda o_, i_: (lambda e: e.copy(out=o_, in_=i_)))(
                            Sbf[:, gblk % 3, br * 2:br * 2 + 2, :], Sst[:, br * 2:br * 2 + 2, :]),
                            ["S%d" % (br * 2), "S%d" % (br * 2 + 1)], [("Sb", gblk % 3, br)])
                        pend.append((lambda a, b, c, d, e_: (lambda: o_ops(a, b, c, d, e_)))(c4, bi, boff, bn, gblk))
                    for f_ in pend:
                        f_()
                    for hl in range(4):
                        epilogue(br, hl, bank(4 + hl)[:, 0:n], off, n, tbi, ("ps", 4 + hl))
                else:
                    bi = len(blocks) - 1
                    boff, bn, smp, gblk = blocks[bi]
                    for pair in range(2):
                        sidx = br * 2 + pair
                        sb_ = sidx % 2
                        S0f, tmpS = (SX[sidx % 2], SX[1 - sidx % 2])
                        kS0f, kTmp = ("SX", sidx % 2), ("SX", 1 - sidx % 2)
                        S0b3 = S0bf[:, sb_, :].rearrange("p (s v) -> p s v", s=16)
                        S0f3 = S0f[:].rearrange("p (s v) -> p s v", s=16)
                        tmp3 = tmpS[:].rearrange("p (s v) -> p s v", s=16)
                        for h2 in range(2):
                            hl = pair * 2 + h2
                            pr_ = slice(h2 * 64, h2 * 64 + 64)
                            slot = scnt[0] % 4
                            scnt[0] += 1
                            sbk = (0, 2)[slot % 2]
                            psS = bank(sbk)[0:64, 0:64]
                            mm(psS, qk[pr_, kb_i + pair, boff:boff + 64], qk[pr_, qb + pair, boff:boff + 64],
                               True, True, rkeys_all(br), [("ps", sbk)])
                            pts = slot + 4 * br
                            tt(PT[0:64, pts, 0:64], psS, masks[0:64, 1, 0:64], ALU.mult,
                               [("ps", sbk), "masks"], [("PT", pts // 4)])
                            if cut <= 4.93:
                                continue
                            po = bank(1)[:, hl * 64:(hl + 1) * 64]
                            vc = br * 512 + hl * 128
                            mm(po, vtm[0:64, bi, vc:vc + 128], PT[0:64, pts, 0:64], True, cut <= 4.935,
                               [("v", bi, br), ("PT", pts // 4)], [("ps", 1)])
                            po_ = slice((1 - h2) * 64, (1 - h2) * 64 + 64)
                            S.op("pool", (lambda a_: (lambda e: e.memset(a_, 0.0)))(qm[po_].rearrange("p a b -> p (a b)")),
                                 (), ["qm"])
                            tt(qm[pr_], qk[pr_, qb + pair, boff:boff + 64].unsqueeze(1).to_broadcast([64, 16, 64]),
                               maskq[pr_], ALU.mult, rkeys_all(br) + ["maskq"], ["qm"])
                            for s_ in range(16):
                                if cut <= 4.935:
                                    continue
                                mm(po, S0b3[:, s_, :], qm[:, s_, :], False, s_ == 15,
                                   [("S0bf", sb_), "qm"], [("ps", 1)])
                            if cut <= 4.94:
                                continue
                            V3 = Vblk[:].rearrange("p (s v) -> p s v", s=16)
                            tt(V3, vtm[0:64, bi, vc:vc + 128].unsqueeze(1).to_broadcast([64, 16, 128]),
                               smask[:, :].unsqueeze(2).to_broadcast([64, 16, 128]), ALU.mult,
                               [("v", bi, br), "smask"], ["Vblk"], eng="pool")
                            if cut <= 4.95:
                                continue
                            kc0 = br * 256 + pair * 128
                            for q4 in range(4):
                                mm(PSB[:, q4 * 512:(q4 + 1) * 512], kh[0:64, bi, kc0:kc0 + 128],
                                   Vblk[:, q4 * 512:(q4 + 1) * 512], True, True, [("kh", bi), "Vblk"], [("ps", 4 + q4)])
                            if h2 == 0:
                                if br == 0:
                                    tt(tmp3, S0f3, decSA[:, pair, :].unsqueeze(2).to_broadcast([128, 16, 128]),
                                       ALU.mult, [kS0f, "decSA"], [kTmp])
                                else:
                                    tsmul(tmpS[:, :], S0f[:, :], gains[:, G_DECR + 2 + pair:G_DECR + 3 + pair],
                                          [kS0f, "gains"], [kTmp])
                            tt(S0f[pr_, :], tmpS[pr_, :], PSB[pr_, :], ALU.add,
                               [kTmp] + [("ps", 4 + q) for q in range(4)], [kS0f + (h2,)])
                        if sidx < 3:
                            s0_load(sidx + 1)
                        dst = st_out[br][:, pair * 2:pair * 2 + 2].rearrange("s h d v -> (h d) s v")
                        dma("sp", dst, S0f3, [kS0f, kS0f + (0,), kS0f + (1,)], (), ("s0o", sidx % 2))
                    for hl in range(4):
                        if cut <= 4.98:
                            continue
                        epilogue(br, hl, bank(1)[:, hl * 64:(hl + 1) * 64], off, n, tbi, ("ps", 1), ((2, 3), (0, 5)))
        if hi == 1:
            for br in range(2):
                for pair in range(2):
                    sidx = br * 2 + pair
                    dma("sp", sp_out[br][pair * 128:(pair + 1) * 128, :], Sst[:, sidx, :], ["S%d" % sidx], (), "spo")
        wrelease(igt[0])
        wrelease(igt[1])
        if cut <= 5:
            return bail()
        S.barrier(exclude=("pe",))
        cnt = 0
        for dmg in range(2):
            iga, wga, kga = wnext(8, 512)
            igr, wgr, kgr = wnext(8, 512)
            iwo, wwo, kwo = wnext(8, 512)
            for tbi, (off, n) in enumerate(tbs):
                okeys = lambda br: [("qk", br, tbi, br * 4 + j) for j in range(4)]
                for dl in range(4):
                    dm = dmg * 4 + dl
                    b0 = (cnt % 2) * 4
                    cnt += 1
                    pga, pgr, pma, pmr = bank(b0), bank(b0 + 1), bank(b0 + 2), bank(b0 + 3)
                    cs = slice(dl * 128, (dl + 1) * 128)
                    for kc in range(8):
                        mm(pga[:, 0:n], wga[:, kc, cs], u[:, kc, off:off + n], kc == 0, kc == 7,
                           [kga, ("u", kc, tbi)], [("ps", b0)])
                    for kc in range(8):
                        mm(pgr[:, 0:n], wgr[:, kc, cs], u[:, kc, off:off + n], kc == 0, kc == 7,
                           [kgr, ("u", kc, tbi)], [("ps", b0 + 1)])
                    for fc in range(4):
                        mm(pma[:, 0:n], wwo[:, fc, cs], qk[:, fc, off:off + n], fc == 0, fc == 3,
                           [kwo] + okeys(0), [("ps", b0 + 2)])
                    for fc in range(4, 8):
                        mm(pmr[:, 0:n], wwo[:, fc, cs], qk[:, fc, off:off + n], fc == 4, fc == 7,
                           [kwo] + okeys(1), [("ps", b0 + 3)])
                    act(tA[:, 0:n], pga[:, 0:n], AF.Tanh, [("ps", b0)], ["tA"], scale=0.5)
                    act(tB[:, 0:n], pgr[:, 0:n], AF.Tanh, [("ps", b0 + 1)], ["tB"], scale=0.5)
                    stt(tC[:, 0:n], tA[:, 0:n], 1.0, pma[:, 0:n], ALU.add, ALU.mult, ["tA", ("ps", b0 + 2)], ["tC"])
                    stt(tD[:, 0:n], tB[:, 0:n], 1.0, pmr[:, 0:n], ALU.add, ALU.mult, ["tB", ("ps", b0 + 3)], ["tD"])
                    tt(tA[:, 0:n], tC[:, 0:n], tD[:, 0:n], ALU.add, ["tC", "tD"], ["tA"])
                    stt(h[:, dm, off:off + n], tA[:, 0:n], 0.5, h[:, dm, off:off + n], ALU.mult, ALU.add,
                        ["tA", ("h", dm, tbi)], [("h", dm, tbi)])
                    if next_gcol is not None and dmg == 1:
                        if tbi == 0 and dl == 3:
                            norm_part1(sq8m, 0, tbs[0][0], tbs[0][1])
                        elif tbi == 1 and dl == 1:
                            norm_part2(sq8m, 0, tbs[0][0], tbs[0][1], next_gcol)
            wrelease(iga)
            wrelease(igr)
            wrelease(iwo)

    def ple(tok0, tbs, lazy):
        cnt = 0
        for dmg in range(2):
            ig, wg_, kg_ = wnext(8, 512)
            ip, wp_, kp_ = wnext(2, 512)
            for tbi, (off, n) in enumerate(tbs):
                if dmg == 0 and tbi > 0:
                    lazy(tbi)
                for dl in range(4):
                    dm = dmg * 4 + dl
                    b0 = (cnt % 2) * 2
                    cnt += 1
                    pg, pp = bank(b0), bank(b0 + 1)
                    cs = slice(dl * 128, (dl + 1) * 128)
                    for kc in range(8):
                        mm(pg[:, 0:n], wg_[:, kc, cs], u[:, kc, off:off + n], kc == 0, kc == 7,
                           [kg_, ("u", kc, tbi)], [("ps", b0)])
                    for kc in range(2):
                        mm(pp[:, 0:n], wp_[:, kc, cs], pb[:, kc, off:off + n], kc == 0, kc == 1,
                           [kp_, "pb"], [("ps", b0 + 1)])
                    sb_ = cnt % 2
                    act(stmp[:, sb_, 0:n], pg[:, 0:n], AF.Tanh, [("ps", b0)], [("stmp", sb_)], scale=0.5)
                    stt(stmp2[:, sb_, 0:n], stmp[:, sb_, 0:n], 1.0, pp[:, 0:n], ALU.add, ALU.mult,
                        [("stmp", sb_), ("ps", b0 + 1)], [("stmp2", sb_)])
                    stt(h[:, dm, off:off + n], stmp2[:, sb_, 0:n], 0.5, h[:, dm, off:off + n], ALU.mult, ALU.add,
                        [("stmp2", sb_), ("h", dm, tbi)], [("h", dm, tbi)])
            wrelease(ig)
            wrelease(ip)

    def skip_tiles(k):
        for _ in range(k):
            i = st["cur"]
            st["cur"] += 1
            wrelease(i)

    yv = yT.rearrange("(k p) t -> p k t", p=128)
    xv = xT.rearrange("(k p) t -> p k t", p=128)
    halves = [(0, [(0, 512), (512, 512)]), (1024, [(0, 512), (512, 512), (1024, 64)])]
    for hi, (tok0, tbs) in enumerate(halves[:nhalves]):
        T = sum(n for _, n in tbs)
        blocks = [(b * 128, 128, False, (tok0 + b * 128) // 128) for b in range(8)]
        if hi == 1:
            blocks.append((1024, 64, True, 16))
        def xload(tok0_, tbi, off, n):
            dma("sp", h[:, :, off:off + n], xv[:, :, tok0_ + off:tok0_ + off + n], (),
                [("h", kc, tbi) for kc in range(8)], ("x", tbi))

        if hi == 0:
            for tbi, (off, n) in enumerate(tbs):
                xload(tok0, tbi, off, n)
            late_consts()

        def mk_lazy(gcol):
            return lambda tbi: norm_u_tb(tbi, tbs[tbi][0], tbs[tbi][1], gcol)

        norm_u_tb(0, tbs[0][0], tbs[0][1], G_FFN1)
        ffn(tbs, mk_lazy(G_FFN1), G_MIX if (stage >= 2 and cut >= 99) else None)
        if stage >= 2:
            if cut < 99:
                norm_u_tb(0, tbs[0][0], tbs[0][1], G_MIX)
            S.barrier(exclude=("pe",))
            mixing(hi, tok0, tbs, blocks, mk_lazy(G_MIX), G_FFN2 if stage >= 3 else None)
        else:
            skip_tiles(15)
        if stage >= 3:
            if cut < 99:
                norm_u_tb(0, tbs[0][0], tbs[0][1], G_FFN2)
            S.barrier(exclude=("pe",))
            if stage >= 4:
                T_ = sum(n for _, n in tbs)
                dma("pool", pb[:, :, 0:T_], pT.rearrange("(k p) t -> p k t", p=128)[:, :, tok0:tok0 + T_], (),
                    ["pb"], "pb")
            ffn(tbs, mk_lazy(G_FFN2), G_PLE if stage >= 4 else None)
        else:
            S.barrier(exclude=("pe",))
            skip_tiles(20)
        if stage >= 4:
            ple(tok0, tbs, mk_lazy(G_PLE))
        else:
            skip_tiles(4)
        if stage >= 5:
            for tbi, (off, n) in enumerate(tbs):
                pn = bank(6)
                for kc in range(8):
                    b = kc % 2
                    act(sq[:, b, 0:n], h[:, kc, off:off + n], AF.Square, [("h", kc, tbi)], [("sq", b)])
                    mm(pn[:, 0:n], ones_bf[:], sq[:, b, 0:n], kc == 0, kc == 7, ["ones_bf", ("sq", b)], [("ps", 6)])
                rb = tbi % 2
                act(rstd[:, rb, 0:n], pn[:, 0:n], AF.Ln, [("ps", 6)], [("rstd", rb)], scale=1.0 / D, bias=EPS)
                act(rstd[:, rb, 0:n], rstd[:, rb, 0:n], AF.Exp, [("rstd", rb)], [("rstd", rb)], scale=-0.5)
                for kc in range(8):
                    stt(yout[:, rb, kc, 0:n], h[:, kc, off:off + n], gains[:, G_FIN + kc:G_FIN + kc + 1],
                        rstd[:, rb, 0:n], ALU.mult, ALU.mult,
                        [("h", kc, tbi), "gains", ("rstd", rb)], [("yout", rb)])
                dma("sp", yv[:, :, tok0 + off:tok0 + off + n], yout[:, rb, :, 0:n], [("yout", rb)], (), ("yo", rb))
                if stage >= 5 and hi + 1 < len(halves[:nhalves]):
                    ntok0, ntbs = halves[hi + 1]
                    xload(ntok0, tbi, ntbs[tbi][0], ntbs[tbi][1])
                    if tbi == len(tbs) - 1:
                        for t2 in range(len(tbs), len(ntbs)):
                            xload(ntok0, t2, ntbs[t2][0], ntbs[t2][1])
        else:
            for tbi, (off, n) in enumerate(tbs):
                dma("sp", yv[:, :, tok0 + off:tok0 + off + n], h[:, :, off:off + n],
                    [("h", kc, tbi) for kc in range(8)], (), "yo")
            if hi + 1 < len(halves[:nhalves]):
                ntok0, ntbs = halves[hi + 1]
                for t2 in range(len(ntbs)):
                    xload(ntok0, t2, ntbs[t2][0], ntbs[t2][1])
    S.emit(nc)
    return nc


_CONST_CACHE = {}


def _prep_inputs(inp):
    f32 = np.float32
    inp = {k: np.asarray(v) for k, v in inp.items()}
    wflat, sched = _weight_tiles(inp)
    ct = _CONST_CACHE.get("ct")
    if ct is None:
        ct = _const_tables()
        _CONST_CACHE["ct"] = ct

    def col8(v):
        return np.asarray(v, f32).reshape(-1, 128).T

    cpack = np.zeros((128, 52), f32)
    cpack[:, 0:8] = col8(inp["norm_ffn1"][0])
    cpack[:, 8:16] = col8(inp["norm_mix"][0])
    cpack[:, 16:24] = col8(inp["norm_ffn2"][0])
    cpack[:, 24:32] = col8(inp["norm_ple"][0])
    cpack[:, 32:40] = col8(inp["norm_final"])
    cpack[:, 40:44] = col8(inp["gn_gla"][0])
    cpack[:, 44:48] = col8(inp["gn_ret"][0])
    cpack[:, 48:52] = ct["decr"]
    xp, xs = inp["x_prompt"], inp["x_sample"]
    pp, psm = inp["p_prompt"][0], inp["p_sample"][0]
    in_maps = []
    for c in range(NCORES):
        xc = np.concatenate([xp[c], xs[16 * c:16 * c + 16].reshape(NSAMP, D)], axis=0)
        pc = np.concatenate([pp[c], psm[16 * c:16 * c + 16].reshape(NSAMP, 256)], axis=0)
        in_maps.append({
            "xT": np.ascontiguousarray(xc.T.astype(f32)),
            "pT": np.ascontiguousarray(pc.T.astype(f32)),
            "wflat": wflat,
            "cpack": cpack,
            "cmat": ct["cmat"], "masks": ct["masks"], "smask": ct["smask"], "maskq": ct["maskq"],
            "wup": np.ascontiguousarray(inp["w_alpha_up"][0].astype(f32)),
            "balpha": np.ascontiguousarray(inp["b_alpha"][0].reshape(1, 256).astype(f32)),
            "rtab": ct["rtab"], "ttab": ct["ttab"],
            "sg_in": np.ascontiguousarray(inp["state_gla"][0, 16 * c:16 * c + 16].astype(f32)),
            "sr_in": np.ascontiguousarray(inp["state_ret"][0, 16 * c:16 * c + 16].astype(f32)),
        })
    return in_maps


def _run(inp, stage=5):
    in_maps = _prep_inputs(inp)
    nc = build_program(stage)
    res = run_bass_kernel_spmd(nc, in_maps, core_ids=list(range(NCORES)))
    return res.results


def kernel(**inputs):
    results = _run(inputs, 5)
    f32 = np.float32
    y_prompt = np.zeros((8, TP, D), f32)
    y_sample = np.zeros((128, 4, D), f32)
    gp = np.zeros((1, 8, 4, 64, 128), f32); rp = np.zeros((1, 8, 4, 64, 128), f32)
    gs = np.zeros((1, 128, 4, 64, 128), f32); rs = np.zeros((1, 128, 4, 64, 128), f32)
    for c, r in enumerate(results):
        yc = np.asarray(r["yT"]).T
        y_prompt[c] = yc[:TP]
        y_sample[16 * c:16 * c + 16] = yc[TP:].reshape(16, 4, D)
        gp[0, c] = np.asarray(r["sgp"]).reshape(4, 64, 128)
        rp[0, c] = np.asarray(r["srp"]).reshape(4, 64, 128)
        gs[0, 16 * c:16 * c + 16] = np.asarray(r["sgs"])
        rs[0, 16 * c:16 * c + 16] = np.asarray(r["srs"])
    return (y_prompt, y_sample, gp, rp, gs, rs)
```

```python
import numpy as np
import ml_dtypes
import concourse.bass as bass
import concourse.mybir as mybir
from concourse.bass_utils import run_bass_kernel_spmd
from contextlib import ExitStack

F32 = mybir.dt.float32
BF16 = mybir.dt.bfloat16
AF = mybir.ActivationFunctionType
ALU = mybir.AluOpType

NCORES = 8
D = 1024
DFF = 2816
NFC = 22
TP = 2048
NSAMP = 64
TALL = TP + NSAMP
EPS = 1e-6
RING = 5
SLOT = 4096

ENGS = ("pe", "act", "dve", "pool", "sp")


class Op:
    __slots__ = ("eng", "fn", "reads", "writes", "dsem", "chan", "deps", "signaled", "count", "idx")

    def __init__(self, eng, fn, reads, writes, dsem):
        self.eng = eng
        self.fn = fn
        self.reads = tuple(reads)
        self.writes = tuple(writes)
        self.dsem = dsem
        self.chan = ("dma", dsem) if dsem is not None else ("eng", eng)
        self.deps = []
        self.signaled = dsem is not None
        self.count = 0


class Sched:
    def __init__(self):
        self.ops = []
        self.dma_sems = []
        self.barriers = []

    def op(self, eng, fn, reads=(), writes=(), dsem=None):
        o = Op(eng, fn, reads, writes, dsem)
        o.idx = len(self.ops)
        self.ops.append(o)
        if dsem is not None and dsem not in self.dma_sems:
            self.dma_sems.append(dsem)
        return o

    def barrier(self, exclude=()):
        self.barriers.append((len(self.ops), tuple(exclude)))

    def analyze(self):
        last_w, last_r = {}, {}
        waited = {e: {} for e in ENGS}
        chan_last = {}
        pending = {e: None for e in ENGS}
        bars = list(self.barriers)
        bi = 0
        ops = self.ops
        for o in ops:
            while bi < len(bars) and bars[bi][0] <= o.idx:
                snap = {ch: i for ch, i in chan_last.items()
                        if not (ch[0] == "dma" and isinstance(ch[1], tuple) and ch[1][0] == "w")}
                for e in ENGS:
                    if e in bars[bi][1]:
                        continue
                    if pending[e] is None:
                        pending[e] = snap
                    else:
                        m = dict(pending[e])
                        m.update(snap)
                        pending[e] = m
                bi += 1
            deps = {}

            def add(d):
                for ch, i in d.items():
                    if deps.get(ch, -1) < i:
                        deps[ch] = i

            if pending[o.eng] is not None:
                add(pending[o.eng])
                pending[o.eng] = None
            for k in o.reads:
                add(last_w.get(k, {}))
            for k in o.writes:
                add(last_w.get(k, {}))
                add(last_r.get(k, {}))
            w = waited[o.eng]
            for ch, i in deps.items():
                if ch == ("eng", "pe") and o.eng == "pe":
                    continue
                if w.get(ch, -1) >= i:
                    continue
                w[ch] = i
                o.deps.append(i)
                ops[i].signaled = True
            for k in o.reads:
                last_r.setdefault(k, {})[o.chan] = o.idx
            for k in o.writes:
                last_w[k] = {o.chan: o.idx}
                last_r[k] = {}
            chan_last[o.chan] = o.idx
        cnt = {}
        for o in ops:
            if o.signaled:
                inc = 16 if o.dsem is not None else 1
                cnt[o.chan] = cnt.get(o.chan, 0) + inc
                o.count = cnt[o.chan]
        self.final_counts = cnt

    def emit(self, nc, final_wait_eng="sp"):
        self.analyze()
        with ExitStack() as es:
            sems = {}
            for e in ENGS:
                sems[("eng", e)] = es.enter_context(nc.semaphore("sem_" + e))
            for i, d in enumerate(self.dma_sems):
                sems[("dma", d)] = es.enter_context(nc.semaphore("dsem_%d" % i))
            block = es.enter_context(nc.Block())
            ops = self.ops

            def run(engname):
                def body(eng):
                    for o in ops:
                        if o.eng != engname:
                            continue
                        for i in o.deps:
                            d = ops[i]
                            eng.wait_ge(sems[d.chan], d.count)
                        ins = o.fn(eng)
                        if o.signaled:
                            ins.then_inc(sems[o.chan], 16 if o.dsem is not None else 1)
                    if engname == final_wait_eng:
                        for ch, c in self.final_counts.items():
                            eng.wait_ge(sems[ch], c)
                return body

            block.tensor(run("pe"))
            block.scalar(run("act"))
            block.vector(run("dve"))
            block.gpsimd(run("pool"))
            block.sync(run("sp"))


def _tile_kc(W, cols):
    nk = W.shape[0] // 128
    sub = W[:, cols]
    return np.ascontiguousarray(sub.reshape(nk, 128, sub.shape[1]).transpose(1, 0, 2).reshape(128, -1))


def _swap_heads(idx):
    idx = np.asarray(idx).reshape(-1, 2, 32)
    return idx[:, ::-1, :].reshape(-1)


_QA, _KA, _VA, _RA = np.arange(0, 256), np.arange(256, 512), np.arange(512, 1024), np.arange(1024, 1536)
_QR, _KR, _VR, _GR = np.arange(1536, 1792), np.arange(1792, 2048), np.arange(2048, 2560), np.arange(2560, 3072)
_AL = np.arange(3072, 3088)
_GA, _GRT = np.arange(3088, 4112), np.arange(4112, 5136)


def _weight_tiles(inp):
    tiles = []

    def ffn(w_in, w_out):
        for g in range(6):
            wd = 512 if g < 5 else 256
            tiles.append(_tile_kc(w_in, np.arange(g * 512, g * 512 + wd)))
            tiles.append(_tile_kc(w_in, np.arange(DFF + g * 512, DFF + g * 512 + wd)))
        for dm in range(8):
            tiles.append(_tile_kc(w_out, np.arange(dm * 128, dm * 128 + 128)))

    ffn(inp["w_ffn1_in"][0], inp["w_ffn1_out"][0])
    wi = inp["w_in"][0]
    tiles.append(_tile_kc(wi, _AL))
    tiles.append(_tile_kc(wi, np.concatenate([_QA, _KA])))
    tiles.append(_tile_kc(wi, np.concatenate([_QR, _KR])))
    tiles.append(_tile_kc(wi, np.concatenate([_swap_heads(_QR), _swap_heads(_KR)])))
    tiles.append(_tile_kc(wi, np.concatenate([_KA, _KR])))
    tiles.append(_tile_kc(wi, _VA))
    tiles.append(_tile_kc(wi, _VR))
    tiles.append(_tile_kc(wi, _RA))
    tiles.append(_tile_kc(wi, _GR))
    wo = inp["w_out"][0]
    for dmg in range(2):
        c = np.arange(dmg * 512, dmg * 512 + 512)
        tiles.append(_tile_kc(wi, _GA[c]))
        tiles.append(_tile_kc(wi, _GRT[c]))
        tiles.append(_tile_kc(wo, c))
    ffn(inp["w_ffn2_in"][0], inp["w_ffn2_out"][0])
    for dmg in range(2):
        c = np.arange(dmg * 512, dmg * 512 + 512)
        tiles.append(_tile_kc(inp["w_ple_gate"][0], c))
        tiles.append(_tile_kc(inp["w_ple_proj"][0], c))
    sched = []
    off = 0
    for t in tiles:
        assert t.shape[1] <= SLOT
        sched.append((off, t.shape[1]))
        off += t.shape[1]
    return np.ascontiguousarray(np.concatenate(tiles, axis=1).astype(np.float32)), sched


def _weight_schedule():
    Ls = []

    def ffn():
        for g in range(6):
            wd = 512 if g < 5 else 256
            Ls.extend([8 * wd, 8 * wd])
        Ls.extend([NFC * 128] * 8)

    ffn()
    Ls.append(8 * 16)
    Ls.extend([8 * 512] * 8)
    Ls.extend([8 * 512] * 6)
    ffn()
    Ls.extend([8 * 512, 2 * 512] * 2)
    sched, off = [], 0
    for L in Ls:
        sched.append((off, L))
        off += L
    return sched, off


def _const_tables():
    f32 = np.float32
    half = 32
    freq = (f32(10000.0) ** (-(np.arange(half, dtype=f32) / f32(half)))).astype(f32)
    pos = np.concatenate([np.arange(TP), 16384 + (np.arange(NSAMP) % 4)]).astype(f32)
    il = np.concatenate([np.arange(TP) % 128, np.arange(NSAMP) % 4]).astype(np.float64)
    nchunk = np.concatenate([np.full(TP, 128.0), np.full(NSAMP, 4.0)])
    ang = (pos[:, None] * freq[None, :]).astype(f32).astype(np.float64)
    cos, sin = np.cos(ang), np.sin(ang)
    lg = np.log1p(-np.exp2(-5.0 - np.arange(4, dtype=np.float64)))
    d = np.arange(64)
    sgn = np.where(d < 32, -1.0, 1.0)
    CQ = np.zeros((2, 128, TALL)); SQ = np.zeros_like(CQ); CK = np.zeros_like(CQ); SK = np.zeros_like(CQ)
    for hp in range(2):
        for h2 in range(2):
            h = hp * 2 + h2
            up = np.exp((il + 1.0) * lg[h])[None, :]
            dn = np.exp(-(il + 1.0) * lg[h])[None, :] * 0.125
            c = cos[:, d % 32].T
            s = sin[:, d % 32].T * sgn[:, None]
            sl = slice(h2 * 64, h2 * 64 + 64)
            CQ[hp, sl] = c * up; SQ[hp, sl] = s * up; CK[hp, sl] = c * dn; SK[hp, sl] = s * dn
    rtab = np.zeros((5, 128, 8, 512), f32)
    for gtb in range(5):
        t0, n = (gtb * 512, 512) if gtb < 4 else (TP, NSAMP)
        for hp in range(2):
            for j, A in enumerate((CQ, SQ, CK, SK)):
                rtab[gtb, :, hp * 4 + j, :n] = A[hp, :, t0:t0 + n]
    ttab = np.zeros((17, 128, 2, 256), f32)
    for b in range(17):
        t0, n = (b * 128, 128) if b < 16 else (TP, NSAMP)
        for h in range(4):
            k = np.exp((nchunk[t0:t0 + n] - 1.0 - il[t0:t0 + n]) * lg[h])[:, None] * 0.125
            ttab[b, :n, 0, h * 64:(h + 1) * 64] = cos[t0:t0 + n][:, d % 32] * k
            ttab[b, :n, 1, h * 64:(h + 1) * 64] = sin[t0:t0 + n][:, d % 32] * sgn[None, :] * k
    decr = np.zeros((128, 4), f32)
    for hp in range(2):
        for h2 in range(2):
            h = hp * 2 + h2
            decr[h2 * 64:(h2 + 1) * 64, hp] = np.exp(128.0 * lg[h])
            decr[h2 * 64:(h2 + 1) * 64, 2 + hp] = np.exp(4.0 * lg[h])
    j = np.arange(128)[:, None]; i = np.arange(128)[None, :]
    cmat = np.zeros((128, 4, 128), f32)
    cmat[:, 0] = np.where(j <= i, -1.0 / 16, 0.0)
    cmat[:, 1] = np.where(j > i, -1.0 / 16, 0.0)
    same = (j // 4 == i // 4) & (j < 64) & (i < 64)
    cmat[:, 2] = np.where(same & (j <= i), -1.0 / 16, 0.0)
    cmat[:, 3] = np.where(same & (j > i), -1.0 / 16, 0.0)
    masks = np.zeros((128, 2, 128), f32)
    masks[:, 0] = (j <= i)
    masks[:, 1] = same & (j <= i)
    smask = (np.arange(64)[:, None] // 4 == np.arange(16)[None, :]).astype(f32)
    maskq = np.broadcast_to((np.arange(16)[:, None] == np.arange(64)[None, :] // 4).astype(f32).reshape(1, 1024),
                            (128, 1024)).copy()
    return dict(rtab=rtab.reshape(5, 128, 8 * 512), ttab=ttab.reshape(17, 128, 512), decr=decr,
                cmat=cmat.reshape(128, 512), masks=masks.reshape(128, 256), smask=smask, maskq=maskq)


def build_program(stage=5, nhalves=2, cut=99):
    nc = bass.Bass("TRN2", target_bir_lowering=False)
    S = Sched()
    wsched1, WTOT = _weight_schedule()

    def din(name, shape, dt=F32):
        return nc.dram_tensor(name, list(shape), dt, kind="ExternalInput").ap()

    def dout(name, shape, dt=F32):
        return nc.dram_tensor(name, list(shape), dt, kind="ExternalOutput").ap()

    xT = din("xT", [D, TALL]); pT = din("pT", [256, TALL])
    wflat = din("wflat", [128, WTOT])
    cpack = din("cpack", [128, 52])
    cmat_d = din("cmat", [128, 512]); masks_d = din("masks", [128, 256]); smask_d = din("smask", [64, 16])
    maskq_d = din("maskq", [128, 1024])
    wup_d = din("wup", [16, 256]); balpha_d = din("balpha", [1, 256])
    rtab_d = din("rtab", [5, 128, 8 * 512]); ttab_d = din("ttab", [17, 128, 512])
    sg_in = din("sg_in", [16, 4, 64, 128]); sr_in = din("sr_in", [16, 4, 64, 128])
    yT = dout("yT", [D, TALL])
    sgp = dout("sgp", [256, 128]); srp = dout("srp", [256, 128])
    sgs = dout("sgs", [16, 4, 64, 128]); srs = dout("srs", [16, 4, 64, 128])
    st_in = (sg_in, sr_in); st_out = (sgs, srs); sp_out = (sgp, srp)

    SB_BASE, SB_END = 16512, 229376
    ptr = [SB_BASE]

    def esz(dt):
        return 2 if dt == BF16 else 4

    def palloc(name, shape, dt):
        nbytes = int(np.prod(shape[1:])) * esz(dt)
        nbytes = (nbytes + 31) // 32 * 32
        t = nc.alloc_sbuf_tensor_at(name, list(shape), dt, offset=ptr[0])
        ptr[0] += nbytes
        return t

    TM = 1088
    h = palloc("h", [128, 8, TM], F32)
    u = palloc("u", [128, 8, TM], BF16)
    ring = [palloc("ring%d" % i, [128, SLOT], BF16) for i in range(RING)]
    gains = palloc("gains", [128, 52], F32)
    cmat = palloc("cmat_s", [128, 4, 128], BF16)
    masks = palloc("masks_s", [128, 2, 128], F32)
    smask = palloc("smask_s", [64, 16], F32)
    maskq = palloc("maskq_s", [128, 16, 64], BF16)
    wup = palloc("wup_s", [16, 256], BF16)
    balpha = palloc("balpha_s", [1, 256], BF16)
    ones_bf = palloc("ones_bf", [128, 128], BF16)
    ones_row = palloc("ones_row", [1, 128], BF16)
    Sst = palloc("Sst", [128, 4, 128], F32)
    Sbf = palloc("Sbf", [128, 3, 4, 128], BF16)
    decA = palloc("decA", [128, 2, 9], F32)
    decSA = palloc("decSA", [128, 2, 16], F32)
    sq = palloc("sq", [128, 2, 512], BF16)
    rstd = palloc("rstd", [128, 2, 512], F32)
    UB = ptr[0]
    USZ = SB_END - UB

    def ualloc(name, shape, dt, off):
        nbytes = int(np.prod(shape[1:])) * esz(dt)
        assert off % 32 == 0 and off + nbytes <= USZ, (name, off, nbytes, USZ)
        return nc.alloc_sbuf_tensor_at(name, list(shape), dt, offset=UB + off), off + (nbytes + 31) // 32 * 32

    gbuf, o = ualloc("gbuf", [128, NFC, TM], BF16, 0)
    stmp, o = ualloc("stmp", [128, 2, 512], F32, o)
    pb, o_pb = ualloc("pb", [128, 2, TM], BF16, o)
    stmp2, _ = ualloc("stmp2", [128, 2, 512], F32, o_pb)
    yout, _ = ualloc("yout", [128, 2, 8, 512], F32, o_pb + 4096)
    sq8f, _ = ualloc("sq8f", [128, 8, 512], BF16, o_pb + 4096 + 32768)
    sq8m, _ = ualloc("sq8m", [128, 8, 512], BF16, 17408)
    qk, o = ualloc("qk", [128, 8, TM], BF16, 0)
    vtm, o = ualloc("vtm", [128, 9, 1024], BF16, o)
    kh, o = ualloc("kh", [128, 9, 512], BF16, o)
    o_tr = o
    ed, o = ualloc("ed", [128, 9, 256], F32, o)
    alowT, o = ualloc("alowT", [16, TM], BF16, o)
    spb, o = ualloc("spb", [128, 2, 256], F32, o)
    sphi, o = ualloc("sphi", [128, 2, 256], BF16, o)
    splo, o = ualloc("splo", [128, 2, 256], BF16, o)
    etmp, o = ualloc("etmp", [128, 2, 256], F32, o)
    eb, o = ualloc("eb", [128, 2, 512], F32, o)
    enb, o = ualloc("enb", [128, 2, 512], F32, o)
    rtab, o = ualloc("rtab_s", [128, 8, 512], F32, o)
    ttab, o = ualloc("ttab_s", [128, 3, 512], F32, o)
    tA, o = ualloc("tA", [128, 512], F32, o)
    tB, o = ualloc("tB", [128, 512], F32, o)
    tC, o = ualloc("tC", [128, 512], F32, o)
    tD, o = ualloc("tD", [128, 512], F32, o)
    o = o_tr
    PT, o = ualloc("PT", [128, 8, 128], BF16, o)
    osb, o = ualloc("osb", [128, 2, 512], F32, o)
    sqb, o = ualloc("sqb", [128, 2, 512], BF16, o)
    rs2, o = ualloc("rs2", [128, 2, 512], F32, o)
    ee, o = ualloc("ee", [128, 2, 512], F32, o)
    tt1, o = ualloc("tt1", [128, 2, 512], F32, o)
    tt2, o = ualloc("tt2", [128, 2, 512], F32, o)
    S0bf, o = ualloc("S0bf", [128, 2, 16 * 128], BF16, o)
    S0f_buf, o = ualloc("S0f", [128, 16 * 128], F32, o)
    tmpS_buf, o = ualloc("tmpS", [128, 16 * 128], F32, o)
    Vblk, o = ualloc("Vblk", [64, 16 * 128], BF16, o)
    qm, o = ualloc("qm", [128, 16, 64], BF16, o)
    print("SBUF union size", USZ, "P2 end", o)

    PSA = nc.alloc_psum_tensor("PSA", [128, 2048], F32)
    PSB = nc.alloc_psum_tensor("PSB", [128, 2048], F32)

    def bank(b):
        t = PSA if b < 4 else PSB
        return t[:, (b % 4) * 512:(b % 4) * 512 + 512]

    def mm(out, lhsT, rhs, start, stop, reads, writes):
        S.op("pe", lambda e: e.matmul(out, lhsT=lhsT, rhs=rhs, start=start, stop=stop), reads, writes)

    def act(out, in_, func, reads, writes, scale=1.0, bias=0.0):
        S.op("act", lambda e: e.activation(out=out, in_=in_, func=func, bias=bias, scale=scale), reads, writes)

    def tt(out, in0, in1, op, reads, writes, eng="dve"):
        S.op(eng, lambda e: e.tensor_tensor(out=out, in0=in0, in1=in1, op=op), reads, writes)

    def stt(out, in0, scalar, in1, op0, op1, reads, writes):
        S.op("dve", lambda e: e.scalar_tensor_tensor(out=out, in0=in0, scalar=scalar, in1=in1, op0=op0, op1=op1),
             reads, writes)

    def tsmul(out, in0, scalar, reads, writes):
        S.op("dve", lambda e: e.tensor_scalar(out=out, in0=in0, scalar1=scalar, scalar2=None, op0=ALU.mult),
             reads, writes)

    def dma(eng, out, in_, reads, writes, dsem):
        S.op(eng, lambda e: e.dma_start(out=out, in_=in_), reads, writes, dsem=dsem)

    def memset(ap, val, writes):
        S.op("dve", lambda e: e.memset(ap, val), (), writes)

    full_sched = wsched1 + wsched1
    st = {"cur": 0}

    def issue(i):
        if i >= len(full_sched):
            return
        off, L = full_sched[i]
        slot = i % RING
        dma("pool", ring[slot][:, 0:L], wflat[:, off:off + L], (), [("ring", slot)], ("w", slot))

    def wnext(nk, ncols):
        i = st["cur"]
        st["cur"] += 1
        off, L = full_sched[i]
        assert L == nk * ncols, (i, L, nk, ncols)
        slot = i % RING
        return i, ring[slot][:, 0:L].rearrange("p (k c) -> p k c", k=nk), ("ring", slot)

    def wrelease(i):
        issue(i + RING)

    dma("sp", gains[:], cpack[:], (), ["gains"], "c0")

    def late_consts():
        dma("pool", cmat[:].rearrange("p a b -> p (a b)"), cmat_d[:], (), ["cmat"], "c1")
        dma("sp", masks[:].rearrange("p a b -> p (a b)"), masks_d[:], (), ["masks"], "c2")
        dma("sp", smask[:], smask_d[:], (), ["smask"], "c3")
        dma("pool", maskq[:].rearrange("p a b -> p (a b)"), maskq_d[:], (), ["maskq"], "c6")
        dma("pool", wup[:], wup_d[:], (), ["wup"], "c4")
        dma("pool", balpha[:], balpha_d[:], (), ["balpha"], "c5")
    memset(ones_bf[:], 1.0, ["ones_bf"])
    memset(ones_row[:], 1.0, ["ones_row"])
    memset(Sst[:].rearrange("p a b -> p (a b)"), 0.0, ["S0", "S1", "S2", "S3"])
    memset(Sbf[:].rearrange("p t a b -> p (t a b)"), 0.0, [("Sb", t_, b_) for t_ in range(3) for b_ in range(2)])
    for i in range(RING):
        issue(i)

    G_FFN1, G_MIX, G_FFN2, G_PLE, G_FIN, G_GNA, G_GNR, G_DECR = 0, 8, 16, 24, 32, 40, 44, 48

    def rmsnorm(tbs, gcol, dst_fn, dst_keys):
        for tbi, (off, n) in enumerate(tbs):
            pn = bank(6)
            for kc in range(8):
                b = kc % 2
                act(sq[:, b, 0:n], h[:, kc, off:off + n], AF.Square, [("h", kc, tbi)], [("sq", b)])
                mm(pn[:, 0:n], ones_bf[:], sq[:, b, 0:n], kc == 0, kc == 7, ["ones_bf", ("sq", b)], [("ps", 6)])
            rb = tbi % 2
            act(rstd[:, rb, 0:n], pn[:, 0:n], AF.Ln, [("ps", 6)], [("rstd", rb)], scale=1.0 / D, bias=EPS)
            act(rstd[:, rb, 0:n], rstd[:, rb, 0:n], AF.Exp, [("rstd", rb)], [("rstd", rb)], scale=-0.5)
            for kc in range(8):
                stt(dst_fn(kc, tbi, off, n), h[:, kc, off:off + n], gains[:, gcol + kc:gcol + kc + 1],
                    rstd[:, rb, 0:n], ALU.mult, ALU.mult,
                    [("h", kc, tbi), "gains", ("rstd", rb)], dst_keys(kc, tbi))

    def norm_to_u(tbs, gcol):
        rmsnorm(tbs, gcol, lambda kc, tbi, off, n: u[:, kc, off:off + n], lambda kc, tbi: [("u", kc, tbi)])

    def norm_u_tb(tbi, off, n, gcol):
        pn = bank(6)
        for kc in range(8):
            b = kc % 2
            act(sq[:, b, 0:n], h[:, kc, off:off + n], AF.Square, [("h", kc, tbi)], [("sq", b)])
            mm(pn[:, 0:n], ones_bf[:], sq[:, b, 0:n], kc == 0, kc == 7, ["ones_bf", ("sq", b)], [("ps", 6)])
        rb = tbi % 2
        act(rstd[:, rb, 0:n], pn[:, 0:n], AF.Ln, [("ps", 6)], [("rstd", rb)], scale=1.0 / D, bias=EPS)
        act(rstd[:, rb, 0:n], rstd[:, rb, 0:n], AF.Exp, [("rstd", rb)], [("rstd", rb)], scale=-0.5)
        for kc in range(8):
            stt(u[:, kc, off:off + n], h[:, kc, off:off + n], gains[:, gcol + kc:gcol + kc + 1],
                rstd[:, rb, 0:n], ALU.mult, ALU.mult,
                [("h", kc, tbi), "gains", ("rstd", rb)], [("u", kc, tbi)])

    def norm_part1(sq8, tbi, off, n):
        for kc in range(8):
            act(sq8[:, kc, 0:n], h[:, kc, off:off + n], AF.Square, [("h", kc, tbi)], [("sq8", kc)])

    def norm_part2(sq8, tbi, off, n, gcol):
        pn = bank(6)
        for kc in range(8):
            mm(pn[:, 0:n], ones_bf[:], sq8[:, kc, 0:n], kc == 0, kc == 7, ["ones_bf", ("sq8", kc)], [("ps", 6)])
        rb = tbi % 2
        act(rstd[:, rb, 0:n], pn[:, 0:n], AF.Ln, [("ps", 6)], [("rstd", rb)], scale=1.0 / D, bias=EPS)
        act(rstd[:, rb, 0:n], rstd[:, rb, 0:n], AF.Exp, [("rstd", rb)], [("rstd", rb)], scale=-0.5)
        for kc in range(8):
            stt(u[:, kc, off:off + n], h[:, kc, off:off + n], gains[:, gcol + kc:gcol + kc + 1],
                rstd[:, rb, 0:n], ALU.mult, ALU.mult,
                [("h", kc, tbi), "gains", ("rstd", rb)], [("u", kc, tbi)])

    def ffn(tbs, lazy, next_gcol=None):
        cnt = 0
        for g in range(6):
            nf = 4 if g < 5 else 2
            ia, wa, ka = wnext(8, nf * 128)
            ib, wb, kb = wnext(8, nf * 128)
            for tbi, (off, n) in enumerate(tbs):
                if g == 0 and tbi > 0:
                    lazy(tbi)
                for fl in range(nf):
                    fc = g * 4 + fl
                    pa, pbk = bank(cnt % 2), bank(2 + cnt % 2)
                    ka_, kb_ = ("ps", cnt % 2), ("ps", 2 + cnt % 2)
                    for kc in range(8):
                        mm(pa[:, 0:n], wa[:, kc, fl * 128:(fl + 1) * 128], u[:, kc, off:off + n], kc == 0, kc == 7,
                           [ka, ("u", kc, tbi)], [ka_])
                    for kc in range(8):
                        mm(pbk[:, 0:n], wb[:, kc, fl * 128:(fl + 1) * 128], u[:, kc, off:off + n], kc == 0, kc == 7,
                           [kb, ("u", kc, tbi)], [kb_])
                    sb_ = cnt % 2
                    act(stmp[:, sb_, 0:n], pa[:, 0:n], AF.Silu, [ka_], [("stmp", sb_)])
                    tt(gbuf[:, fc, off:off + n], stmp[:, sb_, 0:n], pbk[:, 0:n], ALU.mult,
                       [("stmp", sb_), kb_], [("g", fc, tbi)])
                    cnt += 1
            wrelease(ia)
            wrelease(ib)
        cnt = 0
        for dm in range(8):
            io, wo, ko = wnext(NFC, 128)
            for tbi, (off, n) in enumerate(tbs):
                po = bank(4 + cnt % 2)
                kp = ("ps", 4 + cnt % 2)
                for fc in range(NFC):
                    mm(po[:, 0:n], wo[:, fc, :], gbuf[:, fc, off:off + n], fc == 0, fc == NFC - 1,
                       [ko, ("g", fc, tbi)], [kp])
                stt(h[:, dm, off:off + n], po[:, 0:n], 0.5, h[:, dm, off:off + n], ALU.mult, ALU.add,
                    [kp, ("h", dm, tbi)], [("h", dm, tbi)])
                cnt += 1
                if next_gcol is not None and dm == 7:
                    if tbi == 0:
                        norm_part1(sq8f, 0, tbs[0][0], tbs[0][1])
                    elif tbi == 1:
                        norm_part2(sq8f, 0, tbs[0][0], tbs[0][1], next_gcol)
            wrelease(io)

    def mixing(hi, tok0, tbs, blocks, lazy, next_gcol=None):
        mix_start = st["cur"]

        def bail():
            skip_tiles(mix_start + 15 - st["cur"])

        T = sum(n for _, n in tbs)
        def rtab_load(tbi_, hp_):
            off_, n_ = tbs[tbi_]
            gtb = 4 if n_ == 64 else (tok0 + off_) // 512
            src = rtab_d[gtb].rearrange("p (a b) -> p a b", a=8)[:, hp_ * 4:(hp_ + 1) * 4, 0:n_]
            dma("sp", rtab[:, hp_ * 4:(hp_ + 1) * 4, 0:n_], src, (), [("rtab", hp_)], ("rt", hp_))

        def ttab_load(bi_):
            dma("sp", ttab[:, bi_ % 3, :], ttab_d[blocks[bi_][3]], (), [("ttab", bi_ % 3)], ("tt", bi_ % 3))

        rtab_load(0, 0)
        rtab_load(0, 1)
        for bi_ in range(min(3, len(blocks))):
            ttab_load(bi_)
        ial, wal, kal = wnext(8, 16)
        i0, w0, k0 = wnext(8, 512)
        for tbi, (off, n) in enumerate(tbs):
            if tbi > 0:
                lazy(tbi)
            pA = bank(7)
            for kc in range(8):
                mm(pA[0:16, 0:n], wal[:, kc, 0:16], u[:, kc, off:off + n], kc == 0, kc == 7,
                   [kal, ("u", kc, tbi)], [("ps", 7)])
            S.op("act", (lambda o_, i_: (lambda e: e.copy(out=o_, in_=i_)))(alowT[0:16, off:off + n], pA[0:16, 0:n]),
                 [("ps", 7)], [("alowT", tbi)])
            pend_b = []
            for bi, (boff, bn, smp, gblk) in enumerate(blocks):
                if not (off <= boff < off + n) or cut <= 0.2:
                    continue
                lo = boff - off
                dp = bi % 2
                px = bank(6 + dp)
                kpx = ("ps", 6 + dp)
                mm(px[0:bn, 0:256], alowT[0:16, boff:boff + bn], wup[:, :], True, False,
                   [("alowT", tbi), "wup"], [kpx])
                mm(px[0:bn, 0:256], ones_row[0:1, 0:bn], balpha[0:1, :], False, True,
                   ["ones_row", "balpha"], [kpx])
                act(etmp[0:bn, dp, :], px[0:bn, 0:256], AF.Exp, [kpx], [("etmp", dp)], scale=-1.0)
                act(spb[0:bn, dp, :], etmp[0:bn, dp, :], AF.Ln, [("etmp", dp)], [("spb", dp)], bias=1.0)
                if cut <= 0.4:
                    continue
                S.op("dve", (lambda o_, i_: (lambda e: e.tensor_copy(out=o_, in_=i_)))(sphi[0:bn, dp, :], spb[0:bn, dp, :]),
                     [("spb", dp)], [("sphi", dp)])
                tt(splo[0:bn, dp, :], spb[0:bn, dp, :], sphi[0:bn, dp, :], ALU.subtract,
                   [("spb", dp), ("sphi", dp)], [("splo", dp)])
                def part_b(bi=bi, boff=boff, bn=bn, smp=smp, lo=lo, dp=dp):
                    ci = 2 if smp else 0
                    b5 = 4 + bi % 2
                    p5 = bank(b5)
                    for hp in range(2):
                        for xi, (spx, kx) in enumerate(((sphi, ("sphi", dp)), (splo, ("splo", dp)))):
                            mm(p5[:, hp * 128:hp * 128 + bn], spx[0:bn, dp, hp * 128:(hp + 1) * 128],
                               cmat[0:bn, ci, 0:bn], xi == 0, xi == 1, [kx, "cmat"], [("ps", b5)])
                    for xi, (spx, kx) in enumerate(((sphi, ("sphi", dp)), (splo, ("splo", dp)))):
                        mm(p5[0:bn, 256:512], cmat[0:bn, ci + 1, 0:bn], spx[0:bn, dp, 0:256], xi == 0, xi == 1,
                           [kx, "cmat"], [("ps", b5)])
                    for hp in range(2):
                        src = p5[:, hp * 128:hp * 128 + bn]
                        act(eb[:, hp, lo:lo + bn], src, AF.Exp, [("ps", b5)], [("eb", hp)])
                        act(enb[:, hp, lo:lo + bn], src, AF.Exp, [("ps", b5)], [("enb", hp)], scale=-1.0)
                        if smp:
                            act(decSA[:, hp, :], p5[:, hp * 128 + 3:hp * 128 + 64:4], AF.Exp, [("ps", b5)], ["decSA"])
                        else:
                            act(decA[:, hp, bi:bi + 1], p5[:, hp * 128 + bn - 1:hp * 128 + bn], AF.Exp,
                                [("ps", b5)], [("decA", hp, bi)])
                    act(ed[0:bn, bi, :], p5[0:bn, 256:512], AF.Exp, [("ps", b5)], [("ed", bi)])

                for f_ in pend_b:
                    f_()
                pend_b = [part_b]
            for f_ in pend_b:
                f_()
            pend_b = []
            for ch in range(4):
                if cut <= 0.8:
                    continue
                pq = bank(ch)
                for kc in range(8):
                    mm(pq[:, 0:n], w0[:, kc, ch * 128:(ch + 1) * 128], u[:, kc, off:off + n], kc == 0, kc == 7,
                       [k0, ("u", kc, tbi)], [("ps", ch)])
                if cut <= 0.85:
                    continue
                if ch < 2:
                    stt(qk[:, ch, off:off + n], pq[:, 0:n], 0.125, eb[:, ch, 0:n], ALU.mult, ALU.mult,
                        [("ps", ch), ("eb", ch)], [("qk", 0, tbi, ch)])
                elif cut <= 0.9:
                    continue
                elif cut <= 0.95:
                    tt(qk[:, ch, off:off + n], pq[:, 0:n], eb[:, ch - 2, 0:n], ALU.mult,
                       [("ps", ch), ("eb", ch - 2)], [("qk", 0, tbi, ch)])
                elif cut <= 0.97:
                    stt(qk[:, ch, off:off + n], pq[:, 0:n], 1.0, enb[:, ch - 2, 0:n], ALU.mult, ALU.mult,
                        [("ps", ch), ("enb", ch - 2)], [("qk", 0, tbi, ch)])
                else:
                    tt(qk[:, ch, off:off + n], pq[:, 0:n], enb[:, ch - 2, 0:n], ALU.mult,
                       [("ps", ch), ("enb", ch - 2)], [("qk", 0, tbi, ch)])
        wrelease(ial)
        wrelease(i0)
        if cut <= 1:
            return bail()
        i1, w1, k1 = wnext(8, 512)
        i2, w2, k2 = wnext(8, 512)
        cnt = 0
        for tbi, (off, n) in enumerate(tbs):
            for hp in range(2):
                for isk in range(2):
                    ch = hp + 2 * isk
                    b0 = (cnt % 2) * 2
                    pa, pbk = bank(b0), bank(b0 + 1)
                    for kc in range(8):
                        mm(pa[:, 0:n], w1[:, kc, ch * 128:(ch + 1) * 128], u[:, kc, off:off + n], kc == 0, kc == 7,
                           [k1, ("u", kc, tbi)], [("ps", b0)])
                    for kc in range(8):
                        mm(pbk[:, 0:n], w2[:, kc, ch * 128:(ch + 1) * 128], u[:, kc, off:off + n], kc == 0, kc == 7,
                           [k2, ("u", kc, tbi)], [("ps", b0 + 1)])
                    tt(tA[:, 0:n], pa[:, 0:n], rtab[:, hp * 4 + 2 * isk, 0:n], ALU.mult, [("ps", b0), ("rtab", hp)], ["tA"])
                    tt(tB[:, 0:n], pbk[:, 0:n], rtab[:, hp * 4 + 2 * isk + 1, 0:n], ALU.mult,
                       [("ps", b0 + 1), ("rtab", hp)], ["tB"])
                    qi = 4 + 2 * isk + hp
                    tt(qk[:, qi, off:off + n], tA[:, 0:n], tB[:, 0:n], ALU.add, ["tA", "tB"], [("qk", 1, tbi, qi)])
                    cnt += 1
                if tbi + 1 < len(tbs):
                    rtab_load(tbi + 1, hp)
        wrelease(i1)
        wrelease(i2)
        if cut <= 2:
            return bail()
        i3, w3, k3 = wnext(8, 512)
        for bi, (boff, bn, smp, gblk) in enumerate(blocks):
            tbi = min(boff // 512, len(tbs) - 1)
            tb_ = bi % 3
            pk = bank(bi % 2)
            kp = ("ps", bi % 2)
            for kc in range(8):
                mm(pk[0:bn, :], u[:, kc, boff:boff + bn], w3[:, kc, :], kc == 0, kc == 7, [k3, ("u", kc, tbi)], [kp])
            tt(kh[0:bn, bi, 0:256], pk[0:bn, 0:256], ed[0:bn, bi, :], ALU.mult, [kp, ("ed", bi)], [("kh", bi)])
            pr = pk[0:bn, 256:512].rearrange("p (h s e) -> p h s e", h=4, s=2)
            Ct = ttab[0:bn, tb_, 0:256].rearrange("p (h s e) -> p h s e", h=4, s=2)
            St = ttab[0:bn, tb_, 256:512].rearrange("p (h s e) -> p h s e", h=4, s=2)
            kr = kh[0:bn, bi, 256:512].rearrange("p (h s e) -> p h s e", h=4, s=2)
            tA4 = tA[0:bn, 0:256].rearrange("p (h s e) -> p h s e", h=4, s=2)
            tB4 = tB[0:bn, 0:256].rearrange("p (h s e) -> p h s e", h=4, s=2)
            for s_ in range(2):
                tt(tA4[:, :, s_, :], pr[:, :, s_, :], Ct[:, :, s_, :], ALU.mult, [kp, ("ttab", tb_)], ["tA"])
                tt(tB4[:, :, s_, :], pr[:, :, 1 - s_, :], St[:, :, s_, :], ALU.mult, [kp, ("ttab", tb_)], ["tB"])
                tt(kr[:, :, s_, :], tA4[:, :, s_, :], tB4[:, :, s_, :], ALU.add, ["tA", "tB"], [("kh", bi)])
            if bi + 3 < len(blocks):
                ttab_load(bi + 3)
        wrelease(i3)
        if cut <= 3:
            return bail()
        for vi in range(2):
            iv, wv, kv = wnext(8, 512)
            for bi, (boff, bn, smp, gblk) in enumerate(blocks):
                tbi = min(boff // 512, len(tbs) - 1)
                pv = bank(2 + bi % 2)
                kp = ("ps", 2 + bi % 2)
                for kc in range(8):
                    mm(pv[0:bn, :], u[:, kc, boff:boff + bn], wv[:, kc, :], kc == 0, kc == 7,
                       [kv, ("u", kc, tbi)], [kp])
                S.op("act", (lambda o_, i_: (lambda e: e.copy(out=o_, in_=i_)))(
                    vtm[0:bn, bi, vi * 512:(vi + 1) * 512], pv[0:bn, :]), [kp], [("v", bi, vi)])
            wrelease(iv)
        if cut <= 4:
            return bail()
        S.barrier(exclude=("pe",))
        igt = []
        wg = []
        kg = []
        for br in range(2):
            i_, w_, k_ = wnext(8, 512)
            igt.append(i_); wg.append(w_); kg.append(k_)
        ecnt = [0]

        def epilogue(br, hl, po, off, n, tbi, pokey, ebanks=((2, 3), (0, 1))):
            ob = ecnt[0] % 2
            ecnt[0] += 1
            tbk, rbk = ebanks[ob]
            act(sqb[:, ob, 0:n], po, AF.Square, [pokey], [("sqb", ob)])
            pR = bank(rbk)
            for kc in range(8):
                mm(pR[:, 0:n], wg[br][:, kc, hl * 128:(hl + 1) * 128], u[:, kc, off:off + n], kc == 0, kc == 7,
                   [kg[br], ("u", kc, tbi)], [("ps", rbk)])
            pT_ = bank(tbk)
            mm(pT_[:, 0:n], ones_bf[:], sqb[:, ob, 0:n], True, True, ["ones_bf", ("sqb", ob)], [("ps", tbk)])
            act(rs2[:, ob, 0:n], pT_[:, 0:n], AF.Ln, [("ps", tbk)], [("rs2", ob)], scale=1.0 / 128, bias=EPS)
            act(ee[:, ob, 0:n], pR[:, 0:n], AF.Exp, [("ps", rbk)], [("ee", ob)], scale=-1.0)
            act(ee[:, ob, 0:n], ee[:, ob, 0:n], AF.Ln, [("ee", ob)], [("ee", ob)], bias=1.0)
            stt(tt2[:, ob, 0:n], rs2[:, ob, 0:n], -0.5, ee[:, ob, 0:n], ALU.mult, ALU.subtract,
                [("rs2", ob), ("ee", ob)], [("tt2", ob)])
            act(tt2[:, ob, 0:n], tt2[:, ob, 0:n], AF.Exp, [("tt2", ob)], [("tt2", ob)])
            gc = (G_GNA if br == 0 else G_GNR) + hl
            stt(tt1[:, ob, 0:n], po, gains[:, gc:gc + 1], tt2[:, ob, 0:n], ALU.mult, ALU.mult,
                [pokey, "gains", ("tt2", ob)], [("tt1", ob)])
            qi = br * 4 + hl
            wkeys = [("qk", br, tbi, br * 4 + j) for j in range(4)]
            tt(qk[:, qi, off:off + n], tt1[:, ob, 0:n], pR[:, 0:n], ALU.mult, [("tt1", ob), ("ps", rbk)], wkeys)

        scnt = [0]
        SX = (S0f_buf, tmpS_buf)

        def s0_load(sidx):
            br_, pair_ = sidx // 2, sidx % 2
            src = st_in[br_][:, pair_ * 2:pair_ * 2 + 2].rearrange("s h d v -> (h d) s v")
            sb_ = sidx % 2
            dma("pool", S0bf[:, sb_, :].rearrange("p (s v) -> p s v", s=16), src, (), [("S0bf", sb_)], ("s0b", sb_))
            x_ = SX[sidx % 2]
            kx = ("SX", sidx % 2)
            dma("sp", x_[:].rearrange("p (s v) -> p s v", s=16), src, (), [kx, kx + (0,), kx + (1,)], ("s0f", sidx % 2))

        if hi == 1:
            s0_load(0)
        for tbi, (off, n) in enumerate(tbs):
            smp_tb = (n == 64)
            rkeys_all = lambda br: [("qk", br, tbi, br * 4 + j) for j in range(4)]
            for br in range(2):
                qb, kb_i = br * 4, br * 4 + 2
                if not smp_tb:
                    tblocks = [(bi, b) for bi, b in enumerate(blocks) if off <= b[0] < off + n]
                    pend = []

                    def o_ops(c4, bi, boff, bn, gblk):
                        par = c4 % 2
                        for hl in range(4):
                            pair, h2 = hl // 2, hl % 2
                            sidx = br * 2 + pair
                            pr_ = slice(h2 * 64, h2 * 64 + 64)
                            po = bank(4 + hl)[:, c4 * 128:(c4 + 1) * 128]
                            vc = br * 512 + hl * 128
                            mm(po, vtm[:, bi, vc:vc + 128], PT[:, par * 4 + h2 * 2 + pair, :], True, False,
                               [("v", bi, br), ("PT", par)], [("ps", 4 + hl)])
                            mm(po, Sbf[pr_, (gblk - 1) % 3, sidx, :], qk[pr_, qb + pair, boff:boff + bn], False, True,
                               [("Sb", (gblk - 1) % 3, br)] + rkeys_all(br), [("ps", 4 + hl)])

                    for c4, (bi, (boff, bn, smp, gblk)) in enumerate(tblocks):
                        par = c4 % 2
                        ubk = (3, 1)[par]
                        pu = bank(ubk)
                        for pair in range(2):
                            kc0 = br * 256 + pair * 128
                            vc0 = br * 512 + pair * 256
                            mm(pu[:, pair * 256:(pair + 1) * 256], kh[:, bi, kc0:kc0 + 128], vtm[:, bi, vc0:vc0 + 256],
                               True, True, [("kh", bi), ("v", bi, br)], [("ps", ubk)])
                        for hl in (0, 2, 1, 3):
                            pair, h2 = hl // 2, hl % 2
                            pr_ = slice(h2 * 64, h2 * 64 + 64)
                            sb2 = (0, 2)[h2]
                            mm(bank(sb2)[:, pair * 128:(pair + 1) * 128], qk[pr_, kb_i + pair, boff:boff + bn],
                               qk[pr_, qb + pair, boff:boff + bn], True, True, rkeys_all(br), [("ps", sb2)])
                        for f_ in pend:
                            f_()
                        pend = []
                        for h2 in range(2):
                            sb2 = (0, 2)[h2]
                            tt(PT[:, par * 4 + h2 * 2:par * 4 + h2 * 2 + 2, :],
                               bank(sb2)[:, 0:256].rearrange("p (a b) -> p a b", a=2),
                               masks[:, 0, :].unsqueeze(1).to_broadcast([128, 2, 128]), ALU.mult,
                               [("ps", sb2), "masks"], [("PT", par)])
                        for pair in range(2):
                            sidx = br * 2 + pair
                            for h2 in range(2):
                                pr_ = slice(h2 * 64, h2 * 64 + 64)
                                if br == 0:
                                    dsc = decA[pr_, pair, bi:bi + 1]
                                    dk_ = [("decA", pair, bi)]
                                else:
                                    dsc = gains[pr_, G_DECR + pair:G_DECR + pair + 1]
                                    dk_ = ["gains"]
                                stt(Sst[pr_, sidx, :], Sst[pr_, sidx, :], dsc,
                                    pu[pr_, pair * 256 + h2 * 128:pair * 256 + (h2 + 1) * 128],
                                    ALU.mult, ALU.add, ["S%d" % sidx, ("ps", ubk)] + dk_, ["S%d" % sidx])
                        S.op("act", (lambda o_, i_: (lambda e: e.copy(out=o_, in_=i_)))(
                            Sbf[:, gblk % 3, br * 2:br * 2 + 2, :], Sst[:, br * 2:br * 2 + 2, :]),
                            ["S%d" % (br * 2), "S%d" % (br * 2 + 1)], [("Sb", gblk % 3, br)])
                        pend.append((lambda a, b, c, d, e_: (lambda: o_ops(a, b, c, d, e_)))(c4, bi, boff, bn, gblk))
                    for f_ in pend:
                        f_()
                    for hl in range(4):
                        eb_ = ((2, 3), (0, 1)) if hl < 2 else ((2, 4), (0, 5))
                        epilogue(br, hl, bank(4 + hl)[:, 0:n], off, n, tbi, ("ps", 4 + hl), eb_)
                else:
                    bi = len(blocks) - 1
                    boff, bn, smp, gblk = blocks[bi]
                    for pair in range(2):
                        sidx = br * 2 + pair
                        sb_ = sidx % 2
                        S0f, tmpS = (SX[sidx % 2], SX[1 - sidx % 2])
                        kS0f, kTmp = ("SX", sidx % 2), ("SX", 1 - sidx % 2)
                        S0b3 = S0bf[:, sb_, :].rearrange("p (s v) -> p s v", s=16)
                        S0f3 = S0f[:].rearrange("p (s v) -> p s v", s=16)
                        tmp3 = tmpS[:].rearrange("p (s v) -> p s v", s=16)
                        for h2 in range(2):
                            hl = pair * 2 + h2
                            pr_ = slice(h2 * 64, h2 * 64 + 64)
                            slot = scnt[0] % 4
                            scnt[0] += 1
                            sbk = (0, 2)[slot % 2]
                            psS = bank(sbk)[0:64, 0:64]
                            mm(psS, qk[pr_, kb_i + pair, boff:boff + 64], qk[pr_, qb + pair, boff:boff + 64],
                               True, True, rkeys_all(br), [("ps", sbk)])
                            pts = slot + 4 * br
                            tt(PT[0:64, pts, 0:64], psS, masks[0:64, 1, 0:64], ALU.mult,
                               [("ps", sbk), "masks"], [("PT", pts // 4)])
                            if cut <= 4.93:
                                continue
                            po = bank(1)[:, hl * 64:(hl + 1) * 64]
                            vc = br * 512 + hl * 128
                            mm(po, vtm[0:64, bi, vc:vc + 128], PT[0:64, pts, 0:64], True, cut <= 4.935,
                               [("v", bi, br), ("PT", pts // 4)], [("ps", 1)])
                            po_ = slice((1 - h2) * 64, (1 - h2) * 64 + 64)
                            S.op("pool", (lambda a_: (lambda e: e.memset(a_, 0.0)))(qm[po_].rearrange("p a b -> p (a b)")),
                                 (), ["qm"])
                            tt(qm[pr_], qk[pr_, qb + pair, boff:boff + 64].unsqueeze(1).to_broadcast([64, 16, 64]),
                               maskq[pr_], ALU.mult, rkeys_all(br) + ["maskq"], ["qm"])
                            for s_ in range(16):
                                if cut <= 4.935:
                                    continue
                                mm(po, S0b3[:, s_, :], qm[:, s_, :], False, s_ == 15,
                                   [("S0bf", sb_), "qm"], [("ps", 1)])
                            if cut <= 4.94:
                                continue
                            V3 = Vblk[:].rearrange("p (s v) -> p s v", s=16)
                            tt(V3, vtm[0:64, bi, vc:vc + 128].unsqueeze(1).to_broadcast([64, 16, 128]),
                               smask[:, :].unsqueeze(2).to_broadcast([64, 16, 128]), ALU.mult,
                               [("v", bi, br), "smask"], ["Vblk"], eng="pool")
                            if cut <= 4.95:
                                continue
                            kc0 = br * 256 + pair * 128
                            for q4 in range(4):
                                mm(PSB[:, q4 * 512:(q4 + 1) * 512], kh[0:64, bi, kc0:kc0 + 128],
                                   Vblk[:, q4 * 512:(q4 + 1) * 512], True, True, [("kh", bi), "Vblk"], [("ps", 4 + q4)])
                            if h2 == 0:
                                if br == 0:
                                    tt(tmp3, S0f3, decSA[:, pair, :].unsqueeze(2).to_broadcast([128, 16, 128]),
                                       ALU.mult, [kS0f, "decSA"], [kTmp])
                                else:
                                    tsmul(tmpS[:, :], S0f[:, :], gains[:, G_DECR + 2 + pair:G_DECR + 3 + pair],
                                          [kS0f, "gains"], [kTmp])
                            tt(S0f[pr_, :], tmpS[pr_, :], PSB[pr_, :], ALU.add,
                               [kTmp] + [("ps", 4 + q) for q in range(4)], [kS0f + (h2,)])
                        if sidx < 3:
                            s0_load(sidx + 1)
                        dst = st_out[br][:, pair * 2:pair * 2 + 2].rearrange("s h d v -> (h d) s v")
                        dma("sp", dst, S0f3, [kS0f, kS0f + (0,), kS0f + (1,)], (), ("s0o", sidx % 2))
                    for hl in range(4):
                        if cut <= 4.98:
                            continue
                        epilogue(br, hl, bank(1)[:, hl * 64:(hl + 1) * 64], off, n, tbi, ("ps", 1), ((2, 3), (0, 5)))
        if hi == 1:
            for br in range(2):
                for pair in range(2):
                    sidx = br * 2 + pair
                    dma("sp", sp_out[br][pair * 128:(pair + 1) * 128, :], Sst[:, sidx, :], ["S%d" % sidx], (), "spo")
        wrelease(igt[0])
        wrelease(igt[1])
        if cut <= 5:
            return bail()
        S.barrier(exclude=("pe",))
        cnt = 0
        for dmg in range(2):
            iga, wga, kga = wnext(8, 512)
            igr, wgr, kgr = wnext(8, 512)
            iwo, wwo, kwo = wnext(8, 512)
            for tbi, (off, n) in enumerate(tbs):
                okeys = lambda br: [("qk", br, tbi, br * 4 + j) for j in range(4)]
                for dl in range(4):
                    dm = dmg * 4 + dl
                    b0 = (cnt % 2) * 4
                    cnt += 1
                    pga, pgr, pma, pmr = bank(b0), bank(b0 + 1), bank(b0 + 2), bank(b0 + 3)
                    cs = slice(dl * 128, (dl + 1) * 128)
                    for kc in range(8):
                        mm(pga[:, 0:n], wga[:, kc, cs], u[:, kc, off:off + n], kc == 0, kc == 7,
                           [kga, ("u", kc, tbi)], [("ps", b0)])
                    for kc in range(8):
                        mm(pgr[:, 0:n], wgr[:, kc, cs], u[:, kc, off:off + n], kc == 0, kc == 7,
                           [kgr, ("u", kc, tbi)], [("ps", b0 + 1)])
                    for fc in range(4):
                        mm(pma[:, 0:n], wwo[:, fc, cs], qk[:, fc, off:off + n], fc == 0, fc == 3,
                           [kwo] + okeys(0), [("ps", b0 + 2)])
                    for fc in range(4, 8):
                        mm(pmr[:, 0:n], wwo[:, fc, cs], qk[:, fc, off:off + n], fc == 4, fc == 7,
                           [kwo] + okeys(1), [("ps", b0 + 3)])
                    act(tA[:, 0:n], pga[:, 0:n], AF.Tanh, [("ps", b0)], ["tA"], scale=0.5)
                    act(tB[:, 0:n], pgr[:, 0:n], AF.Tanh, [("ps", b0 + 1)], ["tB"], scale=0.5)
                    stt(tC[:, 0:n], tA[:, 0:n], 1.0, pma[:, 0:n], ALU.add, ALU.mult, ["tA", ("ps", b0 + 2)], ["tC"])
                    stt(tD[:, 0:n], tB[:, 0:n], 1.0, pmr[:, 0:n], ALU.add, ALU.mult, ["tB", ("ps", b0 + 3)], ["tD"])
                    tt(tA[:, 0:n], tC[:, 0:n], tD[:, 0:n], ALU.add, ["tC", "tD"], ["tA"])
                    stt(h[:, dm, off:off + n], tA[:, 0:n], 0.5, h[:, dm, off:off + n], ALU.mult, ALU.add,
                        ["tA", ("h", dm, tbi)], [("h", dm, tbi)])
                    if next_gcol is not None and dmg == 1:
                        if tbi == 0 and dl == 3:
                            norm_part1(sq8m, 0, tbs[0][0], tbs[0][1])
                        elif tbi == 1 and dl == 1:
                            norm_part2(sq8m, 0, tbs[0][0], tbs[0][1], next_gcol)
            wrelease(iga)
            wrelease(igr)
            wrelease(iwo)

    def ple(tok0, tbs, lazy):
        cnt = 0
        for dmg in range(2):
            ig, wg_, kg_ = wnext(8, 512)
            ip, wp_, kp_ = wnext(2, 512)
            for tbi, (off, n) in enumerate(tbs):
                if dmg == 0 and tbi > 0:
                    lazy(tbi)
                for dl in range(4):
                    dm = dmg * 4 + dl
                    b0 = (cnt % 2) * 2
                    cnt += 1
                    pg, pp = bank(b0), bank(b0 + 1)
                    cs = slice(dl * 128, (dl + 1) * 128)
                    for kc in range(8):
                        mm(pg[:, 0:n], wg_[:, kc, cs], u[:, kc, off:off + n], kc == 0, kc == 7,
                           [kg_, ("u", kc, tbi)], [("ps", b0)])
                    for kc in range(2):
                        mm(pp[:, 0:n], wp_[:, kc, cs], pb[:, kc, off:off + n], kc == 0, kc == 1,
                           [kp_, "pb"], [("ps", b0 + 1)])
                    sb_ = cnt % 2
                    act(stmp[:, sb_, 0:n], pg[:, 0:n], AF.Tanh, [("ps", b0)], [("stmp", sb_)], scale=0.5)
                    stt(stmp2[:, sb_, 0:n], stmp[:, sb_, 0:n], 1.0, pp[:, 0:n], ALU.add, ALU.mult,
                        [("stmp", sb_), ("ps", b0 + 1)], [("stmp2", sb_)])
                    stt(h[:, dm, off:off + n], stmp2[:, sb_, 0:n], 0.5, h[:, dm, off:off + n], ALU.mult, ALU.add,
                        [("stmp2", sb_), ("h", dm, tbi)], [("h", dm, tbi)])
            wrelease(ig)
            wrelease(ip)

    def skip_tiles(k):
        for _ in range(k):
            i = st["cur"]
            st["cur"] += 1
            wrelease(i)

    yv = yT.rearrange("(k p) t -> p k t", p=128)
    xv = xT.rearrange("(k p) t -> p k t", p=128)
    halves = [(0, [(0, 512), (512, 512)]), (1024, [(0, 512), (512, 512), (1024, 64)])]
    for hi, (tok0, tbs) in enumerate(halves[:nhalves]):
        T = sum(n for _, n in tbs)
        blocks = [(b * 128, 128, False, (tok0 + b * 128) // 128) for b in range(8)]
        if hi == 1:
            blocks.append((1024, 64, True, 16))
        def xload(tok0_, tbi, off, n):
            dma("sp", h[:, :, off:off + n], xv[:, :, tok0_ + off:tok0_ + off + n], (),
                [("h", kc, tbi) for kc in range(8)], ("x", tbi))

        if hi == 0:
            for tbi, (off, n) in enumerate(tbs):
                xload(tok0, tbi, off, n)
            late_consts()

        def mk_lazy(gcol):
            return lambda tbi: norm_u_tb(tbi, tbs[tbi][0], tbs[tbi][1], gcol)

        norm_u_tb(0, tbs[0][0], tbs[0][1], G_FFN1)
        ffn(tbs, mk_lazy(G_FFN1), G_MIX if (stage >= 2 and cut >= 99) else None)
        if stage >= 2:
            if cut < 99:
                norm_u_tb(0, tbs[0][0], tbs[0][1], G_MIX)
            S.barrier(exclude=("pe",))
            mixing(hi, tok0, tbs, blocks, mk_lazy(G_MIX), G_FFN2 if stage >= 3 else None)
        else:
            skip_tiles(15)
        if stage >= 3:
            if cut < 99:
                norm_u_tb(0, tbs[0][0], tbs[0][1], G_FFN2)
            S.barrier(exclude=("pe",))
            if stage >= 4:
                T_ = sum(n for _, n in tbs)
                dma("pool", pb[:, :, 0:T_], pT.rearrange("(k p) t -> p k t", p=128)[:, :, tok0:tok0 + T_], (),
                    ["pb"], "pb")
            ffn(tbs, mk_lazy(G_FFN2), G_PLE if stage >= 4 else None)
        else:
            S.barrier(exclude=("pe",))
            skip_tiles(20)
        if stage >= 4:
            ple(tok0, tbs, mk_lazy(G_PLE))
        else:
            skip_tiles(4)
        if stage >= 5:
            for tbi, (off, n) in enumerate(tbs):
                pn = bank(6)
                for kc in range(8):
                    b = kc % 2
                    act(sq[:, b, 0:n], h[:, kc, off:off + n], AF.Square, [("h", kc, tbi)], [("sq", b)])
                    mm(pn[:, 0:n], ones_bf[:], sq[:, b, 0:n], kc == 0, kc == 7, ["ones_bf", ("sq", b)], [("ps", 6)])
                rb = tbi % 2
                act(rstd[:, rb, 0:n], pn[:, 0:n], AF.Ln, [("ps", 6)], [("rstd", rb)], scale=1.0 / D, bias=EPS)
                act(rstd[:, rb, 0:n], rstd[:, rb, 0:n], AF.Exp, [("rstd", rb)], [("rstd", rb)], scale=-0.5)
                for kc in range(8):
                    stt(yout[:, rb, kc, 0:n], h[:, kc, off:off + n], gains[:, G_FIN + kc:G_FIN + kc + 1],
                        rstd[:, rb, 0:n], ALU.mult, ALU.mult,
                        [("h", kc, tbi), "gains", ("rstd", rb)], [("yout", rb)])
                dma("sp", yv[:, :, tok0 + off:tok0 + off + n], yout[:, rb, :, 0:n], [("yout", rb)], (), ("yo", rb))
                if stage >= 5 and hi + 1 < len(halves[:nhalves]):
                    ntok0, ntbs = halves[hi + 1]
                    xload(ntok0, tbi, ntbs[tbi][0], ntbs[tbi][1])
                    if tbi == len(tbs) - 1:
                        for t2 in range(len(tbs), len(ntbs)):
                            xload(ntok0, t2, ntbs[t2][0], ntbs[t2][1])
        else:
            for tbi, (off, n) in enumerate(tbs):
                dma("sp", yv[:, :, tok0 + off:tok0 + off + n], h[:, :, off:off + n],
                    [("h", kc, tbi) for kc in range(8)], (), "yo")
            if hi + 1 < len(halves[:nhalves]):
                ntok0, ntbs = halves[hi + 1]
                for t2 in range(len(ntbs)):
                    xload(ntok0, t2, ntbs[t2][0], ntbs[t2][1])
    S.emit(nc)
    return nc


_CONST_CACHE = {}


def _prep_inputs(inp):
    f32 = np.float32
    inp = {k: np.asarray(v) for k, v in inp.items()}
    wflat, sched = _weight_tiles(inp)
    ct = _CONST_CACHE.get("ct")
    if ct is None:
        ct = _const_tables()
        _CONST_CACHE["ct"] = ct

    def col8(v):
        return np.asarray(v, f32).reshape(-1, 128).T

    cpack = np.zeros((128, 52), f32)
    cpack[:, 0:8] = col8(inp["norm_ffn1"][0])
    cpack[:, 8:16] = col8(inp["norm_mix"][0])
    cpack[:, 16:24] = col8(inp["norm_ffn2"][0])
    cpack[:, 24:32] = col8(inp["norm_ple"][0])
    cpack[:, 32:40] = col8(inp["norm_final"])
    cpack[:, 40:44] = col8(inp["gn_gla"][0])
    cpack[:, 44:48] = col8(inp["gn_ret"][0])
    cpack[:, 48:52] = ct["decr"]
    xp, xs = inp["x_prompt"], inp["x_sample"]
    pp, psm = inp["p_prompt"][0], inp["p_sample"][0]
    in_maps = []
    for c in range(NCORES):
        xc = np.concatenate([xp[c], xs[16 * c:16 * c + 16].reshape(NSAMP, D)], axis=0)
        pc = np.concatenate([pp[c], psm[16 * c:16 * c + 16].reshape(NSAMP, 256)], axis=0)
        in_maps.append({
            "xT": np.ascontiguousarray(xc.T.astype(f32)),
            "pT": np.ascontiguousarray(pc.T.astype(f32)),
            "wflat": wflat,
            "cpack": cpack,
            "cmat": ct["cmat"], "masks": ct["masks"], "smask": ct["smask"], "maskq": ct["maskq"],
            "wup": np.ascontiguousarray(inp["w_alpha_up"][0].astype(f32)),
            "balpha": np.ascontiguousarray(inp["b_alpha"][0].reshape(1, 256).astype(f32)),
            "rtab": ct["rtab"], "ttab": ct["ttab"],
            "sg_in": np.ascontiguousarray(inp["state_gla"][0, 16 * c:16 * c + 16].astype(f32)),
            "sr_in": np.ascontiguousarray(inp["state_ret"][0, 16 * c:16 * c + 16].astype(f32)),
        })
    return in_maps


def _run(inp, stage=5):
    in_maps = _prep_inputs(inp)
    nc = build_program(stage)
    res = run_bass_kernel_spmd(nc, in_maps, core_ids=list(range(NCORES)))
    return res.results


def kernel(**inputs):
    results = _run(inputs, 5)
    f32 = np.float32
    y_prompt = np.zeros((8, TP, D), f32)
    y_sample = np.zeros((128, 4, D), f32)
    gp = np.zeros((1, 8, 4, 64, 128), f32); rp = np.zeros((1, 8, 4, 64, 128), f32)
    gs = np.zeros((1, 128, 4, 64, 128), f32); rs = np.zeros((1, 128, 4, 64, 128), f32)
    for c, r in enumerate(results):
        yc = np.asarray(r["yT"]).T
        y_prompt[c] = yc[:TP]
        y_sample[16 * c:16 * c + 16] = yc[TP:].reshape(16, 4, D)
        gp[0, c] = np.asarray(r["sgp"]).reshape(4, 64, 128)
        rp[0, c] = np.asarray(r["srp"]).reshape(4, 64, 128)
        gs[0, 16 * c:16 * c + 16] = np.asarray(r["sgs"])
        rs[0, 16 * c:16 * c + 16] = np.asarray(r["srs"])
    return (y_prompt, y_sample, gp, rp, gs, rs)
```

```python
import numpy as np
import ml_dtypes
import concourse.bass as bass
import concourse.mybir as mybir
from concourse.bass_utils import run_bass_kernel_spmd
from contextlib import ExitStack

F32 = mybir.dt.float32
BF16 = mybir.dt.bfloat16
AF = mybir.ActivationFunctionType
ALU = mybir.AluOpType

NCORES = 8
D = 1024
DFF = 2816
NFC = 22
TP = 2048
NSAMP = 64
TALL = TP + NSAMP
EPS = 1e-6
RING = 5
SLOT = 4096

ENGS = ("pe", "act", "dve", "pool", "sp")


class Op:
    __slots__ = ("eng", "fn", "reads", "writes", "dsem", "chan", "deps", "signaled", "count", "idx")

    def __init__(self, eng, fn, reads, writes, dsem):
        self.eng = eng
        self.fn = fn
        self.reads = tuple(reads)
        self.writes = tuple(writes)
        self.dsem = dsem
        self.chan = ("dma", dsem) if dsem is not None else ("eng", eng)
        self.deps = []
        self.signaled = dsem is not None
        self.count = 0


class Sched:
    def __init__(self):
        self.ops = []
        self.dma_sems = []
        self.barriers = []

    def op(self, eng, fn, reads=(), writes=(), dsem=None):
        o = Op(eng, fn, reads, writes, dsem)
        o.idx = len(self.ops)
        self.ops.append(o)
        if dsem is not None and dsem not in self.dma_sems:
            self.dma_sems.append(dsem)
        return o

    def barrier(self, exclude=()):
        self.barriers.append((len(self.ops), tuple(exclude)))

    def analyze(self):
        last_w, last_r = {}, {}
        waited = {e: {} for e in ENGS}
        chan_last = {}
        pending = {e: None for e in ENGS}
        bars = list(self.barriers)
        bi = 0
        ops = self.ops
        for o in ops:
            while bi < len(bars) and bars[bi][0] <= o.idx:
                snap = {ch: i for ch, i in chan_last.items()
                        if not (ch[0] == "dma" and isinstance(ch[1], tuple) and ch[1][0] == "w")}
                for e in ENGS:
                    if e in bars[bi][1]:
                        continue
                    if pending[e] is None:
                        pending[e] = snap
                    else:
                        m = dict(pending[e])
                        m.update(snap)
                        pending[e] = m
                bi += 1
            deps = {}

            def add(d):
                for ch, i in d.items():
                    if deps.get(ch, -1) < i:
                        deps[ch] = i

            if pending[o.eng] is not None:
                add(pending[o.eng])
                pending[o.eng] = None
            for k in o.reads:
                add(last_w.get(k, {}))
            for k in o.writes:
                add(last_w.get(k, {}))
                add(last_r.get(k, {}))
            w = waited[o.eng]
            for ch, i in deps.items():
                if ch == ("eng", "pe") and o.eng == "pe":
                    continue
                if w.get(ch, -1) >= i:
                    continue
                w[ch] = i
                o.deps.append(i)
                ops[i].signaled = True
            for k in o.reads:
                last_r.setdefault(k, {})[o.chan] = o.idx
            for k in o.writes:
                last_w[k] = {o.chan: o.idx}
                last_r[k] = {}
            chan_last[o.chan] = o.idx
        cnt = {}
        for o in ops:
            if o.signaled:
                inc = 16 if o.dsem is not None else 1
                cnt[o.chan] = cnt.get(o.chan, 0) + inc
                o.count = cnt[o.chan]
        self.final_counts = cnt

    def emit(self, nc, final_wait_eng="sp"):
        self.analyze()
        with ExitStack() as es:
            sems = {}
            for e in ENGS:
                sems[("eng", e)] = es.enter_context(nc.semaphore("sem_" + e))
            for i, d in enumerate(self.dma_sems):
                sems[("dma", d)] = es.enter_context(nc.semaphore("dsem_%d" % i))
            block = es.enter_context(nc.Block())
            ops = self.ops

            def run(engname):
                def body(eng):
                    for o in ops:
                        if o.eng != engname:
                            continue
                        for i in o.deps:
                            d = ops[i]
                            eng.wait_ge(sems[d.chan], d.count)
                        ins = o.fn(eng)
                        if o.signaled:
                            ins.then_inc(sems[o.chan], 16 if o.dsem is not None else 1)
                    if engname == final_wait_eng:
                        for ch, c in self.final_counts.items():
                            eng.wait_ge(sems[ch], c)
                return body

            block.tensor(run("pe"))
            block.scalar(run("act"))
            block.vector(run("dve"))
            block.gpsimd(run("pool"))
            block.sync(run("sp"))


def _tile_kc(W, cols):
    nk = W.shape[0] // 128
    sub = W[:, cols]
    return np.ascontiguousarray(sub.reshape(nk, 128, sub.shape[1]).transpose(1, 0, 2).reshape(128, -1))


def _swap_heads(idx):
    idx = np.asarray(idx).reshape(-1, 2, 32)
    return idx[:, ::-1, :].reshape(-1)


_QA, _KA, _VA, _RA = np.arange(0, 256), np.arange(256, 512), np.arange(512, 1024), np.arange(1024, 1536)
_QR, _KR, _VR, _GR = np.arange(1536, 1792), np.arange(1792, 2048), np.arange(2048, 2560), np.arange(2560, 3072)
_AL = np.arange(3072, 3088)
_GA, _GRT = np.arange(3088, 4112), np.arange(4112, 5136)


def _weight_tiles(inp):
    tiles = []

    def ffn(w_in, w_out):
        for g in range(6):
            wd = 512 if g < 5 else 256
            tiles.append(_tile_kc(w_in, np.arange(g * 512, g * 512 + wd)))
            tiles.append(_tile_kc(w_in, np.arange(DFF + g * 512, DFF + g * 512 + wd)))
        for dm in range(8):
            tiles.append(_tile_kc(w_out, np.arange(dm * 128, dm * 128 + 128)))

    ffn(inp["w_ffn1_in"][0], inp["w_ffn1_out"][0])
    wi = inp["w_in"][0]
    tiles.append(_tile_kc(wi, _AL))
    tiles.append(_tile_kc(wi, np.concatenate([_QA, _KA])))
    tiles.append(_tile_kc(wi, np.concatenate([_QR, _KR])))
    tiles.append(_tile_kc(wi, np.concatenate([_swap_heads(_QR), _swap_heads(_KR)])))
    tiles.append(_tile_kc(wi, np.concatenate([_KA, _KR])))
    tiles.append(_tile_kc(wi, _VA))
    tiles.append(_tile_kc(wi, _VR))
    tiles.append(_tile_kc(wi, _RA))
    tiles.append(_tile_kc(wi, _GR))
    wo = inp["w_out"][0]
    for dmg in range(2):
        c = np.arange(dmg * 512, dmg * 512 + 512)
        tiles.append(_tile_kc(wi, _GA[c]))
        tiles.append(_tile_kc(wi, _GRT[c]))
        tiles.append(_tile_kc(wo, c))
    ffn(inp["w_ffn2_in"][0], inp["w_ffn2_out"][0])
    for dmg in range(2):
        c = np.arange(dmg * 512, dmg * 512 + 512)
        tiles.append(_tile_kc(inp["w_ple_gate"][0], c))
        tiles.append(_tile_kc(inp["w_ple_proj"][0], c))
    sched = []
    off = 0
    for t in tiles:
        assert t.shape[1] <= SLOT
        sched.append((off, t.shape[1]))
        off += t.shape[1]
    return np.ascontiguousarray(np.concatenate(tiles, axis=1).astype(np.float32)), sched


def _weight_schedule():
    Ls = []

    def ffn():
        for g in range(6):
            wd = 512 if g < 5 else 256
            Ls.extend([8 * wd, 8 * wd])
        Ls.extend([NFC * 128] * 8)

    ffn()
    Ls.append(8 * 16)
    Ls.extend([8 * 512] * 8)
    Ls.extend([8 * 512] * 6)
    ffn()
    Ls.extend([8 * 512, 2 * 512] * 2)
    sched, off = [], 0
    for L in Ls:
        sched.append((off, L))
        off += L
    return sched, off


def _const_tables():
    f32 = np.float32
    half = 32
    freq = 10000.0 ** (-(np.arange(half, dtype=np.float64) / float(half)))
    pos = np.concatenate([np.arange(TP), 16384 + (np.arange(NSAMP) % 4)]).astype(np.float64)
    il = np.concatenate([np.arange(TP) % 128, np.arange(NSAMP) % 4]).astype(np.float64)
    nchunk = np.concatenate([np.full(TP, 128.0), np.full(NSAMP, 4.0)])
    ang = pos[:, None] * freq[None, :]
    cos, sin = np.cos(ang), np.sin(ang)
    lg = np.log1p(-np.exp2(-5.0 - np.arange(4, dtype=np.float64)))
    d = np.arange(64)
    sgn = np.where(d < 32, -1.0, 1.0)
    CQ = np.zeros((2, 128, TALL)); SQ = np.zeros_like(CQ); CK = np.zeros_like(CQ); SK = np.zeros_like(CQ)
    for hp in range(2):
        for h2 in range(2):
            h = hp * 2 + h2
            up = np.exp((il + 1.0) * lg[h])[None, :]
            dn = np.exp(-(il + 1.0) * lg[h])[None, :] * 0.125
            c = cos[:, d % 32].T
            s = sin[:, d % 32].T * sgn[:, None]
            sl = slice(h2 * 64, h2 * 64 + 64)
            CQ[hp, sl] = c * up; SQ[hp, sl] = s * up; CK[hp, sl] = c * dn; SK[hp, sl] = s * dn
    rtab = np.zeros((5, 128, 8, 512), f32)
    for gtb in range(5):
        t0, n = (gtb * 512, 512) if gtb < 4 else (TP, NSAMP)
        for hp in range(2):
            for j, A in enumerate((CQ, SQ, CK, SK)):
                rtab[gtb, :, hp * 4 + j, :n] = A[hp, :, t0:t0 + n]
    ttab = np.zeros((17, 128, 2, 256), f32)
    for b in range(17):
        t0, n = (b * 128, 128) if b < 16 else (TP, NSAMP)
        for h in range(4):
            k = np.exp((nchunk[t0:t0 + n] - 1.0 - il[t0:t0 + n]) * lg[h])[:, None] * 0.125
            ttab[b, :n, 0, h * 64:(h + 1) * 64] = cos[t0:t0 + n][:, d % 32] * k
            ttab[b, :n, 1, h * 64:(h + 1) * 64] = sin[t0:t0 + n][:, d % 32] * sgn[None, :] * k
    decr = np.zeros((128, 4), f32)
    for hp in range(2):
        for h2 in range(2):
            h = hp * 2 + h2
            decr[h2 * 64:(h2 + 1) * 64, hp] = np.exp(128.0 * lg[h])
            decr[h2 * 64:(h2 + 1) * 64, 2 + hp] = np.exp(4.0 * lg[h])
    j = np.arange(128)[:, None]; i = np.arange(128)[None, :]
    cmat = np.zeros((128, 4, 128), f32)
    cmat[:, 0] = np.where(j <= i, -1.0 / 16, 0.0)
    cmat[:, 1] = np.where(j > i, -1.0 / 16, 0.0)
    same = (j // 4 == i // 4) & (j < 64) & (i < 64)
    cmat[:, 2] = np.where(same & (j <= i), -1.0 / 16, 0.0)
    cmat[:, 3] = np.where(same & (j > i), -1.0 / 16, 0.0)
    masks = np.zeros((128, 2, 128), f32)
    masks[:, 0] = (j <= i)
    masks[:, 1] = same & (j <= i)
    smask = (np.arange(64)[:, None] // 4 == np.arange(16)[None, :]).astype(f32)
    maskq = np.broadcast_to((np.arange(16)[:, None] == np.arange(64)[None, :] // 4).astype(f32).reshape(1, 1024),
                            (128, 1024)).copy()
    return dict(rtab=rtab.reshape(5, 128, 8 * 512), ttab=ttab.reshape(17, 128, 512), decr=decr,
                cmat=cmat.reshape(128, 512), masks=masks.reshape(128, 256), smask=smask, maskq=maskq)


def build_program(stage=5, nhalves=2, cut=99):
    nc = bass.Bass("TRN2", target_bir_lowering=False)
    S = Sched()
    wsched1, WTOT = _weight_schedule()

    def din(name, shape, dt=F32):
        return nc.dram_tensor(name, list(shape), dt, kind="ExternalInput").ap()

    def dout(name, shape, dt=F32):
        return nc.dram_tensor(name, list(shape), dt, kind="ExternalOutput").ap()

    xT = din("xT", [D, TALL]); pT = din("pT", [256, TALL])
    wflat = din("wflat", [128, WTOT])
    cpack = din("cpack", [128, 52])
    cmat_d = din("cmat", [128, 512]); masks_d = din("masks", [128, 256]); smask_d = din("smask", [64, 16])
    maskq_d = din("maskq", [128, 1024])
    wup_d = din("wup", [16, 256]); balpha_d = din("balpha", [1, 256])
    rtab_d = din("rtab", [5, 128, 8 * 512]); ttab_d = din("ttab", [17, 128, 512])
    sg_in = din("sg_in", [16, 4, 64, 128]); sr_in = din("sr_in", [16, 4, 64, 128])
    yT = dout("yT", [D, TALL])
    sgp = dout("sgp", [256, 128]); srp = dout("srp", [256, 128])
    sgs = dout("sgs", [16, 4, 64, 128]); srs = dout("srs", [16, 4, 64, 128])
    st_in = (sg_in, sr_in); st_out = (sgs, srs); sp_out = (sgp, srp)

    SB_BASE, SB_END = 16512, 229376
    ptr = [SB_BASE]

    def esz(dt):
        return 2 if dt == BF16 else 4

    def palloc(name, shape, dt):
        nbytes = int(np.prod(shape[1:])) * esz(dt)
        nbytes = (nbytes + 31) // 32 * 32
        t = nc.alloc_sbuf_tensor_at(name, list(shape), dt, offset=ptr[0])
        ptr[0] += nbytes
        return t

    TM = 1088
    h = palloc("h", [128, 8, TM], F32)
    u = palloc("u", [128, 8, TM], BF16)
    ring = [palloc("ring%d" % i, [128, SLOT], BF16) for i in range(RING)]
    gains = palloc("gains", [128, 52], F32)
    cmat = palloc("cmat_s", [128, 4, 128], BF16)
    masks = palloc("masks_s", [128, 2, 128], F32)
    smask = palloc("smask_s", [64, 16], F32)
    maskq = palloc("maskq_s", [128, 16, 64], BF16)
    wup = palloc("wup_s", [16, 256], BF16)
    balpha = palloc("balpha_s", [1, 256], BF16)
    ones_bf = palloc("ones_bf", [128, 128], BF16)
    ones_row = palloc("ones_row", [1, 128], BF16)
    Sst = palloc("Sst", [128, 4, 128], F32)
    Sbf = palloc("Sbf", [128, 3, 4, 128], BF16)
    decA = palloc("decA", [128, 2, 9], F32)
    decSA = palloc("decSA", [128, 2, 16], F32)
    sq = palloc("sq", [128, 2, 512], BF16)
    rstd = palloc("rstd", [128, 2, 512], F32)
    UB = ptr[0]
    USZ = SB_END - UB

    def ualloc(name, shape, dt, off):
        nbytes = int(np.prod(shape[1:])) * esz(dt)
        assert off % 32 == 0 and off + nbytes <= USZ, (name, off, nbytes, USZ)
        return nc.alloc_sbuf_tensor_at(name, list(shape), dt, offset=UB + off), off + (nbytes + 31) // 32 * 32

    gbuf, o = ualloc("gbuf", [128, NFC, TM], BF16, 0)
    stmp, o = ualloc("stmp", [128, 2, 512], F32, o)
    pb, o_pb = ualloc("pb", [128, 2, TM], BF16, o)
    stmp2, _ = ualloc("stmp2", [128, 2, 512], F32, o_pb)
    yout, _ = ualloc("yout", [128, 2, 8, 512], F32, o_pb + 4096)
    sq8f, _ = ualloc("sq8f", [128, 8, 512], BF16, o_pb + 4096 + 32768)
    sq8m, _ = ualloc("sq8m", [128, 8, 512], BF16, 17408)
    qk, o = ualloc("qk", [128, 8, TM], BF16, 0)
    vtm, o = ualloc("vtm", [128, 9, 1024], BF16, o)
    kh, o = ualloc("kh", [128, 9, 512], BF16, o)
    o_tr = o
    ed, o = ualloc("ed", [128, 9, 256], F32, o)
    alowT, o = ualloc("alowT", [16, TM], BF16, o)
    spb, o = ualloc("spb", [128, 2, 256], F32, o)
    sphi, o = ualloc("sphi", [128, 2, 256], BF16, o)
    splo, o = ualloc("splo", [128, 2, 256], BF16, o)
    etmp, o = ualloc("etmp", [128, 2, 256], F32, o)
    eb, o = ualloc("eb", [128, 2, 512], F32, o)
    enb, o = ualloc("enb", [128, 2, 512], F32, o)
    rtab, o = ualloc("rtab_s", [128, 8, 512], F32, o)
    ttab, o = ualloc("ttab_s", [128, 3, 512], F32, o)
    tA, o = ualloc("tA", [128, 512], F32, o)
    tB, o = ualloc("tB", [128, 512], F32, o)
    tC, o = ualloc("tC", [128, 512], F32, o)
    tD, o = ualloc("tD", [128, 512], F32, o)
    o = o_tr
    PT, o = ualloc("PT", [128, 8, 128], BF16, o)
    osb, o = ualloc("osb", [128, 2, 512], F32, o)
    sqb, o = ualloc("sqb", [128, 2, 512], BF16, o)
    rs2, o = ualloc("rs2", [128, 2, 512], F32, o)
    ee, o = ualloc("ee", [128, 2, 512], F32, o)
    tt1, o = ualloc("tt1", [128, 2, 512], F32, o)
    tt2, o = ualloc("tt2", [128, 2, 512], F32, o)
    S0bf, o = ualloc("S0bf", [128, 2, 16 * 128], BF16, o)
    S0f_buf, o = ualloc("S0f", [128, 16 * 128], F32, o)
    tmpS_buf, o = ualloc("tmpS", [128, 16 * 128], F32, o)
    Vblk, o = ualloc("Vblk", [64, 16 * 128], BF16, o)
    qm, o = ualloc("qm", [128, 16, 64], BF16, o)
    print("SBUF union size", USZ, "P2 end", o)

    PSA = nc.alloc_psum_tensor("PSA", [128, 2048], F32)
    PSB = nc.alloc_psum_tensor("PSB", [128, 2048], F32)

    def bank(b):
        t = PSA if b < 4 else PSB
        return t[:, (b % 4) * 512:(b % 4) * 512 + 512]

    def mm(out, lhsT, rhs, start, stop, reads, writes):
        S.op("pe", lambda e: e.matmul(out, lhsT=lhsT, rhs=rhs, start=start, stop=stop), reads, writes)

    def act(out, in_, func, reads, writes, scale=1.0, bias=0.0):
        S.op("act", lambda e: e.activation(out=out, in_=in_, func=func, bias=bias, scale=scale), reads, writes)

    def tt(out, in0, in1, op, reads, writes, eng="dve"):
        S.op(eng, lambda e: e.tensor_tensor(out=out, in0=in0, in1=in1, op=op), reads, writes)

    def stt(out, in0, scalar, in1, op0, op1, reads, writes):
        S.op("dve", lambda e: e.scalar_tensor_tensor(out=out, in0=in0, scalar=scalar, in1=in1, op0=op0, op1=op1),
             reads, writes)

    def tsmul(out, in0, scalar, reads, writes):
        S.op("dve", lambda e: e.tensor_scalar(out=out, in0=in0, scalar1=scalar, scalar2=None, op0=ALU.mult),
             reads, writes)

    def dma(eng, out, in_, reads, writes, dsem):
        S.op(eng, lambda e: e.dma_start(out=out, in_=in_), reads, writes, dsem=dsem)

    def memset(ap, val, writes):
        S.op("dve", lambda e: e.memset(ap, val), (), writes)

    full_sched = wsched1 + wsched1
    st = {"cur": 0}

    def issue(i):
        if i >= len(full_sched):
            return
        off, L = full_sched[i]
        slot = i % RING
        dma("pool", ring[slot][:, 0:L], wflat[:, off:off + L], (), [("ring", slot)], ("w", slot))

    def wnext(nk, ncols):
        i = st["cur"]
        st["cur"] += 1
        off, L = full_sched[i]
        assert L == nk * ncols, (i, L, nk, ncols)
        slot = i % RING
        return i, ring[slot][:, 0:L].rearrange("p (k c) -> p k c", k=nk), ("ring", slot)

    def wrelease(i):
        issue(i + RING)

    dma("sp", gains[:], cpack[:], (), ["gains"], "c0")

    def late_consts():
        dma("pool", cmat[:].rearrange("p a b -> p (a b)"), cmat_d[:], (), ["cmat"], "c1")
        dma("sp", masks[:].rearrange("p a b -> p (a b)"), masks_d[:], (), ["masks"], "c2")
        dma("sp", smask[:], smask_d[:], (), ["smask"], "c3")
        dma("pool", maskq[:].rearrange("p a b -> p (a b)"), maskq_d[:], (), ["maskq"], "c6")
        dma("pool", wup[:], wup_d[:], (), ["wup"], "c4")
        dma("pool", balpha[:], balpha_d[:], (), ["balpha"], "c5")
    memset(ones_bf[:], 1.0, ["ones_bf"])
    memset(ones_row[:], 1.0, ["ones_row"])
    memset(Sst[:].rearrange("p a b -> p (a b)"), 0.0, ["S0", "S1", "S2", "S3"])
    memset(Sbf[:].rearrange("p t a b -> p (t a b)"), 0.0, [("Sb", t_, b_) for t_ in range(3) for b_ in range(2)])
    for i in range(RING):
        issue(i)

    G_FFN1, G_MIX, G_FFN2, G_PLE, G_FIN, G_GNA, G_GNR, G_DECR = 0, 8, 16, 24, 32, 40, 44, 48

    def rmsnorm(tbs, gcol, dst_fn, dst_keys):
        for tbi, (off, n) in enumerate(tbs):
            pn = bank(6)
            for kc in range(8):
                b = kc % 2
                act(sq[:, b, 0:n], h[:, kc, off:off + n], AF.Square, [("h", kc, tbi)], [("sq", b)])
                mm(pn[:, 0:n], ones_bf[:], sq[:, b, 0:n], kc == 0, kc == 7, ["ones_bf", ("sq", b)], [("ps", 6)])
            rb = tbi % 2
            act(rstd[:, rb, 0:n], pn[:, 0:n], AF.Ln, [("ps", 6)], [("rstd", rb)], scale=1.0 / D, bias=EPS)
            act(rstd[:, rb, 0:n], rstd[:, rb, 0:n], AF.Exp, [("rstd", rb)], [("rstd", rb)], scale=-0.5)
            for kc in range(8):
                stt(dst_fn(kc, tbi, off, n), h[:, kc, off:off + n], gains[:, gcol + kc:gcol + kc + 1],
                    rstd[:, rb, 0:n], ALU.mult, ALU.mult,
                    [("h", kc, tbi), "gains", ("rstd", rb)], dst_keys(kc, tbi))

    def norm_to_u(tbs, gcol):
        rmsnorm(tbs, gcol, lambda kc, tbi, off, n: u[:, kc, off:off + n], lambda kc, tbi: [("u", kc, tbi)])

    def norm_u_tb(tbi, off, n, gcol):
        pn = bank(6)
        for kc in range(8):
            b = kc % 2
            act(sq[:, b, 0:n], h[:, kc, off:off + n], AF.Square, [("h", kc, tbi)], [("sq", b)])
            mm(pn[:, 0:n], ones_bf[:], sq[:, b, 0:n], kc == 0, kc == 7, ["ones_bf", ("sq", b)], [("ps", 6)])
        rb = tbi % 2
        act(rstd[:, rb, 0:n], pn[:, 0:n], AF.Ln, [("ps", 6)], [("rstd", rb)], scale=1.0 / D, bias=EPS)
        act(rstd[:, rb, 0:n], rstd[:, rb, 0:n], AF.Exp, [("rstd", rb)], [("rstd", rb)], scale=-0.5)
        for kc in range(8):
            stt(u[:, kc, off:off + n], h[:, kc, off:off + n], gains[:, gcol + kc:gcol + kc + 1],
                rstd[:, rb, 0:n], ALU.mult, ALU.mult,
                [("h", kc, tbi), "gains", ("rstd", rb)], [("u", kc, tbi)])

    def norm_part1(sq8, tbi, off, n):
        for kc in range(8):
            act(sq8[:, kc, 0:n], h[:, kc, off:off + n], AF.Square, [("h", kc, tbi)], [("sq8", kc)])

    def norm_part2(sq8, tbi, off, n, gcol):
        pn = bank(6)
        for kc in range(8):
            mm(pn[:, 0:n], ones_bf[:], sq8[:, kc, 0:n], kc == 0, kc == 7, ["ones_bf", ("sq8", kc)], [("ps", 6)])
        rb = tbi % 2
        act(rstd[:, rb, 0:n], pn[:, 0:n], AF.Ln, [("ps", 6)], [("rstd", rb)], scale=1.0 / D, bias=EPS)
        act(rstd[:, rb, 0:n], rstd[:, rb, 0:n], AF.Exp, [("rstd", rb)], [("rstd", rb)], scale=-0.5)
        for kc in range(8):
            stt(u[:, kc, off:off + n], h[:, kc, off:off + n], gains[:, gcol + kc:gcol + kc + 1],
                rstd[:, rb, 0:n], ALU.mult, ALU.mult,
                [("h", kc, tbi), "gains", ("rstd", rb)], [("u", kc, tbi)])

    def ffn(tbs, lazy, next_gcol=None):
        cnt = 0
        for g in range(6):
            nf = 4 if g < 5 else 2
            ia, wa, ka = wnext(8, nf * 128)
            ib, wb, kb = wnext(8, nf * 128)
            for tbi, (off, n) in enumerate(tbs):
                if g == 0 and tbi > 0:
                    lazy(tbi)
                for fl in range(nf):
                    fc = g * 4 + fl
                    pa, pbk = bank(cnt % 2), bank(2 + cnt % 2)
                    ka_, kb_ = ("ps", cnt % 2), ("ps", 2 + cnt % 2)
                    for kc in range(8):
                        mm(pa[:, 0:n], wa[:, kc, fl * 128:(fl + 1) * 128], u[:, kc, off:off + n], kc == 0, kc == 7,
                           [ka, ("u", kc, tbi)], [ka_])
                    for kc in range(8):
                        mm(pbk[:, 0:n], wb[:, kc, fl * 128:(fl + 1) * 128], u[:, kc, off:off + n], kc == 0, kc == 7,
                           [kb, ("u", kc, tbi)], [kb_])
                    sb_ = cnt % 2
                    act(stmp[:, sb_, 0:n], pa[:, 0:n], AF.Silu, [ka_], [("stmp", sb_)])
                    tt(gbuf[:, fc, off:off + n], stmp[:, sb_, 0:n], pbk[:, 0:n], ALU.mult,
                       [("stmp", sb_), kb_], [("g", fc, tbi)])
                    cnt += 1
            wrelease(ia)
            wrelease(ib)
        cnt = 0
        for dm in range(8):
            io, wo, ko = wnext(NFC, 128)
            for tbi, (off, n) in enumerate(tbs):
                po = bank(4 + cnt % 2)
                kp = ("ps", 4 + cnt % 2)
                for fc in range(NFC):
                    mm(po[:, 0:n], wo[:, fc, :], gbuf[:, fc, off:off + n], fc == 0, fc == NFC - 1,
                       [ko, ("g", fc, tbi)], [kp])
                stt(h[:, dm, off:off + n], po[:, 0:n], 0.5, h[:, dm, off:off + n], ALU.mult, ALU.add,
                    [kp, ("h", dm, tbi)], [("h", dm, tbi)])
                cnt += 1
                if next_gcol is not None and dm == 7:
                    if tbi == 0:
                        norm_part1(sq8f, 0, tbs[0][0], tbs[0][1])
                    elif tbi == 1:
                        norm_part2(sq8f, 0, tbs[0][0], tbs[0][1], next_gcol)
            wrelease(io)

    def mixing(hi, tok0, tbs, blocks, lazy, next_gcol=None):
        mix_start = st["cur"]

        def bail():
            skip_tiles(mix_start + 15 - st["cur"])

        T = sum(n for _, n in tbs)
        def rtab_load(tbi_, hp_):
            off_, n_ = tbs[tbi_]
            gtb = 4 if n_ == 64 else (tok0 + off_) // 512
            src = rtab_d[gtb].rearrange("p (a b) -> p a b", a=8)[:, hp_ * 4:(hp_ + 1) * 4, 0:n_]
            dma("sp", rtab[:, hp_ * 4:(hp_ + 1) * 4, 0:n_], src, (), [("rtab", hp_)], ("rt", hp_))

        def ttab_load(bi_):
            dma("sp", ttab[:, bi_ % 3, :], ttab_d[blocks[bi_][3]], (), [("ttab", bi_ % 3)], ("tt", bi_ % 3))

        rtab_load(0, 0)
        rtab_load(0, 1)
        for bi_ in range(min(3, len(blocks))):
            ttab_load(bi_)
        ial, wal, kal = wnext(8, 16)
        i0, w0, k0 = wnext(8, 512)
        for tbi, (off, n) in enumerate(tbs):
            if tbi > 0:
                lazy(tbi)
            pA = bank(7)
            for kc in range(8):
                mm(pA[0:16, 0:n], wal[:, kc, 0:16], u[:, kc, off:off + n], kc == 0, kc == 7,
                   [kal, ("u", kc, tbi)], [("ps", 7)])
            S.op("act", (lambda o_, i_: (lambda e: e.copy(out=o_, in_=i_)))(alowT[0:16, off:off + n], pA[0:16, 0:n]),
                 [("ps", 7)], [("alowT", tbi)])
            pend_b = []
            for bi, (boff, bn, smp, gblk) in enumerate(blocks):
                if not (off <= boff < off + n) or cut <= 0.2:
                    continue
                lo = boff - off
                dp = bi % 2
                px = bank(6 + dp)
                kpx = ("ps", 6 + dp)
                mm(px[0:bn, 0:256], alowT[0:16, boff:boff + bn], wup[:, :], True, False,
                   [("alowT", tbi), "wup"], [kpx])
                mm(px[0:bn, 0:256], ones_row[0:1, 0:bn], balpha[0:1, :], False, True,
                   ["ones_row", "balpha"], [kpx])
                act(etmp[0:bn, dp, :], px[0:bn, 0:256], AF.Exp, [kpx], [("etmp", dp)], scale=-1.0)
                act(spb[0:bn, dp, :], etmp[0:bn, dp, :], AF.Ln, [("etmp", dp)], [("spb", dp)], bias=1.0)
                if cut <= 0.4:
                    continue
                S.op("dve", (lambda o_, i_: (lambda e: e.tensor_copy(out=o_, in_=i_)))(sphi[0:bn, dp, :], spb[0:bn, dp, :]),
                     [("spb", dp)], [("sphi", dp)])
                tt(splo[0:bn, dp, :], spb[0:bn, dp, :], sphi[0:bn, dp, :], ALU.subtract,
                   [("spb", dp), ("sphi", dp)], [("splo", dp)])
                def part_b(bi=bi, boff=boff, bn=bn, smp=smp, lo=lo, dp=dp):
                    ci = 2 if smp else 0
                    b5 = 4 + bi % 2
                    p5 = bank(b5)
                    for hp in range(2):
                        for xi, (spx, kx) in enumerate(((sphi, ("sphi", dp)), (splo, ("splo", dp)))):
                            mm(p5[:, hp * 128:hp * 128 + bn], spx[0:bn, dp, hp * 128:(hp + 1) * 128],
                               cmat[0:bn, ci, 0:bn], xi == 0, xi == 1, [kx, "cmat"], [("ps", b5)])
                    for xi, (spx, kx) in enumerate(((sphi, ("sphi", dp)), (splo, ("splo", dp)))):
                        mm(p5[0:bn, 256:512], cmat[0:bn, ci + 1, 0:bn], spx[0:bn, dp, 0:256], xi == 0, xi == 1,
                           [kx, "cmat"], [("ps", b5)])
                    for hp in range(2):
                        src = p5[:, hp * 128:hp * 128 + bn]
                        act(eb[:, hp, lo:lo + bn], src, AF.Exp, [("ps", b5)], [("eb", hp)])
                        act(enb[:, hp, lo:lo + bn], src, AF.Exp, [("ps", b5)], [("enb", hp)], scale=-1.0)
                        if smp:
                            act(decSA[:, hp, :], p5[:, hp * 128 + 3:hp * 128 + 64:4], AF.Exp, [("ps", b5)], ["decSA"])
                        else:
                            act(decA[:, hp, bi:bi + 1], p5[:, hp * 128 + bn - 1:hp * 128 + bn], AF.Exp,
                                [("ps", b5)], [("decA", hp, bi)])
                    act(ed[0:bn, bi, :], p5[0:bn, 256:512], AF.Exp, [("ps", b5)], [("ed", bi)])

                for f_ in pend_b:
                    f_()
                pend_b = [part_b]
            for f_ in pend_b:
                f_()
            pend_b = []
            for ch in range(4):
                if cut <= 0.8:
                    continue
                pq = bank(ch)
                for kc in range(8):
                    mm(pq[:, 0:n], w0[:, kc, ch * 128:(ch + 1) * 128], u[:, kc, off:off + n], kc == 0, kc == 7,
                       [k0, ("u", kc, tbi)], [("ps", ch)])
                if cut <= 0.85:
                    continue
                if ch < 2:
                    stt(qk[:, ch, off:off + n], pq[:, 0:n], 0.125, eb[:, ch, 0:n], ALU.mult, ALU.mult,
                        [("ps", ch), ("eb", ch)], [("qk", 0, tbi, ch)])
                elif cut <= 0.9:
                    continue
                elif cut <= 0.95:
                    tt(qk[:, ch, off:off + n], pq[:, 0:n], eb[:, ch - 2, 0:n], ALU.mult,
                       [("ps", ch), ("eb", ch - 2)], [("qk", 0, tbi, ch)])
                elif cut <= 0.97:
                    stt(qk[:, ch, off:off + n], pq[:, 0:n], 1.0, enb[:, ch - 2, 0:n], ALU.mult, ALU.mult,
                        [("ps", ch), ("enb", ch - 2)], [("qk", 0, tbi, ch)])
                else:
                    tt(qk[:, ch, off:off + n], pq[:, 0:n], enb[:, ch - 2, 0:n], ALU.mult,
                       [("ps", ch), ("enb", ch - 2)], [("qk", 0, tbi, ch)])
        wrelease(ial)
        wrelease(i0)
        if cut <= 1:
            return bail()
        i1, w1, k1 = wnext(8, 512)
        i2, w2, k2 = wnext(8, 512)
        cnt = 0
        for tbi, (off, n) in enumerate(tbs):
            for hp in range(2):
                for isk in range(2):
                    ch = hp + 2 * isk
                    b0 = (cnt % 2) * 2
                    pa, pbk = bank(b0), bank(b0 + 1)
                    for kc in range(8):
                        mm(pa[:, 0:n], w1[:, kc, ch * 128:(ch + 1) * 128], u[:, kc, off:off + n], kc == 0, kc == 7,
                           [k1, ("u", kc, tbi)], [("ps", b0)])
                    for kc in range(8):
                        mm(pbk[:, 0:n], w2[:, kc, ch * 128:(ch + 1) * 128], u[:, kc, off:off + n], kc == 0, kc == 7,
                           [k2, ("u", kc, tbi)], [("ps", b0 + 1)])
                    tt(tA[:, 0:n], pa[:, 0:n], rtab[:, hp * 4 + 2 * isk, 0:n], ALU.mult, [("ps", b0), ("rtab", hp)], ["tA"])
                    tt(tB[:, 0:n], pbk[:, 0:n], rtab[:, hp * 4 + 2 * isk + 1, 0:n], ALU.mult,
                       [("ps", b0 + 1), ("rtab", hp)], ["tB"])
                    qi = 4 + 2 * isk + hp
                    tt(qk[:, qi, off:off + n], tA[:, 0:n], tB[:, 0:n], ALU.add, ["tA", "tB"], [("qk", 1, tbi, qi)])
                    cnt += 1
                if tbi + 1 < len(tbs):
                    rtab_load(tbi + 1, hp)
        wrelease(i1)
        wrelease(i2)
        if cut <= 2:
            return bail()
        i3, w3, k3 = wnext(8, 512)
        for bi, (boff, bn, smp, gblk) in enumerate(blocks):
            tbi = min(boff // 512, len(tbs) - 1)
            tb_ = bi % 3
            pk = bank(bi % 2)
            kp = ("ps", bi % 2)
            for kc in range(8):
                mm(pk[0:bn, :], u[:, kc, boff:boff + bn], w3[:, kc, :], kc == 0, kc == 7, [k3, ("u", kc, tbi)], [kp])
            tt(kh[0:bn, bi, 0:256], pk[0:bn, 0:256], ed[0:bn, bi, :], ALU.mult, [kp, ("ed", bi)], [("kh", bi)])
            pr = pk[0:bn, 256:512].rearrange("p (h s e) -> p h s e", h=4, s=2)
            Ct = ttab[0:bn, tb_, 0:256].rearrange("p (h s e) -> p h s e", h=4, s=2)
            St = ttab[0:bn, tb_, 256:512].rearrange("p (h s e) -> p h s e", h=4, s=2)
            kr = kh[0:bn, bi, 256:512].rearrange("p (h s e) -> p h s e", h=4, s=2)
            tA4 = tA[0:bn, 0:256].rearrange("p (h s e) -> p h s e", h=4, s=2)
            tB4 = tB[0:bn, 0:256].rearrange("p (h s e) -> p h s e", h=4, s=2)
            for s_ in range(2):
                tt(tA4[:, :, s_, :], pr[:, :, s_, :], Ct[:, :, s_, :], ALU.mult, [kp, ("ttab", tb_)], ["tA"])
                tt(tB4[:, :, s_, :], pr[:, :, 1 - s_, :], St[:, :, s_, :], ALU.mult, [kp, ("ttab", tb_)], ["tB"])
                tt(kr[:, :, s_, :], tA4[:, :, s_, :], tB4[:, :, s_, :], ALU.add, ["tA", "tB"], [("kh", bi)])
            if bi + 3 < len(blocks):
                ttab_load(bi + 3)
        wrelease(i3)
        if cut <= 3:
            return bail()
        for vi in range(2):
            iv, wv, kv = wnext(8, 512)
            for bi, (boff, bn, smp, gblk) in enumerate(blocks):
                tbi = min(boff // 512, len(tbs) - 1)
                pv = bank(2 + bi % 2)
                kp = ("ps", 2 + bi % 2)
                for kc in range(8):
                    mm(pv[0:bn, :], u[:, kc, boff:boff + bn], wv[:, kc, :], kc == 0, kc == 7,
                       [kv, ("u", kc, tbi)], [kp])
                S.op("act", (lambda o_, i_: (lambda e: e.copy(out=o_, in_=i_)))(
                    vtm[0:bn, bi, vi * 512:(vi + 1) * 512], pv[0:bn, :]), [kp], [("v", bi, vi)])
            wrelease(iv)
        if cut <= 4:
            return bail()
        S.barrier(exclude=("pe",))
        igt = []
        wg = []
        kg = []
        for br in range(2):
            i_, w_, k_ = wnext(8, 512)
            igt.append(i_); wg.append(w_); kg.append(k_)
        ecnt = [0]

        def epilogue(br, hl, po, off, n, tbi, pokey, ebanks=((2, 3), (0, 1))):
            ob = ecnt[0] % 2
            ecnt[0] += 1
            tbk, rbk = ebanks[ob]
            act(sqb[:, ob, 0:n], po, AF.Square, [pokey], [("sqb", ob)])
            pR = bank(rbk)
            for kc in range(8):
                mm(pR[:, 0:n], wg[br][:, kc, hl * 128:(hl + 1) * 128], u[:, kc, off:off + n], kc == 0, kc == 7,
                   [kg[br], ("u", kc, tbi)], [("ps", rbk)])
            pT_ = bank(tbk)
            mm(pT_[:, 0:n], ones_bf[:], sqb[:, ob, 0:n], True, True, ["ones_bf", ("sqb", ob)], [("ps", tbk)])
            act(rs2[:, ob, 0:n], pT_[:, 0:n], AF.Ln, [("ps", tbk)], [("rs2", ob)], scale=1.0 / 128, bias=EPS)
            act(ee[:, ob, 0:n], pR[:, 0:n], AF.Exp, [("ps", rbk)], [("ee", ob)], scale=-1.0)
            act(ee[:, ob, 0:n], ee[:, ob, 0:n], AF.Ln, [("ee", ob)], [("ee", ob)], bias=1.0)
            stt(tt2[:, ob, 0:n], rs2[:, ob, 0:n], -0.5, ee[:, ob, 0:n], ALU.mult, ALU.subtract,
                [("rs2", ob), ("ee", ob)], [("tt2", ob)])
            act(tt2[:, ob, 0:n], tt2[:, ob, 0:n], AF.Exp, [("tt2", ob)], [("tt2", ob)])
            gc = (G_GNA if br == 0 else G_GNR) + hl
            stt(tt1[:, ob, 0:n], po, gains[:, gc:gc + 1], tt2[:, ob, 0:n], ALU.mult, ALU.mult,
                [pokey, "gains", ("tt2", ob)], [("tt1", ob)])
            qi = br * 4 + hl
            wkeys = [("qk", br, tbi, br * 4 + j) for j in range(4)]
            tt(qk[:, qi, off:off + n], tt1[:, ob, 0:n], pR[:, 0:n], ALU.mult, [("tt1", ob), ("ps", rbk)], wkeys)

        scnt = [0]
        SX = (S0f_buf, tmpS_buf)

        def s0_load(sidx):
            br_, pair_ = sidx // 2, sidx % 2
            src = st_in[br_][:, pair_ * 2:pair_ * 2 + 2].rearrange("s h d v -> (h d) s v")
            sb_ = sidx % 2
            dma("pool", S0bf[:, sb_, :].rearrange("p (s v) -> p s v", s=16), src, (), [("S0bf", sb_)], ("s0b", sb_))
            x_ = SX[sidx % 2]
            kx = ("SX", sidx % 2)
            dma("sp", x_[:].rearrange("p (s v) -> p s v", s=16), src, (), [kx, kx + (0,), kx + (1,)], ("s0f", sidx % 2))

        if hi == 1:
            s0_load(0)
        for tbi, (off, n) in enumerate(tbs):
            smp_tb = (n == 64)
            rkeys_all = lambda br: [("qk", br, tbi, br * 4 + j) for j in range(4)]
            for br in range(2):
                qb, kb_i = br * 4, br * 4 + 2
                if not smp_tb:
                    tblocks = [(bi, b) for bi, b in enumerate(blocks) if off <= b[0] < off + n]
                    pend = []

                    def o_ops(c4, bi, boff, bn, gblk):
                        par = c4 % 2
                        for hl in range(4):
                            pair, h2 = hl // 2, hl % 2
                            sidx = br * 2 + pair
                            pr_ = slice(h2 * 64, h2 * 64 + 64)
                            po = bank(4 + hl)[:, c4 * 128:(c4 + 1) * 128]
                            vc = br * 512 + hl * 128
                            mm(po, vtm[:, bi, vc:vc + 128], PT[:, par * 4 + h2 * 2 + pair, :], True, False,
                               [("v", bi, br), ("PT", par)], [("ps", 4 + hl)])
                            mm(po, Sbf[pr_, (gblk - 1) % 3, sidx, :], qk[pr_, qb + pair, boff:boff + bn], False, True,
                               [("Sb", (gblk - 1) % 3, br)] + rkeys_all(br), [("ps", 4 + hl)])

                    for c4, (bi, (boff, bn, smp, gblk)) in enumerate(tblocks):
                        par = c4 % 2
                        ubk = (3, 1)[par]
                        pu = bank(ubk)
                        for pair in range(2):
                            kc0 = br * 256 + pair * 128
                            vc0 = br * 512 + pair * 256
                            mm(pu[:, pair * 256:(pair + 1) * 256], kh[:, bi, kc0:kc0 + 128], vtm[:, bi, vc0:vc0 + 256],
                               True, True, [("kh", bi), ("v", bi, br)], [("ps", ubk)])
                        for hl in (0, 2, 1, 3):
                            pair, h2 = hl // 2, hl % 2
                            pr_ = slice(h2 * 64, h2 * 64 + 64)
                            sb2 = (0, 2)[h2]
                            mm(bank(sb2)[:, pair * 128:(pair + 1) * 128], qk[pr_, kb_i + pair, boff:boff + bn],
                               qk[pr_, qb + pair, boff:boff + bn], True, True, rkeys_all(br), [("ps", sb2)])
                        for f_ in pend:
                            f_()
                        pend = []
                        for h2 in range(2):
                            sb2 = (0, 2)[h2]
                            tt(PT[:, par * 4 + h2 * 2:par * 4 + h2 * 2 + 2, :],
                               bank(sb2)[:, 0:256].rearrange("p (a b) -> p a b", a=2),
                               masks[:, 0, :].unsqueeze(1).to_broadcast([128, 2, 128]), ALU.mult,
                               [("ps", sb2), "masks"], [("PT", par)])
                        for pair in range(2):
                            sidx = br * 2 + pair
                            for h2 in range(2):
                                pr_ = slice(h2 * 64, h2 * 64 + 64)
                                if br == 0:
                                    dsc = decA[pr_, pair, bi:bi + 1]
                                    dk_ = [("decA", pair, bi)]
                                else:
                                    dsc = gains[pr_, G_DECR + pair:G_DECR + pair + 1]
                                    dk_ = ["gains"]
                                stt(Sst[pr_, sidx, :], Sst[pr_, sidx, :], dsc,
                                    pu[pr_, pair * 256 + h2 * 128:pair * 256 + (h2 + 1) * 128],
                                    ALU.mult, ALU.add, ["S%d" % sidx, ("ps", ubk)] + dk_, ["S%d" % sidx])
                        S.op("act", (lambda o_, i_: (lambda e: e.copy(out=o_, in_=i_)))(
                            Sbf[:, gblk % 3, br * 2:br * 2 + 2, :], Sst[:, br * 2:br * 2 + 2, :]),
                            ["S%d" % (br * 2), "S%d" % (br * 2 + 1)], [("Sb", gblk % 3, br)])
                        pend.append((lambda a, b, c, d, e_: (lambda: o_ops(a, b, c, d, e_)))(c4, bi, boff, bn, gblk))
                    for f_ in pend:
                        f_()
                    for hl in range(4):
                        eb_ = ((2, 3), (0, 1)) if hl < 2 else ((2, 4), (0, 5))
                        epilogue(br, hl, bank(4 + hl)[:, 0:n], off, n, tbi, ("ps", 4 + hl), eb_)
                else:
                    bi = len(blocks) - 1
                    boff, bn, smp, gblk = blocks[bi]
                    for pair in range(2):
                        sidx = br * 2 + pair
                        sb_ = sidx % 2
                        S0f, tmpS = (SX[sidx % 2], SX[1 - sidx % 2])
                        kS0f, kTmp = ("SX", sidx % 2), ("SX", 1 - sidx % 2)
                        S0b3 = S0bf[:, sb_, :].rearrange("p (s v) -> p s v", s=16)
                        S0f3 = S0f[:].rearrange("p (s v) -> p s v", s=16)
                        tmp3 = tmpS[:].rearrange("p (s v) -> p s v", s=16)
                        for h2 in range(2):
                            hl = pair * 2 + h2
                            pr_ = slice(h2 * 64, h2 * 64 + 64)
                            slot = scnt[0] % 4
                            scnt[0] += 1
                            sbk = (0, 2)[slot % 2]
                            psS = bank(sbk)[0:64, 0:64]
                            mm(psS, qk[pr_, kb_i + pair, boff:boff + 64], qk[pr_, qb + pair, boff:boff + 64],
                               True, True, rkeys_all(br), [("ps", sbk)])
                            pts = slot + 4 * br
                            tt(PT[0:64, pts, 0:64], psS, masks[0:64, 1, 0:64], ALU.mult,
                               [("ps", sbk), "masks"], [("PT", pts // 4)])
                            if cut <= 4.93:
                                continue
                            po = bank(1)[:, hl * 64:(hl + 1) * 64]
                            vc = br * 512 + hl * 128
                            mm(po, vtm[0:64, bi, vc:vc + 128], PT[0:64, pts, 0:64], True, cut <= 4.935,
                               [("v", bi, br), ("PT", pts // 4)], [("ps", 1)])
                            po_ = slice((1 - h2) * 64, (1 - h2) * 64 + 64)
                            S.op("pool", (lambda a_: (lambda e: e.memset(a_, 0.0)))(qm[po_].rearrange("p a b -> p (a b)")),
                                 (), ["qm"])
                            tt(qm[pr_], qk[pr_, qb + pair, boff:boff + 64].unsqueeze(1).to_broadcast([64, 16, 64]),
                               maskq[pr_], ALU.mult, rkeys_all(br) + ["maskq"], ["qm"])
                            for s_ in range(16):
                                if cut <= 4.935:
                                    continue
                                mm(po, S0b3[:, s_, :], qm[:, s_, :], False, s_ == 15,
                                   [("S0bf", sb_), "qm"], [("ps", 1)])
                            if cut <= 4.94:
                                continue
                            V3 = Vblk[:].rearrange("p (s v) -> p s v", s=16)
                            tt(V3, vtm[0:64, bi, vc:vc + 128].unsqueeze(1).to_broadcast([64, 16, 128]),
                               smask[:, :].unsqueeze(2).to_broadcast([64, 16, 128]), ALU.mult,
                               [("v", bi, br), "smask"], ["Vblk"], eng="pool")
                            if cut <= 4.95:
                                continue
                            kc0 = br * 256 + pair * 128
                            for q4 in range(4):
                                mm(PSB[:, q4 * 512:(q4 + 1) * 512], kh[0:64, bi, kc0:kc0 + 128],
                                   Vblk[:, q4 * 512:(q4 + 1) * 512], True, True, [("kh", bi), "Vblk"], [("ps", 4 + q4)])
                            if h2 == 0:
                                if br == 0:
                                    tt(tmp3, S0f3, decSA[:, pair, :].unsqueeze(2).to_broadcast([128, 16, 128]),
                                       ALU.mult, [kS0f, "decSA"], [kTmp])
                                else:
                                    tsmul(tmpS[:, :], S0f[:, :], gains[:, G_DECR + 2 + pair:G_DECR + 3 + pair],
                                          [kS0f, "gains"], [kTmp])
                            tt(S0f[pr_, :], tmpS[pr_, :], PSB[pr_, :], ALU.add,
                               [kTmp] + [("ps", 4 + q) for q in range(4)], [kS0f + (h2,)])
                        if sidx < 3:
                            s0_load(sidx + 1)
                        dst = st_out[br][:, pair * 2:pair * 2 + 2].rearrange("s h d v -> (h d) s v")
                        dma("sp", dst, S0f3, [kS0f, kS0f + (0,), kS0f + (1,)], (), ("s0o", sidx % 2))
                    for hl in range(4):
                        if cut <= 4.98:
                            continue
                        epilogue(br, hl, bank(1)[:, hl * 64:(hl + 1) * 64], off, n, tbi, ("ps", 1), ((2, 3), (0, 5)))
        if hi == 1:
            for br in range(2):
                for pair in range(2):
                    sidx = br * 2 + pair
                    dma("sp", sp_out[br][pair * 128:(pair + 1) * 128, :], Sst[:, sidx, :], ["S%d" % sidx], (), "spo")
        wrelease(igt[0])
        wrelease(igt[1])
        if cut <= 5:
            return bail()
        S.barrier(exclude=("pe",))
        cnt = 0
        for dmg in range(2):
            iga, wga, kga = wnext(8, 512)
            igr, wgr, kgr = wnext(8, 512)
            iwo, wwo, kwo = wnext(8, 512)
            for tbi, (off, n) in enumerate(tbs):
                okeys = lambda br: [("qk", br, tbi, br * 4 + j) for j in range(4)]
                for dl in range(4):
                    dm = dmg * 4 + dl
                    b0 = (cnt % 2) * 4
                    cnt += 1
                    pga, pgr, pma, pmr = bank(b0), bank(b0 + 1), bank(b0 + 2), bank(b0 + 3)
                    cs = slice(dl * 128, (dl + 1) * 128)
                    for kc in range(8):
                        mm(pga[:, 0:n], wga[:, kc, cs], u[:, kc, off:off + n], kc == 0, kc == 7,
                           [kga, ("u", kc, tbi)], [("ps", b0)])
                    for kc in range(8):
                        mm(pgr[:, 0:n], wgr[:, kc, cs], u[:, kc, off:off + n], kc == 0, kc == 7,
                           [kgr, ("u", kc, tbi)], [("ps", b0 + 1)])
                    for fc in range(4):
                        mm(pma[:, 0:n], wwo[:, fc, cs], qk[:, fc, off:off + n], fc == 0, fc == 3,
                           [kwo] + okeys(0), [("ps", b0 + 2)])
                    for fc in range(4, 8):
                        mm(pmr[:, 0:n], wwo[:, fc, cs], qk[:, fc, off:off + n], fc == 4, fc == 7,
                           [kwo] + okeys(1), [("ps", b0 + 3)])
                    act(tA[:, 0:n], pga[:, 0:n], AF.Tanh, [("ps", b0)], ["tA"], scale=0.5)
                    act(tB[:, 0:n], pgr[:, 0:n], AF.Tanh, [("ps", b0 + 1)], ["tB"], scale=0.5)
                    stt(tC[:, 0:n], tA[:, 0:n], 1.0, pma[:, 0:n], ALU.add, ALU.mult, ["tA", ("ps", b0 + 2)], ["tC"])
                    stt(tD[:, 0:n], tB[:, 0:n], 1.0, pmr[:, 0:n], ALU.add, ALU.mult, ["tB", ("ps", b0 + 3)], ["tD"])
                    tt(tA[:, 0:n], tC[:, 0:n], tD[:, 0:n], ALU.add, ["tC", "tD"], ["tA"])
                    stt(h[:, dm, off:off + n], tA[:, 0:n], 0.5, h[:, dm, off:off + n], ALU.mult, ALU.add,
                        ["tA", ("h", dm, tbi)], [("h", dm, tbi)])
                    if next_gcol is not None and dmg == 1:
                        if tbi == 0 and dl == 3:
                            norm_part1(sq8m, 0, tbs[0][0], tbs[0][1])
                        elif tbi == 1 and dl == 1:
                            norm_part2(sq8m, 0, tbs[0][0], tbs[0][1], next_gcol)
            wrelease(iga)
            wrelease(igr)
            wrelease(iwo)

    def ple(tok0, tbs, lazy):
        cnt = 0
        for dmg in range(2):
            ig, wg_, kg_ = wnext(8, 512)
            ip, wp_, kp_ = wnext(2, 512)
            for tbi, (off, n) in enumerate(tbs):
                if dmg == 0 and tbi > 0:
                    lazy(tbi)
                for dl in range(4):
                    dm = dmg * 4 + dl
                    b0 = (cnt % 2) * 2
                    cnt += 1
                    pg, pp = bank(b0), bank(b0 + 1)
                    cs = slice(dl * 128, (dl + 1) * 128)
                    for kc in range(8):
                        mm(pg[:, 0:n], wg_[:, kc, cs], u[:, kc, off:off + n], kc == 0, kc == 7,
                           [kg_, ("u", kc, tbi)], [("ps", b0)])
                    for kc in range(2):
                        mm(pp[:, 0:n], wp_[:, kc, cs], pb[:, kc, off:off + n], kc == 0, kc == 1,
                           [kp_, "pb"], [("ps", b0 + 1)])
                    sb_ = cnt % 2
                    act(stmp[:, sb_, 0:n], pg[:, 0:n], AF.Tanh, [("ps", b0)], [("stmp", sb_)], scale=0.5)
                    stt(stmp2[:, sb_, 0:n], stmp[:, sb_, 0:n], 1.0, pp[:, 0:n], ALU.add, ALU.mult,
                        [("stmp", sb_), ("ps", b0 + 1)], [("stmp2", sb_)])
                    stt(h[:, dm, off:off + n], stmp2[:, sb_, 0:n], 0.5, h[:, dm, off:off + n], ALU.mult, ALU.add,
                        [("stmp2", sb_), ("h", dm, tbi)], [("h", dm, tbi)])
            wrelease(ig)
            wrelease(ip)

    def skip_tiles(k):
        for _ in range(k):
            i = st["cur"]
            st["cur"] += 1
            wrelease(i)

    yv = yT.rearrange("(k p) t -> p k t", p=128)
    xv = xT.rearrange("(k p) t -> p k t", p=128)
    halves = [(0, [(0, 512), (512, 512)]), (1024, [(0, 512), (512, 512), (1024, 64)])]
    for hi, (tok0, tbs) in enumerate(halves[:nhalves]):
        T = sum(n for _, n in tbs)
        blocks = [(b * 128, 128, False, (tok0 + b * 128) // 128) for b in range(8)]
        if hi == 1:
            blocks.append((1024, 64, True, 16))
        def xload(tok0_, tbi, off, n):
            dma("sp", h[:, :, off:off + n], xv[:, :, tok0_ + off:tok0_ + off + n], (),
                [("h", kc, tbi) for kc in range(8)], ("x", tbi))

        if hi == 0:
            for tbi, (off, n) in enumerate(tbs):
                xload(tok0, tbi, off, n)
            late_consts()

        def mk_lazy(gcol):
            return lambda tbi: norm_u_tb(tbi, tbs[tbi][0], tbs[tbi][1], gcol)

        norm_u_tb(0, tbs[0][0], tbs[0][1], G_FFN1)
        ffn(tbs, mk_lazy(G_FFN1), G_MIX if (stage >= 2 and cut >= 99) else None)
        if stage >= 2:
            if cut < 99:
                norm_u_tb(0, tbs[0][0], tbs[0][1], G_MIX)
            S.barrier(exclude=("pe",))
            mixing(hi, tok0, tbs, blocks, mk_lazy(G_MIX), G_FFN2 if stage >= 3 else None)
        else:
            skip_tiles(15)
        if stage >= 3:
            if cut < 99:
                norm_u_tb(0, tbs[0][0], tbs[0][1], G_FFN2)
            S.barrier(exclude=("pe",))
            if stage >= 4:
                T_ = sum(n for _, n in tbs)
                dma("pool", pb[:, :, 0:T_], pT.rearrange("(k p) t -> p k t", p=128)[:, :, tok0:tok0 + T_], (),
                    ["pb"], "pb")
            ffn(tbs, mk_lazy(G_FFN2), G_PLE if stage >= 4 else None)
        else:
            S.barrier(exclude=("pe",))
            skip_tiles(20)
        if stage >= 4:
            ple(tok0, tbs, mk_lazy(G_PLE))
        else:
            skip_tiles(4)
        if stage >= 5:
            for tbi, (off, n) in enumerate(tbs):
                pn = bank(6)
                for kc in range(8):
                    b = kc % 2
                    act(sq[:, b, 0:n], h[:, kc, off:off + n], AF.Square, [("h", kc, tbi)], [("sq", b)])
                    mm(pn[:, 0:n], ones_bf[:], sq[:, b, 0:n], kc == 0, kc == 7, ["ones_bf", ("sq", b)], [("ps", 6)])
                rb = tbi % 2
                act(rstd[:, rb, 0:n], pn[:, 0:n], AF.Ln, [("ps", 6)], [("rstd", rb)], scale=1.0 / D, bias=EPS)
                act(rstd[:, rb, 0:n], rstd[:, rb, 0:n], AF.Exp, [("rstd", rb)], [("rstd", rb)], scale=-0.5)
                for kc in range(8):
                    stt(yout[:, rb, kc, 0:n], h[:, kc, off:off + n], gains[:, G_FIN + kc:G_FIN + kc + 1],
                        rstd[:, rb, 0:n], ALU.mult, ALU.mult,
                        [("h", kc, tbi), "gains", ("rstd", rb)], [("yout", rb)])
                dma("sp", yv[:, :, tok0 + off:tok0 + off + n], yout[:, rb, :, 0:n], [("yout", rb)], (), ("yo", rb))
                if stage >= 5 and hi + 1 < len(halves[:nhalves]):
                    ntok0, ntbs = halves[hi + 1]
                    xload(ntok0, tbi, ntbs[tbi][0], ntbs[tbi][1])
                    if tbi == len(tbs) - 1:
                        for t2 in range(len(tbs), len(ntbs)):
                            xload(ntok0, t2, ntbs[t2][0], ntbs[t2][1])
        else:
            for tbi, (off, n) in enumerate(tbs):
                dma("sp", yv[:, :, tok0 + off:tok0 + off + n], h[:, :, off:off + n],
                    [("h", kc, tbi) for kc in range(8)], (), "yo")
            if hi + 1 < len(halves[:nhalves]):
                ntok0, ntbs = halves[hi + 1]
                for t2 in range(len(ntbs)):
                    xload(ntok0, t2, ntbs[t2][0], ntbs[t2][1])
    S.emit(nc)
    return nc


_CONST_CACHE = {}


def _prep_inputs(inp):
    f32 = np.float32
    inp = {k: np.asarray(v) for k, v in inp.items()}
    wflat, sched = _weight_tiles(inp)
    ct = _CONST_CACHE.get("ct")
    if ct is None:
        ct = _const_tables()
        _CONST_CACHE["ct"] = ct

    def col8(v):
        return np.asarray(v, f32).reshape(-1, 128).T

    cpack = np.zeros((128, 52), f32)
    cpack[:, 0:8] = col8(inp["norm_ffn1"][0])
    cpack[:, 8:16] = col8(inp["norm_mix"][0])
    cpack[:, 16:24] = col8(inp["norm_ffn2"][0])
    cpack[:, 24:32] = col8(inp["norm_ple"][0])
    cpack[:, 32:40] = col8(inp["norm_final"])
    cpack[:, 40:44] = col8(inp["gn_gla"][0])
    cpack[:, 44:48] = col8(inp["gn_ret"][0])
    cpack[:, 48:52] = ct["decr"]
    xp, xs = inp["x_prompt"], inp["x_sample"]
    pp, psm = inp["p_prompt"][0], inp["p_sample"][0]
    in_maps = []
    for c in range(NCORES):
        xc = np.concatenate([xp[c], xs[16 * c:16 * c + 16].reshape(NSAMP, D)], axis=0)
        pc = np.concatenate([pp[c], psm[16 * c:16 * c + 16].reshape(NSAMP, 256)], axis=0)
        in_maps.append({
            "xT": np.ascontiguousarray(xc.T.astype(f32)),
            "pT": np.ascontiguousarray(pc.T.astype(f32)),
            "wflat": wflat,
            "cpack": cpack,
            "cmat": ct["cmat"], "masks": ct["masks"], "smask": ct["smask"], "maskq": ct["maskq"],
            "wup": np.ascontiguousarray(inp["w_alpha_up"][0].astype(f32)),
            "balpha": np.ascontiguousarray(inp["b_alpha"][0].reshape(1, 256).astype(f32)),
            "rtab": ct["rtab"], "ttab": ct["ttab"],
            "sg_in": np.ascontiguousarray(inp["state_gla"][0, 16 * c:16 * c + 16].astype(f32)),
            "sr_in": np.ascontiguousarray(inp["state_ret"][0, 16 * c:16 * c + 16].astype(f32)),
        })
    return in_maps


def _run(inp, stage=5):
    in_maps = _prep_inputs(inp)
    nc = build_program(stage)
    res = run_bass_kernel_spmd(nc, in_maps, core_ids=list(range(NCORES)))
    return res.results


def kernel(**inputs):
    results = _run(inputs, 5)
    f32 = np.float32
    y_prompt = np.zeros((8, TP, D), f32)
    y_sample = np.zeros((128, 4, D), f32)
    gp = np.zeros((1, 8, 4, 64, 128), f32); rp = np.zeros((1, 8, 4, 64, 128), f32)
    gs = np.zeros((1, 128, 4, 64, 128), f32); rs = np.zeros((1, 128, 4, 64, 128), f32)
    for c, r in enumerate(results):
        yc = np.asarray(r["yT"]).T
        y_prompt[c] = yc[:TP]
        y_sample[16 * c:16 * c + 16] = yc[TP:].reshape(16, 4, D)
        gp[0, c] = np.asarray(r["sgp"]).reshape(4, 64, 128)
        rp[0, c] = np.asarray(r["srp"]).reshape(4, 64, 128)
        gs[0, 16 * c:16 * c + 16] = np.asarray(r["sgs"])
        rs[0, 16 * c:16 * c + 16] = np.asarray(r["srs"])
    return (y_prompt, y_sample, gp, rp, gs, rs)
```

```python
import numpy as np
import ml_dtypes
import concourse.bass as bass
import concourse.mybir as mybir
from concourse.bass_utils import run_bass_kernel_spmd
from contextlib import ExitStack

F32 = mybir.dt.float32
BF16 = mybir.dt.bfloat16
AF = mybir.ActivationFunctionType
ALU = mybir.AluOpType

NCORES = 8
D = 1024
DFF = 2816
NFC = 22
TP = 2048
NSAMP = 64
TALL = TP + NSAMP
EPS = 1e-6
RING = 5
SLOT = 4096

ENGS = ("pe", "act", "dve", "pool", "sp")


class Op:
    __slots__ = ("eng", "fn", "reads", "writes", "dsem", "chan", "deps", "signaled", "count", "idx")

    def __init__(self, eng, fn, reads, writes, dsem):
        self.eng = eng
        self.fn = fn
        self.reads = tuple(reads)
        self.writes = tuple(writes)
        self.dsem = dsem
        self.chan = ("dma", dsem) if dsem is not None else ("eng", eng)
        self.deps = []
        self.signaled = dsem is not None
        self.count = 0


class Sched:
    def __init__(self):
        self.ops = []
        self.dma_sems = []
        self.barriers = []

    def op(self, eng, fn, reads=(), writes=(), dsem=None):
        o = Op(eng, fn, reads, writes, dsem)
        o.idx = len(self.ops)
        self.ops.append(o)
        if dsem is not None and dsem not in self.dma_sems:
            self.dma_sems.append(dsem)
        return o

    def barrier(self, exclude=()):
        self.barriers.append((len(self.ops), tuple(exclude)))

    def analyze(self):
        last_w, last_r = {}, {}
        waited = {e: {} for e in ENGS}
        chan_last = {}
        pending = {e: None for e in ENGS}
        bars = list(self.barriers)
        bi = 0
        ops = self.ops
        for o in ops:
            while bi < len(bars) and bars[bi][0] <= o.idx:
                snap = {ch: i for ch, i in chan_last.items()
                        if not (ch[0] == "dma" and isinstance(ch[1], tuple) and ch[1][0] == "w")}
                for e in ENGS:
                    if e in bars[bi][1]:
                        continue
                    if pending[e] is None:
                        pending[e] = snap
                    else:
                        m = dict(pending[e])
                        m.update(snap)
                        pending[e] = m
                bi += 1
            deps = {}

            def add(d):
                for ch, i in d.items():
                    if deps.get(ch, -1) < i:
                        deps[ch] = i

            if pending[o.eng] is not None:
                add(pending[o.eng])
                pending[o.eng] = None
            for k in o.reads:
                add(last_w.get(k, {}))
            for k in o.writes:
                add(last_w.get(k, {}))
                add(last_r.get(k, {}))
            w = waited[o.eng]
            for ch, i in deps.items():
                if ch == ("eng", "pe") and o.eng == "pe":
                    continue
                if w.get(ch, -1) >= i:
                    continue
                w[ch] = i
                o.deps.append(i)
                ops[i].signaled = True
            for k in o.reads:
                last_r.setdefault(k, {})[o.chan] = o.idx
            for k in o.writes:
                last_w[k] = {o.chan: o.idx}
                last_r[k] = {}
            chan_last[o.chan] = o.idx
        cnt = {}
        for o in ops:
            if o.signaled:
                inc = 16 if o.dsem is not None else 1
                cnt[o.chan] = cnt.get(o.chan, 0) + inc
                o.count = cnt[o.chan]
        self.final_counts = cnt

    def emit(self, nc, final_wait_eng="sp"):
        self.analyze()
        with ExitStack() as es:
            sems = {}
            for e in ENGS:
                sems[("eng", e)] = es.enter_context(nc.semaphore("sem_" + e))
            for i, d in enumerate(self.dma_sems):
                sems[("dma", d)] = es.enter_context(nc.semaphore("dsem_%d" % i))
            block = es.enter_context(nc.Block())
            ops = self.ops

            def run(engname):
                def body(eng):
                    for o in ops:
                        if o.eng != engname:
                            continue
                        for i in o.deps:
                            d = ops[i]
                            eng.wait_ge(sems[d.chan], d.count)
                        ins = o.fn(eng)
                        if o.signaled:
                            ins.then_inc(sems[o.chan], 16 if o.dsem is not None else 1)
                    if engname == final_wait_eng:
                        for ch, c in self.final_counts.items():
                            eng.wait_ge(sems[ch], c)
                return body

            block.tensor(run("pe"))
            block.scalar(run("act"))
            block.vector(run("dve"))
            block.gpsimd(run("pool"))
            block.sync(run("sp"))


def _tile_kc(W, cols):
    nk = W.shape[0] // 128
    sub = W[:, cols]
    return np.ascontiguousarray(sub.reshape(nk, 128, sub.shape[1]).transpose(1, 0, 2).reshape(128, -1))


def _swap_heads(idx):
    idx = np.asarray(idx).reshape(-1, 2, 32)
    return idx[:, ::-1, :].reshape(-1)


_QA, _KA, _VA, _RA = np.arange(0, 256), np.arange(256, 512), np.arange(512, 1024), np.arange(1024, 1536)
_QR, _KR, _VR, _GR = np.arange(1536, 1792), np.arange(1792, 2048), np.arange(2048, 2560), np.arange(2560, 3072)
_AL = np.arange(3072, 3088)
_GA, _GRT = np.arange(3088, 4112), np.arange(4112, 5136)


def _weight_tiles(inp):
    tiles = []

    def ffn(w_in, w_out):
        for g in range(6):
            wd = 512 if g < 5 else 256
            tiles.append(_tile_kc(w_in, np.arange(g * 512, g * 512 + wd)))
            tiles.append(_tile_kc(w_in, np.arange(DFF + g * 512, DFF + g * 512 + wd)))
        for dm in range(8):
            tiles.append(_tile_kc(w_out, np.arange(dm * 128, dm * 128 + 128)))

    ffn(inp["w_ffn1_in"][0], inp["w_ffn1_out"][0])
    wi = inp["w_in"][0]
    tiles.append(_tile_kc(wi, _AL))
    tiles.append(_tile_kc(wi, np.concatenate([_QA, _KA])))
    tiles.append(_tile_kc(wi, np.concatenate([_QR, _KR])))
    tiles.append(_tile_kc(wi, np.concatenate([_swap_heads(_QR), _swap_heads(_KR)])))
    tiles.append(_tile_kc(wi, np.concatenate([_KA, _KR])))
    tiles.append(_tile_kc(wi, _VA))
    tiles.append(_tile_kc(wi, _VR))
    tiles.append(_tile_kc(wi, _RA))
    tiles.append(_tile_kc(wi, _GR))
    wo = inp["w_out"][0]
    for dmg in range(2):
        c = np.arange(dmg * 512, dmg * 512 + 512)
        tiles.append(_tile_kc(wi, _GA[c]))
        tiles.append(_tile_kc(wi, _GRT[c]))
        tiles.append(_tile_kc(wo, c))
    ffn(inp["w_ffn2_in"][0], inp["w_ffn2_out"][0])
    for dmg in range(2):
        c = np.arange(dmg * 512, dmg * 512 + 512)
        tiles.append(_tile_kc(inp["w_ple_gate"][0], c))
        tiles.append(_tile_kc(inp["w_ple_proj"][0], c))
    sched = []
    off = 0
    for t in tiles:
        assert t.shape[1] <= SLOT
        sched.append((off, t.shape[1]))
        off += t.shape[1]
    return np.ascontiguousarray(np.concatenate(tiles, axis=1).astype(np.float32)), sched


def _weight_schedule():
    Ls = []

    def ffn():
        for g in range(6):
            wd = 512 if g < 5 else 256
            Ls.extend([8 * wd, 8 * wd])
        Ls.extend([NFC * 128] * 8)

    ffn()
    Ls.append(8 * 16)
    Ls.extend([8 * 512] * 8)
    Ls.extend([8 * 512] * 6)
    ffn()
    Ls.extend([8 * 512, 2 * 512] * 2)
    sched, off = [], 0
    for L in Ls:
        sched.append((off, L))
        off += L
    return sched, off


def _const_tables():
    f32 = np.float32
    half = 32
    freq = (10000.0 ** (-(np.arange(half, dtype=np.float64) / float(half)))).astype(f32)
    pos = np.concatenate([np.arange(TP), 16384 + (np.arange(NSAMP) % 4)]).astype(f32)
    il = np.concatenate([np.arange(TP) % 128, np.arange(NSAMP) % 4]).astype(np.float64)
    nchunk = np.concatenate([np.full(TP, 128.0), np.full(NSAMP, 4.0)])
    ang = (pos[:, None] * freq[None, :]).astype(f32).astype(np.float64)
    cos, sin = np.cos(ang), np.sin(ang)
    lg = np.log1p(-np.exp2(-5.0 - np.arange(4, dtype=np.float64)))
    d = np.arange(64)
    sgn = np.where(d < 32, -1.0, 1.0)
    CQ = np.zeros((2, 128, TALL)); SQ = np.zeros_like(CQ); CK = np.zeros_like(CQ); SK = np.zeros_like(CQ)
    for hp in range(2):
        for h2 in range(2):
            h = hp * 2 + h2
            up = np.exp((il + 1.0) * lg[h])[None, :]
            dn = np.exp(-(il + 1.0) * lg[h])[None, :] * 0.125
            c = cos[:, d % 32].T
            s = sin[:, d % 32].T * sgn[:, None]
            sl = slice(h2 * 64, h2 * 64 + 64)
            CQ[hp, sl] = c * up; SQ[hp, sl] = s * up; CK[hp, sl] = c * dn; SK[hp, sl] = s * dn
    rtab = np.zeros((5, 128, 8, 512), f32)
    for gtb in range(5):
        t0, n = (gtb * 512, 512) if gtb < 4 else (TP, NSAMP)
        for hp in range(2):
            for j, A in enumerate((CQ, SQ, CK, SK)):
                rtab[gtb, :, hp * 4 + j, :n] = A[hp, :, t0:t0 + n]
    ttab = np.zeros((17, 128, 2, 256), f32)
    for b in range(17):
        t0, n = (b * 128, 128) if b < 16 else (TP, NSAMP)
        for h in range(4):
            k = np.exp((nchunk[t0:t0 + n] - 1.0 - il[t0:t0 + n]) * lg[h])[:, None] * 0.125
            ttab[b, :n, 0, h * 64:(h + 1) * 64] = cos[t0:t0 + n][:, d % 32] * k
            ttab[b, :n, 1, h * 64:(h + 1) * 64] = sin[t0:t0 + n][:, d % 32] * sgn[None, :] * k
    decr = np.zeros((128, 4), f32)
    for hp in range(2):
        for h2 in range(2):
            h = hp * 2 + h2
            decr[h2 * 64:(h2 + 1) * 64, hp] = np.exp(128.0 * lg[h])
            decr[h2 * 64:(h2 + 1) * 64, 2 + hp] = np.exp(4.0 * lg[h])
    j = np.arange(128)[:, None]; i = np.arange(128)[None, :]
    cmat = np.zeros((128, 4, 128), f32)
    cmat[:, 0] = np.where(j <= i, -1.0 / 16, 0.0)
    cmat[:, 1] = np.where(j > i, -1.0 / 16, 0.0)
    same = (j // 4 == i // 4) & (j < 64) & (i < 64)
    cmat[:, 2] = np.where(same & (j <= i), -1.0 / 16, 0.0)
    cmat[:, 3] = np.where(same & (j > i), -1.0 / 16, 0.0)
    masks = np.zeros((128, 2, 128), f32)
    masks[:, 0] = (j <= i)
    masks[:, 1] = same & (j <= i)
    smask = (np.arange(64)[:, None] // 4 == np.arange(16)[None, :]).astype(f32)
    maskq = np.broadcast_to((np.arange(16)[:, None] == np.arange(64)[None, :] // 4).astype(f32).reshape(1, 1024),
                            (128, 1024)).copy()
    return dict(rtab=rtab.reshape(5, 128, 8 * 512), ttab=ttab.reshape(17, 128, 512), decr=decr,
                cmat=cmat.reshape(128, 512), masks=masks.reshape(128, 256), smask=smask, maskq=maskq)


def build_program(stage=5, nhalves=2, cut=99):
    nc = bass.Bass("TRN2", target_bir_lowering=False)
    S = Sched()
    wsched1, WTOT = _weight_schedule()

    def din(name, shape, dt=F32):
        return nc.dram_tensor(name, list(shape), dt, kind="ExternalInput").ap()

    def dout(name, shape, dt=F32):
        return nc.dram_tensor(name, list(shape), dt, kind="ExternalOutput").ap()

    xT = din("xT", [D, TALL]); pT = din("pT", [256, TALL])
    wflat = din("wflat", [128, WTOT])
    cpack = din("cpack", [128, 52])
    cmat_d = din("cmat", [128, 512]); masks_d = din("masks", [128, 256]); smask_d = din("smask", [64, 16])
    maskq_d = din("maskq", [128, 1024])
    wup_d = din("wup", [16, 256]); balpha_d = din("balpha", [1, 256])
    rtab_d = din("rtab", [5, 128, 8 * 512]); ttab_d = din("ttab", [17, 128, 512])
    sg_in = din("sg_in", [16, 4, 64, 128]); sr_in = din("sr_in", [16, 4, 64, 128])
    yT = dout("yT", [D, TALL])
    sgp = dout("sgp", [256, 128]); srp = dout("srp", [256, 128])
    sgs = dout("sgs", [16, 4, 64, 128]); srs = dout("srs", [16, 4, 64, 128])
    st_in = (sg_in, sr_in); st_out = (sgs, srs); sp_out = (sgp, srp)

    SB_BASE, SB_END = 16512, 229376
    ptr = [SB_BASE]

    def esz(dt):
        return 2 if dt == BF16 else 4

    def palloc(name, shape, dt):
        nbytes = int(np.prod(shape[1:])) * esz(dt)
        nbytes = (nbytes + 31) // 32 * 32
        t = nc.alloc_sbuf_tensor_at(name, list(shape), dt, offset=ptr[0])
        ptr[0] += nbytes
        return t

    TM = 1088
    h = palloc("h", [128, 8, TM], F32)
    u = palloc("u", [128, 8, TM], BF16)
    ring = [palloc("ring%d" % i, [128, SLOT], BF16) for i in range(RING)]
    gains = palloc("gains", [128, 52], F32)
    cmat = palloc("cmat_s", [128, 4, 128], BF16)
    masks = palloc("masks_s", [128, 2, 128], F32)
    smask = palloc("smask_s", [64, 16], F32)
    maskq = palloc("maskq_s", [128, 16, 64], BF16)
    wup = palloc("wup_s", [16, 256], BF16)
    balpha = palloc("balpha_s", [1, 256], BF16)
    ones_bf = palloc("ones_bf", [128, 128], BF16)
    ones_row = palloc("ones_row", [1, 128], BF16)
    Sst = palloc("Sst", [128, 4, 128], F32)
    Sbf = palloc("Sbf", [128, 3, 4, 128], BF16)
    decA = palloc("decA", [128, 2, 9], F32)
    decSA = palloc("decSA", [128, 2, 16], F32)
    sq = palloc("sq", [128, 2, 512], BF16)
    rstd = palloc("rstd", [128, 2, 512], F32)
    UB = ptr[0]
    USZ = SB_END - UB

    def ualloc(name, shape, dt, off):
        nbytes = int(np.prod(shape[1:])) * esz(dt)
        assert off % 32 == 0 and off + nbytes <= USZ, (name, off, nbytes, USZ)
        return nc.alloc_sbuf_tensor_at(name, list(shape), dt, offset=UB + off), off + (nbytes + 31) // 32 * 32

    gbuf, o = ualloc("gbuf", [128, NFC, TM], BF16, 0)
    stmp, o = ualloc("stmp", [128, 2, 512], F32, o)
    pb, o_pb = ualloc("pb", [128, 2, TM], BF16, o)
    stmp2, _ = ualloc("stmp2", [128, 2, 512], F32, o_pb)
    yout, _ = ualloc("yout", [128, 2, 8, 512], F32, o_pb + 4096)
    sq8f, _ = ualloc("sq8f", [128, 8, 512], BF16, o_pb + 4096 + 32768)
    sq8m, _ = ualloc("sq8m", [128, 8, 512], BF16, 17408)
    qk, o = ualloc("qk", [128, 8, TM], BF16, 0)
    vtm, o = ualloc("vtm", [128, 9, 1024], BF16, o)
    kh, o = ualloc("kh", [128, 9, 512], BF16, o)
    o_tr = o
    ed, o = ualloc("ed", [128, 9, 256], F32, o)
    alowT, o = ualloc("alowT", [16, TM], BF16, o)
    spb, o = ualloc("spb", [128, 2, 256], F32, o)
    sphi, o = ualloc("sphi", [128, 2, 256], BF16, o)
    splo, o = ualloc("splo", [128, 2, 256], BF16, o)
    etmp, o = ualloc("etmp", [128, 2, 256], F32, o)
    eb, o = ualloc("eb", [128, 2, 512], F32, o)
    enb, o = ualloc("enb", [128, 2, 512], F32, o)
    rtab, o = ualloc("rtab_s", [128, 8, 512], F32, o)
    ttab, o = ualloc("ttab_s", [128, 3, 512], F32, o)
    tA, o = ualloc("tA", [128, 512], F32, o)
    tB, o = ualloc("tB", [128, 512], F32, o)
    tC, o = ualloc("tC", [128, 512], F32, o)
    tD, o = ualloc("tD", [128, 512], F32, o)
    o = o_tr
    PT, o = ualloc("PT", [128, 8, 128], BF16, o)
    osb, o = ualloc("osb", [128, 2, 512], F32, o)
    sqb, o = ualloc("sqb", [128, 2, 512], BF16, o)
    rs2, o = ualloc("rs2", [128, 2, 512], F32, o)
    ee, o = ualloc("ee", [128, 2, 512], F32, o)
    tt1, o = ualloc("tt1", [128, 2, 512], F32, o)
    tt2, o = ualloc("tt2", [128, 2, 512], F32, o)
    S0bf, o = ualloc("S0bf", [128, 2, 16 * 128], BF16, o)
    S0f_buf, o = ualloc("S0f", [128, 16 * 128], F32, o)
    tmpS_buf, o = ualloc("tmpS", [128, 16 * 128], F32, o)
    Vblk, o = ualloc("Vblk", [64, 16 * 128], BF16, o)
    qm, o = ualloc("qm", [128, 16, 64], BF16, o)
    print("SBUF union size", USZ, "P2 end", o)

    PSA = nc.alloc_psum_tensor("PSA", [128, 2048], F32)
    PSB = nc.alloc_psum_tensor("PSB", [128, 2048], F32)

    def bank(b):
        t = PSA if b < 4 else PSB
        return t[:, (b % 4) * 512:(b % 4) * 512 + 512]

    def mm(out, lhsT, rhs, start, stop, reads, writes):
        S.op("pe", lambda e: e.matmul(out, lhsT=lhsT, rhs=rhs, start=start, stop=stop), reads, writes)

    def act(out, in_, func, reads, writes, scale=1.0, bias=0.0):
        S.op("act", lambda e: e.activation(out=out, in_=in_, func=func, bias=bias, scale=scale), reads, writes)

    def tt(out, in0, in1, op, reads, writes, eng="dve"):
        S.op(eng, lambda e: e.tensor_tensor(out=out, in0=in0, in1=in1, op=op), reads, writes)

    def stt(out, in0, scalar, in1, op0, op1, reads, writes):
        S.op("dve", lambda e: e.scalar_tensor_tensor(out=out, in0=in0, scalar=scalar, in1=in1, op0=op0, op1=op1),
             reads, writes)

    def tsmul(out, in0, scalar, reads, writes):
        S.op("dve", lambda e: e.tensor_scalar(out=out, in0=in0, scalar1=scalar, scalar2=None, op0=ALU.mult),
             reads, writes)

    def dma(eng, out, in_, reads, writes, dsem):
        S.op(eng, lambda e: e.dma_start(out=out, in_=in_), reads, writes, dsem=dsem)

    def memset(ap, val, writes):
        S.op("dve", lambda e: e.memset(ap, val), (), writes)

    full_sched = wsched1 + wsched1
    st = {"cur": 0}

    def issue(i):
        if i >= len(full_sched):
            return
        off, L = full_sched[i]
        slot = i % RING
        dma("pool", ring[slot][:, 0:L], wflat[:, off:off + L], (), [("ring", slot)], ("w", slot))

    def wnext(nk, ncols):
        i = st["cur"]
        st["cur"] += 1
        off, L = full_sched[i]
        assert L == nk * ncols, (i, L, nk, ncols)
        slot = i % RING
        return i, ring[slot][:, 0:L].rearrange("p (k c) -> p k c", k=nk), ("ring", slot)

    def wrelease(i):
        issue(i + RING)

    dma("sp", gains[:], cpack[:], (), ["gains"], "c0")

    def late_consts():
        dma("pool", cmat[:].rearrange("p a b -> p (a b)"), cmat_d[:], (), ["cmat"], "c1")
        dma("sp", masks[:].rearrange("p a b -> p (a b)"), masks_d[:], (), ["masks"], "c2")
        dma("sp", smask[:], smask_d[:], (), ["smask"], "c3")
        dma("pool", maskq[:].rearrange("p a b -> p (a b)"), maskq_d[:], (), ["maskq"], "c6")
        dma("pool", wup[:], wup_d[:], (), ["wup"], "c4")
        dma("pool", balpha[:], balpha_d[:], (), ["balpha"], "c5")
    memset(ones_bf[:], 1.0, ["ones_bf"])
    memset(ones_row[:], 1.0, ["ones_row"])
    memset(Sst[:].rearrange("p a b -> p (a b)"), 0.0, ["S0", "S1", "S2", "S3"])
    memset(Sbf[:].rearrange("p t a b -> p (t a b)"), 0.0, [("Sb", t_, b_) for t_ in range(3) for b_ in range(2)])
    for i in range(RING):
        issue(i)

    G_FFN1, G_MIX, G_FFN2, G_PLE, G_FIN, G_GNA, G_GNR, G_DECR = 0, 8, 16, 24, 32, 40, 44, 48

    def rmsnorm(tbs, gcol, dst_fn, dst_keys):
        for tbi, (off, n) in enumerate(tbs):
            pn = bank(6)
            for kc in range(8):
                b = kc % 2
                act(sq[:, b, 0:n], h[:, kc, off:off + n], AF.Square, [("h", kc, tbi)], [("sq", b)])
                mm(pn[:, 0:n], ones_bf[:], sq[:, b, 0:n], kc == 0, kc == 7, ["ones_bf", ("sq", b)], [("ps", 6)])
            rb = tbi % 2
            act(rstd[:, rb, 0:n], pn[:, 0:n], AF.Ln, [("ps", 6)], [("rstd", rb)], scale=1.0 / D, bias=EPS)
            act(rstd[:, rb, 0:n], rstd[:, rb, 0:n], AF.Exp, [("rstd", rb)], [("rstd", rb)], scale=-0.5)
            for kc in range(8):
                stt(dst_fn(kc, tbi, off, n), h[:, kc, off:off + n], gains[:, gcol + kc:gcol + kc + 1],
                    rstd[:, rb, 0:n], ALU.mult, ALU.mult,
                    [("h", kc, tbi), "gains", ("rstd", rb)], dst_keys(kc, tbi))

    def norm_to_u(tbs, gcol):
        rmsnorm(tbs, gcol, lambda kc, tbi, off, n: u[:, kc, off:off + n], lambda kc, tbi: [("u", kc, tbi)])

    def norm_u_tb(tbi, off, n, gcol):
        pn = bank(6)
        for kc in range(8):
            b = kc % 2
            act(sq[:, b, 0:n], h[:, kc, off:off + n], AF.Square, [("h", kc, tbi)], [("sq", b)])
            mm(pn[:, 0:n], ones_bf[:], sq[:, b, 0:n], kc == 0, kc == 7, ["ones_bf", ("sq", b)], [("ps", 6)])
        rb = tbi % 2
        act(rstd[:, rb, 0:n], pn[:, 0:n], AF.Ln, [("ps", 6)], [("rstd", rb)], scale=1.0 / D, bias=EPS)
        act(rstd[:, rb, 0:n], rstd[:, rb, 0:n], AF.Exp, [("rstd", rb)], [("rstd", rb)], scale=-0.5)
        for kc in range(8):
            stt(u[:, kc, off:off + n], h[:, kc, off:off + n], gains[:, gcol + kc:gcol + kc + 1],
                rstd[:, rb, 0:n], ALU.mult, ALU.mult,
                [("h", kc, tbi), "gains", ("rstd", rb)], [("u", kc, tbi)])

    def norm_part1(sq8, tbi, off, n):
        for kc in range(8):
            act(sq8[:, kc, 0:n], h[:, kc, off:off + n], AF.Square, [("h", kc, tbi)], [("sq8", kc)])

    def norm_part2(sq8, tbi, off, n, gcol):
        pn = bank(6)
        for kc in range(8):
            mm(pn[:, 0:n], ones_bf[:], sq8[:, kc, 0:n], kc == 0, kc == 7, ["ones_bf", ("sq8", kc)], [("ps", 6)])
        rb = tbi % 2
        act(rstd[:, rb, 0:n], pn[:, 0:n], AF.Ln, [("ps", 6)], [("rstd", rb)], scale=1.0 / D, bias=EPS)
        act(rstd[:, rb, 0:n], rstd[:, rb, 0:n], AF.Exp, [("rstd", rb)], [("rstd", rb)], scale=-0.5)
        for kc in range(8):
            stt(u[:, kc, off:off + n], h[:, kc, off:off + n], gains[:, gcol + kc:gcol + kc + 1],
                rstd[:, rb, 0:n], ALU.mult, ALU.mult,
                [("h", kc, tbi), "gains", ("rstd", rb)], [("u", kc, tbi)])

    def ffn(tbs, lazy, next_gcol=None):
        cnt = 0
        for g in range(6):
            nf = 4 if g < 5 else 2
            ia, wa, ka = wnext(8, nf * 128)
            ib, wb, kb = wnext(8, nf * 128)
            for tbi, (off, n) in enumerate(tbs):
                if g == 0 and tbi > 0:
                    lazy(tbi)
                for fl in range(nf):
                    fc = g * 4 + fl
                    pa, pbk = bank(cnt % 2), bank(2 + cnt % 2)
                    ka_, kb_ = ("ps", cnt % 2), ("ps", 2 + cnt % 2)
                    for kc in range(8):
                        mm(pa[:, 0:n], wa[:, kc, fl * 128:(fl + 1) * 128], u[:, kc, off:off + n], kc == 0, kc == 7,
                           [ka, ("u", kc, tbi)], [ka_])
                    for kc in range(8):
                        mm(pbk[:, 0:n], wb[:, kc, fl * 128:(fl + 1) * 128], u[:, kc, off:off + n], kc == 0, kc == 7,
                           [kb, ("u", kc, tbi)], [kb_])
                    sb_ = cnt % 2
                    act(stmp[:, sb_, 0:n], pa[:, 0:n], AF.Silu, [ka_], [("stmp", sb_)])
                    tt(gbuf[:, fc, off:off + n], stmp[:, sb_, 0:n], pbk[:, 0:n], ALU.mult,
                       [("stmp", sb_), kb_], [("g", fc, tbi)])
                    cnt += 1
            wrelease(ia)
            wrelease(ib)
        cnt = 0
        for dm in range(8):
            io, wo, ko = wnext(NFC, 128)
            for tbi, (off, n) in enumerate(tbs):
                po = bank(4 + cnt % 2)
                kp = ("ps", 4 + cnt % 2)
                for fc in range(NFC):
                    mm(po[:, 0:n], wo[:, fc, :], gbuf[:, fc, off:off + n], fc == 0, fc == NFC - 1,
                       [ko, ("g", fc, tbi)], [kp])
                stt(h[:, dm, off:off + n], po[:, 0:n], 0.5, h[:, dm, off:off + n], ALU.mult, ALU.add,
                    [kp, ("h", dm, tbi)], [("h", dm, tbi)])
                cnt += 1
                if next_gcol is not None and dm == 7:
                    if tbi == 0:
                        norm_part1(sq8f, 0, tbs[0][0], tbs[0][1])
                    elif tbi == 1:
                        norm_part2(sq8f, 0, tbs[0][0], tbs[0][1], next_gcol)
            wrelease(io)

    def mixing(hi, tok0, tbs, blocks, lazy, next_gcol=None):
        mix_start = st["cur"]

        def bail():
            skip_tiles(mix_start + 15 - st["cur"])

        T = sum(n for _, n in tbs)
        def rtab_load(tbi_, hp_):
            off_, n_ = tbs[tbi_]
            gtb = 4 if n_ == 64 else (tok0 + off_) // 512
            src = rtab_d[gtb].rearrange("p (a b) -> p a b", a=8)[:, hp_ * 4:(hp_ + 1) * 4, 0:n_]
            dma("sp", rtab[:, hp_ * 4:(hp_ + 1) * 4, 0:n_], src, (), [("rtab", hp_)], ("rt", hp_))

        def ttab_load(bi_):
            dma("sp", ttab[:, bi_ % 3, :], ttab_d[blocks[bi_][3]], (), [("ttab", bi_ % 3)], ("tt", bi_ % 3))

        rtab_load(0, 0)
        rtab_load(0, 1)
        for bi_ in range(min(3, len(blocks))):
            ttab_load(bi_)
        ial, wal, kal = wnext(8, 16)
        i0, w0, k0 = wnext(8, 512)
        for tbi, (off, n) in enumerate(tbs):
            if tbi > 0:
                lazy(tbi)
            pA = bank(7)
            for kc in range(8):
                mm(pA[0:16, 0:n], wal[:, kc, 0:16], u[:, kc, off:off + n], kc == 0, kc == 7,
                   [kal, ("u", kc, tbi)], [("ps", 7)])
            S.op("act", (lambda o_, i_: (lambda e: e.copy(out=o_, in_=i_)))(alowT[0:16, off:off + n], pA[0:16, 0:n]),
                 [("ps", 7)], [("alowT", tbi)])
            pend_b = []
            for bi, (boff, bn, smp, gblk) in enumerate(blocks):
                if not (off <= boff < off + n) or cut <= 0.2:
                    continue
                lo = boff - off
                dp = bi % 2
                px = bank(6 + dp)
                kpx = ("ps", 6 + dp)
                mm(px[0:bn, 0:256], alowT[0:16, boff:boff + bn], wup[:, :], True, False,
                   [("alowT", tbi), "wup"], [kpx])
                mm(px[0:bn, 0:256], ones_row[0:1, 0:bn], balpha[0:1, :], False, True,
                   ["ones_row", "balpha"], [kpx])
                act(etmp[0:bn, dp, :], px[0:bn, 0:256], AF.Exp, [kpx], [("etmp", dp)], scale=-1.0)
                act(spb[0:bn, dp, :], etmp[0:bn, dp, :], AF.Ln, [("etmp", dp)], [("spb", dp)], bias=1.0)
                if cut <= 0.4:
                    continue
                S.op("dve", (lambda o_, i_: (lambda e: e.tensor_copy(out=o_, in_=i_)))(sphi[0:bn, dp, :], spb[0:bn, dp, :]),
                     [("spb", dp)], [("sphi", dp)])
                tt(splo[0:bn, dp, :], spb[0:bn, dp, :], sphi[0:bn, dp, :], ALU.subtract,
                   [("spb", dp), ("sphi", dp)], [("splo", dp)])
                def part_b(bi=bi, boff=boff, bn=bn, smp=smp, lo=lo, dp=dp):
                    ci = 2 if smp else 0
                    b5 = 4 + bi % 2
                    p5 = bank(b5)
                    for hp in range(2):
                        for xi, (spx, kx) in enumerate(((sphi, ("sphi", dp)), (splo, ("splo", dp)))):
                            mm(p5[:, hp * 128:hp * 128 + bn], spx[0:bn, dp, hp * 128:(hp + 1) * 128],
                               cmat[0:bn, ci, 0:bn], xi == 0, xi == 1, [kx, "cmat"], [("ps", b5)])
                    for xi, (spx, kx) in enumerate(((sphi, ("sphi", dp)), (splo, ("splo", dp)))):
                        mm(p5[0:bn, 256:512], cmat[0:bn, ci + 1, 0:bn], spx[0:bn, dp, 0:256], xi == 0, xi == 1,
                           [kx, "cmat"], [("ps", b5)])
                    for hp in range(2):
                        src = p5[:, hp * 128:hp * 128 + bn]
                        act(eb[:, hp, lo:lo + bn], src, AF.Exp, [("ps", b5)], [("eb", hp)])
                        act(enb[:, hp, lo:lo + bn], src, AF.Exp, [("ps", b5)], [("enb", hp)], scale=-1.0)
                        if smp:
                            act(decSA[:, hp, :], p5[:, hp * 128 + 3:hp * 128 + 64:4], AF.Exp, [("ps", b5)], ["decSA"])
                        else:
                            act(decA[:, hp, bi:bi + 1], p5[:, hp * 128 + bn - 1:hp * 128 + bn], AF.Exp,
                                [("ps", b5)], [("decA", hp, bi)])
                    act(ed[0:bn, bi, :], p5[0:bn, 256:512], AF.Exp, [("ps", b5)], [("ed", bi)])

                for f_ in pend_b:
                    f_()
                pend_b = [part_b]
            for f_ in pend_b:
                f_()
            pend_b = []
            for ch in range(4):
                if cut <= 0.8:
                    continue
                pq = bank(ch)
                for kc in range(8):
                    mm(pq[:, 0:n], w0[:, kc, ch * 128:(ch + 1) * 128], u[:, kc, off:off + n], kc == 0, kc == 7,
                       [k0, ("u", kc, tbi)], [("ps", ch)])
                if cut <= 0.85:
                    continue
                if ch < 2:
                    stt(qk[:, ch, off:off + n], pq[:, 0:n], 0.125, eb[:, ch, 0:n], ALU.mult, ALU.mult,
                        [("ps", ch), ("eb", ch)], [("qk", 0, tbi, ch)])
                elif cut <= 0.9:
                    continue
                elif cut <= 0.95:
                    tt(qk[:, ch, off:off + n], pq[:, 0:n], eb[:, ch - 2, 0:n], ALU.mult,
                       [("ps", ch), ("eb", ch - 2)], [("qk", 0, tbi, ch)])
                elif cut <= 0.97:
                    stt(qk[:, ch, off:off + n], pq[:, 0:n], 1.0, enb[:, ch - 2, 0:n], ALU.mult, ALU.mult,
                        [("ps", ch), ("enb", ch - 2)], [("qk", 0, tbi, ch)])
                else:
                    tt(qk[:, ch, off:off + n], pq[:, 0:n], enb[:, ch - 2, 0:n], ALU.mult,
                       [("ps", ch), ("enb", ch - 2)], [("qk", 0, tbi, ch)])
        wrelease(ial)
        wrelease(i0)
        if cut <= 1:
            return bail()
        i1, w1, k1 = wnext(8, 512)
        i2, w2, k2 = wnext(8, 512)
        cnt = 0
        for tbi, (off, n) in enumerate(tbs):
            for hp in range(2):
                for isk in range(2):
                    ch = hp + 2 * isk
                    b0 = (cnt % 2) * 2
                    pa, pbk = bank(b0), bank(b0 + 1)
                    for kc in range(8):
                        mm(pa[:, 0:n], w1[:, kc, ch * 128:(ch + 1) * 128], u[:, kc, off:off + n], kc == 0, kc == 7,
                           [k1, ("u", kc, tbi)], [("ps", b0)])
                    for kc in range(8):
                        mm(pbk[:, 0:n], w2[:, kc, ch * 128:(ch + 1) * 128], u[:, kc, off:off + n], kc == 0, kc == 7,
                           [k2, ("u", kc, tbi)], [("ps", b0 + 1)])
                    tt(tA[:, 0:n], pa[:, 0:n], rtab[:, hp * 4 + 2 * isk, 0:n], ALU.mult, [("ps", b0), ("rtab", hp)], ["tA"])
                    tt(tB[:, 0:n], pbk[:, 0:n], rtab[:, hp * 4 + 2 * isk + 1, 0:n], ALU.mult,
                       [("ps", b0 + 1), ("rtab", hp)], ["tB"])
                    qi = 4 + 2 * isk + hp
                    tt(qk[:, qi, off:off + n], tA[:, 0:n], tB[:, 0:n], ALU.add, ["tA", "tB"], [("qk", 1, tbi, qi)])
                    cnt += 1
                if tbi + 1 < len(tbs):
                    rtab_load(tbi + 1, hp)
        wrelease(i1)
        wrelease(i2)
        if cut <= 2:
            return bail()
        i3, w3, k3 = wnext(8, 512)
        for bi, (boff, bn, smp, gblk) in enumerate(blocks):
            tbi = min(boff // 512, len(tbs) - 1)
            tb_ = bi % 3
            pk = bank(bi % 2)
            kp = ("ps", bi % 2)
            for kc in range(8):
                mm(pk[0:bn, :], u[:, kc, boff:boff + bn], w3[:, kc, :], kc == 0, kc == 7, [k3, ("u", kc, tbi)], [kp])
            tt(kh[0:bn, bi, 0:256], pk[0:bn, 0:256], ed[0:bn, bi, :], ALU.mult, [kp, ("ed", bi)], [("kh", bi)])
            pr = pk[0:bn, 256:512].rearrange("p (h s e) -> p h s e", h=4, s=2)
            Ct = ttab[0:bn, tb_, 0:256].rearrange("p (h s e) -> p h s e", h=4, s=2)
            St = ttab[0:bn, tb_, 256:512].rearrange("p (h s e) -> p h s e", h=4, s=2)
            kr = kh[0:bn, bi, 256:512].rearrange("p (h s e) -> p h s e", h=4, s=2)
            tA4 = tA[0:bn, 0:256].rearrange("p (h s e) -> p h s e", h=4, s=2)
            tB4 = tB[0:bn, 0:256].rearrange("p (h s e) -> p h s e", h=4, s=2)
            for s_ in range(2):
                tt(tA4[:, :, s_, :], pr[:, :, s_, :], Ct[:, :, s_, :], ALU.mult, [kp, ("ttab", tb_)], ["tA"])
                tt(tB4[:, :, s_, :], pr[:, :, 1 - s_, :], St[:, :, s_, :], ALU.mult, [kp, ("ttab", tb_)], ["tB"])
                tt(kr[:, :, s_, :], tA4[:, :, s_, :], tB4[:, :, s_, :], ALU.add, ["tA", "tB"], [("kh", bi)])
            if bi + 3 < len(blocks):
                ttab_load(bi + 3)
        wrelease(i3)
        if cut <= 3:
            return bail()
        for vi in range(2):
            iv, wv, kv = wnext(8, 512)
            for bi, (boff, bn, smp, gblk) in enumerate(blocks):
                tbi = min(boff // 512, len(tbs) - 1)
                pv = bank(2 + bi % 2)
                kp = ("ps", 2 + bi % 2)
                for kc in range(8):
                    mm(pv[0:bn, :], u[:, kc, boff:boff + bn], wv[:, kc, :], kc == 0, kc == 7,
                       [kv, ("u", kc, tbi)], [kp])
                S.op("act", (lambda o_, i_: (lambda e: e.copy(out=o_, in_=i_)))(
                    vtm[0:bn, bi, vi * 512:(vi + 1) * 512], pv[0:bn, :]), [kp], [("v", bi, vi)])
            wrelease(iv)
        if cut <= 4:
            return bail()
        S.barrier(exclude=("pe",))
        igt = []
        wg = []
        kg = []
        for br in range(2):
            i_, w_, k_ = wnext(8, 512)
            igt.append(i_); wg.append(w_); kg.append(k_)
        ecnt = [0]

        def epilogue(br, hl, po, off, n, tbi, pokey, ebanks=((2, 3), (0, 1))):
            ob = ecnt[0] % 2
            ecnt[0] += 1
            tbk, rbk = ebanks[ob]
            act(sqb[:, ob, 0:n], po, AF.Square, [pokey], [("sqb", ob)])
            pR = bank(rbk)
            for kc in range(8):
                mm(pR[:, 0:n], wg[br][:, kc, hl * 128:(hl + 1) * 128], u[:, kc, off:off + n], kc == 0, kc == 7,
                   [kg[br], ("u", kc, tbi)], [("ps", rbk)])
            pT_ = bank(tbk)
            mm(pT_[:, 0:n], ones_bf[:], sqb[:, ob, 0:n], True, True, ["ones_bf", ("sqb", ob)], [("ps", tbk)])
            act(rs2[:, ob, 0:n], pT_[:, 0:n], AF.Ln, [("ps", tbk)], [("rs2", ob)], scale=1.0 / 128, bias=EPS)
            act(ee[:, ob, 0:n], pR[:, 0:n], AF.Exp, [("ps", rbk)], [("ee", ob)], scale=-1.0)
            act(ee[:, ob, 0:n], ee[:, ob, 0:n], AF.Ln, [("ee", ob)], [("ee", ob)], bias=1.0)
            stt(tt2[:, ob, 0:n], rs2[:, ob, 0:n], -0.5, ee[:, ob, 0:n], ALU.mult, ALU.subtract,
                [("rs2", ob), ("ee", ob)], [("tt2", ob)])
            act(tt2[:, ob, 0:n], tt2[:, ob, 0:n], AF.Exp, [("tt2", ob)], [("tt2", ob)])
            gc = (G_GNA if br == 0 else G_GNR) + hl
            stt(tt1[:, ob, 0:n], po, gains[:, gc:gc + 1], tt2[:, ob, 0:n], ALU.mult, ALU.mult,
                [pokey, "gains", ("tt2", ob)], [("tt1", ob)])
            qi = br * 4 + hl
            wkeys = [("qk", br, tbi, br * 4 + j) for j in range(4)]
            tt(qk[:, qi, off:off + n], tt1[:, ob, 0:n], pR[:, 0:n], ALU.mult, [("tt1", ob), ("ps", rbk)], wkeys)

        scnt = [0]
        SX = (S0f_buf, tmpS_buf)

        def s0_load(sidx):
            br_, pair_ = sidx // 2, sidx % 2
            src = st_in[br_][:, pair_ * 2:pair_ * 2 + 2].rearrange("s h d v -> (h d) s v")
            sb_ = sidx % 2
            dma("pool", S0bf[:, sb_, :].rearrange("p (s v) -> p s v", s=16), src, (), [("S0bf", sb_)], ("s0b", sb_))
            x_ = SX[sidx % 2]
            kx = ("SX", sidx % 2)
            dma("sp", x_[:].rearrange("p (s v) -> p s v", s=16), src, (), [kx, kx + (0,), kx + (1,)], ("s0f", sidx % 2))

        if hi == 1:
            s0_load(0)
        for tbi, (off, n) in enumerate(tbs):
            smp_tb = (n == 64)
            rkeys_all = lambda br: [("qk", br, tbi, br * 4 + j) for j in range(4)]
            for br in range(2):
                qb, kb_i = br * 4, br * 4 + 2
                if not smp_tb:
                    tblocks = [(bi, b) for bi, b in enumerate(blocks) if off <= b[0] < off + n]
                    pend = []

                    def o_ops(c4, bi, boff, bn, gblk):
                        par = c4 % 2
                        for hl in range(4):
                            pair, h2 = hl // 2, hl % 2
                            sidx = br * 2 + pair
                            pr_ = slice(h2 * 64, h2 * 64 + 64)
                            po = bank(4 + hl)[:, c4 * 128:(c4 + 1) * 128]
                            vc = br * 512 + hl * 128
                            mm(po, vtm[:, bi, vc:vc + 128], PT[:, par * 4 + h2 * 2 + pair, :], True, False,
                               [("v", bi, br), ("PT", par)], [("ps", 4 + hl)])
                            mm(po, Sbf[pr_, (gblk - 1) % 3, sidx, :], qk[pr_, qb + pair, boff:boff + bn], False, True,
                               [("Sb", (gblk - 1) % 3, br)] + rkeys_all(br), [("ps", 4 + hl)])

                    for c4, (bi, (boff, bn, smp, gblk)) in enumerate(tblocks):
                        par = c4 % 2
                        ubk = (3, 1)[par]
                        pu = bank(ubk)
                        for pair in range(2):
                            kc0 = br * 256 + pair * 128
                            vc0 = br * 512 + pair * 256
                            mm(pu[:, pair * 256:(pair + 1) * 256], kh[:, bi, kc0:kc0 + 128], vtm[:, bi, vc0:vc0 + 256],
                               True, True, [("kh", bi), ("v", bi, br)], [("ps", ubk)])
                        for hl in (0, 2, 1, 3):
                            pair, h2 = hl // 2, hl % 2
                            pr_ = slice(h2 * 64, h2 * 64 + 64)
                            sb2 = (0, 2)[h2]
                            mm(bank(sb2)[:, pair * 128:(pair + 1) * 128], qk[pr_, kb_i + pair, boff:boff + bn],
                               qk[pr_, qb + pair, boff:boff + bn], True, True, rkeys_all(br), [("ps", sb2)])
                        for f_ in pend:
                            f_()
                        pend = []
                        for h2 in range(2):
                            sb2 = (0, 2)[h2]
                            tt(PT[:, par * 4 + h2 * 2:par * 4 + h2 * 2 + 2, :],
                               bank(sb2)[:, 0:256].rearrange("p (a b) -> p a b", a=2),
                               masks[:, 0, :].unsqueeze(1).to_broadcast([128, 2, 128]), ALU.mult,
                               [("ps", sb2), "masks"], [("PT", par)])
                        for pair in range(2):
                            sidx = br * 2 + pair
                            for h2 in range(2):
                                pr_ = slice(h2 * 64, h2 * 64 + 64)
                                if br == 0:
                                    dsc = decA[pr_, pair, bi:bi + 1]
                                    dk_ = [("decA", pair, bi)]
                                else:
                                    dsc = gains[pr_, G_DECR + pair:G_DECR + pair + 1]
                                    dk_ = ["gains"]
                                stt(Sst[pr_, sidx, :], Sst[pr_, sidx, :], dsc,
                                    pu[pr_, pair * 256 + h2 * 128:pair * 256 + (h2 + 1) * 128],
                                    ALU.mult, ALU.add, ["S%d" % sidx, ("ps", ubk)] + dk_, ["S%d" % sidx])
                        S.op("act", (lambda o_, i_: (lambda e: e.copy(out=o_, in_=i_)))(
                            Sbf[:, gblk % 3, br * 2:br * 2 + 2, :], Sst[:, br * 2:br * 2 + 2, :]),
                            ["S%d" % (br * 2), "S%d" % (br * 2 + 1)], [("Sb", gblk % 3, br)])
                        pend.append((lambda a, b, c, d, e_: (lambda: o_ops(a, b, c, d, e_)))(c4, bi, boff, bn, gblk))
                    for f_ in pend:
                        f_()
                    for hl in range(4):
                        eb_ = ((2, 3), (0, 1)) if hl < 2 else ((2, 4), (0, 5))
                        epilogue(br, hl, bank(4 + hl)[:, 0:n], off, n, tbi, ("ps", 4 + hl), eb_)
                else:
                    bi = len(blocks) - 1
                    boff, bn, smp, gblk = blocks[bi]
                    for pair in range(2):
                        sidx = br * 2 + pair
                        sb_ = sidx % 2
                        S0f, tmpS = (SX[sidx % 2], SX[1 - sidx % 2])
                        kS0f, kTmp = ("SX", sidx % 2), ("SX", 1 - sidx % 2)
                        S0b3 = S0bf[:, sb_, :].rearrange("p (s v) -> p s v", s=16)
                        S0f3 = S0f[:].rearrange("p (s v) -> p s v", s=16)
                        tmp3 = tmpS[:].rearrange("p (s v) -> p s v", s=16)
                        for h2 in range(2):
                            hl = pair * 2 + h2
                            pr_ = slice(h2 * 64, h2 * 64 + 64)
                            slot = scnt[0] % 4
                            scnt[0] += 1
                            sbk = (0, 2)[slot % 2]
                            psS = bank(sbk)[0:64, 0:64]
                            mm(psS, qk[pr_, kb_i + pair, boff:boff + 64], qk[pr_, qb + pair, boff:boff + 64],
                               True, True, rkeys_all(br), [("ps", sbk)])
                            pts = slot + 4 * br
                            tt(PT[0:64, pts, 0:64], psS, masks[0:64, 1, 0:64], ALU.mult,
                               [("ps", sbk), "masks"], [("PT", pts // 4)])
                            if cut <= 4.93:
                                continue
                            po = bank(1)[:, hl * 64:(hl + 1) * 64]
                            vc = br * 512 + hl * 128
                            mm(po, vtm[0:64, bi, vc:vc + 128], PT[0:64, pts, 0:64], True, cut <= 4.935,
                               [("v", bi, br), ("PT", pts // 4)], [("ps", 1)])
                            po_ = slice((1 - h2) * 64, (1 - h2) * 64 + 64)
                            S.op("pool", (lambda a_: (lambda e: e.memset(a_, 0.0)))(qm[po_].rearrange("p a b -> p (a b)")),
                                 (), ["qm"])
                            tt(qm[pr_], qk[pr_, qb + pair, boff:boff + 64].unsqueeze(1).to_broadcast([64, 16, 64]),
                               maskq[pr_], ALU.mult, rkeys_all(br) + ["maskq"], ["qm"])
                            for s_ in range(16):
                                if cut <= 4.935:
                                    continue
                                mm(po, S0b3[:, s_, :], qm[:, s_, :], False, s_ == 15,
                                   [("S0bf", sb_), "qm"], [("ps", 1)])
                            if cut <= 4.94:
                                continue
                            V3 = Vblk[:].rearrange("p (s v) -> p s v", s=16)
                            tt(V3, vtm[0:64, bi, vc:vc + 128].unsqueeze(1).to_broadcast([64, 16, 128]),
                               smask[:, :].unsqueeze(2).to_broadcast([64, 16, 128]), ALU.mult,
                               [("v", bi, br), "smask"], ["Vblk"], eng="pool")
                            if cut <= 4.95:
                                continue
                            kc0 = br * 256 + pair * 128
                            for q4 in range(4):
                                mm(PSB[:, q4 * 512:(q4 + 1) * 512], kh[0:64, bi, kc0:kc0 + 128],
                                   Vblk[:, q4 * 512:(q4 + 1) * 512], True, True, [("kh", bi), "Vblk"], [("ps", 4 + q4)])
                            if h2 == 0:
                                if br == 0:
                                    tt(tmp3, S0f3, decSA[:, pair, :].unsqueeze(2).to_broadcast([128, 16, 128]),
                                       ALU.mult, [kS0f, "decSA"], [kTmp])
                                else:
                                    tsmul(tmpS[:, :], S0f[:, :], gains[:, G_DECR + 2 + pair:G_DECR + 3 + pair],
                                          [kS0f, "gains"], [kTmp])
                            tt(S0f[pr_, :], tmpS[pr_, :], PSB[pr_, :], ALU.add,
                               [kTmp] + [("ps", 4 + q) for q in range(4)], [kS0f + (h2,)])
                        if sidx < 3:
                            s0_load(sidx + 1)
                        dst = st_out[br][:, pair * 2:pair * 2 + 2].rearrange("s h d v -> (h d) s v")
                        dma("sp", dst, S0f3, [kS0f, kS0f + (0,), kS0f + (1,)], (), ("s0o", sidx % 2))
                    for hl in range(4):
                        if cut <= 4.98:
                            continue
                        epilogue(br, hl, bank(1)[:, hl * 64:(hl + 1) * 64], off, n, tbi, ("ps", 1), ((2, 3), (0, 5)))
        if hi == 1:
            for br in range(2):
                for pair in range(2):
                    sidx = br * 2 + pair
                    dma("sp", sp_out[br][pair * 128:(pair + 1) * 128, :], Sst[:, sidx, :], ["S%d" % sidx], (), "spo")
        wrelease(igt[0])
        wrelease(igt[1])
        if cut <= 5:
            return bail()
        S.barrier(exclude=("pe",))
        cnt = 0
        for dmg in range(2):
            iga, wga, kga = wnext(8, 512)
            igr, wgr, kgr = wnext(8, 512)
            iwo, wwo, kwo = wnext(8, 512)
            for tbi, (off, n) in enumerate(tbs):
                okeys = lambda br: [("qk", br, tbi, br * 4 + j) for j in range(4)]
                for dl in range(4):
                    dm = dmg * 4 + dl
                    b0 = (cnt % 2) * 4
                    cnt += 1
                    pga, pgr, pma, pmr = bank(b0), bank(b0 + 1), bank(b0 + 2), bank(b0 + 3)
                    cs = slice(dl * 128, (dl + 1) * 128)
                    for kc in range(8):
                        mm(pga[:, 0:n], wga[:, kc, cs], u[:, kc, off:off + n], kc == 0, kc == 7,
                           [kga, ("u", kc, tbi)], [("ps", b0)])
                    for kc in range(8):
                        mm(pgr[:, 0:n], wgr[:, kc, cs], u[:, kc, off:off + n], kc == 0, kc == 7,
                           [kgr, ("u", kc, tbi)], [("ps", b0 + 1)])
                    for fc in range(4):
                        mm(pma[:, 0:n], wwo[:, fc, cs], qk[:, fc, off:off + n], fc == 0, fc == 3,
                           [kwo] + okeys(0), [("ps", b0 + 2)])
                    for fc in range(4, 8):
                        mm(pmr[:, 0:n], wwo[:, fc, cs], qk[:, fc, off:off + n], fc == 4, fc == 7,
                           [kwo] + okeys(1), [("ps", b0 + 3)])
                    act(tA[:, 0:n], pga[:, 0:n], AF.Tanh, [("ps", b0)], ["tA"], scale=0.5)
                    act(tB[:, 0:n], pgr[:, 0:n], AF.Tanh, [("ps", b0 + 1)], ["tB"], scale=0.5)
                    stt(tC[:, 0:n], tA[:, 0:n], 1.0, pma[:, 0:n], ALU.add, ALU.mult, ["tA", ("ps", b0 + 2)], ["tC"])
                    stt(tD[:, 0:n], tB[:, 0:n], 1.0, pmr[:, 0:n], ALU.add, ALU.mult, ["tB", ("ps", b0 + 3)], ["tD"])
                    tt(tA[:, 0:n], tC[:, 0:n], tD[:, 0:n], ALU.add, ["tC", "tD"], ["tA"])
                    stt(h[:, dm, off:off + n], tA[:, 0:n], 0.5, h[:, dm, off:off + n], ALU.mult, ALU.add,
                        ["tA", ("h", dm, tbi)], [("h", dm, tbi)])
                    if next_gcol is not None and dmg == 1:
                        if tbi == 0 and dl == 3:
                            norm_part1(sq8m, 0, tbs[0][0], tbs[0][1])
                        elif tbi == 1 and dl == 1:
                            norm_part2(sq8m, 0, tbs[0][0], tbs[0][1], next_gcol)
            wrelease(iga)
            wrelease(igr)
            wrelease(iwo)

    def ple(tok0, tbs, lazy):
        cnt = 0
        for dmg in range(2):
            ig, wg_, kg_ = wnext(8, 512)
            ip, wp_, kp_ = wnext(2, 512)
            for tbi, (off, n) in enumerate(tbs):
                if dmg == 0 and tbi > 0:
                    lazy(tbi)
                for dl in range(4):
                    dm = dmg * 4 + dl
                    b0 = (cnt % 2) * 2
                    cnt += 1
                    pg, pp = bank(b0), bank(b0 + 1)
                    cs = slice(dl * 128, (dl + 1) * 128)
                    for kc in range(8):
                        mm(pg[:, 0:n], wg_[:, kc, cs], u[:, kc, off:off + n], kc == 0, kc == 7,
                           [kg_, ("u", kc, tbi)], [("ps", b0)])
                    for kc in range(2):
                        mm(pp[:, 0:n], wp_[:, kc, cs], pb[:, kc, off:off + n], kc == 0, kc == 1,
                           [kp_, "pb"], [("ps", b0 + 1)])
                    sb_ = cnt % 2
                    act(stmp[:, sb_, 0:n], pg[:, 0:n], AF.Tanh, [("ps", b0)], [("stmp", sb_)], scale=0.5)
                    stt(stmp2[:, sb_, 0:n], stmp[:, sb_, 0:n], 1.0, pp[:, 0:n], ALU.add, ALU.mult,
                        [("stmp", sb_), ("ps", b0 + 1)], [("stmp2", sb_)])
                    stt(h[:, dm, off:off + n], stmp2[:, sb_, 0:n], 0.5, h[:, dm, off:off + n], ALU.mult, ALU.add,
                        [("stmp2", sb_), ("h", dm, tbi)], [("h", dm, tbi)])
            wrelease(ig)
            wrelease(ip)

    def skip_tiles(k):
        for _ in range(k):
            i = st["cur"]
            st["cur"] += 1
            wrelease(i)

    yv = yT.rearrange("(k p) t -> p k t", p=128)
    xv = xT.rearrange("(k p) t -> p k t", p=128)
    halves = [(0, [(0, 512), (512, 512)]), (1024, [(0, 512), (512, 512), (1024, 64)])]
    for hi, (tok0, tbs) in enumerate(halves[:nhalves]):
        T = sum(n for _, n in tbs)
        blocks = [(b * 128, 128, False, (tok0 + b * 128) // 128) for b in range(8)]
        if hi == 1:
            blocks.append((1024, 64, True, 16))
        def xload(tok0_, tbi, off, n):
            dma("sp", h[:, :, off:off + n], xv[:, :, tok0_ + off:tok0_ + off + n], (),
                [("h", kc, tbi) for kc in range(8)], ("x", tbi))

        if hi == 0:
            for tbi, (off, n) in enumerate(tbs):
                xload(tok0, tbi, off, n)
            late_consts()

        def mk_lazy(gcol):
            return lambda tbi: norm_u_tb(tbi, tbs[tbi][0], tbs[tbi][1], gcol)

        norm_u_tb(0, tbs[0][0], tbs[0][1], G_FFN1)
        ffn(tbs, mk_lazy(G_FFN1), G_MIX if (stage >= 2 and cut >= 99) else None)
        if stage >= 2:
            if cut < 99:
                norm_u_tb(0, tbs[0][0], tbs[0][1], G_MIX)
            S.barrier(exclude=("pe",))
            mixing(hi, tok0, tbs, blocks, mk_lazy(G_MIX), G_FFN2 if stage >= 3 else None)
        else:
            skip_tiles(15)
        if stage >= 3:
            if cut < 99:
                norm_u_tb(0, tbs[0][0], tbs[0][1], G_FFN2)
            S.barrier(exclude=("pe",))
            if stage >= 4:
                T_ = sum(n for _, n in tbs)
                dma("pool", pb[:, :, 0:T_], pT.rearrange("(k p) t -> p k t", p=128)[:, :, tok0:tok0 + T_], (),
                    ["pb"], "pb")
            ffn(tbs, mk_lazy(G_FFN2), G_PLE if stage >= 4 else None)
        else:
            S.barrier(exclude=("pe",))
            skip_tiles(20)
        if stage >= 4:
            ple(tok0, tbs, mk_lazy(G_PLE))
        else:
            skip_tiles(4)
        if stage >= 5:
            for tbi, (off, n) in enumerate(tbs):
                pn = bank(6)
                for kc in range(8):
                    b = kc % 2
                    act(sq[:, b, 0:n], h[:, kc, off:off + n], AF.Square, [("h", kc, tbi)], [("sq", b)])
                    mm(pn[:, 0:n], ones_bf[:], sq[:, b, 0:n], kc == 0, kc == 7, ["ones_bf", ("sq", b)], [("ps", 6)])
                rb = tbi % 2
                act(rstd[:, rb, 0:n], pn[:, 0:n], AF.Ln, [("ps", 6)], [("rstd", rb)], scale=1.0 / D, bias=EPS)
                act(rstd[:, rb, 0:n], rstd[:, rb, 0:n], AF.Exp, [("rstd", rb)], [("rstd", rb)], scale=-0.5)
                for kc in range(8):
                    stt(yout[:, rb, kc, 0:n], h[:, kc, off:off + n], gains[:, G_FIN + kc:G_FIN + kc + 1],
                        rstd[:, rb, 0:n], ALU.mult, ALU.mult,
                        [("h", kc, tbi), "gains", ("rstd", rb)], [("yout", rb)])
                dma("sp", yv[:, :, tok0 + off:tok0 + off + n], yout[:, rb, :, 0:n], [("yout", rb)], (), ("yo", rb))
                if stage >= 5 and hi + 1 < len(halves[:nhalves]):
                    ntok0, ntbs = halves[hi + 1]
                    xload(ntok0, tbi, ntbs[tbi][0], ntbs[tbi][1])
                    if tbi == len(tbs) - 1:
                        for t2 in range(len(tbs), len(ntbs)):
                            xload(ntok0, t2, ntbs[t2][0], ntbs[t2][1])
        else:
            for tbi, (off, n) in enumerate(tbs):
                dma("sp", yv[:, :, tok0 + off:tok0 + off + n], h[:, :, off:off + n],
                    [("h", kc, tbi) for kc in range(8)], (), "yo")
            if hi + 1 < len(halves[:nhalves]):
                ntok0, ntbs = halves[hi + 1]
                for t2 in range(len(ntbs)):
                    xload(ntok0, t2, ntbs[t2][0], ntbs[t2][1])
    S.emit(nc)
    return nc


_CONST_CACHE = {}


def _prep_inputs(inp):
    f32 = np.float32
    inp = {k: np.asarray(v) for k, v in inp.items()}
    wflat, sched = _weight_tiles(inp)
    ct = _CONST_CACHE.get("ct")
    if ct is None:
        ct = _const_tables()
        _CONST_CACHE["ct"] = ct

    def col8(v):
        return np.asarray(v, f32).reshape(-1, 128).T

    cpack = np.zeros((128, 52), f32)
    cpack[:, 0:8] = col8(inp["norm_ffn1"][0])
    cpack[:, 8:16] = col8(inp["norm_mix"][0])
    cpack[:, 16:24] = col8(inp["norm_ffn2"][0])
    cpack[:, 24:32] = col8(inp["norm_ple"][0])
    cpack[:, 32:40] = col8(inp["norm_final"])
    cpack[:, 40:44] = col8(inp["gn_gla"][0])
    cpack[:, 44:48] = col8(inp["gn_ret"][0])
    cpack[:, 48:52] = ct["decr"]
    xp, xs = inp["x_prompt"], inp["x_sample"]
    pp, psm = inp["p_prompt"][0], inp["p_sample"][0]
    in_maps = []
    for c in range(NCORES):
        xc = np.concatenate([xp[c], xs[16 * c:16 * c + 16].reshape(NSAMP, D)], axis=0)
        pc = np.concatenate([pp[c], psm[16 * c:16 * c + 16].reshape(NSAMP, 256)], axis=0)
        in_maps.append({
            "xT": np.ascontiguousarray(xc.T.astype(f32)),
            "pT": np.ascontiguousarray(pc.T.astype(f32)),
            "wflat": wflat,
            "cpack": cpack,
            "cmat": ct["cmat"], "masks": ct["masks"], "smask": ct["smask"], "maskq": ct["maskq"],
            "wup": np.ascontiguousarray(inp["w_alpha_up"][0].astype(f32)),
            "balpha": np.ascontiguousarray(inp["b_alpha"][0].reshape(1, 256).astype(f32)),
            "rtab": ct["rtab"], "ttab": ct["ttab"],
            "sg_in": np.ascontiguousarray(inp["state_gla"][0, 16 * c:16 * c + 16].astype(f32)),
            "sr_in": np.ascontiguousarray(inp["state_ret"][0, 16 * c:16 * c + 16].astype(f32)),
        })
    return in_maps


def _run(inp, stage=5):
    in_maps = _prep_inputs(inp)
    nc = build_program(stage)
    res = run_bass_kernel_spmd(nc, in_maps, core_ids=list(range(NCORES)))
    return res.results


def kernel(**inputs):
    results = _run(inputs, 5)
    f32 = np.float32
    y_prompt = np.zeros((8, TP, D), f32)
    y_sample = np.zeros((128, 4, D), f32)
    gp = np.zeros((1, 8, 4, 64, 128), f32); rp = np.zeros((1, 8, 4, 64, 128), f32)
    gs = np.zeros((1, 128, 4, 64, 128), f32); rs = np.zeros((1, 128, 4, 64, 128), f32)
    for c, r in enumerate(results):
        yc = np.asarray(r["yT"]).T
        y_prompt[c] = yc[:TP]
        y_sample[16 * c:16 * c + 16] = yc[TP:].reshape(16, 4, D)
        gp[0, c] = np.asarray(r["sgp"]).reshape(4, 64, 128)
        rp[0, c] = np.asarray(r["srp"]).reshape(4, 64, 128)
        gs[0, 16 * c:16 * c + 16] = np.asarray(r["sgs"])
        rs[0, 16 * c:16 * c + 16] = np.asarray(r["srs"])
    return (y_prompt, y_sample, gp, rp, gs, rs)
```

```python
import numpy as np
import ml_dtypes
import concourse.bass as bass
import concourse.mybir as mybir
from concourse.bass_utils import run_bass_kernel_spmd
from contextlib import ExitStack

F32 = mybir.dt.float32
BF16 = mybir.dt.bfloat16
AF = mybir.ActivationFunctionType
ALU = mybir.AluOpType

NCORES = 8
D = 1024
DFF = 2816
NFC = 22
TP = 2048
NSAMP = 64
TALL = TP + NSAMP
EPS = 1e-6
RING = 5
SLOT = 4096

ENGS = ("pe", "act", "dve", "pool", "sp")


class Op:
    __slots__ = ("eng", "fn", "reads", "writes", "dsem", "chan", "deps", "signaled", "count", "idx")

    def __init__(self, eng, fn, reads, writes, dsem):
        self.eng = eng
        self.fn = fn
        self.reads = tuple(reads)
        self.writes = tuple(writes)
        self.dsem = dsem
        self.chan = ("dma", dsem) if dsem is not None else ("eng", eng)
        self.deps = []
        self.signaled = dsem is not None
        self.count = 0


class Sched:
    def __init__(self):
        self.ops = []
        self.dma_sems = []
        self.barriers = []

    def op(self, eng, fn, reads=(), writes=(), dsem=None):
        o = Op(eng, fn, reads, writes, dsem)
        o.idx = len(self.ops)
        self.ops.append(o)
        if dsem is not None and dsem not in self.dma_sems:
            self.dma_sems.append(dsem)
        return o

    def barrier(self, exclude=()):
        self.barriers.append((len(self.ops), tuple(exclude)))

    def analyze(self):
        last_w, last_r = {}, {}
        waited = {e: {} for e in ENGS}
        chan_last = {}
        pending = {e: None for e in ENGS}
        bars = list(self.barriers)
        bi = 0
        ops = self.ops
        for o in ops:
            while bi < len(bars) and bars[bi][0] <= o.idx:
                snap = {ch: i for ch, i in chan_last.items()
                        if not (ch[0] == "dma" and isinstance(ch[1], tuple) and ch[1][0] == "w")}
                for e in ENGS:
                    if e in bars[bi][1]:
                        continue
                    if pending[e] is None:
                        pending[e] = snap
                    else:
                        m = dict(pending[e])
                        m.update(snap)
                        pending[e] = m
                bi += 1
            deps = {}

            def add(d):
                for ch, i in d.items():
                    if deps.get(ch, -1) < i:
                        deps[ch] = i

            if pending[o.eng] is not None:
                add(pending[o.eng])
                pending[o.eng] = None
            for k in o.reads:
                add(last_w.get(k, {}))
            for k in o.writes:
                add(last_w.get(k, {}))
                add(last_r.get(k, {}))
            w = waited[o.eng]
            for ch, i in deps.items():
                if ch == ("eng", "pe") and o.eng == "pe":
                    continue
                if w.get(ch, -1) >= i:
                    continue
                w[ch] = i
                o.deps.append(i)
                ops[i].signaled = True
            for k in o.reads:
                last_r.setdefault(k, {})[o.chan] = o.idx
            for k in o.writes:
                last_w[k] = {o.chan: o.idx}
                last_r[k] = {}
            chan_last[o.chan] = o.idx
        cnt = {}
        for o in ops:
            if o.signaled:
                inc = 16 if o.dsem is not None else 1
                cnt[o.chan] = cnt.get(o.chan, 0) + inc
                o.count = cnt[o.chan]
        self.final_counts = cnt

    def emit(self, nc, final_wait_eng="sp"):
        self.analyze()
        with ExitStack() as es:
            sems = {}
            for e in ENGS:
                sems[("eng", e)] = es.enter_context(nc.semaphore("sem_" + e))
            for i, d in enumerate(self.dma_sems):
                sems[("dma", d)] = es.enter_context(nc.semaphore("dsem_%d" % i))
            block = es.enter_context(nc.Block())
            ops = self.ops

            def run(engname):
                def body(eng):
                    for o in ops:
                        if o.eng != engname:
                            continue
                        for i in o.deps:
                            d = ops[i]
                            eng.wait_ge(sems[d.chan], d.count)
                        ins = o.fn(eng)
                        if o.signaled:
                            ins.then_inc(sems[o.chan], 16 if o.dsem is not None else 1)
                    if engname == final_wait_eng:
                        for ch, c in self.final_counts.items():
                            eng.wait_ge(sems[ch], c)
                return body

            block.tensor(run("pe"))
            block.scalar(run("act"))
            block.vector(run("dve"))
            block.gpsimd(run("pool"))
            block.sync(run("sp"))


def _tile_kc(W, cols):
    nk = W.shape[0] // 128
    sub = W[:, cols]
    return np.ascontiguousarray(sub.reshape(nk, 128, sub.shape[1]).transpose(1, 0, 2).reshape(128, -1))


def _swap_heads(idx):
    idx = np.asarray(idx).reshape(-1, 2, 32)
    return idx[:, ::-1, :].reshape(-1)


_QA, _KA, _VA, _RA = np.arange(0, 256), np.arange(256, 512), np.arange(512, 1024), np.arange(1024, 1536)
_QR, _KR, _VR, _GR = np.arange(1536, 1792), np.arange(1792, 2048), np.arange(2048, 2560), np.arange(2560, 3072)
_AL = np.arange(3072, 3088)
_GA, _GRT = np.arange(3088, 4112), np.arange(4112, 5136)


def _weight_tiles(inp):
    tiles = []

    def ffn(w_in, w_out):
        for g in range(6):
            wd = 512 if g < 5 else 256
            tiles.append(_tile_kc(w_in, np.arange(g * 512, g * 512 + wd)))
            tiles.append(_tile_kc(w_in, np.arange(DFF + g * 512, DFF + g * 512 + wd)))
        for dm in range(8):
            tiles.append(_tile_kc(w_out, np.arange(dm * 128, dm * 128 + 128)))

    ffn(inp["w_ffn1_in"][0], inp["w_ffn1_out"][0])
    wi = inp["w_in"][0]
    tiles.append(_tile_kc(wi, _AL))
    tiles.append(_tile_kc(wi, np.concatenate([_QA, _KA])))
    tiles.append(_tile_kc(wi, np.concatenate([_QR, _KR])))
    tiles.append(_tile_kc(wi, np.concatenate([_swap_heads(_QR), _swap_heads(_KR)])))
    tiles.append(_tile_kc(wi, np.concatenate([_KA, _KR])))
    tiles.append(_tile_kc(wi, _VA))
    tiles.append(_tile_kc(wi, _VR))
    tiles.append(_tile_kc(wi, _RA))
    tiles.append(_tile_kc(wi, _GR))
    wo = inp["w_out"][0]
    for dmg in range(2):
        c = np.arange(dmg * 512, dmg * 512 + 512)
        tiles.append(_tile_kc(wi, _GA[c]))
        tiles.append(_tile_kc(wi, _GRT[c]))
        tiles.append(_tile_kc(wo, c))
    ffn(inp["w_ffn2_in"][0], inp["w_ffn2_out"][0])
    for dmg in range(2):
        c = np.arange(dmg * 512, dmg * 512 + 512)
        tiles.append(_tile_kc(inp["w_ple_gate"][0], c))
        tiles.append(_tile_kc(inp["w_ple_proj"][0], c))
    sched = []
    off = 0
    for t in tiles:
        assert t.shape[1] <= SLOT
        sched.append((off, t.shape[1]))
        off += t.shape[1]
    return np.ascontiguousarray(np.concatenate(tiles, axis=1).astype(np.float32)), sched


def _weight_schedule():
    Ls = []

    def ffn():
        for g in range(6):
            wd = 512 if g < 5 else 256
            Ls.extend([8 * wd, 8 * wd])
        Ls.extend([NFC * 128] * 8)

    ffn()
    Ls.append(8 * 16)
    Ls.extend([8 * 512] * 8)
    Ls.extend([8 * 512] * 6)
    ffn()
    Ls.extend([8 * 512, 2 * 512] * 2)
    sched, off = [], 0
    for L in Ls:
        sched.append((off, L))
        off += L
    return sched, off


def _const_tables():
    f32 = np.float32
    half = 32
    freq = (10000.0 ** (-(np.arange(half, dtype=np.float64) / float(half)))).astype(f32)
    pos = np.concatenate([np.arange(TP), 16384 + (np.arange(NSAMP) % 4)]).astype(f32)
    il = np.concatenate([np.arange(TP) % 128, np.arange(NSAMP) % 4]).astype(np.float64)
    nchunk = np.concatenate([np.full(TP, 128.0), np.full(NSAMP, 4.0)])
    ang = (pos[:, None] * freq[None, :]).astype(f32).astype(np.float64)
    cos, sin = np.cos(ang), np.sin(ang)
    lg = np.log1p(-np.exp2(-5.0 - np.arange(4, dtype=np.float64)))
    d = np.arange(64)
    sgn = np.where(d < 32, -1.0, 1.0)
    CQ = np.zeros((2, 128, TALL)); SQ = np.zeros_like(CQ); CK = np.zeros_like(CQ); SK = np.zeros_like(CQ)
    for hp in range(2):
        for h2 in range(2):
            h = hp * 2 + h2
            up = np.exp((il + 1.0) * lg[h])[None, :]
            dn = np.exp(-(il + 1.0) * lg[h])[None, :] * 0.125
            c = cos[:, d % 32].T
            s = sin[:, d % 32].T * sgn[:, None]
            sl = slice(h2 * 64, h2 * 64 + 64)
            CQ[hp, sl] = c * up; SQ[hp, sl] = s * up; CK[hp, sl] = c * dn; SK[hp, sl] = s * dn
    rtab = np.zeros((5, 128, 8, 512), f32)
    for gtb in range(5):
        t0, n = (gtb * 512, 512) if gtb < 4 else (TP, NSAMP)
        for hp in range(2):
            for j, A in enumerate((CQ, SQ, CK, SK)):
                rtab[gtb, :, hp * 4 + j, :n] = A[hp, :, t0:t0 + n]
    ttab = np.zeros((17, 128, 2, 256), f32)
    for b in range(17):
        t0, n = (b * 128, 128) if b < 16 else (TP, NSAMP)
        for h in range(4):
            k = np.exp((nchunk[t0:t0 + n] - 1.0 - il[t0:t0 + n]) * lg[h])[:, None] * 0.125
            ttab[b, :n, 0, h * 64:(h + 1) * 64] = cos[t0:t0 + n][:, d % 32] * k
            ttab[b, :n, 1, h * 64:(h + 1) * 64] = sin[t0:t0 + n][:, d % 32] * sgn[None, :] * k
    decr = np.zeros((128, 4), f32)
    for hp in range(2):
        for h2 in range(2):
            h = hp * 2 + h2
            decr[h2 * 64:(h2 + 1) * 64, hp] = np.exp(128.0 * lg[h])
            decr[h2 * 64:(h2 + 1) * 64, 2 + hp] = np.exp(4.0 * lg[h])
    j = np.arange(128)[:, None]; i = np.arange(128)[None, :]
    cmat = np.zeros((128, 4, 128), f32)
    cmat[:, 0] = np.where(j <= i, -1.0 / 16, 0.0)
    cmat[:, 1] = np.where(j > i, -1.0 / 16, 0.0)
    same = (j // 4 == i // 4) & (j < 64) & (i < 64)
    cmat[:, 2] = np.where(same & (j <= i), -1.0 / 16, 0.0)
    cmat[:, 3] = np.where(same & (j > i), -1.0 / 16, 0.0)
    masks = np.zeros((128, 2, 128), f32)
    masks[:, 0] = (j <= i)
    masks[:, 1] = same & (j <= i)
    smask = (np.arange(64)[:, None] // 4 == np.arange(16)[None, :]).astype(f32)
    maskq = np.broadcast_to((np.arange(16)[:, None] == np.arange(64)[None, :] // 4).astype(f32).reshape(1, 1024),
                            (128, 1024)).copy()
    return dict(rtab=rtab.reshape(5, 128, 8 * 512), ttab=ttab.reshape(17, 128, 512), decr=decr,
                cmat=cmat.reshape(128, 512), masks=masks.reshape(128, 256), smask=smask, maskq=maskq)


def build_program(stage=5, nhalves=2, cut=99):
    nc = bass.Bass("TRN2", target_bir_lowering=False)
    S = Sched()
    wsched1, WTOT = _weight_schedule()

    def din(name, shape, dt=F32):
        return nc.dram_tensor(name, list(shape), dt, kind="ExternalInput").ap()

    def dout(name, shape, dt=F32):
        return nc.dram_tensor(name, list(shape), dt, kind="ExternalOutput").ap()

    xT = din("xT", [D, TALL]); pT = din("pT", [256, TALL])
    wflat = din("wflat", [128, WTOT])
    cpack = din("cpack", [128, 52])
    cmat_d = din("cmat", [128, 512]); masks_d = din("masks", [128, 256]); smask_d = din("smask", [64, 16])
    maskq_d = din("maskq", [128, 1024])
    wup_d = din("wup", [16, 256]); balpha_d = din("balpha", [1, 256])
    rtab_d = din("rtab", [5, 128, 8 * 512]); ttab_d = din("ttab", [17, 128, 512])
    sg_in = din("sg_in", [16, 4, 64, 128]); sr_in = din("sr_in", [16, 4, 64, 128])
    yT = dout("yT", [D, TALL])
    sgp = dout("sgp", [256, 128]); srp = dout("srp", [256, 128])
    sgs = dout("sgs", [16, 4, 64, 128]); srs = dout("srs", [16, 4, 64, 128])
    st_in = (sg_in, sr_in); st_out = (sgs, srs); sp_out = (sgp, srp)

    SB_BASE, SB_END = 16512, 229376
    ptr = [SB_BASE]

    def esz(dt):
        return 2 if dt == BF16 else 4

    def palloc(name, shape, dt):
        nbytes = int(np.prod(shape[1:])) * esz(dt)
        nbytes = (nbytes + 31) // 32 * 32
        t = nc.alloc_sbuf_tensor_at(name, list(shape), dt, offset=ptr[0])
        ptr[0] += nbytes
        return t

    TM = 1088
    h = palloc("h", [128, 8, TM], F32)
    u = palloc("u", [128, 8, TM], BF16)
    ring = [palloc("ring%d" % i, [128, SLOT], BF16) for i in range(RING)]
    gains = palloc("gains", [128, 52], F32)
    cmat = palloc("cmat_s", [128, 4, 128], BF16)
    masks = palloc("masks_s", [128, 2, 128], F32)
    smask = palloc("smask_s", [64, 16], F32)
    maskq = palloc("maskq_s", [128, 16, 64], BF16)
    wup = palloc("wup_s", [16, 256], BF16)
    balpha = palloc("balpha_s", [1, 256], BF16)
    ones_bf = palloc("ones_bf", [128, 128], BF16)
    ones_row = palloc("ones_row", [1, 128], BF16)
    Sst = palloc("Sst", [128, 4, 128], F32)
    Sbf = palloc("Sbf", [128, 3, 4, 128], BF16)
    decA = palloc("decA", [128, 2, 9], F32)
    decSA = palloc("decSA", [128, 2, 16], F32)
    sq = palloc("sq", [128, 2, 512], BF16)
    rstd = palloc("rstd", [128, 2, 512], F32)
    UB = ptr[0]
    USZ = SB_END - UB

    def ualloc(name, shape, dt, off):
        nbytes = int(np.prod(shape[1:])) * esz(dt)
        assert off % 32 == 0 and off + nbytes <= USZ, (name, off, nbytes, USZ)
        return nc.alloc_sbuf_tensor_at(name, list(shape), dt, offset=UB + off), off + (nbytes + 31) // 32 * 32

    gbuf, o = ualloc("gbuf", [128, NFC, TM], BF16, 0)
    stmp, o = ualloc("stmp", [128, 2, 512], F32, o)
    pb, o_pb = ualloc("pb", [128, 2, TM], BF16, o)
    stmp2, _ = ualloc("stmp2", [128, 2, 512], F32, o_pb)
    yout, _ = ualloc("yout", [128, 2, 8, 512], F32, o_pb + 4096)
    sq8f, _ = ualloc("sq8f", [128, 8, 512], BF16, o_pb + 4096 + 32768)
    sq8m, _ = ualloc("sq8m", [128, 8, 512], BF16, 17408)
    qk, o = ualloc("qk", [128, 8, TM], BF16, 0)
    vtm, o = ualloc("vtm", [128, 9, 1024], BF16, o)
    kh, o = ualloc("kh", [128, 9, 512], BF16, o)
    o_tr = o
    ed, o = ualloc("ed", [128, 9, 256], F32, o)
    alowT, o = ualloc("alowT", [16, TM], BF16, o)
    spb, o = ualloc("spb", [128, 2, 256], F32, o)
    sphi, o = ualloc("sphi", [128, 2, 256], BF16, o)
    splo, o = ualloc("splo", [128, 2, 256], BF16, o)
    etmp, o = ualloc("etmp", [128, 2, 256], F32, o)
    eb, o = ualloc("eb", [128, 2, 512], F32, o)
    enb, o = ualloc("enb", [128, 2, 512], F32, o)
    rtab, o = ualloc("rtab_s", [128, 8, 512], F32, o)
    ttab, o = ualloc("ttab_s", [128, 3, 512], F32, o)
    tA, o = ualloc("tA", [128, 512], F32, o)
    tB, o = ualloc("tB", [128, 512], F32, o)
    tC, o = ualloc("tC", [128, 512], F32, o)
    tD, o = ualloc("tD", [128, 512], F32, o)
    o = o_tr
    PT, o = ualloc("PT", [128, 8, 128], BF16, o)
    osb, o = ualloc("osb", [128, 2, 512], F32, o)
    sqb, o = ualloc("sqb", [128, 2, 512], BF16, o)
    rs2, o = ualloc("rs2", [128, 2, 512], F32, o)
    ee, o = ualloc("ee", [128, 2, 512], F32, o)
    tt1, o = ualloc("tt1", [128, 2, 512], F32, o)
    tt2, o = ualloc("tt2", [128, 2, 512], F32, o)
    S0bf, o = ualloc("S0bf", [128, 2, 16 * 128], BF16, o)
    S0f_buf, o = ualloc("S0f", [128, 16 * 128], F32, o)
    tmpS_buf, o = ualloc("tmpS", [128, 16 * 128], F32, o)
    Vblk, o = ualloc("Vblk", [64, 16 * 128], BF16, o)
    qm, o = ualloc("qm", [128, 16, 64], BF16, o)
    print("SBUF union size", USZ, "P2 end", o)

    PSA = nc.alloc_psum_tensor("PSA", [128, 2048], F32)
    PSB = nc.alloc_psum_tensor("PSB", [128, 2048], F32)

    def bank(b):
        t = PSA if b < 4 else PSB
        return t[:, (b % 4) * 512:(b % 4) * 512 + 512]

    def mm(out, lhsT, rhs, start, stop, reads, writes):
        S.op("pe", lambda e: e.matmul(out, lhsT=lhsT, rhs=rhs, start=start, stop=stop), reads, writes)

    def act(out, in_, func, reads, writes, scale=1.0, bias=0.0):
        S.op("act", lambda e: e.activation(out=out, in_=in_, func=func, bias=bias, scale=scale), reads, writes)

    def tt(out, in0, in1, op, reads, writes, eng="dve"):
        S.op(eng, lambda e: e.tensor_tensor(out=out, in0=in0, in1=in1, op=op), reads, writes)

    def stt(out, in0, scalar, in1, op0, op1, reads, writes):
        S.op("dve", lambda e: e.scalar_tensor_tensor(out=out, in0=in0, scalar=scalar, in1=in1, op0=op0, op1=op1),
             reads, writes)

    def tsmul(out, in0, scalar, reads, writes):
        S.op("dve", lambda e: e.tensor_scalar(out=out, in0=in0, scalar1=scalar, scalar2=None, op0=ALU.mult),
             reads, writes)

    def dma(eng, out, in_, reads, writes, dsem):
        S.op(eng, lambda e: e.dma_start(out=out, in_=in_), reads, writes, dsem=dsem)

    def memset(ap, val, writes):
        S.op("dve", lambda e: e.memset(ap, val), (), writes)

    full_sched = wsched1 + wsched1
    st = {"cur": 0}

    def issue(i, after=()):
        if i >= len(full_sched):
            return
        off, L = full_sched[i]
        slot = i % RING
        dma("pool", ring[slot][:, 0:L], wflat[:, off:off + L], after, [("ring", slot)], ("w", slot))

    def wnext(nk, ncols):
        i = st["cur"]
        st["cur"] += 1
        off, L = full_sched[i]
        assert L == nk * ncols, (i, L, nk, ncols)
        slot = i % RING
        return i, ring[slot][:, 0:L].rearrange("p (k c) -> p k c", k=nk), ("ring", slot)

    def wrelease(i):
        issue(i + RING)

    dma("sp", gains[:], cpack[:], (), ["gains"], "c0")

    def late_consts():
        dma("pool", cmat[:].rearrange("p a b -> p (a b)"), cmat_d[:], (), ["cmat"], "c1")
        dma("sp", masks[:].rearrange("p a b -> p (a b)"), masks_d[:], (), ["masks"], "c2")
        dma("sp", smask[:], smask_d[:], (), ["smask"], "c3")
        dma("pool", maskq[:].rearrange("p a b -> p (a b)"), maskq_d[:], (), ["maskq"], "c6")
        dma("pool", wup[:], wup_d[:], (), ["wup"], "c4")
        dma("pool", balpha[:], balpha_d[:], (), ["balpha"], "c5")
    memset(ones_bf[:], 1.0, ["ones_bf"])
    memset(ones_row[:], 1.0, ["ones_row"])
    memset(Sst[:].rearrange("p a b -> p (a b)"), 0.0, ["S0", "S1", "S2", "S3"])
    memset(Sbf[:].rearrange("p t a b -> p (t a b)"), 0.0, [("Sb", t_, b_) for t_ in range(3) for b_ in range(2)])
    for i in range(2):
        issue(i)

    G_FFN1, G_MIX, G_FFN2, G_PLE, G_FIN, G_GNA, G_GNR, G_DECR = 0, 8, 16, 24, 32, 40, 44, 48

    def rmsnorm(tbs, gcol, dst_fn, dst_keys):
        for tbi, (off, n) in enumerate(tbs):
            pn = bank(6)
            for kc in range(8):
                b = kc % 2
                act(sq[:, b, 0:n], h[:, kc, off:off + n], AF.Square, [("h", kc, tbi)], [("sq", b)])
                mm(pn[:, 0:n], ones_bf[:], sq[:, b, 0:n], kc == 0, kc == 7, ["ones_bf", ("sq", b)], [("ps", 6)])
            rb = tbi % 2
            act(rstd[:, rb, 0:n], pn[:, 0:n], AF.Ln, [("ps", 6)], [("rstd", rb)], scale=1.0 / D, bias=EPS)
            act(rstd[:, rb, 0:n], rstd[:, rb, 0:n], AF.Exp, [("rstd", rb)], [("rstd", rb)], scale=-0.5)
            for kc in range(8):
                stt(dst_fn(kc, tbi, off, n), h[:, kc, off:off + n], gains[:, gcol + kc:gcol + kc + 1],
                    rstd[:, rb, 0:n], ALU.mult, ALU.mult,
                    [("h", kc, tbi), "gains", ("rstd", rb)], dst_keys(kc, tbi))

    def norm_to_u(tbs, gcol):
        rmsnorm(tbs, gcol, lambda kc, tbi, off, n: u[:, kc, off:off + n], lambda kc, tbi: [("u", kc, tbi)])

    def norm_u_tb(tbi, off, n, gcol):
        pn = bank(6)
        for kc in range(8):
            b = kc % 2
            act(sq[:, b, 0:n], h[:, kc, off:off + n], AF.Square, [("h", kc, tbi)], [("sq", b)])
            mm(pn[:, 0:n], ones_bf[:], sq[:, b, 0:n], kc == 0, kc == 7, ["ones_bf", ("sq", b)], [("ps", 6)])
        rb = tbi % 2
        act(rstd[:, rb, 0:n], pn[:, 0:n], AF.Ln, [("ps", 6)], [("rstd", rb)], scale=1.0 / D, bias=EPS)
        act(rstd[:, rb, 0:n], rstd[:, rb, 0:n], AF.Exp, [("rstd", rb)], [("rstd", rb)], scale=-0.5)
        for kc in range(8):
            stt(u[:, kc, off:off + n], h[:, kc, off:off + n], gains[:, gcol + kc:gcol + kc + 1],
                rstd[:, rb, 0:n], ALU.mult, ALU.mult,
                [("h", kc, tbi), "gains", ("rstd", rb)], [("u", kc, tbi)])

    def norm_part1(sq8, tbi, off, n):
        for kc in range(8):
            act(sq8[:, kc, 0:n], h[:, kc, off:off + n], AF.Square, [("h", kc, tbi)], [("sq8", kc)])

    def norm_part2(sq8, tbi, off, n, gcol):
        pn = bank(6)
        for kc in range(8):
            mm(pn[:, 0:n], ones_bf[:], sq8[:, kc, 0:n], kc == 0, kc == 7, ["ones_bf", ("sq8", kc)], [("ps", 6)])
        rb = tbi % 2
        act(rstd[:, rb, 0:n], pn[:, 0:n], AF.Ln, [("ps", 6)], [("rstd", rb)], scale=1.0 / D, bias=EPS)
        act(rstd[:, rb, 0:n], rstd[:, rb, 0:n], AF.Exp, [("rstd", rb)], [("rstd", rb)], scale=-0.5)
        for kc in range(8):
            stt(u[:, kc, off:off + n], h[:, kc, off:off + n], gains[:, gcol + kc:gcol + kc + 1],
                rstd[:, rb, 0:n], ALU.mult, ALU.mult,
                [("h", kc, tbi), "gains", ("rstd", rb)], [("u", kc, tbi)])

    def ffn(tbs, lazy, next_gcol=None):
        cnt = 0
        for g in range(6):
            nf = 4 if g < 5 else 2
            ia, wa, ka = wnext(8, nf * 128)
            ib, wb, kb = wnext(8, nf * 128)
            for tbi, (off, n) in enumerate(tbs):
                if g == 0 and tbi > 0:
                    lazy(tbi)
                for fl in range(nf):
                    fc = g * 4 + fl
                    pa, pbk = bank(cnt % 2), bank(2 + cnt % 2)
                    ka_, kb_ = ("ps", cnt % 2), ("ps", 2 + cnt % 2)
                    for kc in range(8):
                        mm(pa[:, 0:n], wa[:, kc, fl * 128:(fl + 1) * 128], u[:, kc, off:off + n], kc == 0, kc == 7,
                           [ka, ("u", kc, tbi)], [ka_])
                    for kc in range(8):
                        mm(pbk[:, 0:n], wb[:, kc, fl * 128:(fl + 1) * 128], u[:, kc, off:off + n], kc == 0, kc == 7,
                           [kb, ("u", kc, tbi)], [kb_])
                    sb_ = cnt % 2
                    act(stmp[:, sb_, 0:n], pa[:, 0:n], AF.Silu, [ka_], [("stmp", sb_)])
                    tt(gbuf[:, fc, off:off + n], stmp[:, sb_, 0:n], pbk[:, 0:n], ALU.mult,
                       [("stmp", sb_), kb_], [("g", fc, tbi)])
                    cnt += 1
            wrelease(ia)
            wrelease(ib)
        cnt = 0
        for dm in range(8):
            io, wo, ko = wnext(NFC, 128)
            for tbi, (off, n) in enumerate(tbs):
                po = bank(4 + cnt % 2)
                kp = ("ps", 4 + cnt % 2)
                for fc in range(NFC):
                    mm(po[:, 0:n], wo[:, fc, :], gbuf[:, fc, off:off + n], fc == 0, fc == NFC - 1,
                       [ko, ("g", fc, tbi)], [kp])
                stt(h[:, dm, off:off + n], po[:, 0:n], 0.5, h[:, dm, off:off + n], ALU.mult, ALU.add,
                    [kp, ("h", dm, tbi)], [("h", dm, tbi)])
                cnt += 1
                if next_gcol is not None and dm == 7:
                    if tbi == 0:
                        norm_part1(sq8f, 0, tbs[0][0], tbs[0][1])
                    elif tbi == 1:
                        norm_part2(sq8f, 0, tbs[0][0], tbs[0][1], next_gcol)
            wrelease(io)

    def mixing(hi, tok0, tbs, blocks, lazy, next_gcol=None):
        mix_start = st["cur"]

        def bail():
            skip_tiles(mix_start + 15 - st["cur"])

        T = sum(n for _, n in tbs)
        def rtab_load(tbi_, hp_):
            off_, n_ = tbs[tbi_]
            gtb = 4 if n_ == 64 else (tok0 + off_) // 512
            src = rtab_d[gtb].rearrange("p (a b) -> p a b", a=8)[:, hp_ * 4:(hp_ + 1) * 4, 0:n_]
            dma("sp", rtab[:, hp_ * 4:(hp_ + 1) * 4, 0:n_], src, (), [("rtab", hp_)], ("rt", hp_))

        def ttab_load(bi_):
            dma("sp", ttab[:, bi_ % 3, :], ttab_d[blocks[bi_][3]], (), [("ttab", bi_ % 3)], ("tt", bi_ % 3))

        rtab_load(0, 0)
        rtab_load(0, 1)
        for bi_ in range(min(3, len(blocks))):
            ttab_load(bi_)
        ial, wal, kal = wnext(8, 16)
        i0, w0, k0 = wnext(8, 512)
        for tbi, (off, n) in enumerate(tbs):
            if tbi > 0:
                lazy(tbi)
            pA = bank(7)
            for kc in range(8):
                mm(pA[0:16, 0:n], wal[:, kc, 0:16], u[:, kc, off:off + n], kc == 0, kc == 7,
                   [kal, ("u", kc, tbi)], [("ps", 7)])
            S.op("act", (lambda o_, i_: (lambda e: e.copy(out=o_, in_=i_)))(alowT[0:16, off:off + n], pA[0:16, 0:n]),
                 [("ps", 7)], [("alowT", tbi)])
            pend_b = []
            for bi, (boff, bn, smp, gblk) in enumerate(blocks):
                if not (off <= boff < off + n) or cut <= 0.2:
                    continue
                lo = boff - off
                dp = bi % 2
                px = bank(6 + dp)
                kpx = ("ps", 6 + dp)
                mm(px[0:bn, 0:256], alowT[0:16, boff:boff + bn], wup[:, :], True, False,
                   [("alowT", tbi), "wup"], [kpx])
                mm(px[0:bn, 0:256], ones_row[0:1, 0:bn], balpha[0:1, :], False, True,
                   ["ones_row", "balpha"], [kpx])
                act(etmp[0:bn, dp, :], px[0:bn, 0:256], AF.Exp, [kpx], [("etmp", dp)], scale=-1.0)
                act(spb[0:bn, dp, :], etmp[0:bn, dp, :], AF.Ln, [("etmp", dp)], [("spb", dp)], bias=1.0)
                if cut <= 0.4:
                    continue
                S.op("dve", (lambda o_, i_: (lambda e: e.tensor_copy(out=o_, in_=i_)))(sphi[0:bn, dp, :], spb[0:bn, dp, :]),
                     [("spb", dp)], [("sphi", dp)])
                tt(splo[0:bn, dp, :], spb[0:bn, dp, :], sphi[0:bn, dp, :], ALU.subtract,
                   [("spb", dp), ("sphi", dp)], [("splo", dp)])
                def part_b(bi=bi, boff=boff, bn=bn, smp=smp, lo=lo, dp=dp):
                    ci = 2 if smp else 0
                    b5 = 4 + bi % 2
                    p5 = bank(b5)
                    for hp in range(2):
                        for xi, (spx, kx) in enumerate(((sphi, ("sphi", dp)), (splo, ("splo", dp)))):
                            mm(p5[:, hp * 128:hp * 128 + bn], spx[0:bn, dp, hp * 128:(hp + 1) * 128],
                               cmat[0:bn, ci, 0:bn], xi == 0, xi == 1, [kx, "cmat"], [("ps", b5)])
                    for xi, (spx, kx) in enumerate(((sphi, ("sphi", dp)), (splo, ("splo", dp)))):
                        mm(p5[0:bn, 256:512], cmat[0:bn, ci + 1, 0:bn], spx[0:bn, dp, 0:256], xi == 0, xi == 1,
                           [kx, "cmat"], [("ps", b5)])
                    for hp in range(2):
                        src = p5[:, hp * 128:hp * 128 + bn]
                        act(eb[:, hp, lo:lo + bn], src, AF.Exp, [("ps", b5)], [("eb", hp)])
                        act(enb[:, hp, lo:lo + bn], src, AF.Exp, [("ps", b5)], [("enb", hp)], scale=-1.0)
                        if smp:
                            act(decSA[:, hp, :], p5[:, hp * 128 + 3:hp * 128 + 64:4], AF.Exp, [("ps", b5)], ["decSA"])
                        else:
                            act(decA[:, hp, bi:bi + 1], p5[:, hp * 128 + bn - 1:hp * 128 + bn], AF.Exp,
                                [("ps", b5)], [("decA", hp, bi)])
                    act(ed[0:bn, bi, :], p5[0:bn, 256:512], AF.Exp, [("ps", b5)], [("ed", bi)])

                for f_ in pend_b:
                    f_()
                pend_b = [part_b]
            for f_ in pend_b:
                f_()
            pend_b = []
            for ch in range(4):
                if cut <= 0.8:
                    continue
                pq = bank(ch)
                for kc in range(8):
                    mm(pq[:, 0:n], w0[:, kc, ch * 128:(ch + 1) * 128], u[:, kc, off:off + n], kc == 0, kc == 7,
                       [k0, ("u", kc, tbi)], [("ps", ch)])
                if cut <= 0.85:
                    continue
                if ch < 2:
                    stt(qk[:, ch, off:off + n], pq[:, 0:n], 0.125, eb[:, ch, 0:n], ALU.mult, ALU.mult,
                        [("ps", ch), ("eb", ch)], [("qk", 0, tbi, ch)])
                elif cut <= 0.9:
                    continue
                elif cut <= 0.95:
                    tt(qk[:, ch, off:off + n], pq[:, 0:n], eb[:, ch - 2, 0:n], ALU.mult,
                       [("ps", ch), ("eb", ch - 2)], [("qk", 0, tbi, ch)])
                elif cut <= 0.97:
                    stt(qk[:, ch, off:off + n], pq[:, 0:n], 1.0, enb[:, ch - 2, 0:n], ALU.mult, ALU.mult,
                        [("ps", ch), ("enb", ch - 2)], [("qk", 0, tbi, ch)])
                else:
                    tt(qk[:, ch, off:off + n], pq[:, 0:n], enb[:, ch - 2, 0:n], ALU.mult,
                       [("ps", ch), ("enb", ch - 2)], [("qk", 0, tbi, ch)])
        wrelease(ial)
        wrelease(i0)
        if cut <= 1:
            return bail()
        i1, w1, k1 = wnext(8, 512)
        i2, w2, k2 = wnext(8, 512)
        cnt = 0
        for tbi, (off, n) in enumerate(tbs):
            for hp in range(2):
                for isk in range(2):
                    ch = hp + 2 * isk
                    b0 = (cnt % 2) * 2
                    pa, pbk = bank(b0), bank(b0 + 1)
                    for kc in range(8):
                        mm(pa[:, 0:n], w1[:, kc, ch * 128:(ch + 1) * 128], u[:, kc, off:off + n], kc == 0, kc == 7,
                           [k1, ("u", kc, tbi)], [("ps", b0)])
                    for kc in range(8):
                        mm(pbk[:, 0:n], w2[:, kc, ch * 128:(ch + 1) * 128], u[:, kc, off:off + n], kc == 0, kc == 7,
                           [k2, ("u", kc, tbi)], [("ps", b0 + 1)])
                    tt(tA[:, 0:n], pa[:, 0:n], rtab[:, hp * 4 + 2 * isk, 0:n], ALU.mult, [("ps", b0), ("rtab", hp)], ["tA"])
                    tt(tB[:, 0:n], pbk[:, 0:n], rtab[:, hp * 4 + 2 * isk + 1, 0:n], ALU.mult,
                       [("ps", b0 + 1), ("rtab", hp)], ["tB"])
                    qi = 4 + 2 * isk + hp
                    tt(qk[:, qi, off:off + n], tA[:, 0:n], tB[:, 0:n], ALU.add, ["tA", "tB"], [("qk", 1, tbi, qi)])
                    cnt += 1
                if tbi + 1 < len(tbs):
                    rtab_load(tbi + 1, hp)
        wrelease(i1)
        wrelease(i2)
        if cut <= 2:
            return bail()
        i3, w3, k3 = wnext(8, 512)
        for bi, (boff, bn, smp, gblk) in enumerate(blocks):
            tbi = min(boff // 512, len(tbs) - 1)
            tb_ = bi % 3
            pk = bank(bi % 2)
            kp = ("ps", bi % 2)
            for kc in range(8):
                mm(pk[0:bn, :], u[:, kc, boff:boff + bn], w3[:, kc, :], kc == 0, kc == 7, [k3, ("u", kc, tbi)], [kp])
            tt(kh[0:bn, bi, 0:256], pk[0:bn, 0:256], ed[0:bn, bi, :], ALU.mult, [kp, ("ed", bi)], [("kh", bi)])
            pr = pk[0:bn, 256:512].rearrange("p (h s e) -> p h s e", h=4, s=2)
            Ct = ttab[0:bn, tb_, 0:256].rearrange("p (h s e) -> p h s e", h=4, s=2)
            St = ttab[0:bn, tb_, 256:512].rearrange("p (h s e) -> p h s e", h=4, s=2)
            kr = kh[0:bn, bi, 256:512].rearrange("p (h s e) -> p h s e", h=4, s=2)
            tA4 = tA[0:bn, 0:256].rearrange("p (h s e) -> p h s e", h=4, s=2)
            tB4 = tB[0:bn, 0:256].rearrange("p (h s e) -> p h s e", h=4, s=2)
            for s_ in range(2):
                tt(tA4[:, :, s_, :], pr[:, :, s_, :], Ct[:, :, s_, :], ALU.mult, [kp, ("ttab", tb_)], ["tA"])
                tt(tB4[:, :, s_, :], pr[:, :, 1 - s_, :], St[:, :, s_, :], ALU.mult, [kp, ("ttab", tb_)], ["tB"])
                tt(kr[:, :, s_, :], tA4[:, :, s_, :], tB4[:, :, s_, :], ALU.add, ["tA", "tB"], [("kh", bi)])
            if bi + 3 < len(blocks):
                ttab_load(bi + 3)
        wrelease(i3)
        if cut <= 3:
            return bail()
        for vi in range(2):
            iv, wv, kv = wnext(8, 512)
            for bi, (boff, bn, smp, gblk) in enumerate(blocks):
                tbi = min(boff // 512, len(tbs) - 1)
                pv = bank(2 + bi % 2)
                kp = ("ps", 2 + bi % 2)
                for kc in range(8):
                    mm(pv[0:bn, :], u[:, kc, boff:boff + bn], wv[:, kc, :], kc == 0, kc == 7,
                       [kv, ("u", kc, tbi)], [kp])
                S.op("act", (lambda o_, i_: (lambda e: e.copy(out=o_, in_=i_)))(
                    vtm[0:bn, bi, vi * 512:(vi + 1) * 512], pv[0:bn, :]), [kp], [("v", bi, vi)])
            wrelease(iv)
        if cut <= 4:
            return bail()
        S.barrier(exclude=("pe",))
        igt = []
        wg = []
        kg = []
        for br in range(2):
            i_, w_, k_ = wnext(8, 512)
            igt.append(i_); wg.append(w_); kg.append(k_)
        ecnt = [0]

        def epilogue(br, hl, po, off, n, tbi, pokey, ebanks=((2, 3), (0, 1))):
            ob = ecnt[0] % 2
            ecnt[0] += 1
            tbk, rbk = ebanks[ob]
            act(sqb[:, ob, 0:n], po, AF.Square, [pokey], [("sqb", ob)])
            pR = bank(rbk)
            for kc in range(8):
                mm(pR[:, 0:n], wg[br][:, kc, hl * 128:(hl + 1) * 128], u[:, kc, off:off + n], kc == 0, kc == 7,
                   [kg[br], ("u", kc, tbi)], [("ps", rbk)])
            pT_ = bank(tbk)
            mm(pT_[:, 0:n], ones_bf[:], sqb[:, ob, 0:n], True, True, ["ones_bf", ("sqb", ob)], [("ps", tbk)])
            act(rs2[:, ob, 0:n], pT_[:, 0:n], AF.Ln, [("ps", tbk)], [("rs2", ob)], scale=1.0 / 128, bias=EPS)
            act(ee[:, ob, 0:n], pR[:, 0:n], AF.Exp, [("ps", rbk)], [("ee", ob)], scale=-1.0)
            act(ee[:, ob, 0:n], ee[:, ob, 0:n], AF.Ln, [("ee", ob)], [("ee", ob)], bias=1.0)
            stt(tt2[:, ob, 0:n], rs2[:, ob, 0:n], -0.5, ee[:, ob, 0:n], ALU.mult, ALU.subtract,
                [("rs2", ob), ("ee", ob)], [("tt2", ob)])
            act(tt2[:, ob, 0:n], tt2[:, ob, 0:n], AF.Exp, [("tt2", ob)], [("tt2", ob)])
            gc = (G_GNA if br == 0 else G_GNR) + hl
            stt(tt1[:, ob, 0:n], po, gains[:, gc:gc + 1], tt2[:, ob, 0:n], ALU.mult, ALU.mult,
                [pokey, "gains", ("tt2", ob)], [("tt1", ob)])
            qi = br * 4 + hl
            wkeys = [("qk", br, tbi, br * 4 + j) for j in range(4)]
            tt(qk[:, qi, off:off + n], tt1[:, ob, 0:n], pR[:, 0:n], ALU.mult, [("tt1", ob), ("ps", rbk)], wkeys)

        scnt = [0]
        SX = (S0f_buf, tmpS_buf)

        def s0_load(sidx):
            br_, pair_ = sidx // 2, sidx % 2
            src = st_in[br_][:, pair_ * 2:pair_ * 2 + 2].rearrange("s h d v -> (h d) s v")
            sb_ = sidx % 2
            dma("pool", S0bf[:, sb_, :].rearrange("p (s v) -> p s v", s=16), src, (), [("S0bf", sb_)], ("s0b", sb_))
            x_ = SX[sidx % 2]
            kx = ("SX", sidx % 2)
            dma("sp", x_[:].rearrange("p (s v) -> p s v", s=16), src, (), [kx, kx + (0,), kx + (1,)], ("s0f", sidx % 2))

        if hi == 1:
            s0_load(0)
        for tbi, (off, n) in enumerate(tbs):
            smp_tb = (n == 64)
            rkeys_all = lambda br: [("qk", br, tbi, br * 4 + j) for j in range(4)]
            for br in range(2):
                qb, kb_i = br * 4, br * 4 + 2
                if not smp_tb:
                    tblocks = [(bi, b) for bi, b in enumerate(blocks) if off <= b[0] < off + n]
                    pend = []

                    def o_ops(c4, bi, boff, bn, gblk):
                        par = c4 % 2
                        for hl in range(4):
                            pair, h2 = hl // 2, hl % 2
                            sidx = br * 2 + pair
                            pr_ = slice(h2 * 64, h2 * 64 + 64)
                            po = bank(4 + hl)[:, c4 * 128:(c4 + 1) * 128]
                            vc = br * 512 + hl * 128
                            mm(po, vtm[:, bi, vc:vc + 128], PT[:, par * 4 + h2 * 2 + pair, :], True, False,
                               [("v", bi, br), ("PT", par)], [("ps", 4 + hl)])
                            mm(po, Sbf[pr_, (gblk - 1) % 3, sidx, :], qk[pr_, qb + pair, boff:boff + bn], False, True,
                               [("Sb", (gblk - 1) % 3, br)] + rkeys_all(br), [("ps", 4 + hl)])

                    for c4, (bi, (boff, bn, smp, gblk)) in enumerate(tblocks):
                        par = c4 % 2
                        ubk = (3, 1)[par]
                        pu = bank(ubk)
                        for pair in range(2):
                            kc0 = br * 256 + pair * 128
                            vc0 = br * 512 + pair * 256
                            mm(pu[:, pair * 256:(pair + 1) * 256], kh[:, bi, kc0:kc0 + 128], vtm[:, bi, vc0:vc0 + 256],
                               True, True, [("kh", bi), ("v", bi, br)], [("ps", ubk)])
                        for hl in (0, 2, 1, 3):
                            pair, h2 = hl // 2, hl % 2
                            pr_ = slice(h2 * 64, h2 * 64 + 64)
                            sb2 = (0, 2)[h2]
                            mm(bank(sb2)[:, pair * 128:(pair + 1) * 128], qk[pr_, kb_i + pair, boff:boff + bn],
                               qk[pr_, qb + pair, boff:boff + bn], True, True, rkeys_all(br), [("ps", sb2)])
                        for f_ in pend:
                            f_()
                        pend = []
                        for h2 in range(2):
                            sb2 = (0, 2)[h2]
                            tt(PT[:, par * 4 + h2 * 2:par * 4 + h2 * 2 + 2, :],
                               bank(sb2)[:, 0:256].rearrange("p (a b) -> p a b", a=2),
                               masks[:, 0, :].unsqueeze(1).to_broadcast([128, 2, 128]), ALU.mult,
                               [("ps", sb2), "masks"], [("PT", par)])
                        for pair in range(2):
                            sidx = br * 2 + pair
                            for h2 in range(2):
                                pr_ = slice(h2 * 64, h2 * 64 + 64)
                                if br == 0:
                                    dsc = decA[pr_, pair, bi:bi + 1]
                                    dk_ = [("decA", pair, bi)]
                                else:
                                    dsc = gains[pr_, G_DECR + pair:G_DECR + pair + 1]
                                    dk_ = ["gains"]
                                stt(Sst[pr_, sidx, :], Sst[pr_, sidx, :], dsc,
                                    pu[pr_, pair * 256 + h2 * 128:pair * 256 + (h2 + 1) * 128],
                                    ALU.mult, ALU.add, ["S%d" % sidx, ("ps", ubk)] + dk_, ["S%d" % sidx])
                        S.op("act", (lambda o_, i_: (lambda e: e.copy(out=o_, in_=i_)))(
                            Sbf[:, gblk % 3, br * 2:br * 2 + 2, :], Sst[:, br * 2:br * 2 + 2, :]),
                            ["S%d" % (br * 2), "S%d" % (br * 2 + 1)], [("Sb", gblk % 3, br)])
                        pend.append((lambda a, b, c, d, e_: (lambda: o_ops(a, b, c, d, e_)))(c4, bi, boff, bn, gblk))
                    for f_ in pend:
                        f_()
                    for hl in range(4):
                        eb_ = ((2, 3), (0, 1)) if hl < 2 else ((2, 4), (0, 5))
                        epilogue(br, hl, bank(4 + hl)[:, 0:n], off, n, tbi, ("ps", 4 + hl), eb_)
                else:
                    bi = len(blocks) - 1
                    boff, bn, smp, gblk = blocks[bi]
                    for pair in range(2):
                        sidx = br * 2 + pair
                        sb_ = sidx % 2
                        S0f, tmpS = (SX[sidx % 2], SX[1 - sidx % 2])
                        kS0f, kTmp = ("SX", sidx % 2), ("SX", 1 - sidx % 2)
                        S0b3 = S0bf[:, sb_, :].rearrange("p (s v) -> p s v", s=16)
                        S0f3 = S0f[:].rearrange("p (s v) -> p s v", s=16)
                        tmp3 = tmpS[:].rearrange("p (s v) -> p s v", s=16)
                        for h2 in range(2):
                            hl = pair * 2 + h2
                            pr_ = slice(h2 * 64, h2 * 64 + 64)
                            slot = scnt[0] % 4
                            scnt[0] += 1
                            sbk = (0, 2)[slot % 2]
                            psS = bank(sbk)[0:64, 0:64]
                            mm(psS, qk[pr_, kb_i + pair, boff:boff + 64], qk[pr_, qb + pair, boff:boff + 64],
                               True, True, rkeys_all(br), [("ps", sbk)])
                            pts = slot + 4 * br
                            tt(PT[0:64, pts, 0:64], psS, masks[0:64, 1, 0:64], ALU.mult,
                               [("ps", sbk), "masks"], [("PT", pts // 4)])
                            if cut <= 4.93:
                                continue
                            po = bank(1)[:, hl * 64:(hl + 1) * 64]
                            vc = br * 512 + hl * 128
                            mm(po, vtm[0:64, bi, vc:vc + 128], PT[0:64, pts, 0:64], True, cut <= 4.935,
                               [("v", bi, br), ("PT", pts // 4)], [("ps", 1)])
                            po_ = slice((1 - h2) * 64, (1 - h2) * 64 + 64)
                            S.op("pool", (lambda a_: (lambda e: e.memset(a_, 0.0)))(qm[po_].rearrange("p a b -> p (a b)")),
                                 (), ["qm"])
                            tt(qm[pr_], qk[pr_, qb + pair, boff:boff + 64].unsqueeze(1).to_broadcast([64, 16, 64]),
                               maskq[pr_], ALU.mult, rkeys_all(br) + ["maskq"], ["qm"])
                            for s_ in range(16):
                                if cut <= 4.935:
                                    continue
                                mm(po, S0b3[:, s_, :], qm[:, s_, :], False, s_ == 15,
                                   [("S0bf", sb_), "qm"], [("ps", 1)])
                            if cut <= 4.94:
                                continue
                            V3 = Vblk[:].rearrange("p (s v) -> p s v", s=16)
                            tt(V3, vtm[0:64, bi, vc:vc + 128].unsqueeze(1).to_broadcast([64, 16, 128]),
                               smask[:, :].unsqueeze(2).to_broadcast([64, 16, 128]), ALU.mult,
                               [("v", bi, br), "smask"], ["Vblk"], eng="pool")
                            if cut <= 4.95:
                                continue
                            kc0 = br * 256 + pair * 128
                            for q4 in range(4):
                                mm(PSB[:, q4 * 512:(q4 + 1) * 512], kh[0:64, bi, kc0:kc0 + 128],
                                   Vblk[:, q4 * 512:(q4 + 1) * 512], True, True, [("kh", bi), "Vblk"], [("ps", 4 + q4)])
                            if h2 == 0:
                                if br == 0:
                                    tt(tmp3, S0f3, decSA[:, pair, :].unsqueeze(2).to_broadcast([128, 16, 128]),
                                       ALU.mult, [kS0f, "decSA"], [kTmp])
                                else:
                                    tsmul(tmpS[:, :], S0f[:, :], gains[:, G_DECR + 2 + pair:G_DECR + 3 + pair],
                                          [kS0f, "gains"], [kTmp])
                            tt(S0f[pr_, :], tmpS[pr_, :], PSB[pr_, :], ALU.add,
                               [kTmp] + [("ps", 4 + q) for q in range(4)], [kS0f + (h2,)])
                        if sidx < 3:
                            s0_load(sidx + 1)
                        dst = st_out[br][:, pair * 2:pair * 2 + 2].rearrange("s h d v -> (h d) s v")
                        dma("sp", dst, S0f3, [kS0f, kS0f + (0,), kS0f + (1,)], (), ("s0o", sidx % 2))
                    for hl in range(4):
                        if cut <= 4.98:
                            continue
                        epilogue(br, hl, bank(1)[:, hl * 64:(hl + 1) * 64], off, n, tbi, ("ps", 1), ((2, 3), (0, 5)))
        if hi == 1:
            for br in range(2):
                for pair in range(2):
                    sidx = br * 2 + pair
                    dma("sp", sp_out[br][pair * 128:(pair + 1) * 128, :], Sst[:, sidx, :], ["S%d" % sidx], (), "spo")
        wrelease(igt[0])
        wrelease(igt[1])
        if cut <= 5:
            return bail()
        S.barrier(exclude=("pe",))
        cnt = 0
        for dmg in range(2):
            iga, wga, kga = wnext(8, 512)
            igr, wgr, kgr = wnext(8, 512)
            iwo, wwo, kwo = wnext(8, 512)
            for tbi, (off, n) in enumerate(tbs):
                okeys = lambda br: [("qk", br, tbi, br * 4 + j) for j in range(4)]
                for dl in range(4):
                    dm = dmg * 4 + dl
                    b0 = (cnt % 2) * 4
                    cnt += 1
                    pga, pgr, pma, pmr = bank(b0), bank(b0 + 1), bank(b0 + 2), bank(b0 + 3)
                    cs = slice(dl * 128, (dl + 1) * 128)
                    for kc in range(8):
                        mm(pga[:, 0:n], wga[:, kc, cs], u[:, kc, off:off + n], kc == 0, kc == 7,
                           [kga, ("u", kc, tbi)], [("ps", b0)])
                    for kc in range(8):
                        mm(pgr[:, 0:n], wgr[:, kc, cs], u[:, kc, off:off + n], kc == 0, kc == 7,
                           [kgr, ("u", kc, tbi)], [("ps", b0 + 1)])
                    for fc in range(4):
                        mm(pma[:, 0:n], wwo[:, fc, cs], qk[:, fc, off:off + n], fc == 0, fc == 3,
                           [kwo] + okeys(0), [("ps", b0 + 2)])
                    for fc in range(4, 8):
                        mm(pmr[:, 0:n], wwo[:, fc, cs], qk[:, fc, off:off + n], fc == 4, fc == 7,
                           [kwo] + okeys(1), [("ps", b0 + 3)])
                    act(tA[:, 0:n], pga[:, 0:n], AF.Tanh, [("ps", b0)], ["tA"], scale=0.5)
                    act(tB[:, 0:n], pgr[:, 0:n], AF.Tanh, [("ps", b0 + 1)], ["tB"], scale=0.5)
                    stt(tC[:, 0:n], tA[:, 0:n], 1.0, pma[:, 0:n], ALU.add, ALU.mult, ["tA", ("ps", b0 + 2)], ["tC"])
                    stt(tD[:, 0:n], tB[:, 0:n], 1.0, pmr[:, 0:n], ALU.add, ALU.mult, ["tB", ("ps", b0 + 3)], ["tD"])
                    tt(tA[:, 0:n], tC[:, 0:n], tD[:, 0:n], ALU.add, ["tC", "tD"], ["tA"])
                    stt(h[:, dm, off:off + n], tA[:, 0:n], 0.5, h[:, dm, off:off + n], ALU.mult, ALU.add,
                        ["tA", ("h", dm, tbi)], [("h", dm, tbi)])
                    if next_gcol is not None and dmg == 1:
                        if tbi == 0 and dl == 3:
                            norm_part1(sq8m, 0, tbs[0][0], tbs[0][1])
                        elif tbi == 1 and dl == 1:
                            norm_part2(sq8m, 0, tbs[0][0], tbs[0][1], next_gcol)
            wrelease(iga)
            wrelease(igr)
            wrelease(iwo)

    def ple(tok0, tbs, lazy):
        cnt = 0
        for dmg in range(2):
            ig, wg_, kg_ = wnext(8, 512)
            ip, wp_, kp_ = wnext(2, 512)
            for tbi, (off, n) in enumerate(tbs):
                if dmg == 0 and tbi > 0:
                    lazy(tbi)
                for dl in range(4):
                    dm = dmg * 4 + dl
                    b0 = (cnt % 2) * 2
                    cnt += 1
                    pg, pp = bank(b0), bank(b0 + 1)
                    cs = slice(dl * 128, (dl + 1) * 128)
                    for kc in range(8):
                        mm(pg[:, 0:n], wg_[:, kc, cs], u[:, kc, off:off + n], kc == 0, kc == 7,
                           [kg_, ("u", kc, tbi)], [("ps", b0)])
                    for kc in range(2):
                        mm(pp[:, 0:n], wp_[:, kc, cs], pb[:, kc, off:off + n], kc == 0, kc == 1,
                           [kp_, "pb"], [("ps", b0 + 1)])
                    sb_ = cnt % 2
                    act(stmp[:, sb_, 0:n], pg[:, 0:n], AF.Tanh, [("ps", b0)], [("stmp", sb_)], scale=0.5)
                    stt(stmp2[:, sb_, 0:n], stmp[:, sb_, 0:n], 1.0, pp[:, 0:n], ALU.add, ALU.mult,
                        [("stmp", sb_), ("ps", b0 + 1)], [("stmp2", sb_)])
                    stt(h[:, dm, off:off + n], stmp2[:, sb_, 0:n], 0.5, h[:, dm, off:off + n], ALU.mult, ALU.add,
                        [("stmp2", sb_), ("h", dm, tbi)], [("h", dm, tbi)])
            wrelease(ig)
            wrelease(ip)

    def skip_tiles(k):
        for _ in range(k):
            i = st["cur"]
            st["cur"] += 1
            wrelease(i)

    yv = yT.rearrange("(k p) t -> p k t", p=128)
    xv = xT.rearrange("(k p) t -> p k t", p=128)
    halves = [(0, [(0, 512), (512, 512)]), (1024, [(0, 512), (512, 512), (1024, 64)])]
    for hi, (tok0, tbs) in enumerate(halves[:nhalves]):
        T = sum(n for _, n in tbs)
        blocks = [(b * 128, 128, False, (tok0 + b * 128) // 128) for b in range(8)]
        if hi == 1:
            blocks.append((1024, 64, True, 16))
        def xload(tok0_, tbi, off, n):
            dma("sp", h[:, :, off:off + n], xv[:, :, tok0_ + off:tok0_ + off + n], (),
                [("h", kc, tbi) for kc in range(8)], ("x", tbi))

        if hi == 0:
            for tbi, (off, n) in enumerate(tbs):
                xload(tok0, tbi, off, n)
            for i in range(2, RING):
                issue(i, after=[("h", 0, 0)])
            late_consts()

        def mk_lazy(gcol):
            return lambda tbi: norm_u_tb(tbi, tbs[tbi][0], tbs[tbi][1], gcol)

        norm_u_tb(0, tbs[0][0], tbs[0][1], G_FFN1)
        ffn(tbs, mk_lazy(G_FFN1), G_MIX if (stage >= 2 and cut >= 99) else None)
        if stage >= 2:
            if cut < 99:
                norm_u_tb(0, tbs[0][0], tbs[0][1], G_MIX)
            S.barrier(exclude=("pe",))
            mixing(hi, tok0, tbs, blocks, mk_lazy(G_MIX), G_FFN2 if stage >= 3 else None)
        else:
            skip_tiles(15)
        if stage >= 3:
            if cut < 99:
                norm_u_tb(0, tbs[0][0], tbs[0][1], G_FFN2)
            S.barrier(exclude=("pe",))
            if stage >= 4:
                T_ = sum(n for _, n in tbs)
                dma("pool", pb[:, :, 0:T_], pT.rearrange("(k p) t -> p k t", p=128)[:, :, tok0:tok0 + T_], (),
                    ["pb"], "pb")
            ffn(tbs, mk_lazy(G_FFN2), G_PLE if stage >= 4 else None)
        else:
            S.barrier(exclude=("pe",))
            skip_tiles(20)
        if stage >= 4:
            ple(tok0, tbs, mk_lazy(G_PLE))
        else:
            skip_tiles(4)
        if stage >= 5:
            for tbi, (off, n) in enumerate(tbs):
                pn = bank(6)
                for kc in range(8):
                    b = kc % 2
                    act(sq[:, b, 0:n], h[:, kc, off:off + n], AF.Square, [("h", kc, tbi)], [("sq", b)])
                    mm(pn[:, 0:n], ones_bf[:], sq[:, b, 0:n], kc == 0, kc == 7, ["ones_bf", ("sq", b)], [("ps", 6)])
                rb = tbi % 2
                act(rstd[:, rb, 0:n], pn[:, 0:n], AF.Ln, [("ps", 6)], [("rstd", rb)], scale=1.0 / D, bias=EPS)
                act(rstd[:, rb, 0:n], rstd[:, rb, 0:n], AF.Exp, [("rstd", rb)], [("rstd", rb)], scale=-0.5)
                for kc in range(8):
                    stt(yout[:, rb, kc, 0:n], h[:, kc, off:off + n], gains[:, G_FIN + kc:G_FIN + kc + 1],
                        rstd[:, rb, 0:n], ALU.mult, ALU.mult,
                        [("h", kc, tbi), "gains", ("rstd", rb)], [("yout", rb)])
                dma("sp", yv[:, :, tok0 + off:tok0 + off + n], yout[:, rb, :, 0:n], [("yout", rb)], (), ("yo", rb))
                if stage >= 5 and hi + 1 < len(halves[:nhalves]):
                    ntok0, ntbs = halves[hi + 1]
                    xload(ntok0, tbi, ntbs[tbi][0], ntbs[tbi][1])
                    if tbi == len(tbs) - 1:
                        for t2 in range(len(tbs), len(ntbs)):
                            xload(ntok0, t2, ntbs[t2][0], ntbs[t2][1])
        else:
            for tbi, (off, n) in enumerate(tbs):
                dma("sp", yv[:, :, tok0 + off:tok0 + off + n], h[:, :, off:off + n],
                    [("h", kc, tbi) for kc in range(8)], (), "yo")
            if hi + 1 < len(halves[:nhalves]):
                ntok0, ntbs = halves[hi + 1]
                for t2 in range(len(ntbs)):
                    xload(ntok0, t2, ntbs[t2][0], ntbs[t2][1])
    S.emit(nc)
    return nc


_CONST_CACHE = {}


def _prep_inputs(inp):
    f32 = np.float32
    inp = {k: np.asarray(v) for k, v in inp.items()}
    wflat, sched = _weight_tiles(inp)
    ct = _CONST_CACHE.get("ct")
    if ct is None:
        ct = _const_tables()
        _CONST_CACHE["ct"] = ct

    def col8(v):
        return np.asarray(v, f32).reshape(-1, 128).T

    cpack = np.zeros((128, 52), f32)
    cpack[:, 0:8] = col8(inp["norm_ffn1"][0])
    cpack[:, 8:16] = col8(inp["norm_mix"][0])
    cpack[:, 16:24] = col8(inp["norm_ffn2"][0])
    cpack[:, 24:32] = col8(inp["norm_ple"][0])
    cpack[:, 32:40] = col8(inp["norm_final"])
    cpack[:, 40:44] = col8(inp["gn_gla"][0])
    cpack[:, 44:48] = col8(inp["gn_ret"][0])
    cpack[:, 48:52] = ct["decr"]
    xp, xs = inp["x_prompt"], inp["x_sample"]
    pp, psm = inp["p_prompt"][0], inp["p_sample"][0]
    in_maps = []
    for c in range(NCORES):
        xc = np.concatenate([xp[c], xs[16 * c:16 * c + 16].reshape(NSAMP, D)], axis=0)
        pc = np.concatenate([pp[c], psm[16 * c:16 * c + 16].reshape(NSAMP, 256)], axis=0)
        in_maps.append({
            "xT": np.ascontiguousarray(xc.T.astype(f32)),
            "pT": np.ascontiguousarray(pc.T.astype(f32)),
            "wflat": wflat,
            "cpack": cpack,
            "cmat": ct["cmat"], "masks": ct["masks"], "smask": ct["smask"], "maskq": ct["maskq"],
            "wup": np.ascontiguousarray(inp["w_alpha_up"][0].astype(f32)),
            "balpha": np.ascontiguousarray(inp["b_alpha"][0].reshape(1, 256).astype(f32)),
            "rtab": ct["rtab"], "ttab": ct["ttab"],
            "sg_in": np.ascontiguousarray(inp["state_gla"][0, 16 * c:16 * c + 16].astype(f32)),
            "sr_in": np.ascontiguousarray(inp["state_ret"][0, 16 * c:16 * c + 16].astype(f32)),
        })
    return in_maps


def _run(inp, stage=5):
    in_maps = _prep_inputs(inp)
    nc = build_program(stage)
    res = run_bass_kernel_spmd(nc, in_maps, core_ids=list(range(NCORES)))
    return res.results


def kernel(**inputs):
    results = _run(inputs, 5)
    f32 = np.float32
    y_prompt = np.zeros((8, TP, D), f32)
    y_sample = np.zeros((128, 4, D), f32)
    gp = np.zeros((1, 8, 4, 64, 128), f32); rp = np.zeros((1, 8, 4, 64, 128), f32)
    gs = np.zeros((1, 128, 4, 64, 128), f32); rs = np.zeros((1, 128, 4, 64, 128), f32)
    for c, r in enumerate(results):
        yc = np.asarray(r["yT"]).T
        y_prompt[c] = yc[:TP]
        y_sample[16 * c:16 * c + 16] = yc[TP:].reshape(16, 4, D)
        gp[0, c] = np.asarray(r["sgp"]).reshape(4, 64, 128)
        rp[0, c] = np.asarray(r["srp"]).reshape(4, 64, 128)
        gs[0, 16 * c:16 * c + 16] = np.asarray(r["sgs"])
        rs[0, 16 * c:16 * c + 16] = np.asarray(r["srs"])
    return (y_prompt, y_sample, gp, rp, gs, rs)
```

```python
import numpy as np
import ml_dtypes
import concourse.bass as bass
import concourse.mybir as mybir
from concourse.bass_utils import run_bass_kernel_spmd
from contextlib import ExitStack

F32 = mybir.dt.float32
BF16 = mybir.dt.bfloat16
AF = mybir.ActivationFunctionType
ALU = mybir.AluOpType

NCORES = 8
D = 1024
DFF = 2816
NFC = 22
TP = 2048
NSAMP = 64
TALL = TP + NSAMP
EPS = 1e-6
RING = 5
SLOT = 4096

ENGS = ("pe", "act", "dve", "pool", "sp")


class Op:
    __slots__ = ("eng", "fn", "reads", "writes", "dsem", "chan", "deps", "signaled", "count", "idx")

    def __init__(self, eng, fn, reads, writes, dsem):
        self.eng = eng
        self.fn = fn
        self.reads = tuple(reads)
        self.writes = tuple(writes)
        self.dsem = dsem
        self.chan = ("dma", dsem) if dsem is not None else ("eng", eng)
        self.deps = []
        self.signaled = dsem is not None
        self.count = 0


class Sched:
    def __init__(self):
        self.ops = []
        self.dma_sems = []
        self.barriers = []

    def op(self, eng, fn, reads=(), writes=(), dsem=None):
        o = Op(eng, fn, reads, writes, dsem)
        o.idx = len(self.ops)
        self.ops.append(o)
        if dsem is not None and dsem not in self.dma_sems:
            self.dma_sems.append(dsem)
        return o

    def barrier(self, exclude=()):
        self.barriers.append((len(self.ops), tuple(exclude)))

    def analyze(self):
        last_w, last_r = {}, {}
        waited = {e: {} for e in ENGS}
        chan_last = {}
        pending = {e: None for e in ENGS}
        bars = list(self.barriers)
        bi = 0
        ops = self.ops
        for o in ops:
            while bi < len(bars) and bars[bi][0] <= o.idx:
                snap = {ch: i for ch, i in chan_last.items()
                        if not (ch[0] == "dma" and isinstance(ch[1], tuple) and ch[1][0] == "w")}
                for e in ENGS:
                    if e in bars[bi][1]:
                        continue
                    if pending[e] is None:
                        pending[e] = snap
                    else:
                        m = dict(pending[e])
                        m.update(snap)
                        pending[e] = m
                bi += 1
            deps = {}

            def add(d):
                for ch, i in d.items():
                    if deps.get(ch, -1) < i:
                        deps[ch] = i

            if pending[o.eng] is not None:
                add(pending[o.eng])
                pending[o.eng] = None
            for k in o.reads:
                add(last_w.get(k, {}))
            for k in o.writes:
                add(last_w.get(k, {}))
                add(last_r.get(k, {}))
            w = waited[o.eng]
            for ch, i in deps.items():
                if ch == ("eng", "pe") and o.eng == "pe":
                    continue
                if w.get(ch, -1) >= i:
                    continue
                w[ch] = i
                o.deps.append(i)
                ops[i].signaled = True
            for k in o.reads:
                last_r.setdefault(k, {})[o.chan] = o.idx
            for k in o.writes:
                last_w[k] = {o.chan: o.idx}
                last_r[k] = {}
            chan_last[o.chan] = o.idx
        cnt = {}
        for o in ops:
            if o.signaled:
                inc = 16 if o.dsem is not None else 1
                cnt[o.chan] = cnt.get(o.chan, 0) + inc
                o.count = cnt[o.chan]
        self.final_counts = cnt

    def emit(self, nc, final_wait_eng="sp"):
        self.analyze()
        with ExitStack() as es:
            sems = {}
            for e in ENGS:
                sems[("eng", e)] = es.enter_context(nc.semaphore("sem_" + e))
            for i, d in enumerate(self.dma_sems):
                sems[("dma", d)] = es.enter_context(nc.semaphore("dsem_%d" % i))
            block = es.enter_context(nc.Block())
            ops = self.ops

            def run(engname):
                def body(eng):
                    for o in ops:
                        if o.eng != engname:
                            continue
                        for i in o.deps:
                            d = ops[i]
                            eng.wait_ge(sems[d.chan], d.count)
                        ins = o.fn(eng)
                        if o.signaled:
                            ins.then_inc(sems[o.chan], 16 if o.dsem is not None else 1)
                    if engname == final_wait_eng:
                        for ch, c in self.final_counts.items():
                            eng.wait_ge(sems[ch], c)
                return body

            block.tensor(run("pe"))
            block.scalar(run("act"))
            block.vector(run("dve"))
            block.gpsimd(run("pool"))
            block.sync(run("sp"))


def _tile_kc(W, cols):
    nk = W.shape[0] // 128
    sub = W[:, cols]
    return np.ascontiguousarray(sub.reshape(nk, 128, sub.shape[1]).transpose(1, 0, 2).reshape(128, -1))


def _swap_heads(idx):
    idx = np.asarray(idx).reshape(-1, 2, 32)
    return idx[:, ::-1, :].reshape(-1)


_QA, _KA, _VA, _RA = np.arange(0, 256), np.arange(256, 512), np.arange(512, 1024), np.arange(1024, 1536)
_QR, _KR, _VR, _GR = np.arange(1536, 1792), np.arange(1792, 2048), np.arange(2048, 2560), np.arange(2560, 3072)
_AL = np.arange(3072, 3088)
_GA, _GRT = np.arange(3088, 4112), np.arange(4112, 5136)


def _weight_tiles(inp):
    tiles = []

    def ffn(w_in, w_out):
        for g in range(6):
            wd = 512 if g < 5 else 256
            tiles.append(_tile_kc(w_in, np.arange(g * 512, g * 512 + wd)))
            tiles.append(_tile_kc(w_in, np.arange(DFF + g * 512, DFF + g * 512 + wd)))
        for dm in range(8):
            tiles.append(_tile_kc(w_out, np.arange(dm * 128, dm * 128 + 128)))

    ffn(inp["w_ffn1_in"][0], inp["w_ffn1_out"][0])
    wi = inp["w_in"][0]
    tiles.append(_tile_kc(wi, _AL))
    tiles.append(_tile_kc(wi, np.concatenate([_QA, _KA])))
    tiles.append(_tile_kc(wi, np.concatenate([_QR, _KR])))
    tiles.append(_tile_kc(wi, np.concatenate([_swap_heads(_QR), _swap_heads(_KR)])))
    tiles.append(_tile_kc(wi, np.concatenate([_KA, _KR])))
    tiles.append(_tile_kc(wi, _VA))
    tiles.append(_tile_kc(wi, _VR))
    tiles.append(_tile_kc(wi, _RA))
    tiles.append(_tile_kc(wi, _GR))
    wo = inp["w_out"][0]
    for dmg in range(2):
        c = np.arange(dmg * 512, dmg * 512 + 512)
        tiles.append(_tile_kc(wi, _GA[c]))
        tiles.append(_tile_kc(wi, _GRT[c]))
        tiles.append(_tile_kc(wo, c))
    ffn(inp["w_ffn2_in"][0], inp["w_ffn2_out"][0])
    for dmg in range(2):
        c = np.arange(dmg * 512, dmg * 512 + 512)
        tiles.append(_tile_kc(inp["w_ple_gate"][0], c))
        tiles.append(_tile_kc(inp["w_ple_proj"][0], c))
    sched = []
    off = 0
    for t in tiles:
        assert t.shape[1] <= SLOT
        sched.append((off, t.shape[1]))
        off += t.shape[1]
    return np.ascontiguousarray(np.concatenate(tiles, axis=1).astype(np.float32)), sched


def _weight_schedule():
    Ls = []

    def ffn():
        for g in range(6):
            wd = 512 if g < 5 else 256
            Ls.extend([8 * wd, 8 * wd])
        Ls.extend([NFC * 128] * 8)

    ffn()
    Ls.append(8 * 16)
    Ls.extend([8 * 512] * 8)
    Ls.extend([8 * 512] * 6)
    ffn()
    Ls.extend([8 * 512, 2 * 512] * 2)
    sched, off = [], 0
    for L in Ls:
        sched.append((off, L))
        off += L
    return sched, off


def _const_tables():
    f32 = np.float32
    half = 32
    freq = (10000.0 ** (-(np.arange(half, dtype=np.float64) / float(half)))).astype(f32)
    pos = np.concatenate([np.arange(TP), 16384 + (np.arange(NSAMP) % 4)]).astype(f32)
    il = np.concatenate([np.arange(TP) % 128, np.arange(NSAMP) % 4]).astype(np.float64)
    nchunk = np.concatenate([np.full(TP, 128.0), np.full(NSAMP, 4.0)])
    ang = (pos[:, None] * freq[None, :]).astype(f32).astype(np.float64)
    cos, sin = np.cos(ang), np.sin(ang)
    lg = np.log1p(-np.exp2(-5.0 - np.arange(4, dtype=np.float64)))
    d = np.arange(64)
    sgn = np.where(d < 32, -1.0, 1.0)
    CQ = np.zeros((2, 128, TALL)); SQ = np.zeros_like(CQ); CK = np.zeros_like(CQ); SK = np.zeros_like(CQ)
    for hp in range(2):
        for h2 in range(2):
            h = hp * 2 + h2
            up = np.exp((il + 1.0) * lg[h])[None, :]
            dn = np.exp(-(il + 1.0) * lg[h])[None, :] * 0.125
            c = cos[:, d % 32].T
            s = sin[:, d % 32].T * sgn[:, None]
            sl = slice(h2 * 64, h2 * 64 + 64)
            CQ[hp, sl] = c * up; SQ[hp, sl] = s * up; CK[hp, sl] = c * dn; SK[hp, sl] = s * dn
    rtab = np.zeros((5, 128, 8, 512), f32)
    for gtb in range(5):
        t0, n = (gtb * 512, 512) if gtb < 4 else (TP, NSAMP)
        for hp in range(2):
            for j, A in enumerate((CQ, SQ, CK, SK)):
                rtab[gtb, :, hp * 4 + j, :n] = A[hp, :, t0:t0 + n]
    ttab = np.zeros((17, 128, 2, 256), f32)
    for b in range(17):
        t0, n = (b * 128, 128) if b < 16 else (TP, NSAMP)
        for h in range(4):
            k = np.exp((nchunk[t0:t0 + n] - 1.0 - il[t0:t0 + n]) * lg[h])[:, None] * 0.125
            ttab[b, :n, 0, h * 64:(h + 1) * 64] = cos[t0:t0 + n][:, d % 32] * k
            ttab[b, :n, 1, h * 64:(h + 1) * 64] = sin[t0:t0 + n][:, d % 32] * sgn[None, :] * k
    decr = np.zeros((128, 4), f32)
    for hp in range(2):
        for h2 in range(2):
            h = hp * 2 + h2
            decr[h2 * 64:(h2 + 1) * 64, hp] = np.exp(128.0 * lg[h])
            decr[h2 * 64:(h2 + 1) * 64, 2 + hp] = np.exp(4.0 * lg[h])
    j = np.arange(128)[:, None]; i = np.arange(128)[None, :]
    cmat = np.zeros((128, 4, 128), f32)
    cmat[:, 0] = np.where(j <= i, -1.0 / 16, 0.0)
    cmat[:, 1] = np.where(j > i, -1.0 / 16, 0.0)
    same = (j // 4 == i // 4) & (j < 64) & (i < 64)
    cmat[:, 2] = np.where(same & (j <= i), -1.0 / 16, 0.0)
    cmat[:, 3] = np.where(same & (j > i), -1.0 / 16, 0.0)
    masks = np.zeros((128, 2, 128), f32)
    masks[:, 0] = (j <= i)
    masks[:, 1] = same & (j <= i)
    smask = (np.arange(64)[:, None] // 4 == np.arange(16)[None, :]).astype(f32)
    maskq = np.broadcast_to((np.arange(16)[:, None] == np.arange(64)[None, :] // 4).astype(f32).reshape(1, 1024),
                            (128, 1024)).copy()
    return dict(rtab=rtab.reshape(5, 128, 8 * 512), ttab=ttab.reshape(17, 128, 512), decr=decr,
                cmat=cmat.reshape(128, 512), masks=masks.reshape(128, 256), smask=smask, maskq=maskq)


def build_program(stage=5, nhalves=2, cut=99):
    nc = bass.Bass("TRN2", target_bir_lowering=False)
    S = Sched()
    wsched1, WTOT = _weight_schedule()

    def din(name, shape, dt=F32):
        return nc.dram_tensor(name, list(shape), dt, kind="ExternalInput").ap()

    def dout(name, shape, dt=F32):
        return nc.dram_tensor(name, list(shape), dt, kind="ExternalOutput").ap()

    xT = din("xT", [D, TALL]); pT = din("pT", [256, TALL])
    wflat = din("wflat", [128, WTOT])
    cpack = din("cpack", [128, 52])
    cmat_d = din("cmat", [128, 512]); masks_d = din("masks", [128, 256]); smask_d = din("smask", [64, 16])
    maskq_d = din("maskq", [128, 1024])
    wup_d = din("wup", [16, 256]); balpha_d = din("balpha", [1, 256])
    rtab_d = din("rtab", [5, 128, 8 * 512]); ttab_d = din("ttab", [17, 128, 512])
    sg_in = din("sg_in", [16, 4, 64, 128]); sr_in = din("sr_in", [16, 4, 64, 128])
    yT = dout("yT", [D, TALL])
    sgp = dout("sgp", [256, 128]); srp = dout("srp", [256, 128])
    sgs = dout("sgs", [16, 4, 64, 128]); srs = dout("srs", [16, 4, 64, 128])
    st_in = (sg_in, sr_in); st_out = (sgs, srs); sp_out = (sgp, srp)

    SB_BASE, SB_END = 16512, 229376
    ptr = [SB_BASE]

    def esz(dt):
        return 2 if dt == BF16 else 4

    def palloc(name, shape, dt):
        nbytes = int(np.prod(shape[1:])) * esz(dt)
        nbytes = (nbytes + 31) // 32 * 32
        t = nc.alloc_sbuf_tensor_at(name, list(shape), dt, offset=ptr[0])
        ptr[0] += nbytes
        return t

    TM = 1088
    h = palloc("h", [128, 8, TM], F32)
    u = palloc("u", [128, 8, TM], BF16)
    ring = [palloc("ring%d" % i, [128, SLOT], BF16) for i in range(RING)]
    gains = palloc("gains", [128, 52], F32)
    cmat = palloc("cmat_s", [128, 4, 128], BF16)
    masks = palloc("masks_s", [128, 2, 128], F32)
    smask = palloc("smask_s", [64, 16], F32)
    maskq = palloc("maskq_s", [128, 16, 64], BF16)
    wup = palloc("wup_s", [16, 256], BF16)
    balpha = palloc("balpha_s", [1, 256], BF16)
    ones_bf = palloc("ones_bf", [128, 128], BF16)
    ones_row = palloc("ones_row", [1, 128], BF16)
    Sst = palloc("Sst", [128, 4, 128], F32)
    Sbf = palloc("Sbf", [128, 3, 4, 128], BF16)
    decA = palloc("decA", [128, 2, 9], F32)
    decSA = palloc("decSA", [128, 2, 16], F32)
    sq = palloc("sq", [128, 2, 512], BF16)
    rstd = palloc("rstd", [128, 2, 512], F32)
    UB = ptr[0]
    USZ = SB_END - UB

    def ualloc(name, shape, dt, off):
        nbytes = int(np.prod(shape[1:])) * esz(dt)
        assert off % 32 == 0 and off + nbytes <= USZ, (name, off, nbytes, USZ)
        return nc.alloc_sbuf_tensor_at(name, list(shape), dt, offset=UB + off), off + (nbytes + 31) // 32 * 32

    gbuf, o = ualloc("gbuf", [128, NFC, TM], BF16, 0)
    stmp, o = ualloc("stmp", [128, 2, 512], F32, o)
    pb, o_pb = ualloc("pb", [128, 2, TM], BF16, o)
    stmp2, _ = ualloc("stmp2", [128, 2, 512], F32, o_pb)
    yout, _ = ualloc("yout", [128, 2, 8, 512], F32, o_pb + 4096)
    sq8f, _ = ualloc("sq8f", [128, 8, 512], BF16, o_pb + 4096 + 32768)
    sq8m, _ = ualloc("sq8m", [128, 8, 512], BF16, 17408)
    qk, o = ualloc("qk", [128, 8, TM], BF16, 0)
    vtm, o = ualloc("vtm", [128, 9, 1024], BF16, o)
    kh, o = ualloc("kh", [128, 9, 512], BF16, o)
    o_tr = o
    ed, o = ualloc("ed", [128, 9, 256], F32, o)
    alowT, o = ualloc("alowT", [16, TM], BF16, o)
    spb, o = ualloc("spb", [128, 2, 256], F32, o)
    sphi, o = ualloc("sphi", [128, 2, 256], BF16, o)
    splo, o = ualloc("splo", [128, 2, 256], BF16, o)
    etmp, o = ualloc("etmp", [128, 2, 256], F32, o)
    eb, o = ualloc("eb", [128, 2, 512], F32, o)
    enb, o = ualloc("enb", [128, 2, 512], F32, o)
    rtab, o = ualloc("rtab_s", [128, 8, 512], F32, o)
    ttab, o = ualloc("ttab_s", [128, 3, 512], F32, o)
    tA, o = ualloc("tA", [128, 512], F32, o)
    tB, o = ualloc("tB", [128, 512], F32, o)
    tC, o = ualloc("tC", [128, 512], F32, o)
    tD, o = ualloc("tD", [128, 512], F32, o)
    o = o_tr
    PT, o = ualloc("PT", [128, 8, 128], BF16, o)
    osb, o = ualloc("osb", [128, 2, 512], F32, o)
    sqb, o = ualloc("sqb", [128, 2, 512], BF16, o)
    rs2, o = ualloc("rs2", [128, 2, 512], F32, o)
    ee, o = ualloc("ee", [128, 2, 512], F32, o)
    tt1, o = ualloc("tt1", [128, 2, 512], F32, o)
    tt2, o = ualloc("tt2", [128, 2, 512], F32, o)
    S0bf, o = ualloc("S0bf", [128, 2, 16 * 128], BF16, o)
    S0f_buf, o = ualloc("S0f", [128, 16 * 128], F32, o)
    tmpS_buf, o = ualloc("tmpS", [128, 16 * 128], F32, o)
    Vblk, o = ualloc("Vblk", [64, 16 * 128], BF16, o)
    qm, o = ualloc("qm", [128, 16, 64], BF16, o)
    print("SBUF union size", USZ, "P2 end", o)

    PSA = nc.alloc_psum_tensor("PSA", [128, 2048], F32)
    PSB = nc.alloc_psum_tensor("PSB", [128, 2048], F32)

    def bank(b):
        t = PSA if b < 4 else PSB
        return t[:, (b % 4) * 512:(b % 4) * 512 + 512]

    def mm(out, lhsT, rhs, start, stop, reads, writes):
        S.op("pe", lambda e: e.matmul(out, lhsT=lhsT, rhs=rhs, start=start, stop=stop), reads, writes)

    def act(out, in_, func, reads, writes, scale=1.0, bias=0.0):
        S.op("act", lambda e: e.activation(out=out, in_=in_, func=func, bias=bias, scale=scale), reads, writes)

    def tt(out, in0, in1, op, reads, writes, eng="dve"):
        S.op(eng, lambda e: e.tensor_tensor(out=out, in0=in0, in1=in1, op=op), reads, writes)

    def stt(out, in0, scalar, in1, op0, op1, reads, writes):
        S.op("dve", lambda e: e.scalar_tensor_tensor(out=out, in0=in0, scalar=scalar, in1=in1, op0=op0, op1=op1),
             reads, writes)

    def tsmul(out, in0, scalar, reads, writes):
        S.op("dve", lambda e: e.tensor_scalar(out=out, in0=in0, scalar1=scalar, scalar2=None, op0=ALU.mult),
             reads, writes)

    def dma(eng, out, in_, reads, writes, dsem):
        S.op(eng, lambda e: e.dma_start(out=out, in_=in_), reads, writes, dsem=dsem)

    def memset(ap, val, writes):
        S.op("dve", lambda e: e.memset(ap, val), (), writes)

    full_sched = wsched1 + wsched1
    st = {"cur": 0}

    def issue(i, after=()):
        if i >= len(full_sched):
            return
        off, L = full_sched[i]
        slot = i % RING
        dma("pool", ring[slot][:, 0:L], wflat[:, off:off + L], after, [("ring", slot)], ("w", slot))

    def wnext(nk, ncols):
        i = st["cur"]
        st["cur"] += 1
        off, L = full_sched[i]
        assert L == nk * ncols, (i, L, nk, ncols)
        slot = i % RING
        return i, ring[slot][:, 0:L].rearrange("p (k c) -> p k c", k=nk), ("ring", slot)

    def wrelease(i):
        issue(i + RING)

    dma("sp", gains[:], cpack[:], (), ["gains"], "c0")

    def late_consts():
        dma("pool", cmat[:].rearrange("p a b -> p (a b)"), cmat_d[:], (), ["cmat"], "c1")
        dma("sp", masks[:].rearrange("p a b -> p (a b)"), masks_d[:], (), ["masks"], "c2")
        dma("sp", smask[:], smask_d[:], (), ["smask"], "c3")
        dma("pool", maskq[:].rearrange("p a b -> p (a b)"), maskq_d[:], (), ["maskq"], "c6")
        dma("pool", wup[:], wup_d[:], (), ["wup"], "c4")
        dma("pool", balpha[:], balpha_d[:], (), ["balpha"], "c5")
    memset(ones_bf[:], 1.0, ["ones_bf"])
    memset(ones_row[:], 1.0, ["ones_row"])
    memset(Sst[:].rearrange("p a b -> p (a b)"), 0.0, ["S0", "S1", "S2", "S3"])
    memset(Sbf[:].rearrange("p t a b -> p (t a b)"), 0.0, [("Sb", t_, b_) for t_ in range(3) for b_ in range(2)])
    for i in range(2):
        issue(i)

    G_FFN1, G_MIX, G_FFN2, G_PLE, G_FIN, G_GNA, G_GNR, G_DECR = 0, 8, 16, 24, 32, 40, 44, 48

    def rmsnorm(tbs, gcol, dst_fn, dst_keys):
        for tbi, (off, n) in enumerate(tbs):
            pn = bank(6)
            for kc in range(8):
                b = kc % 2
                act(sq[:, b, 0:n], h[:, kc, off:off + n], AF.Square, [("h", kc, tbi)], [("sq", b)])
                mm(pn[:, 0:n], ones_bf[:], sq[:, b, 0:n], kc == 0, kc == 7, ["ones_bf", ("sq", b)], [("ps", 6)])
            rb = tbi % 2
            act(rstd[:, rb, 0:n], pn[:, 0:n], AF.Ln, [("ps", 6)], [("rstd", rb)], scale=1.0 / D, bias=EPS)
            act(rstd[:, rb, 0:n], rstd[:, rb, 0:n], AF.Exp, [("rstd", rb)], [("rstd", rb)], scale=-0.5)
            for kc in range(8):
                stt(dst_fn(kc, tbi, off, n), h[:, kc, off:off + n], gains[:, gcol + kc:gcol + kc + 1],
                    rstd[:, rb, 0:n], ALU.mult, ALU.mult,
                    [("h", kc, tbi), "gains", ("rstd", rb)], dst_keys(kc, tbi))

    def norm_to_u(tbs, gcol):
        rmsnorm(tbs, gcol, lambda kc, tbi, off, n: u[:, kc, off:off + n], lambda kc, tbi: [("u", kc, tbi)])

    def norm_u_tb(tbi, off, n, gcol):
        pn = bank(6)
        for kc in range(8):
            b = kc % 2
            act(sq[:, b, 0:n], h[:, kc, off:off + n], AF.Square, [("h", kc, tbi)], [("sq", b)])
            mm(pn[:, 0:n], ones_bf[:], sq[:, b, 0:n], kc == 0, kc == 7, ["ones_bf", ("sq", b)], [("ps", 6)])
        rb = tbi % 2
        act(rstd[:, rb, 0:n], pn[:, 0:n], AF.Ln, [("ps", 6)], [("rstd", rb)], scale=1.0 / D, bias=EPS)
        act(rstd[:, rb, 0:n], rstd[:, rb, 0:n], AF.Exp, [("rstd", rb)], [("rstd", rb)], scale=-0.5)
        for kc in range(8):
            stt(u[:, kc, off:off + n], h[:, kc, off:off + n], gains[:, gcol + kc:gcol + kc + 1],
                rstd[:, rb, 0:n], ALU.mult, ALU.mult,
                [("h", kc, tbi), "gains", ("rstd", rb)], [("u", kc, tbi)])

    def norm_part1(sq8, tbi, off, n):
        for kc in range(8):
            act(sq8[:, kc, 0:n], h[:, kc, off:off + n], AF.Square, [("h", kc, tbi)], [("sq8", kc)])

    def norm_part2(sq8, tbi, off, n, gcol):
        pn = bank(6)
        for kc in range(8):
            mm(pn[:, 0:n], ones_bf[:], sq8[:, kc, 0:n], kc == 0, kc == 7, ["ones_bf", ("sq8", kc)], [("ps", 6)])
        rb = tbi % 2
        act(rstd[:, rb, 0:n], pn[:, 0:n], AF.Ln, [("ps", 6)], [("rstd", rb)], scale=1.0 / D, bias=EPS)
        act(rstd[:, rb, 0:n], rstd[:, rb, 0:n], AF.Exp, [("rstd", rb)], [("rstd", rb)], scale=-0.5)
        for kc in range(8):
            stt(u[:, kc, off:off + n], h[:, kc, off:off + n], gains[:, gcol + kc:gcol + kc + 1],
                rstd[:, rb, 0:n], ALU.mult, ALU.mult,
                [("h", kc, tbi), "gains", ("rstd", rb)], [("u", kc, tbi)])

    def ffn(tbs, lazy, next_gcol=None):
        cnt = 0
        for g in range(6):
            nf = 4 if g < 5 else 2
            ia, wa, ka = wnext(8, nf * 128)
            ib, wb, kb = wnext(8, nf * 128)
            for tbi, (off, n) in enumerate(tbs):
                if g == 0 and tbi > 0:
                    lazy(tbi)
                for fl in range(nf):
                    fc = g * 4 + fl
                    pa, pbk = bank(cnt % 2), bank(2 + cnt % 2)
                    ka_, kb_ = ("ps", cnt % 2), ("ps", 2 + cnt % 2)
                    for kc in range(8):
                        mm(pa[:, 0:n], wa[:, kc, fl * 128:(fl + 1) * 128], u[:, kc, off:off + n], kc == 0, kc == 7,
                           [ka, ("u", kc, tbi)], [ka_])
                    for kc in range(8):
                        mm(pbk[:, 0:n], wb[:, kc, fl * 128:(fl + 1) * 128], u[:, kc, off:off + n], kc == 0, kc == 7,
                           [kb, ("u", kc, tbi)], [kb_])
                    sb_ = cnt % 2
                    act(stmp[:, sb_, 0:n], pa[:, 0:n], AF.Silu, [ka_], [("stmp", sb_)])
                    tt(gbuf[:, fc, off:off + n], stmp[:, sb_, 0:n], pbk[:, 0:n], ALU.mult,
                       [("stmp", sb_), kb_], [("g", fc, tbi)])
                    cnt += 1
            wrelease(ia)
            wrelease(ib)
        cnt = 0
        for dm in range(8):
            io, wo, ko = wnext(NFC, 128)
            for tbi, (off, n) in enumerate(tbs):
                po = bank(4 + cnt % 2)
                kp = ("ps", 4 + cnt % 2)
                for fc in range(NFC):
                    mm(po[:, 0:n], wo[:, fc, :], gbuf[:, fc, off:off + n], fc == 0, fc == NFC - 1,
                       [ko, ("g", fc, tbi)], [kp])
                stt(h[:, dm, off:off + n], po[:, 0:n], 0.5, h[:, dm, off:off + n], ALU.mult, ALU.add,
                    [kp, ("h", dm, tbi)], [("h", dm, tbi)])
                cnt += 1
                if next_gcol is not None and dm == 7:
                    if tbi == 0:
                        norm_part1(sq8f, 0, tbs[0][0], tbs[0][1])
                    elif tbi == 1:
                        norm_part2(sq8f, 0, tbs[0][0], tbs[0][1], next_gcol)
            wrelease(io)

    def mixing(hi, tok0, tbs, blocks, lazy, next_gcol=None):
        mix_start = st["cur"]

        def bail():
            skip_tiles(mix_start + 15 - st["cur"])

        T = sum(n for _, n in tbs)
        def rtab_load(tbi_, hp_):
            off_, n_ = tbs[tbi_]
            gtb = 4 if n_ == 64 else (tok0 + off_) // 512
            src = rtab_d[gtb].rearrange("p (a b) -> p a b", a=8)[:, hp_ * 4:(hp_ + 1) * 4, 0:n_]
            dma("sp", rtab[:, hp_ * 4:(hp_ + 1) * 4, 0:n_], src, (), [("rtab", hp_)], ("rt", hp_))

        def ttab_load(bi_):
            dma("sp", ttab[:, bi_ % 3, :], ttab_d[blocks[bi_][3]], (), [("ttab", bi_ % 3)], ("tt", bi_ % 3))

        rtab_load(0, 0)
        rtab_load(0, 1)
        for bi_ in range(min(3, len(blocks))):
            ttab_load(bi_)
        ial, wal, kal = wnext(8, 16)
        i0, w0, k0 = wnext(8, 512)
        for tbi, (off, n) in enumerate(tbs):
            if tbi > 0:
                lazy(tbi)
            pA = bank(7)
            for kc in range(8):
                mm(pA[0:16, 0:n], wal[:, kc, 0:16], u[:, kc, off:off + n], kc == 0, kc == 7,
                   [kal, ("u", kc, tbi)], [("ps", 7)])
            S.op("act", (lambda o_, i_: (lambda e: e.copy(out=o_, in_=i_)))(alowT[0:16, off:off + n], pA[0:16, 0:n]),
                 [("ps", 7)], [("alowT", tbi)])
            pend_b = []
            for bi, (boff, bn, smp, gblk) in enumerate(blocks):
                if not (off <= boff < off + n) or cut <= 0.2:
                    continue
                lo = boff - off
                dp = bi % 2
                px = bank(6 + dp)
                kpx = ("ps", 6 + dp)
                mm(px[0:bn, 0:256], alowT[0:16, boff:boff + bn], wup[:, :], True, False,
                   [("alowT", tbi), "wup"], [kpx])
                mm(px[0:bn, 0:256], ones_row[0:1, 0:bn], balpha[0:1, :], False, True,
                   ["ones_row", "balpha"], [kpx])
                act(etmp[0:bn, dp, :], px[0:bn, 0:256], AF.Exp, [kpx], [("etmp", dp)], scale=-1.0)
                act(spb[0:bn, dp, :], etmp[0:bn, dp, :], AF.Ln, [("etmp", dp)], [("spb", dp)], bias=1.0)
                if cut <= 0.4:
                    continue
                S.op("dve", (lambda o_, i_: (lambda e: e.tensor_copy(out=o_, in_=i_)))(sphi[0:bn, dp, :], spb[0:bn, dp, :]),
                     [("spb", dp)], [("sphi", dp)])
                tt(splo[0:bn, dp, :], spb[0:bn, dp, :], sphi[0:bn, dp, :], ALU.subtract,
                   [("spb", dp), ("sphi", dp)], [("splo", dp)])
                def part_b(bi=bi, boff=boff, bn=bn, smp=smp, lo=lo, dp=dp):
                    ci = 2 if smp else 0
                    b5 = 4 + bi % 2
                    p5 = bank(b5)
                    for hp in range(2):
                        for xi, (spx, kx) in enumerate(((sphi, ("sphi", dp)), (splo, ("splo", dp)))):
                            mm(p5[:, hp * 128:hp * 128 + bn], spx[0:bn, dp, hp * 128:(hp + 1) * 128],
                               cmat[0:bn, ci, 0:bn], xi == 0, xi == 1, [kx, "cmat"], [("ps", b5)])
                    for xi, (spx, kx) in enumerate(((sphi, ("sphi", dp)), (splo, ("splo", dp)))):
                        mm(p5[0:bn, 256:512], cmat[0:bn, ci + 1, 0:bn], spx[0:bn, dp, 0:256], xi == 0, xi == 1,
                           [kx, "cmat"], [("ps", b5)])
                    for hp in range(2):
                        src = p5[:, hp * 128:hp * 128 + bn]
                        act(eb[:, hp, lo:lo + bn], src, AF.Exp, [("ps", b5)], [("eb", hp)])
                        act(enb[:, hp, lo:lo + bn], src, AF.Exp, [("ps", b5)], [("enb", hp)], scale=-1.0)
                        if smp:
                            act(decSA[:, hp, :], p5[:, hp * 128 + 3:hp * 128 + 64:4], AF.Exp, [("ps", b5)], ["decSA"])
                        else:
                            act(decA[:, hp, bi:bi + 1], p5[:, hp * 128 + bn - 1:hp * 128 + bn], AF.Exp,
                                [("ps", b5)], [("decA", hp, bi)])
                    act(ed[0:bn, bi, :], p5[0:bn, 256:512], AF.Exp, [("ps", b5)], [("ed", bi)])

                for f_ in pend_b:
                    f_()
                pend_b = [part_b]
            for f_ in pend_b:
                f_()
            pend_b = []
            for ch in range(4):
                if cut <= 0.8:
                    continue
                pq = bank(ch)
                for kc in range(8):
                    mm(pq[:, 0:n], w0[:, kc, ch * 128:(ch + 1) * 128], u[:, kc, off:off + n], kc == 0, kc == 7,
                       [k0, ("u", kc, tbi)], [("ps", ch)])
                if cut <= 0.85:
                    continue
                if ch < 2:
                    stt(qk[:, ch, off:off + n], pq[:, 0:n], 0.125, eb[:, ch, 0:n], ALU.mult, ALU.mult,
                        [("ps", ch), ("eb", ch)], [("qk", 0, tbi, ch)])
                elif cut <= 0.9:
                    continue
                elif cut <= 0.95:
                    tt(qk[:, ch, off:off + n], pq[:, 0:n], eb[:, ch - 2, 0:n], ALU.mult,
                       [("ps", ch), ("eb", ch - 2)], [("qk", 0, tbi, ch)])
                elif cut <= 0.97:
                    stt(qk[:, ch, off:off + n], pq[:, 0:n], 1.0, enb[:, ch - 2, 0:n], ALU.mult, ALU.mult,
                        [("ps", ch), ("enb", ch - 2)], [("qk", 0, tbi, ch)])
                else:
                    tt(qk[:, ch, off:off + n], pq[:, 0:n], enb[:, ch - 2, 0:n], ALU.mult,
                       [("ps", ch), ("enb", ch - 2)], [("qk", 0, tbi, ch)])
        wrelease(ial)
        wrelease(i0)
        if cut <= 1:
            return bail()
        i1, w1, k1 = wnext(8, 512)
        i2, w2, k2 = wnext(8, 512)
        cnt = 0
        for tbi, (off, n) in enumerate(tbs):
            for hp in range(2):
                for isk in range(2):
                    ch = hp + 2 * isk
                    b0 = (cnt % 2) * 2
                    pa, pbk = bank(b0), bank(b0 + 1)
                    for kc in range(8):
                        mm(pa[:, 0:n], w1[:, kc, ch * 128:(ch + 1) * 128], u[:, kc, off:off + n], kc == 0, kc == 7,
                           [k1, ("u", kc, tbi)], [("ps", b0)])
                    for kc in range(8):
                        mm(pbk[:, 0:n], w2[:, kc, ch * 128:(ch + 1) * 128], u[:, kc, off:off + n], kc == 0, kc == 7,
                           [k2, ("u", kc, tbi)], [("ps", b0 + 1)])
                    tt(tA[:, 0:n], pa[:, 0:n], rtab[:, hp * 4 + 2 * isk, 0:n], ALU.mult, [("ps", b0), ("rtab", hp)], ["tA"])
                    tt(tB[:, 0:n], pbk[:, 0:n], rtab[:, hp * 4 + 2 * isk + 1, 0:n], ALU.mult,
                       [("ps", b0 + 1), ("rtab", hp)], ["tB"])
                    qi = 4 + 2 * isk + hp
                    tt(qk[:, qi, off:off + n], tA[:, 0:n], tB[:, 0:n], ALU.add, ["tA", "tB"], [("qk", 1, tbi, qi)])
                    cnt += 1
                if tbi + 1 < len(tbs):
                    rtab_load(tbi + 1, hp)
        wrelease(i1)
        wrelease(i2)
        if cut <= 2:
            return bail()
        i3, w3, k3 = wnext(8, 512)
        for bi, (boff, bn, smp, gblk) in enumerate(blocks):
            tbi = min(boff // 512, len(tbs) - 1)
            tb_ = bi % 3
            pk = bank(bi % 2)
            kp = ("ps", bi % 2)
            for kc in range(8):
                mm(pk[0:bn, :], u[:, kc, boff:boff + bn], w3[:, kc, :], kc == 0, kc == 7, [k3, ("u", kc, tbi)], [kp])
            tt(kh[0:bn, bi, 0:256], pk[0:bn, 0:256], ed[0:bn, bi, :], ALU.mult, [kp, ("ed", bi)], [("kh", bi)])
            pr = pk[0:bn, 256:512].rearrange("p (h s e) -> p h s e", h=4, s=2)
            Ct = ttab[0:bn, tb_, 0:256].rearrange("p (h s e) -> p h s e", h=4, s=2)
            St = ttab[0:bn, tb_, 256:512].rearrange("p (h s e) -> p h s e", h=4, s=2)
            kr = kh[0:bn, bi, 256:512].rearrange("p (h s e) -> p h s e", h=4, s=2)
            tA4 = tA[0:bn, 0:256].rearrange("p (h s e) -> p h s e", h=4, s=2)
            tB4 = tB[0:bn, 0:256].rearrange("p (h s e) -> p h s e", h=4, s=2)
            for s_ in range(2):
                tt(tA4[:, :, s_, :], pr[:, :, s_, :], Ct[:, :, s_, :], ALU.mult, [kp, ("ttab", tb_)], ["tA"])
                tt(tB4[:, :, s_, :], pr[:, :, 1 - s_, :], St[:, :, s_, :], ALU.mult, [kp, ("ttab", tb_)], ["tB"])
                tt(kr[:, :, s_, :], tA4[:, :, s_, :], tB4[:, :, s_, :], ALU.add, ["tA", "tB"], [("kh", bi)])
            if bi + 3 < len(blocks):
                ttab_load(bi + 3)
        wrelease(i3)
        if cut <= 3:
            return bail()
        for vi in range(2):
            iv, wv, kv = wnext(8, 512)
            for bi, (boff, bn, smp, gblk) in enumerate(blocks):
                tbi = min(boff // 512, len(tbs) - 1)
                pv = bank(2 + bi % 2)
                kp = ("ps", 2 + bi % 2)
                for kc in range(8):
                    mm(pv[0:bn, :], u[:, kc, boff:boff + bn], wv[:, kc, :], kc == 0, kc == 7,
                       [kv, ("u", kc, tbi)], [kp])
                S.op("act", (lambda o_, i_: (lambda e: e.copy(out=o_, in_=i_)))(
                    vtm[0:bn, bi, vi * 512:(vi + 1) * 512], pv[0:bn, :]), [kp], [("v", bi, vi)])
            wrelease(iv)
        if cut <= 4:
            return bail()
        S.barrier(exclude=("pe",))
        igt = []
        wg = []
        kg = []
        for br in range(2):
            i_, w_, k_ = wnext(8, 512)
            igt.append(i_); wg.append(w_); kg.append(k_)
        ecnt = [0]

        def epilogue(br, hl, po, off, n, tbi, pokey, ebanks=((2, 3), (0, 1))):
            ob = ecnt[0] % 2
            ecnt[0] += 1
            tbk, rbk = ebanks[ob]
            act(sqb[:, ob, 0:n], po, AF.Square, [pokey], [("sqb", ob)])
            pR = bank(rbk)
            for kc in range(8):
                mm(pR[:, 0:n], wg[br][:, kc, hl * 128:(hl + 1) * 128], u[:, kc, off:off + n], kc == 0, kc == 7,
                   [kg[br], ("u", kc, tbi)], [("ps", rbk)])
            pT_ = bank(tbk)
            mm(pT_[:, 0:n], ones_bf[:], sqb[:, ob, 0:n], True, True, ["ones_bf", ("sqb", ob)], [("ps", tbk)])
            act(rs2[:, ob, 0:n], pT_[:, 0:n], AF.Ln, [("ps", tbk)], [("rs2", ob)], scale=1.0 / 128, bias=EPS)
            act(ee[:, ob, 0:n], pR[:, 0:n], AF.Exp, [("ps", rbk)], [("ee", ob)], scale=-1.0)
            act(ee[:, ob, 0:n], ee[:, ob, 0:n], AF.Ln, [("ee", ob)], [("ee", ob)], bias=1.0)
            stt(tt2[:, ob, 0:n], rs2[:, ob, 0:n], -0.5, ee[:, ob, 0:n], ALU.mult, ALU.subtract,
                [("rs2", ob), ("ee", ob)], [("tt2", ob)])
            act(tt2[:, ob, 0:n], tt2[:, ob, 0:n], AF.Exp, [("tt2", ob)], [("tt2", ob)])
            gc = (G_GNA if br == 0 else G_GNR) + hl
            stt(tt1[:, ob, 0:n], po, gains[:, gc:gc + 1], tt2[:, ob, 0:n], ALU.mult, ALU.mult,
                [pokey, "gains", ("tt2", ob)], [("tt1", ob)])
            qi = br * 4 + hl
            wkeys = [("qk", br, tbi, br * 4 + j) for j in range(4)]
            tt(qk[:, qi, off:off + n], tt1[:, ob, 0:n], pR[:, 0:n], ALU.mult, [("tt1", ob), ("ps", rbk)], wkeys)

        scnt = [0]
        SX = (S0f_buf, tmpS_buf)

        def s0_load(sidx):
            br_, pair_ = sidx // 2, sidx % 2
            src = st_in[br_][:, pair_ * 2:pair_ * 2 + 2].rearrange("s h d v -> (h d) s v")
            sb_ = sidx % 2
            dma("pool", S0bf[:, sb_, :].rearrange("p (s v) -> p s v", s=16), src, (), [("S0bf", sb_)], ("s0b", sb_))
            x_ = SX[sidx % 2]
            kx = ("SX", sidx % 2)
            dma("sp", x_[:].rearrange("p (s v) -> p s v", s=16), src, (), [kx, kx + (0,), kx + (1,)], ("s0f", sidx % 2))

        if hi == 1:
            s0_load(0)
        for tbi, (off, n) in enumerate(tbs):
            smp_tb = (n == 64)
            rkeys_all = lambda br: [("qk", br, tbi, br * 4 + j) for j in range(4)]
            for br in range(2):
                qb, kb_i = br * 4, br * 4 + 2
                if not smp_tb:
                    tblocks = [(bi, b) for bi, b in enumerate(blocks) if off <= b[0] < off + n]
                    pend = []

                    def o_ops(c4, bi, boff, bn, gblk):
                        par = c4 % 2
                        for hl in range(4):
                            pair, h2 = hl // 2, hl % 2
                            sidx = br * 2 + pair
                            pr_ = slice(h2 * 64, h2 * 64 + 64)
                            po = bank(4 + hl)[:, c4 * 128:(c4 + 1) * 128]
                            vc = br * 512 + hl * 128
                            mm(po, vtm[:, bi, vc:vc + 128], PT[:, par * 4 + h2 * 2 + pair, :], True, False,
                               [("v", bi, br), ("PT", par)], [("ps", 4 + hl)])
                            mm(po, Sbf[pr_, (gblk - 1) % 3, sidx, :], qk[pr_, qb + pair, boff:boff + bn], False, True,
                               [("Sb", (gblk - 1) % 3, br)] + rkeys_all(br), [("ps", 4 + hl)])

                    for c4, (bi, (boff, bn, smp, gblk)) in enumerate(tblocks):
                        par = c4 % 2
                        ubk = (3, 1)[par]
                        pu = bank(ubk)
                        for pair in range(2):
                            kc0 = br * 256 + pair * 128
                            vc0 = br * 512 + pair * 256
                            mm(pu[:, pair * 256:(pair + 1) * 256], kh[:, bi, kc0:kc0 + 128], vtm[:, bi, vc0:vc0 + 256],
                               True, True, [("kh", bi), ("v", bi, br)], [("ps", ubk)])
                        for hl in (0, 2, 1, 3):
                            pair, h2 = hl // 2, hl % 2
                            pr_ = slice(h2 * 64, h2 * 64 + 64)
                            sb2 = (0, 2)[h2]
                            mm(bank(sb2)[:, pair * 128:(pair + 1) * 128], qk[pr_, kb_i + pair, boff:boff + bn],
                               qk[pr_, qb + pair, boff:boff + bn], True, True, rkeys_all(br), [("ps", sb2)])
                        for f_ in pend:
                            f_()
                        pend = []
                        for h2 in range(2):
                            sb2 = (0, 2)[h2]
                            tt(PT[:, par * 4 + h2 * 2:par * 4 + h2 * 2 + 2, :],
                               bank(sb2)[:, 0:256].rearrange("p (a b) -> p a b", a=2),
                               masks[:, 0, :].unsqueeze(1).to_broadcast([128, 2, 128]), ALU.mult,
                               [("ps", sb2), "masks"], [("PT", par)])
                        for pair in range(2):
                            sidx = br * 2 + pair
                            for h2 in range(2):
                                pr_ = slice(h2 * 64, h2 * 64 + 64)
                                if br == 0:
                                    dsc = decA[pr_, pair, bi:bi + 1]
                                    dk_ = [("decA", pair, bi)]
                                else:
                                    dsc = gains[pr_, G_DECR + pair:G_DECR + pair + 1]
                                    dk_ = ["gains"]
                                stt(Sst[pr_, sidx, :], Sst[pr_, sidx, :], dsc,
                                    pu[pr_, pair * 256 + h2 * 128:pair * 256 + (h2 + 1) * 128],
                                    ALU.mult, ALU.add, ["S%d" % sidx, ("ps", ubk)] + dk_, ["S%d" % sidx])
                        S.op("act", (lambda o_, i_: (lambda e: e.copy(out=o_, in_=i_)))(
                            Sbf[:, gblk % 3, br * 2:br * 2 + 2, :], Sst[:, br * 2:br * 2 + 2, :]),
                            ["S%d" % (br * 2), "S%d" % (br * 2 + 1)], [("Sb", gblk % 3, br)])
                        pend.append((lambda a, b, c, d, e_: (lambda: o_ops(a, b, c, d, e_)))(c4, bi, boff, bn, gblk))
                    for f_ in pend:
                        f_()
                    for hl in range(4):
                        eb_ = ((2, 3), (0, 1)) if hl < 2 else ((2, 4), (0, 5))
                        epilogue(br, hl, bank(4 + hl)[:, 0:n], off, n, tbi, ("ps", 4 + hl), eb_)
                else:
                    bi = len(blocks) - 1
                    boff, bn, smp, gblk = blocks[bi]
                    for pair in range(2):
                        sidx = br * 2 + pair
                        sb_ = sidx % 2
                        S0f, tmpS = (SX[sidx % 2], SX[1 - sidx % 2])
                        kS0f, kTmp = ("SX", sidx % 2), ("SX", 1 - sidx % 2)
                        S0b3 = S0bf[:, sb_, :].rearrange("p (s v) -> p s v", s=16)
                        S0f3 = S0f[:].rearrange("p (s v) -> p s v", s=16)
                        tmp3 = tmpS[:].rearrange("p (s v) -> p s v", s=16)
                        for h2 in range(2):
                            hl = pair * 2 + h2
                            pr_ = slice(h2 * 64, h2 * 64 + 64)
                            slot = scnt[0] % 4
                            scnt[0] += 1
                            sbk = (0, 2)[slot % 2]
                            psS = bank(sbk)[0:64, 0:64]
                            mm(psS, qk[pr_, kb_i + pair, boff:boff + 64], qk[pr_, qb + pair, boff:boff + 64],
                               True, True, rkeys_all(br), [("ps", sbk)])
                            pts = slot + 4 * br
                            tt(PT[0:64, pts, 0:64], psS, masks[0:64, 1, 0:64], ALU.mult,
                               [("ps", sbk), "masks"], [("PT", pts // 4)])
                            if cut <= 4.93:
                                continue
                            po = bank(1)[:, hl * 64:(hl + 1) * 64]
                            vc = br * 512 + hl * 128
                            mm(po, vtm[0:64, bi, vc:vc + 128], PT[0:64, pts, 0:64], True, cut <= 4.935,
                               [("v", bi, br), ("PT", pts // 4)], [("ps", 1)])
                            po_ = slice((1 - h2) * 64, (1 - h2) * 64 + 64)
                            S.op("pool", (lambda a_: (lambda e: e.memset(a_, 0.0)))(qm[po_].rearrange("p a b -> p (a b)")),
                                 (), ["qm"])
                            tt(qm[pr_], qk[pr_, qb + pair, boff:boff + 64].unsqueeze(1).to_broadcast([64, 16, 64]),
                               maskq[pr_], ALU.mult, rkeys_all(br) + ["maskq"], ["qm"])
                            for s_ in range(16):
                                if cut <= 4.935:
                                    continue
                                mm(po, S0b3[:, s_, :], qm[:, s_, :], False, s_ == 15,
                                   [("S0bf", sb_), "qm"], [("ps", 1)])
                            if cut <= 4.94:
                                continue
                            V3 = Vblk[:].rearrange("p (s v) -> p s v", s=16)
                            tt(V3, vtm[0:64, bi, vc:vc + 128].unsqueeze(1).to_broadcast([64, 16, 128]),
                               smask[:, :].unsqueeze(2).to_broadcast([64, 16, 128]), ALU.mult,
                               [("v", bi, br), "smask"], ["Vblk"], eng="pool")
                            if cut <= 4.95:
                                continue
                            kc0 = br * 256 + pair * 128
                            for q4 in range(4):
                                mm(PSB[:, q4 * 512:(q4 + 1) * 512], kh[0:64, bi, kc0:kc0 + 128],
                                   Vblk[:, q4 * 512:(q4 + 1) * 512], True, True, [("kh", bi), "Vblk"], [("ps", 4 + q4)])
                            if h2 == 0:
                                if br == 0:
                                    tt(tmp3, S0f3, decSA[:, pair, :].unsqueeze(2).to_broadcast([128, 16, 128]),
                                       ALU.mult, [kS0f, "decSA"], [kTmp])
                                else:
                                    tsmul(tmpS[:, :], S0f[:, :], gains[:, G_DECR + 2 + pair:G_DECR + 3 + pair],
                                          [kS0f, "gains"], [kTmp])
                            tt(S0f[pr_, :], tmpS[pr_, :], PSB[pr_, :], ALU.add,
                               [kTmp] + [("ps", 4 + q) for q in range(4)], [kS0f + (h2,)])
                        if sidx < 3:
                            s0_load(sidx + 1)
                        dst = st_out[br][:, pair * 2:pair * 2 + 2].rearrange("s h d v -> (h d) s v")
                        dma("sp", dst, S0f3, [kS0f, kS0f + (0,), kS0f + (1,)], (), ("s0o", sidx % 2))
                    for hl in range(4):
                        if cut <= 4.98:
                            continue
                        epilogue(br, hl, bank(1)[:, hl * 64:(hl + 1) * 64], off, n, tbi, ("ps", 1), ((2, 3), (0, 5)))
        if hi == 1:
            for br in range(2):
                for pair in range(2):
                    sidx = br * 2 + pair
                    dma("sp", sp_out[br][pair * 128:(pair + 1) * 128, :], Sst[:, sidx, :], ["S%d" % sidx], (), "spo")
        wrelease(igt[0])
        wrelease(igt[1])
        if cut <= 5:
            return bail()
        S.barrier(exclude=("pe",))
        cnt = 0
        for dmg in range(2):
            iga, wga, kga = wnext(8, 512)
            igr, wgr, kgr = wnext(8, 512)
            iwo, wwo, kwo = wnext(8, 512)
            for tbi, (off, n) in enumerate(tbs):
                okeys = lambda br: [("qk", br, tbi, br * 4 + j) for j in range(4)]
                for dl in range(4):
                    dm = dmg * 4 + dl
                    b0 = (cnt % 2) * 4
                    cnt += 1
                    pga, pgr, pma, pmr = bank(b0), bank(b0 + 1), bank(b0 + 2), bank(b0 + 3)
                    cs = slice(dl * 128, (dl + 1) * 128)
                    for kc in range(8):
                        mm(pga[:, 0:n], wga[:, kc, cs], u[:, kc, off:off + n], kc == 0, kc == 7,
                           [kga, ("u", kc, tbi)], [("ps", b0)])
                    for kc in range(8):
                        mm(pgr[:, 0:n], wgr[:, kc, cs], u[:, kc, off:off + n], kc == 0, kc == 7,
                           [kgr, ("u", kc, tbi)], [("ps", b0 + 1)])
                    for fc in range(4):
                        mm(pma[:, 0:n], wwo[:, fc, cs], qk[:, fc, off:off + n], fc == 0, fc == 3,
                           [kwo] + okeys(0), [("ps", b0 + 2)])
                    for fc in range(4, 8):
                        mm(pmr[:, 0:n], wwo[:, fc, cs], qk[:, fc, off:off + n], fc == 4, fc == 7,
                           [kwo] + okeys(1), [("ps", b0 + 3)])
                    act(tA[:, 0:n], pga[:, 0:n], AF.Tanh, [("ps", b0)], ["tA"], scale=0.5)
                    act(tB[:, 0:n], pgr[:, 0:n], AF.Tanh, [("ps", b0 + 1)], ["tB"], scale=0.5)
                    stt(tC[:, 0:n], tA[:, 0:n], 1.0, pma[:, 0:n], ALU.add, ALU.mult, ["tA", ("ps", b0 + 2)], ["tC"])
                    stt(tD[:, 0:n], tB[:, 0:n], 1.0, pmr[:, 0:n], ALU.add, ALU.mult, ["tB", ("ps", b0 + 3)], ["tD"])
                    tt(tA[:, 0:n], tC[:, 0:n], tD[:, 0:n], ALU.add, ["tC", "tD"], ["tA"])
                    stt(h[:, dm, off:off + n], tA[:, 0:n], 0.5, h[:, dm, off:off + n], ALU.mult, ALU.add,
                        ["tA", ("h", dm, tbi)], [("h", dm, tbi)])
                    if next_gcol is not None and dmg == 1:
                        if tbi == 0 and dl == 3:
                            norm_part1(sq8m, 0, tbs[0][0], tbs[0][1])
                        elif tbi == 1 and dl == 1:
                            norm_part2(sq8m, 0, tbs[0][0], tbs[0][1], next_gcol)
            wrelease(iga)
            wrelease(igr)
            wrelease(iwo)

    def ple(tok0, tbs, lazy):
        cnt = 0
        for dmg in range(2):
            ig, wg_, kg_ = wnext(8, 512)
            ip, wp_, kp_ = wnext(2, 512)
            for tbi, (off, n) in enumerate(tbs):
                if dmg == 0 and tbi > 0:
                    lazy(tbi)
                for dl in range(4):
                    dm = dmg * 4 + dl
                    b0 = (cnt % 2) * 2
                    cnt += 1
                    pg, pp = bank(b0), bank(b0 + 1)
                    cs = slice(dl * 128, (dl + 1) * 128)
                    for kc in range(8):
                        mm(pg[:, 0:n], wg_[:, kc, cs], u[:, kc, off:off + n], kc == 0, kc == 7,
                           [kg_, ("u", kc, tbi)], [("ps", b0)])
                    for kc in range(2):
                        mm(pp[:, 0:n], wp_[:, kc, cs], pb[:, kc, off:off + n], kc == 0, kc == 1,
                           [kp_, "pb"], [("ps", b0 + 1)])
                    sb_ = cnt % 2
                    act(stmp[:, sb_, 0:n], pg[:, 0:n], AF.Tanh, [("ps", b0)], [("stmp", sb_)], scale=0.5)
                    stt(stmp2[:, sb_, 0:n], stmp[:, sb_, 0:n], 1.0, pp[:, 0:n], ALU.add, ALU.mult,
                        [("stmp", sb_), ("ps", b0 + 1)], [("stmp2", sb_)])
                    stt(h[:, dm, off:off + n], stmp2[:, sb_, 0:n], 0.5, h[:, dm, off:off + n], ALU.mult, ALU.add,
                        [("stmp2", sb_), ("h", dm, tbi)], [("h", dm, tbi)])
            wrelease(ig)
            wrelease(ip)

    def skip_tiles(k):
        for _ in range(k):
            i = st["cur"]
            st["cur"] += 1
            wrelease(i)

    yv = yT.rearrange("(k p) t -> p k t", p=128)
    xv = xT.rearrange("(k p) t -> p k t", p=128)
    halves = [(0, [(0, 512), (512, 512)]), (1024, [(0, 512), (512, 512), (1024, 64)])]
    for hi, (tok0, tbs) in enumerate(halves[:nhalves]):
        T = sum(n for _, n in tbs)
        blocks = [(b * 128, 128, False, (tok0 + b * 128) // 128) for b in range(8)]
        if hi == 1:
            blocks.append((1024, 64, True, 16))
        def xload(tok0_, tbi, off, n, after=()):
            dma("sp", h[:, :, off:off + n], xv[:, :, tok0_ + off:tok0_ + off + n], after,
                [("h", kc, tbi) for kc in range(8)], ("x", tbi))

        if hi == 0:
            for tbi, (off, n) in enumerate(tbs):
                xload(tok0, tbi, off, n, after=([("h", 0, 0)] if tbi > 0 else ()))
            for i in range(2, RING):
                issue(i, after=[("h", 0, 0)])
            late_consts()

        def mk_lazy(gcol):
            return lambda tbi: norm_u_tb(tbi, tbs[tbi][0], tbs[tbi][1], gcol)

        norm_u_tb(0, tbs[0][0], tbs[0][1], G_FFN1)
        ffn(tbs, mk_lazy(G_FFN1), G_MIX if (stage >= 2 and cut >= 99) else None)
        if stage >= 2:
            if cut < 99:
                norm_u_tb(0, tbs[0][0], tbs[0][1], G_MIX)
            S.barrier(exclude=("pe",))
            mixing(hi, tok0, tbs, blocks, mk_lazy(G_MIX), G_FFN2 if stage >= 3 else None)
        else:
            skip_tiles(15)
        if stage >= 3:
            if cut < 99:
                norm_u_tb(0, tbs[0][0], tbs[0][1], G_FFN2)
            S.barrier(exclude=("pe",))
            if stage >= 4:
                T_ = sum(n for _, n in tbs)
                dma("pool", pb[:, :, 0:T_], pT.rearrange("(k p) t -> p k t", p=128)[:, :, tok0:tok0 + T_], (),
                    ["pb"], "pb")
            ffn(tbs, mk_lazy(G_FFN2), G_PLE if stage >= 4 else None)
        else:
            S.barrier(exclude=("pe",))
            skip_tiles(20)
        if stage >= 4:
            ple(tok0, tbs, mk_lazy(G_PLE))
        else:
            skip_tiles(4)
        if stage >= 5:
            for tbi, (off, n) in enumerate(tbs):
                pn = bank(6)
                for kc in range(8):
                    b = kc % 2
                    act(sq[:, b, 0:n], h[:, kc, off:off + n], AF.Square, [("h", kc, tbi)], [("sq", b)])
                    mm(pn[:, 0:n], ones_bf[:], sq[:, b, 0:n], kc == 0, kc == 7, ["ones_bf", ("sq", b)], [("ps", 6)])
                rb = tbi % 2
                act(rstd[:, rb, 0:n], pn[:, 0:n], AF.Ln, [("ps", 6)], [("rstd", rb)], scale=1.0 / D, bias=EPS)
                act(rstd[:, rb, 0:n], rstd[:, rb, 0:n], AF.Exp, [("rstd", rb)], [("rstd", rb)], scale=-0.5)
                for kc in range(8):
                    stt(yout[:, rb, kc, 0:n], h[:, kc, off:off + n], gains[:, G_FIN + kc:G_FIN + kc + 1],
                        rstd[:, rb, 0:n], ALU.mult, ALU.mult,
                        [("h", kc, tbi), "gains", ("rstd", rb)], [("yout", rb)])
                dma("sp", yv[:, :, tok0 + off:tok0 + off + n], yout[:, rb, :, 0:n], [("yout", rb)], (), ("yo", rb))
                if stage >= 5 and hi + 1 < len(halves[:nhalves]):
                    ntok0, ntbs = halves[hi + 1]
                    xload(ntok0, tbi, ntbs[tbi][0], ntbs[tbi][1])
                    if tbi == len(tbs) - 1:
                        for t2 in range(len(tbs), len(ntbs)):
                            xload(ntok0, t2, ntbs[t2][0], ntbs[t2][1])
        else:
            for tbi, (off, n) in enumerate(tbs):
                dma("sp", yv[:, :, tok0 + off:tok0 + off + n], h[:, :, off:off + n],
                    [("h", kc, tbi) for kc in range(8)], (), "yo")
            if hi + 1 < len(halves[:nhalves]):
                ntok0, ntbs = halves[hi + 1]
                for t2 in range(len(ntbs)):
                    xload(ntok0, t2, ntbs[t2][0], ntbs[t2][1])
    S.emit(nc)
    return nc


_CONST_CACHE = {}


def _prep_inputs(inp):
    f32 = np.float32
    inp = {k: np.asarray(v) for k, v in inp.items()}
    wflat, sched = _weight_tiles(inp)
    ct = _CONST_CACHE.get("ct")
    if ct is None:
        ct = _const_tables()
        _CONST_CACHE["ct"] = ct

    def col8(v):
        return np.asarray(v, f32).reshape(-1, 128).T

    cpack = np.zeros((128, 52), f32)
    cpack[:, 0:8] = col8(inp["norm_ffn1"][0])
    cpack[:, 8:16] = col8(inp["norm_mix"][0])
    cpack[:, 16:24] = col8(inp["norm_ffn2"][0])
    cpack[:, 24:32] = col8(inp["norm_ple"][0])
    cpack[:, 32:40] = col8(inp["norm_final"])
    cpack[:, 40:44] = col8(inp["gn_gla"][0])
    cpack[:, 44:48] = col8(inp["gn_ret"][0])
    cpack[:, 48:52] = ct["decr"]
    xp, xs = inp["x_prompt"], inp["x_sample"]
    pp, psm = inp["p_prompt"][0], inp["p_sample"][0]
    in_maps = []
    for c in range(NCORES):
        xc = np.concatenate([xp[c], xs[16 * c:16 * c + 16].reshape(NSAMP, D)], axis=0)
        pc = np.concatenate([pp[c], psm[16 * c:16 * c + 16].reshape(NSAMP, 256)], axis=0)
        in_maps.append({
            "xT": np.ascontiguousarray(xc.T.astype(f32)),
            "pT": np.ascontiguousarray(pc.T.astype(f32)),
            "wflat": wflat,
            "cpack": cpack,
            "cmat": ct["cmat"], "masks": ct["masks"], "smask": ct["smask"], "maskq": ct["maskq"],
            "wup": np.ascontiguousarray(inp["w_alpha_up"][0].astype(f32)),
            "balpha": np.ascontiguousarray(inp["b_alpha"][0].reshape(1, 256).astype(f32)),
            "rtab": ct["rtab"], "ttab": ct["ttab"],
            "sg_in": np.ascontiguousarray(inp["state_gla"][0, 16 * c:16 * c + 16].astype(f32)),
            "sr_in": np.ascontiguousarray(inp["state_ret"][0, 16 * c:16 * c + 16].astype(f32)),
        })
    return in_maps


def _run(inp, stage=5):
    in_maps = _prep_inputs(inp)
    nc = build_program(stage)
    res = run_bass_kernel_spmd(nc, in_maps, core_ids=list(range(NCORES)))
    return res.results


def kernel(**inputs):
    results = _run(inputs, 5)
    f32 = np.float32
    y_prompt = np.zeros((8, TP, D), f32)
    y_sample = np.zeros((128, 4, D), f32)
    gp = np.zeros((1, 8, 4, 64, 128), f32); rp = np.zeros((1, 8, 4, 64, 128), f32)
    gs = np.zeros((1, 128, 4, 64, 128), f32); rs = np.zeros((1, 128, 4, 64, 128), f32)
    for c, r in enumerate(results):
        yc = np.asarray(r["yT"]).T
        y_prompt[c] = yc[:TP]
        y_sample[16 * c:16 * c + 16] = yc[TP:].reshape(16, 4, D)
        gp[0, c] = np.asarray(r["sgp"]).reshape(4, 64, 128)
        rp[0, c] = np.asarray(r["srp"]).reshape(4, 64, 128)
        gs[0, 16 * c:16 * c + 16] = np.asarray(r["sgs"])
        rs[0, 16 * c:16 * c + 16] = np.asarray(r["srs"])
    return (y_prompt, y_sample, gp, rp, gs, rs)
```

```python
import numpy as np
import ml_dtypes
import concourse.bass as bass
import concourse.mybir as mybir
from concourse.bass_utils import run_bass_kernel_spmd
from contextlib import ExitStack

F32 = mybir.dt.float32
BF16 = mybir.dt.bfloat16
AF = mybir.ActivationFunctionType
ALU = mybir.AluOpType

NCORES = 8
D = 1024
DFF = 2816
NFC = 22
TP = 2048
NSAMP = 64
TALL = TP + NSAMP
EPS = 1e-6
RING = 5
SLOT = 4096

ENGS = ("pe", "act", "dve", "pool", "sp")


class Op:
    __slots__ = ("eng", "fn", "reads", "writes", "dsem", "chan", "deps", "signaled", "count", "idx")

    def __init__(self, eng, fn, reads, writes, dsem):
        self.eng = eng
        self.fn = fn
        self.reads = tuple(reads)
        self.writes = tuple(writes)
        self.dsem = dsem
        self.chan = ("dma", dsem) if dsem is not None else ("eng", eng)
        self.deps = []
        self.signaled = dsem is not None
        self.count = 0


class Sched:
    def __init__(self):
        self.ops = []
        self.dma_sems = []
        self.barriers = []

    def op(self, eng, fn, reads=(), writes=(), dsem=None):
        o = Op(eng, fn, reads, writes, dsem)
        o.idx = len(self.ops)
        self.ops.append(o)
        if dsem is not None and dsem not in self.dma_sems:
            self.dma_sems.append(dsem)
        return o

    def barrier(self, exclude=()):
        self.barriers.append((len(self.ops), tuple(exclude)))

    def analyze(self):
        last_w, last_r = {}, {}
        waited = {e: {} for e in ENGS}
        chan_last = {}
        pending = {e: None for e in ENGS}
        bars = list(self.barriers)
        bi = 0
        ops = self.ops
        for o in ops:
            while bi < len(bars) and bars[bi][0] <= o.idx:
                snap = {ch: i for ch, i in chan_last.items()
                        if not (ch[0] == "dma" and isinstance(ch[1], tuple) and ch[1][0] == "w")}
                for e in ENGS:
                    if e in bars[bi][1]:
                        continue
                    if pending[e] is None:
                        pending[e] = snap
                    else:
                        m = dict(pending[e])
                        m.update(snap)
                        pending[e] = m
                bi += 1
            deps = {}

            def add(d):
                for ch, i in d.items():
                    if deps.get(ch, -1) < i:
                        deps[ch] = i

            if pending[o.eng] is not None:
                add(pending[o.eng])
                pending[o.eng] = None
            for k in o.reads:
                add(last_w.get(k, {}))
            for k in o.writes:
                add(last_w.get(k, {}))
                add(last_r.get(k, {}))
            w = waited[o.eng]
            for ch, i in deps.items():
                if ch == ("eng", "pe") and o.eng == "pe":
                    continue
                if w.get(ch, -1) >= i:
                    continue
                w[ch] = i
                o.deps.append(i)
                ops[i].signaled = True
            for k in o.reads:
                last_r.setdefault(k, {})[o.chan] = o.idx
            for k in o.writes:
                last_w[k] = {o.chan: o.idx}
                last_r[k] = {}
            chan_last[o.chan] = o.idx
        cnt = {}
        for o in ops:
            if o.signaled:
                inc = 16 if o.dsem is not None else 1
                cnt[o.chan] = cnt.get(o.chan, 0) + inc
                o.count = cnt[o.chan]
        self.final_counts = cnt

    def emit(self, nc, final_wait_eng="sp"):
        self.analyze()
        with ExitStack() as es:
            sems = {}
            for e in ENGS:
                sems[("eng", e)] = es.enter_context(nc.semaphore("sem_" + e))
            for i, d in enumerate(self.dma_sems):
                sems[("dma", d)] = es.enter_context(nc.semaphore("dsem_%d" % i))
            block = es.enter_context(nc.Block())
            ops = self.ops

            def run(engname):
                def body(eng):
                    for o in ops:
                        if o.eng != engname:
                            continue
                        for i in o.deps:
                            d = ops[i]
                            eng.wait_ge(sems[d.chan], d.count)
                        ins = o.fn(eng)
                        if o.signaled:
                            ins.then_inc(sems[o.chan], 16 if o.dsem is not None else 1)
                    if engname == final_wait_eng:
                        for ch, c in self.final_counts.items():
                            eng.wait_ge(sems[ch], c)
                return body

            block.tensor(run("pe"))
            block.scalar(run("act"))
            block.vector(run("dve"))
            block.gpsimd(run("pool"))
            block.sync(run("sp"))


def _tile_kc(W, cols):
    nk = W.shape[0] // 128
    sub = W[:, cols]
    return np.ascontiguousarray(sub.reshape(nk, 128, sub.shape[1]).transpose(1, 0, 2).reshape(128, -1))


def _swap_heads(idx):
    idx = np.asarray(idx).reshape(-1, 2, 32)
    return idx[:, ::-1, :].reshape(-1)


_QA, _KA, _VA, _RA = np.arange(0, 256), np.arange(256, 512), np.arange(512, 1024), np.arange(1024, 1536)
_QR, _KR, _VR, _GR = np.arange(1536, 1792), np.arange(1792, 2048), np.arange(2048, 2560), np.arange(2560, 3072)
_AL = np.arange(3072, 3088)
_GA, _GRT = np.arange(3088, 4112), np.arange(4112, 5136)


def _weight_tiles(inp):
    tiles = []

    def ffn(w_in, w_out):
        for g in range(6):
            wd = 512 if g < 5 else 256
            tiles.append(_tile_kc(w_in, np.arange(g * 512, g * 512 + wd)))
            tiles.append(_tile_kc(w_in, np.arange(DFF + g * 512, DFF + g * 512 + wd)))
        for dm in range(8):
            tiles.append(_tile_kc(w_out, np.arange(dm * 128, dm * 128 + 128)))

    ffn(inp["w_ffn1_in"][0], inp["w_ffn1_out"][0])
    wi = inp["w_in"][0]
    tiles.append(_tile_kc(wi, _AL))
    tiles.append(_tile_kc(wi, np.concatenate([_QA, _KA])))
    tiles.append(_tile_kc(wi, np.concatenate([_QR, _KR])))
    tiles.append(_tile_kc(wi, np.concatenate([_swap_heads(_QR), _swap_heads(_KR)])))
    tiles.append(_tile_kc(wi, np.concatenate([_KA, _KR])))
    tiles.append(_tile_kc(wi, _VA))
    tiles.append(_tile_kc(wi, _VR))
    tiles.append(_tile_kc(wi, _RA))
    tiles.append(_tile_kc(wi, _GR))
    wo = inp["w_out"][0]
    for dmg in range(2):
        c = np.arange(dmg * 512, dmg * 512 + 512)
        tiles.append(_tile_kc(wi, _GA[c]))
        tiles.append(_tile_kc(wi, _GRT[c]))
        tiles.append(_tile_kc(wo, c))
    ffn(inp["w_ffn2_in"][0], inp["w_ffn2_out"][0])
    for dmg in range(2):
        c = np.arange(dmg * 512, dmg * 512 + 512)
        tiles.append(_tile_kc(inp["w_ple_gate"][0], c))
        tiles.append(_tile_kc(inp["w_ple_proj"][0], c))
    sched = []
    off = 0
    for t in tiles:
        assert t.shape[1] <= SLOT
        sched.append((off, t.shape[1]))
        off += t.shape[1]
    return np.ascontiguousarray(np.concatenate(tiles, axis=1).astype(np.float32)), sched


def _weight_schedule():
    Ls = []

    def ffn():
        for g in range(6):
            wd = 512 if g < 5 else 256
            Ls.extend([8 * wd, 8 * wd])
        Ls.extend([NFC * 128] * 8)

    ffn()
    Ls.append(8 * 16)
    Ls.extend([8 * 512] * 8)
    Ls.extend([8 * 512] * 6)
    ffn()
    Ls.extend([8 * 512, 2 * 512] * 2)
    sched, off = [], 0
    for L in Ls:
        sched.append((off, L))
        off += L
    return sched, off


def _const_tables():
    f32 = np.float32
    half = 32
    freq = (10000.0 ** (-(np.arange(half, dtype=np.float64) / float(half)))).astype(f32)
    pos = np.concatenate([np.arange(TP), 16384 + (np.arange(NSAMP) % 4)]).astype(f32)
    il = np.concatenate([np.arange(TP) % 128, np.arange(NSAMP) % 4]).astype(np.float64)
    nchunk = np.concatenate([np.full(TP, 128.0), np.full(NSAMP, 4.0)])
    ang = (pos[:, None] * freq[None, :]).astype(f32).astype(np.float64)
    cos, sin = np.cos(ang), np.sin(ang)
    lg = np.log1p(-np.exp2(-5.0 - np.arange(4, dtype=np.float64)))
    d = np.arange(64)
    sgn = np.where(d < 32, -1.0, 1.0)
    CQ = np.zeros((2, 128, TALL)); SQ = np.zeros_like(CQ); CK = np.zeros_like(CQ); SK = np.zeros_like(CQ)
    for hp in range(2):
        for h2 in range(2):
            h = hp * 2 + h2
            up = np.exp((il + 1.0) * lg[h])[None, :]
            dn = np.exp(-(il + 1.0) * lg[h])[None, :] * 0.125
            c = cos[:, d % 32].T
            s = sin[:, d % 32].T * sgn[:, None]
            sl = slice(h2 * 64, h2 * 64 + 64)
            CQ[hp, sl] = c * up; SQ[hp, sl] = s * up; CK[hp, sl] = c * dn; SK[hp, sl] = s * dn
    rtab = np.zeros((5, 128, 8, 512), f32)
    for gtb in range(5):
        t0, n = (gtb * 512, 512) if gtb < 4 else (TP, NSAMP)
        for hp in range(2):
            for j, A in enumerate((CQ, SQ, CK, SK)):
                rtab[gtb, :, hp * 4 + j, :n] = A[hp, :, t0:t0 + n]
    ttab = np.zeros((17, 128, 2, 256), f32)
    for b in range(17):
        t0, n = (b * 128, 128) if b < 16 else (TP, NSAMP)
        for h in range(4):
            k = np.exp((nchunk[t0:t0 + n] - 1.0 - il[t0:t0 + n]) * lg[h])[:, None] * 0.125
            ttab[b, :n, 0, h * 64:(h + 1) * 64] = cos[t0:t0 + n][:, d % 32] * k
            ttab[b, :n, 1, h * 64:(h + 1) * 64] = sin[t0:t0 + n][:, d % 32] * sgn[None, :] * k
    decr = np.zeros((128, 4), f32)
    for hp in range(2):
        for h2 in range(2):
            h = hp * 2 + h2
            decr[h2 * 64:(h2 + 1) * 64, hp] = np.exp(128.0 * lg[h])
            decr[h2 * 64:(h2 + 1) * 64, 2 + hp] = np.exp(4.0 * lg[h])
    j = np.arange(128)[:, None]; i = np.arange(128)[None, :]
    cmat = np.zeros((128, 4, 128), f32)
    cmat[:, 0] = np.where(j <= i, -1.0 / 16, 0.0)
    cmat[:, 1] = np.where(j > i, -1.0 / 16, 0.0)
    same = (j // 4 == i // 4) & (j < 64) & (i < 64)
    cmat[:, 2] = np.where(same & (j <= i), -1.0 / 16, 0.0)
    cmat[:, 3] = np.where(same & (j > i), -1.0 / 16, 0.0)
    masks = np.zeros((128, 2, 128), f32)
    masks[:, 0] = (j <= i)
    masks[:, 1] = same & (j <= i)
    smask = (np.arange(64)[:, None] // 4 == np.arange(16)[None, :]).astype(f32)
    maskq = np.broadcast_to((np.arange(16)[:, None] == np.arange(64)[None, :] // 4).astype(f32).reshape(1, 1024),
                            (128, 1024)).copy()
    return dict(rtab=rtab.reshape(5, 128, 8 * 512), ttab=ttab.reshape(17, 128, 512), decr=decr,
                cmat=cmat.reshape(128, 512), masks=masks.reshape(128, 256), smask=smask, maskq=maskq)


def build_program(stage=5, nhalves=2, cut=99):
    nc = bass.Bass("TRN2", target_bir_lowering=False)
    S = Sched()
    wsched1, WTOT = _weight_schedule()

    def din(name, shape, dt=F32):
        return nc.dram_tensor(name, list(shape), dt, kind="ExternalInput").ap()

    def dout(name, shape, dt=F32):
        return nc.dram_tensor(name, list(shape), dt, kind="ExternalOutput").ap()

    xT = din("xT", [D, TALL]); pT = din("pT", [256, TALL])
    wflat = din("wflat", [128, WTOT])
    cpack = din("cpack", [128, 52])
    cmat_d = din("cmat", [128, 512]); masks_d = din("masks", [128, 256]); smask_d = din("smask", [64, 16])
    maskq_d = din("maskq", [128, 1024])
    wup_d = din("wup", [16, 256]); balpha_d = din("balpha", [1, 256])
    rtab_d = din("rtab", [5, 128, 8 * 512]); ttab_d = din("ttab", [17, 128, 512])
    sg_in = din("sg_in", [16, 4, 64, 128]); sr_in = din("sr_in", [16, 4, 64, 128])
    yT = dout("yT", [D, TALL])
    sgp = dout("sgp", [256, 128]); srp = dout("srp", [256, 128])
    sgs = dout("sgs", [16, 4, 64, 128]); srs = dout("srs", [16, 4, 64, 128])
    st_in = (sg_in, sr_in); st_out = (sgs, srs); sp_out = (sgp, srp)

    SB_BASE, SB_END = 16512, 229376
    ptr = [SB_BASE]

    def esz(dt):
        return 2 if dt == BF16 else 4

    def palloc(name, shape, dt):
        nbytes = int(np.prod(shape[1:])) * esz(dt)
        nbytes = (nbytes + 31) // 32 * 32
        t = nc.alloc_sbuf_tensor_at(name, list(shape), dt, offset=ptr[0])
        ptr[0] += nbytes
        return t

    TM = 1088
    h = palloc("h", [128, 8, TM], F32)
    u = palloc("u", [128, 8, TM], BF16)
    ring = [palloc("ring%d" % i, [128, SLOT], BF16) for i in range(RING)]
    gains = palloc("gains", [128, 52], F32)
    cmat = palloc("cmat_s", [128, 4, 128], BF16)
    masks = palloc("masks_s", [128, 2, 128], F32)
    smask = palloc("smask_s", [64, 16], F32)
    maskq = palloc("maskq_s", [128, 16, 64], BF16)
    wup = palloc("wup_s", [16, 256], BF16)
    balpha = palloc("balpha_s", [1, 256], BF16)
    ones_bf = palloc("ones_bf", [128, 128], BF16)
    ones_row = palloc("ones_row", [1, 128], BF16)
    Sst = palloc("Sst", [128, 4, 128], F32)
    Sbf = palloc("Sbf", [128, 3, 4, 128], BF16)
    decA = palloc("decA", [128, 2, 9], F32)
    decSA = palloc("decSA", [128, 2, 16], F32)
    sq = palloc("sq", [128, 2, 512], BF16)
    rstd = palloc("rstd", [128, 2, 512], F32)
    UB = ptr[0]
    USZ = SB_END - UB

    def ualloc(name, shape, dt, off):
        nbytes = int(np.prod(shape[1:])) * esz(dt)
        assert off % 32 == 0 and off + nbytes <= USZ, (name, off, nbytes, USZ)
        return nc.alloc_sbuf_tensor_at(name, list(shape), dt, offset=UB + off), off + (nbytes + 31) // 32 * 32

    gbuf, o = ualloc("gbuf", [128, NFC, TM], BF16, 0)
    stmp, o = ualloc("stmp", [128, 2, 512], F32, o)
    pb, o_pb = ualloc("pb", [128, 2, TM], BF16, o)
    stmp2, _ = ualloc("stmp2", [128, 2, 512], F32, o_pb)
    yout, _ = ualloc("yout", [128, 2, 8, 512], F32, o_pb + 4096)
    sq8f, _ = ualloc("sq8f", [128, 8, 512], BF16, o_pb + 4096 + 32768)
    sq8m, _ = ualloc("sq8m", [128, 8, 512], BF16, 17408)
    qk, o = ualloc("qk", [128, 8, TM], BF16, 0)
    vtm, o = ualloc("vtm", [128, 9, 1024], BF16, o)
    kh, o = ualloc("kh", [128, 9, 512], BF16, o)
    o_tr = o
    ed, o = ualloc("ed", [128, 9, 256], F32, o)
    alowT, o = ualloc("alowT", [16, TM], BF16, o)
    spb, o = ualloc("spb", [128, 2, 256], F32, o)
    sphi, o = ualloc("sphi", [128, 2, 256], BF16, o)
    splo, o = ualloc("splo", [128, 2, 256], BF16, o)
    etmp, o = ualloc("etmp", [128, 2, 256], F32, o)
    eb, o = ualloc("eb", [128, 2, 512], F32, o)
    enb, o = ualloc("enb", [128, 2, 512], F32, o)
    rtab, o = ualloc("rtab_s", [128, 8, 512], F32, o)
    ttab, o = ualloc("ttab_s", [128, 3, 512], F32, o)
    tA, o = ualloc("tA", [128, 512], F32, o)
    tB, o = ualloc("tB", [128, 512], F32, o)
    tC, o = ualloc("tC", [128, 512], F32, o)
    tD, o = ualloc("tD", [128, 512], F32, o)
    o = o_tr
    PT, o = ualloc("PT", [128, 8, 128], BF16, o)
    osb, o = ualloc("osb", [128, 2, 512], F32, o)
    sqb, o = ualloc("sqb", [128, 2, 512], BF16, o)
    rs2, o = ualloc("rs2", [128, 2, 512], F32, o)
    ee, o = ualloc("ee", [128, 2, 512], F32, o)
    tt1, o = ualloc("tt1", [128, 2, 512], F32, o)
    tt2, o = ualloc("tt2", [128, 2, 512], F32, o)
    S0bf, o = ualloc("S0bf", [128, 2, 16 * 128], BF16, o)
    S0f_buf, o = ualloc("S0f", [128, 16 * 128], F32, o)
    tmpS_buf, o = ualloc("tmpS", [128, 16 * 128], F32, o)
    Vblk, o = ualloc("Vblk", [64, 16 * 128], BF16, o)
    qm, o = ualloc("qm", [128, 16, 64], BF16, o)
    print("SBUF union size", USZ, "P2 end", o)

    PSA = nc.alloc_psum_tensor("PSA", [128, 2048], F32)
    PSB = nc.alloc_psum_tensor("PSB", [128, 2048], F32)

    def bank(b):
        t = PSA if b < 4 else PSB
        return t[:, (b % 4) * 512:(b % 4) * 512 + 512]

    def mm(out, lhsT, rhs, start, stop, reads, writes):
        S.op("pe", lambda e: e.matmul(out, lhsT=lhsT, rhs=rhs, start=start, stop=stop), reads, writes)

    def act(out, in_, func, reads, writes, scale=1.0, bias=0.0):
        S.op("act", lambda e: e.activation(out=out, in_=in_, func=func, bias=bias, scale=scale), reads, writes)

    def tt(out, in0, in1, op, reads, writes, eng="dve"):
        S.op(eng, lambda e: e.tensor_tensor(out=out, in0=in0, in1=in1, op=op), reads, writes)

    def stt(out, in0, scalar, in1, op0, op1, reads, writes):
        S.op("dve", lambda e: e.scalar_tensor_tensor(out=out, in0=in0, scalar=scalar, in1=in1, op0=op0, op1=op1),
             reads, writes)

    def tsmul(out, in0, scalar, reads, writes):
        S.op("dve", lambda e: e.tensor_scalar(out=out, in0=in0, scalar1=scalar, scalar2=None, op0=ALU.mult),
             reads, writes)

    def dma(eng, out, in_, reads, writes, dsem):
        S.op(eng, lambda e: e.dma_start(out=out, in_=in_), reads, writes, dsem=dsem)

    def memset(ap, val, writes):
        S.op("dve", lambda e: e.memset(ap, val), (), writes)

    full_sched = wsched1 + wsched1
    st = {"cur": 0}

    def issue(i, after=()):
        if i >= len(full_sched):
            return
        off, L = full_sched[i]
        slot = i % RING
        dma("pool", ring[slot][:, 0:L], wflat[:, off:off + L], after, [("ring", slot)], ("w", slot))

    def wnext(nk, ncols):
        i = st["cur"]
        st["cur"] += 1
        off, L = full_sched[i]
        assert L == nk * ncols, (i, L, nk, ncols)
        slot = i % RING
        return i, ring[slot][:, 0:L].rearrange("p (k c) -> p k c", k=nk), ("ring", slot)

    def wrelease(i):
        issue(i + RING)

    dma("sp", gains[:], cpack[:], (), ["gains"], "c0")

    def late_consts():
        dma("pool", cmat[:].rearrange("p a b -> p (a b)"), cmat_d[:], (), ["cmat"], "c1")
        dma("sp", masks[:].rearrange("p a b -> p (a b)"), masks_d[:], (), ["masks"], "c2")
        dma("sp", smask[:], smask_d[:], (), ["smask"], "c3")
        dma("pool", maskq[:].rearrange("p a b -> p (a b)"), maskq_d[:], (), ["maskq"], "c6")
        dma("pool", wup[:], wup_d[:], (), ["wup"], "c4")
        dma("pool", balpha[:], balpha_d[:], (), ["balpha"], "c5")
    memset(ones_bf[:], 1.0, ["ones_bf"])
    memset(ones_row[:], 1.0, ["ones_row"])
    memset(Sst[:].rearrange("p a b -> p (a b)"), 0.0, ["S0", "S1", "S2", "S3"])
    memset(Sbf[:].rearrange("p t a b -> p (t a b)"), 0.0, [("Sb", t_, b_) for t_ in range(3) for b_ in range(2)])
    for i in range(2):
        issue(i)

    G_FFN1, G_MIX, G_FFN2, G_PLE, G_FIN, G_GNA, G_GNR, G_DECR = 0, 8, 16, 24, 32, 40, 44, 48

    def rmsnorm(tbs, gcol, dst_fn, dst_keys):
        for tbi, (off, n) in enumerate(tbs):
            pn = bank(6)
            for kc in range(8):
                b = kc % 2
                act(sq[:, b, 0:n], h[:, kc, off:off + n], AF.Square, [("h", kc, tbi)], [("sq", b)])
                mm(pn[:, 0:n], ones_bf[:], sq[:, b, 0:n], kc == 0, kc == 7, ["ones_bf", ("sq", b)], [("ps", 6)])
            rb = tbi % 2
            act(rstd[:, rb, 0:n], pn[:, 0:n], AF.Ln, [("ps", 6)], [("rstd", rb)], scale=1.0 / D, bias=EPS)
            act(rstd[:, rb, 0:n], rstd[:, rb, 0:n], AF.Exp, [("rstd", rb)], [("rstd", rb)], scale=-0.5)
            for kc in range(8):
                stt(dst_fn(kc, tbi, off, n), h[:, kc, off:off + n], gains[:, gcol + kc:gcol + kc + 1],
                    rstd[:, rb, 0:n], ALU.mult, ALU.mult,
                    [("h", kc, tbi), "gains", ("rstd", rb)], dst_keys(kc, tbi))

    def norm_to_u(tbs, gcol):
        rmsnorm(tbs, gcol, lambda kc, tbi, off, n: u[:, kc, off:off + n], lambda kc, tbi: [("u", kc, tbi)])

    def norm_u_tb(tbi, off, n, gcol):
        pn = bank(6)
        for kc in range(8):
            b = kc % 2
            act(sq[:, b, 0:n], h[:, kc, off:off + n], AF.Square, [("h", kc, tbi)], [("sq", b)])
            mm(pn[:, 0:n], ones_bf[:], sq[:, b, 0:n], kc == 0, kc == 7, ["ones_bf", ("sq", b)], [("ps", 6)])
        rb = tbi % 2
        act(rstd[:, rb, 0:n], pn[:, 0:n], AF.Ln, [("ps", 6)], [("rstd", rb)], scale=1.0 / D, bias=EPS)
        act(rstd[:, rb, 0:n], rstd[:, rb, 0:n], AF.Exp, [("rstd", rb)], [("rstd", rb)], scale=-0.5)
        for kc in range(8):
            stt(u[:, kc, off:off + n], h[:, kc, off:off + n], gains[:, gcol + kc:gcol + kc + 1],
                rstd[:, rb, 0:n], ALU.mult, ALU.mult,
                [("h", kc, tbi), "gains", ("rstd", rb)], [("u", kc, tbi)])

    def norm_part1(sq8, tbi, off, n):
        for kc in range(8):
            act(sq8[:, kc, 0:n], h[:, kc, off:off + n], AF.Square, [("h", kc, tbi)], [("sq8", kc)])

    def norm_part2(sq8, tbi, off, n, gcol):
        pn = bank(6)
        for kc in range(8):
            mm(pn[:, 0:n], ones_bf[:], sq8[:, kc, 0:n], kc == 0, kc == 7, ["ones_bf", ("sq8", kc)], [("ps", 6)])
        rb = tbi % 2
        act(rstd[:, rb, 0:n], pn[:, 0:n], AF.Ln, [("ps", 6)], [("rstd", rb)], scale=1.0 / D, bias=EPS)
        act(rstd[:, rb, 0:n], rstd[:, rb, 0:n], AF.Exp, [("rstd", rb)], [("rstd", rb)], scale=-0.5)
        for kc in range(8):
            stt(u[:, kc, off:off + n], h[:, kc, off:off + n], gains[:, gcol + kc:gcol + kc + 1],
                rstd[:, rb, 0:n], ALU.mult, ALU.mult,
                [("h", kc, tbi), "gains", ("rstd", rb)], [("u", kc, tbi)])

    def ffn(tbs, lazy, next_gcol=None):
        cnt = 0
        for g in range(6):
            nf = 4 if g < 5 else 2
            ia, wa, ka = wnext(8, nf * 128)
            ib, wb, kb = wnext(8, nf * 128)
            for tbi, (off, n) in enumerate(tbs):
                if g == 0 and tbi > 0:
                    lazy(tbi)
                for fl in range(nf):
                    fc = g * 4 + fl
                    pa, pbk = bank(cnt % 2), bank(2 + cnt % 2)
                    ka_, kb_ = ("ps", cnt % 2), ("ps", 2 + cnt % 2)
                    for kc in range(8):
                        mm(pa[:, 0:n], wa[:, kc, fl * 128:(fl + 1) * 128], u[:, kc, off:off + n], kc == 0, kc == 7,
                           [ka, ("u", kc, tbi)], [ka_])
                    for kc in range(8):
                        mm(pbk[:, 0:n], wb[:, kc, fl * 128:(fl + 1) * 128], u[:, kc, off:off + n], kc == 0, kc == 7,
                           [kb, ("u", kc, tbi)], [kb_])
                    sb_ = cnt % 2
                    act(stmp[:, sb_, 0:n], pa[:, 0:n], AF.Silu, [ka_], [("stmp", sb_)])
                    tt(gbuf[:, fc, off:off + n], stmp[:, sb_, 0:n], pbk[:, 0:n], ALU.mult,
                       [("stmp", sb_), kb_], [("g", fc, tbi)])
                    cnt += 1
            wrelease(ia)
            wrelease(ib)
        cnt = 0
        for dm in range(8):
            io, wo, ko = wnext(NFC, 128)
            for tbi, (off, n) in enumerate(tbs):
                po = bank(4 + cnt % 2)
                kp = ("ps", 4 + cnt % 2)
                for fc in range(NFC):
                    mm(po[:, 0:n], wo[:, fc, :], gbuf[:, fc, off:off + n], fc == 0, fc == NFC - 1,
                       [ko, ("g", fc, tbi)], [kp])
                stt(h[:, dm, off:off + n], po[:, 0:n], 0.5, h[:, dm, off:off + n], ALU.mult, ALU.add,
                    [kp, ("h", dm, tbi)], [("h", dm, tbi)])
                cnt += 1
                if next_gcol is not None and dm == 7:
                    if tbi == 0:
                        norm_part1(sq8f, 0, tbs[0][0], tbs[0][1])
                    elif tbi == 1:
                        norm_part2(sq8f, 0, tbs[0][0], tbs[0][1], next_gcol)
            wrelease(io)

    def mixing(hi, tok0, tbs, blocks, lazy, next_gcol=None):
        mix_start = st["cur"]

        def bail():
            skip_tiles(mix_start + 15 - st["cur"])

        T = sum(n for _, n in tbs)
        def rtab_load(tbi_, hp_):
            off_, n_ = tbs[tbi_]
            gtb = 4 if n_ == 64 else (tok0 + off_) // 512
            src = rtab_d[gtb].rearrange("p (a b) -> p a b", a=8)[:, hp_ * 4:(hp_ + 1) * 4, 0:n_]
            dma("sp", rtab[:, hp_ * 4:(hp_ + 1) * 4, 0:n_], src, (), [("rtab", hp_)], ("rt", hp_))

        def ttab_load(bi_):
            dma("sp", ttab[:, bi_ % 3, :], ttab_d[blocks[bi_][3]], (), [("ttab", bi_ % 3)], ("tt", bi_ % 3))

        rtab_load(0, 0)
        rtab_load(0, 1)
        for bi_ in range(min(3, len(blocks))):
            ttab_load(bi_)
        ial, wal, kal = wnext(8, 16)
        i0, w0, k0 = wnext(8, 512)
        for tbi, (off, n) in enumerate(tbs):
            if tbi > 0:
                lazy(tbi)
            pA = bank(7)
            for kc in range(8):
                mm(pA[0:16, 0:n], wal[:, kc, 0:16], u[:, kc, off:off + n], kc == 0, kc == 7,
                   [kal, ("u", kc, tbi)], [("ps", 7)])
            S.op("act", (lambda o_, i_: (lambda e: e.copy(out=o_, in_=i_)))(alowT[0:16, off:off + n], pA[0:16, 0:n]),
                 [("ps", 7)], [("alowT", tbi)])
            pend_b = []
            for bi, (boff, bn, smp, gblk) in enumerate(blocks):
                if not (off <= boff < off + n) or cut <= 0.2:
                    continue
                lo = boff - off
                dp = bi % 2
                px = bank(6 + dp)
                kpx = ("ps", 6 + dp)
                mm(px[0:bn, 0:256], alowT[0:16, boff:boff + bn], wup[:, :], True, False,
                   [("alowT", tbi), "wup"], [kpx])
                mm(px[0:bn, 0:256], ones_row[0:1, 0:bn], balpha[0:1, :], False, True,
                   ["ones_row", "balpha"], [kpx])
                act(etmp[0:bn, dp, :], px[0:bn, 0:256], AF.Exp, [kpx], [("etmp", dp)], scale=-1.0)
                act(spb[0:bn, dp, :], etmp[0:bn, dp, :], AF.Ln, [("etmp", dp)], [("spb", dp)], bias=1.0)
                if cut <= 0.4:
                    continue
                S.op("dve", (lambda o_, i_: (lambda e: e.tensor_copy(out=o_, in_=i_)))(sphi[0:bn, dp, :], spb[0:bn, dp, :]),
                     [("spb", dp)], [("sphi", dp)])
                tt(splo[0:bn, dp, :], spb[0:bn, dp, :], sphi[0:bn, dp, :], ALU.subtract,
                   [("spb", dp), ("sphi", dp)], [("splo", dp)])
                def part_b(bi=bi, boff=boff, bn=bn, smp=smp, lo=lo, dp=dp):
                    ci = 2 if smp else 0
                    b5 = 4 + bi % 2
                    p5 = bank(b5)
                    for hp in range(2):
                        for xi, (spx, kx) in enumerate(((sphi, ("sphi", dp)), (splo, ("splo", dp)))):
                            mm(p5[:, hp * 128:hp * 128 + bn], spx[0:bn, dp, hp * 128:(hp + 1) * 128],
                               cmat[0:bn, ci, 0:bn], xi == 0, xi == 1, [kx, "cmat"], [("ps", b5)])
                    for xi, (spx, kx) in enumerate(((sphi, ("sphi", dp)), (splo, ("splo", dp)))):
                        mm(p5[0:bn, 256:512], cmat[0:bn, ci + 1, 0:bn], spx[0:bn, dp, 0:256], xi == 0, xi == 1,
                           [kx, "cmat"], [("ps", b5)])
                    for hp in range(2):
                        src = p5[:, hp * 128:hp * 128 + bn]
                        act(eb[:, hp, lo:lo + bn], src, AF.Exp, [("ps", b5)], [("eb", hp)])
                        act(enb[:, hp, lo:lo + bn], src, AF.Exp, [("ps", b5)], [("enb", hp)], scale=-1.0)
                        if smp:
                            act(decSA[:, hp, :], p5[:, hp * 128 + 3:hp * 128 + 64:4], AF.Exp, [("ps", b5)], ["decSA"])
                        else:
                            act(decA[:, hp, bi:bi + 1], p5[:, hp * 128 + bn - 1:hp * 128 + bn], AF.Exp,
                                [("ps", b5)], [("decA", hp, bi)])
                    act(ed[0:bn, bi, :], p5[0:bn, 256:512], AF.Exp, [("ps", b5)], [("ed", bi)])

                for f_ in pend_b:
                    f_()
                pend_b = [part_b]
            for f_ in pend_b:
                f_()
            pend_b = []
            for ch in range(4):
                if cut <= 0.8:
                    continue
                pq = bank(ch)
                for kc in range(8):
                    mm(pq[:, 0:n], w0[:, kc, ch * 128:(ch + 1) * 128], u[:, kc, off:off + n], kc == 0, kc == 7,
                       [k0, ("u", kc, tbi)], [("ps", ch)])
                if cut <= 0.85:
                    continue
                if ch < 2:
                    stt(qk[:, ch, off:off + n], pq[:, 0:n], 0.125, eb[:, ch, 0:n], ALU.mult, ALU.mult,
                        [("ps", ch), ("eb", ch)], [("qk", 0, tbi, ch)])
                elif cut <= 0.9:
                    continue
                elif cut <= 0.95:
                    tt(qk[:, ch, off:off + n], pq[:, 0:n], eb[:, ch - 2, 0:n], ALU.mult,
                       [("ps", ch), ("eb", ch - 2)], [("qk", 0, tbi, ch)])
                elif cut <= 0.97:
                    stt(qk[:, ch, off:off + n], pq[:, 0:n], 1.0, enb[:, ch - 2, 0:n], ALU.mult, ALU.mult,
                        [("ps", ch), ("enb", ch - 2)], [("qk", 0, tbi, ch)])
                else:
                    tt(qk[:, ch, off:off + n], pq[:, 0:n], enb[:, ch - 2, 0:n], ALU.mult,
                       [("ps", ch), ("enb", ch - 2)], [("qk", 0, tbi, ch)])
        wrelease(ial)
        wrelease(i0)
        if cut <= 1:
            return bail()
        i1, w1, k1 = wnext(8, 512)
        i2, w2, k2 = wnext(8, 512)
        cnt = 0
        for tbi, (off, n) in enumerate(tbs):
            for hp in range(2):
                for isk in range(2):
                    ch = hp + 2 * isk
                    b0 = (cnt % 2) * 2
                    pa, pbk = bank(b0), bank(b0 + 1)
                    for kc in range(8):
                        mm(pa[:, 0:n], w1[:, kc, ch * 128:(ch + 1) * 128], u[:, kc, off:off + n], kc == 0, kc == 7,
                           [k1, ("u", kc, tbi)], [("ps", b0)])
                    for kc in range(8):
                        mm(pbk[:, 0:n], w2[:, kc, ch * 128:(ch + 1) * 128], u[:, kc, off:off + n], kc == 0, kc == 7,
                           [k2, ("u", kc, tbi)], [("ps", b0 + 1)])
                    tt(tA[:, 0:n], pa[:, 0:n], rtab[:, hp * 4 + 2 * isk, 0:n], ALU.mult, [("ps", b0), ("rtab", hp)], ["tA"])
                    tt(tB[:, 0:n], pbk[:, 0:n], rtab[:, hp * 4 + 2 * isk + 1, 0:n], ALU.mult,
                       [("ps", b0 + 1), ("rtab", hp)], ["tB"])
                    qi = 4 + 2 * isk + hp
                    tt(qk[:, qi, off:off + n], tA[:, 0:n], tB[:, 0:n], ALU.add, ["tA", "tB"], [("qk", 1, tbi, qi)])
                    cnt += 1
                if tbi + 1 < len(tbs):
                    rtab_load(tbi + 1, hp)
        wrelease(i1)
        wrelease(i2)
        if cut <= 2:
            return bail()
        i3, w3, k3 = wnext(8, 512)
        for bi, (boff, bn, smp, gblk) in enumerate(blocks):
            tbi = min(boff // 512, len(tbs) - 1)
            tb_ = bi % 3
            pk = bank(bi % 2)
            kp = ("ps", bi % 2)
            for kc in range(8):
                mm(pk[0:bn, :], u[:, kc, boff:boff + bn], w3[:, kc, :], kc == 0, kc == 7, [k3, ("u", kc, tbi)], [kp])
            tt(kh[0:bn, bi, 0:256], pk[0:bn, 0:256], ed[0:bn, bi, :], ALU.mult, [kp, ("ed", bi)], [("kh", bi)])
            pr = pk[0:bn, 256:512].rearrange("p (h s e) -> p h s e", h=4, s=2)
            Ct = ttab[0:bn, tb_, 0:256].rearrange("p (h s e) -> p h s e", h=4, s=2)
            St = ttab[0:bn, tb_, 256:512].rearrange("p (h s e) -> p h s e", h=4, s=2)
            kr = kh[0:bn, bi, 256:512].rearrange("p (h s e) -> p h s e", h=4, s=2)
            tA4 = tA[0:bn, 0:256].rearrange("p (h s e) -> p h s e", h=4, s=2)
            tB4 = tB[0:bn, 0:256].rearrange("p (h s e) -> p h s e", h=4, s=2)
            for s_ in range(2):
                tt(tA4[:, :, s_, :], pr[:, :, s_, :], Ct[:, :, s_, :], ALU.mult, [kp, ("ttab", tb_)], ["tA"])
                tt(tB4[:, :, s_, :], pr[:, :, 1 - s_, :], St[:, :, s_, :], ALU.mult, [kp, ("ttab", tb_)], ["tB"])
                tt(kr[:, :, s_, :], tA4[:, :, s_, :], tB4[:, :, s_, :], ALU.add, ["tA", "tB"], [("kh", bi)])
            if bi + 3 < len(blocks):
                ttab_load(bi + 3)
        wrelease(i3)
        if cut <= 3:
            return bail()
        for vi in range(2):
            iv, wv, kv = wnext(8, 512)
            for bi, (boff, bn, smp, gblk) in enumerate(blocks):
                tbi = min(boff // 512, len(tbs) - 1)
                pv = bank(2 + bi % 2)
                kp = ("ps", 2 + bi % 2)
                for kc in range(8):
                    mm(pv[0:bn, :], u[:, kc, boff:boff + bn], wv[:, kc, :], kc == 0, kc == 7,
                       [kv, ("u", kc, tbi)], [kp])
                S.op("act", (lambda o_, i_: (lambda e: e.copy(out=o_, in_=i_)))(
                    vtm[0:bn, bi, vi * 512:(vi + 1) * 512], pv[0:bn, :]), [kp], [("v", bi, vi)])
            wrelease(iv)
        if cut <= 4:
            return bail()
        S.barrier(exclude=("pe",))
        igt = []
        wg = []
        kg = []
        for br in range(2):
            i_, w_, k_ = wnext(8, 512)
            igt.append(i_); wg.append(w_); kg.append(k_)
        ecnt = [0]

        def epilogue(br, hl, po, off, n, tbi, pokey, ebanks=((2, 3), (0, 1))):
            ob = ecnt[0] % 2
            ecnt[0] += 1
            tbk, rbk = ebanks[ob]
            act(sqb[:, ob, 0:n], po, AF.Square, [pokey], [("sqb", ob)])
            pR = bank(rbk)
            for kc in range(8):
                mm(pR[:, 0:n], wg[br][:, kc, hl * 128:(hl + 1) * 128], u[:, kc, off:off + n], kc == 0, kc == 7,
                   [kg[br], ("u", kc, tbi)], [("ps", rbk)])
            pT_ = bank(tbk)
            mm(pT_[:, 0:n], ones_bf[:], sqb[:, ob, 0:n], True, True, ["ones_bf", ("sqb", ob)], [("ps", tbk)])
            act(rs2[:, ob, 0:n], pT_[:, 0:n], AF.Ln, [("ps", tbk)], [("rs2", ob)], scale=1.0 / 128, bias=EPS)
            act(ee[:, ob, 0:n], pR[:, 0:n], AF.Exp, [("ps", rbk)], [("ee", ob)], scale=-1.0)
            act(ee[:, ob, 0:n], ee[:, ob, 0:n], AF.Ln, [("ee", ob)], [("ee", ob)], bias=1.0)
            stt(tt2[:, ob, 0:n], rs2[:, ob, 0:n], -0.5, ee[:, ob, 0:n], ALU.mult, ALU.subtract,
                [("rs2", ob), ("ee", ob)], [("tt2", ob)])
            act(tt2[:, ob, 0:n], tt2[:, ob, 0:n], AF.Exp, [("tt2", ob)], [("tt2", ob)])
            gc = (G_GNA if br == 0 else G_GNR) + hl
            stt(tt1[:, ob, 0:n], po, gains[:, gc:gc + 1], tt2[:, ob, 0:n], ALU.mult, ALU.mult,
                [pokey, "gains", ("tt2", ob)], [("tt1", ob)])
            qi = br * 4 + hl
            wkeys = [("qk", br, tbi, br * 4 + j) for j in range(4)]
            tt(qk[:, qi, off:off + n], tt1[:, ob, 0:n], pR[:, 0:n], ALU.mult, [("tt1", ob), ("ps", rbk)], wkeys)

        scnt = [0]
        SX = (S0f_buf, tmpS_buf)

        def s0_load(sidx):
            br_, pair_ = sidx // 2, sidx % 2
            src = st_in[br_][:, pair_ * 2:pair_ * 2 + 2].rearrange("s h d v -> (h d) s v")
            sb_ = sidx % 2
            dma("pool", S0bf[:, sb_, :].rearrange("p (s v) -> p s v", s=16), src, (), [("S0bf", sb_)], ("s0b", sb_))
            x_ = SX[sidx % 2]
            kx = ("SX", sidx % 2)
            dma("sp", x_[:].rearrange("p (s v) -> p s v", s=16), src, (), [kx, kx + (0,), kx + (1,)], ("s0f", sidx % 2))

        if hi == 1:
            s0_load(0)
        for tbi, (off, n) in enumerate(tbs):
            smp_tb = (n == 64)
            rkeys_all = lambda br: [("qk", br, tbi, br * 4 + j) for j in range(4)]
            for br in range(2):
                qb, kb_i = br * 4, br * 4 + 2
                if not smp_tb:
                    tblocks = [(bi, b) for bi, b in enumerate(blocks) if off <= b[0] < off + n]
                    pend = []

                    def o_ops(c4, bi, boff, bn, gblk):
                        par = c4 % 2
                        for hl in range(4):
                            pair, h2 = hl // 2, hl % 2
                            sidx = br * 2 + pair
                            pr_ = slice(h2 * 64, h2 * 64 + 64)
                            po = bank(4 + hl)[:, c4 * 128:(c4 + 1) * 128]
                            vc = br * 512 + hl * 128
                            mm(po, vtm[:, bi, vc:vc + 128], PT[:, par * 4 + h2 * 2 + pair, :], True, False,
                               [("v", bi, br), ("PT", par)], [("ps", 4 + hl)])
                            mm(po, Sbf[pr_, (gblk - 1) % 3, sidx, :], qk[pr_, qb + pair, boff:boff + bn], False, True,
                               [("Sb", (gblk - 1) % 3, br)] + rkeys_all(br), [("ps", 4 + hl)])

                    for c4, (bi, (boff, bn, smp, gblk)) in enumerate(tblocks):
                        par = c4 % 2
                        ubk = (3, 1)[par]
                        pu = bank(ubk)
                        for pair in range(2):
                            kc0 = br * 256 + pair * 128
                            vc0 = br * 512 + pair * 256
                            mm(pu[:, pair * 256:(pair + 1) * 256], kh[:, bi, kc0:kc0 + 128], vtm[:, bi, vc0:vc0 + 256],
                               True, True, [("kh", bi), ("v", bi, br)], [("ps", ubk)])
                        for hl in (0, 2, 1, 3):
                            pair, h2 = hl // 2, hl % 2
                            pr_ = slice(h2 * 64, h2 * 64 + 64)
                            sb2 = (0, 2)[h2]
                            mm(bank(sb2)[:, pair * 128:(pair + 1) * 128], qk[pr_, kb_i + pair, boff:boff + bn],
                               qk[pr_, qb + pair, boff:boff + bn], True, True, rkeys_all(br), [("ps", sb2)])
                        for f_ in pend:
                            f_()
                        pend = []
                        for h2 in range(2):
                            sb2 = (0, 2)[h2]
                            tt(PT[:, par * 4 + h2 * 2:par * 4 + h2 * 2 + 2, :],
                               bank(sb2)[:, 0:256].rearrange("p (a b) -> p a b", a=2),
                               masks[:, 0, :].unsqueeze(1).to_broadcast([128, 2, 128]), ALU.mult,
                               [("ps", sb2), "masks"], [("PT", par)])
                        for pair in range(2):
                            sidx = br * 2 + pair
                            for h2 in range(2):
                                pr_ = slice(h2 * 64, h2 * 64 + 64)
                                if br == 0:
                                    dsc = decA[pr_, pair, bi:bi + 1]
                                    dk_ = [("decA", pair, bi)]
                                else:
                                    dsc = gains[pr_, G_DECR + pair:G_DECR + pair + 1]
                                    dk_ = ["gains"]
                                stt(Sst[pr_, sidx, :], Sst[pr_, sidx, :], dsc,
                                    pu[pr_, pair * 256 + h2 * 128:pair * 256 + (h2 + 1) * 128],
                                    ALU.mult, ALU.add, ["S%d" % sidx, ("ps", ubk)] + dk_, ["S%d" % sidx])
                        S.op("act", (lambda o_, i_: (lambda e: e.copy(out=o_, in_=i_)))(
                            Sbf[:, gblk % 3, br * 2:br * 2 + 2, :], Sst[:, br * 2:br * 2 + 2, :]),
                            ["S%d" % (br * 2), "S%d" % (br * 2 + 1)], [("Sb", gblk % 3, br)])
                        pend.append((lambda a, b, c, d, e_: (lambda: o_ops(a, b, c, d, e_)))(c4, bi, boff, bn, gblk))
                    for f_ in pend:
                        f_()
                    for hl in range(4):
                        eb_ = ((2, 3), (0, 1)) if hl < 2 else ((2, 4), (0, 5))
                        epilogue(br, hl, bank(4 + hl)[:, 0:n], off, n, tbi, ("ps", 4 + hl), eb_)
                else:
                    bi = len(blocks) - 1
                    boff, bn, smp, gblk = blocks[bi]
                    for pair in range(2):
                        sidx = br * 2 + pair
                        sb_ = sidx % 2
                        S0f, tmpS = (SX[sidx % 2], SX[1 - sidx % 2])
                        kS0f, kTmp = ("SX", sidx % 2), ("SX", 1 - sidx % 2)
                        S0b3 = S0bf[:, sb_, :].rearrange("p (s v) -> p s v", s=16)
                        S0f3 = S0f[:].rearrange("p (s v) -> p s v", s=16)
                        tmp3 = tmpS[:].rearrange("p (s v) -> p s v", s=16)
                        for h2 in range(2):
                            hl = pair * 2 + h2
                            pr_ = slice(h2 * 64, h2 * 64 + 64)
                            slot = scnt[0] % 4
                            scnt[0] += 1
                            sbk = (0, 2)[slot % 2]
                            psS = bank(sbk)[0:64, 0:64]
                            mm(psS, qk[pr_, kb_i + pair, boff:boff + 64], qk[pr_, qb + pair, boff:boff + 64],
                               True, True, rkeys_all(br), [("ps", sbk)])
                            pts = slot + 4 * br
                            tt(PT[0:64, pts, 0:64], psS, masks[0:64, 1, 0:64], ALU.mult,
                               [("ps", sbk), "masks"], [("PT", pts // 4)])
                            if cut <= 4.93:
                                continue
                            po = bank(1)[:, hl * 64:(hl + 1) * 64]
                            vc = br * 512 + hl * 128
                            mm(po, vtm[0:64, bi, vc:vc + 128], PT[0:64, pts, 0:64], True, cut <= 4.935,
                               [("v", bi, br), ("PT", pts // 4)], [("ps", 1)])
                            po_ = slice((1 - h2) * 64, (1 - h2) * 64 + 64)
                            S.op("pool", (lambda a_: (lambda e: e.memset(a_, 0.0)))(qm[po_].rearrange("p a b -> p (a b)")),
                                 (), ["qm"])
                            tt(qm[pr_], qk[pr_, qb + pair, boff:boff + 64].unsqueeze(1).to_broadcast([64, 16, 64]),
                               maskq[pr_], ALU.mult, rkeys_all(br) + ["maskq"], ["qm"])
                            for s_ in range(16):
                                if cut <= 4.935:
                                    continue
                                mm(po, S0b3[:, s_, :], qm[:, s_, :], False, s_ == 15,
                                   [("S0bf", sb_), "qm"], [("ps", 1)])
                            if cut <= 4.94:
                                continue
                            V3 = Vblk[:].rearrange("p (s v) -> p s v", s=16)
                            tt(V3, vtm[0:64, bi, vc:vc + 128].unsqueeze(1).to_broadcast([64, 16, 128]),
                               smask[:, :].unsqueeze(2).to_broadcast([64, 16, 128]), ALU.mult,
                               [("v", bi, br), "smask"], ["Vblk"], eng="pool")
                            if cut <= 4.95:
                                continue
                            kc0 = br * 256 + pair * 128
                            for q4 in range(4):
                                mm(PSB[:, q4 * 512:(q4 + 1) * 512], kh[0:64, bi, kc0:kc0 + 128],
                                   Vblk[:, q4 * 512:(q4 + 1) * 512], True, True, [("kh", bi), "Vblk"], [("ps", 4 + q4)])
                            if h2 == 0:
                                if br == 0:
                                    tt(tmp3, S0f3, decSA[:, pair, :].unsqueeze(2).to_broadcast([128, 16, 128]),
                                       ALU.mult, [kS0f, "decSA"], [kTmp])
                                else:
                                    tsmul(tmpS[:, :], S0f[:, :], gains[:, G_DECR + 2 + pair:G_DECR + 3 + pair],
                                          [kS0f, "gains"], [kTmp])
                            tt(S0f[pr_, :], tmpS[pr_, :], PSB[pr_, :], ALU.add,
                               [kTmp] + [("ps", 4 + q) for q in range(4)], [kS0f + (h2,)])
                        if sidx < 3:
                            s0_load(sidx + 1)
                        dst = st_out[br][:, pair * 2:pair * 2 + 2].rearrange("s h d v -> (h d) s v")
                        dma("sp", dst, S0f3, [kS0f, kS0f + (0,), kS0f + (1,)], (), ("s0o", sidx % 2))
                    for hl in range(4):
                        if cut <= 4.98:
                            continue
                        epilogue(br, hl, bank(1)[:, hl * 64:(hl + 1) * 64], off, n, tbi, ("ps", 1), ((2, 3), (0, 5)))
        if hi == 1:
            for br in range(2):
                for pair in range(2):
                    sidx = br * 2 + pair
                    dma("sp", sp_out[br][pair * 128:(pair + 1) * 128, :], Sst[:, sidx, :], ["S%d" % sidx], (), "spo")
        wrelease(igt[0])
        wrelease(igt[1])
        if cut <= 5:
            return bail()
        S.barrier(exclude=("pe",))
        cnt = 0
        for dmg in range(2):
            iga, wga, kga = wnext(8, 512)
            igr, wgr, kgr = wnext(8, 512)
            iwo, wwo, kwo = wnext(8, 512)
            for tbi, (off, n) in enumerate(tbs):
                okeys = lambda br: [("qk", br, tbi, br * 4 + j) for j in range(4)]
                for dl in range(4):
                    dm = dmg * 4 + dl
                    b0 = (cnt % 2) * 4
                    cnt += 1
                    pga, pgr, pma, pmr = bank(b0), bank(b0 + 1), bank(b0 + 2), bank(b0 + 3)
                    cs = slice(dl * 128, (dl + 1) * 128)
                    for kc in range(8):
                        mm(pga[:, 0:n], wga[:, kc, cs], u[:, kc, off:off + n], kc == 0, kc == 7,
                           [kga, ("u", kc, tbi)], [("ps", b0)])
                    for kc in range(8):
                        mm(pgr[:, 0:n], wgr[:, kc, cs], u[:, kc, off:off + n], kc == 0, kc == 7,
                           [kgr, ("u", kc, tbi)], [("ps", b0 + 1)])
                    for fc in range(4):
                        mm(pma[:, 0:n], wwo[:, fc, cs], qk[:, fc, off:off + n], fc == 0, fc == 3,
                           [kwo] + okeys(0), [("ps", b0 + 2)])
                    for fc in range(4, 8):
                        mm(pmr[:, 0:n], wwo[:, fc, cs], qk[:, fc, off:off + n], fc == 4, fc == 7,
                           [kwo] + okeys(1), [("ps", b0 + 3)])
                    act(tA[:, 0:n], pga[:, 0:n], AF.Tanh, [("ps", b0)], ["tA"], scale=0.5)
                    act(tB[:, 0:n], pgr[:, 0:n], AF.Tanh, [("ps", b0 + 1)], ["tB"], scale=0.5)
                    stt(tC[:, 0:n], tA[:, 0:n], 1.0, pma[:, 0:n], ALU.add, ALU.mult, ["tA", ("ps", b0 + 2)], ["tC"])
                    stt(tD[:, 0:n], tB[:, 0:n], 1.0, pmr[:, 0:n], ALU.add, ALU.mult, ["tB", ("ps", b0 + 3)], ["tD"])
                    tt(tA[:, 0:n], tC[:, 0:n], tD[:, 0:n], ALU.add, ["tC", "tD"], ["tA"])
                    stt(h[:, dm, off:off + n], tA[:, 0:n], 0.5, h[:, dm, off:off + n], ALU.mult, ALU.add,
                        ["tA", ("h", dm, tbi)], [("h", dm, tbi)])
                    if next_gcol is not None and dmg == 1:
                        if tbi == 0 and dl == 3:
                            norm_part1(sq8m, 0, tbs[0][0], tbs[0][1])
                        elif tbi == 1 and dl == 1:
                            norm_part2(sq8m, 0, tbs[0][0], tbs[0][1], next_gcol)
            wrelease(iga)
            wrelease(igr)
            wrelease(iwo)

    def ple(tok0, tbs, lazy):
        cnt = 0
        for dmg in range(2):
            ig, wg_, kg_ = wnext(8, 512)
            ip, wp_, kp_ = wnext(2, 512)
            for tbi, (off, n) in enumerate(tbs):
                if dmg == 0 and tbi > 0:
                    lazy(tbi)
                for dl in range(4):
                    dm = dmg * 4 + dl
                    b0 = (cnt % 2) * 2
                    cnt += 1
                    pg, pp = bank(b0), bank(b0 + 1)
                    cs = slice(dl * 128, (dl + 1) * 128)
                    for kc in range(8):
                        mm(pg[:, 0:n], wg_[:, kc, cs], u[:, kc, off:off + n], kc == 0, kc == 7,
                           [kg_, ("u", kc, tbi)], [("ps", b0)])
                    for kc in range(2):
                        mm(pp[:, 0:n], wp_[:, kc, cs], pb[:, kc, off:off + n], kc == 0, kc == 1,
                           [kp_, "pb"], [("ps", b0 + 1)])
                    sb_ = cnt % 2
                    act(stmp[:, sb_, 0:n], pg[:, 0:n], AF.Tanh, [("ps", b0)], [("stmp", sb_)], scale=0.5)
                    stt(stmp2[:, sb_, 0:n], stmp[:, sb_, 0:n], 1.0, pp[:, 0:n], ALU.add, ALU.mult,
                        [("stmp", sb_), ("ps", b0 + 1)], [("stmp2", sb_)])
                    stt(h[:, dm, off:off + n], stmp2[:, sb_, 0:n], 0.5, h[:, dm, off:off + n], ALU.mult, ALU.add,
                        [("stmp2", sb_), ("h", dm, tbi)], [("h", dm, tbi)])
            wrelease(ig)
            wrelease(ip)

    def skip_tiles(k):
        for _ in range(k):
            i = st["cur"]
            st["cur"] += 1
            wrelease(i)

    yv = yT.rearrange("(k p) t -> p k t", p=128)
    xv = xT.rearrange("(k p) t -> p k t", p=128)
    halves = [(0, [(0, 512), (512, 512)]), (1024, [(0, 512), (512, 512), (1024, 64)])]
    for hi, (tok0, tbs) in enumerate(halves[:nhalves]):
        T = sum(n for _, n in tbs)
        blocks = [(b * 128, 128, False, (tok0 + b * 128) // 128) for b in range(8)]
        if hi == 1:
            blocks.append((1024, 64, True, 16))
        def xload(tok0_, tbi, off, n, after=()):
            dma("sp", h[:, :, off:off + n], xv[:, :, tok0_ + off:tok0_ + off + n], after,
                [("h", kc, tbi) for kc in range(8)], ("x", tbi))

        if hi == 0:
            for tbi, (off, n) in enumerate(tbs):
                xload(tok0, tbi, off, n, after=([("h", 0, 0)] if tbi > 0 else ()))
            for i in range(2, RING):
                issue(i, after=[("h", 0, 0)])
            late_consts()

        def mk_lazy(gcol):
            return lambda tbi: norm_u_tb(tbi, tbs[tbi][0], tbs[tbi][1], gcol)

        norm_u_tb(0, tbs[0][0], tbs[0][1], G_FFN1)
        ffn(tbs, mk_lazy(G_FFN1), G_MIX if (stage >= 2 and cut >= 99) else None)
        if stage >= 2:
            if cut < 99:
                norm_u_tb(0, tbs[0][0], tbs[0][1], G_MIX)
            S.barrier(exclude=("pe",))
            mixing(hi, tok0, tbs, blocks, mk_lazy(G_MIX), G_FFN2 if stage >= 3 else None)
        else:
            skip_tiles(15)
        if stage >= 3:
            if cut < 99:
                norm_u_tb(0, tbs[0][0], tbs[0][1], G_FFN2)
            S.barrier(exclude=("pe",))
            if stage >= 4:
                T_ = sum(n for _, n in tbs)
                dma("pool", pb[:, :, 0:T_], pT.rearrange("(k p) t -> p k t", p=128)[:, :, tok0:tok0 + T_], (),
                    ["pb"], "pb")
            ffn(tbs, mk_lazy(G_FFN2), G_PLE if stage >= 4 else None)
        else:
            S.barrier(exclude=("pe",))
            skip_tiles(20)
        if stage >= 4:
            ple(tok0, tbs, mk_lazy(G_PLE))
        else:
            skip_tiles(4)
        if stage >= 5:
            for tbi, (off, n) in enumerate(tbs):
                pn = bank(6)
                for kc in range(8):
                    b = kc % 2
                    act(sq[:, b, 0:n], h[:, kc, off:off + n], AF.Square, [("h", kc, tbi)], [("sq", b)])
                    mm(pn[:, 0:n], ones_bf[:], sq[:, b, 0:n], kc == 0, kc == 7, ["ones_bf", ("sq", b)], [("ps", 6)])
                rb = tbi % 2
                act(rstd[:, rb, 0:n], pn[:, 0:n], AF.Ln, [("ps", 6)], [("rstd", rb)], scale=1.0 / D, bias=EPS)
                act(rstd[:, rb, 0:n], rstd[:, rb, 0:n], AF.Exp, [("rstd", rb)], [("rstd", rb)], scale=-0.5)
                for kc in range(8):
                    stt(yout[:, rb, kc, 0:n], h[:, kc, off:off + n], gains[:, G_FIN + kc:G_FIN + kc + 1],
                        rstd[:, rb, 0:n], ALU.mult, ALU.mult,
                        [("h", kc, tbi), "gains", ("rstd", rb)], [("yout", rb)])
                if stage >= 5 and hi + 1 < len(halves[:nhalves]):
                    ntok0, ntbs = halves[hi + 1]
                    xload(ntok0, tbi, ntbs[tbi][0], ntbs[tbi][1])
                dma("sp", yv[:, :, tok0 + off:tok0 + off + n], yout[:, rb, :, 0:n], [("yout", rb)], (), ("yo", rb))
                if stage >= 5 and hi + 1 < len(halves[:nhalves]) and tbi == len(tbs) - 1:
                    ntok0, ntbs = halves[hi + 1]
                    for t2 in range(len(tbs), len(ntbs)):
                        xload(ntok0, t2, ntbs[t2][0], ntbs[t2][1])
        else:
            for tbi, (off, n) in enumerate(tbs):
                dma("sp", yv[:, :, tok0 + off:tok0 + off + n], h[:, :, off:off + n],
                    [("h", kc, tbi) for kc in range(8)], (), "yo")
            if hi + 1 < len(halves[:nhalves]):
                ntok0, ntbs = halves[hi + 1]
                for t2 in range(len(ntbs)):
                    xload(ntok0, t2, ntbs[t2][0], ntbs[t2][1])
    S.emit(nc)
    return nc


_CONST_CACHE = {}


def _prep_inputs(inp):
    f32 = np.float32
    inp = {k: np.asarray(v) for k, v in inp.items()}
    wflat, sched = _weight_tiles(inp)
    ct = _CONST_CACHE.get("ct")
    if ct is None:
        ct = _const_tables()
        _CONST_CACHE["ct"] = ct

    def col8(v):
        return np.asarray(v, f32).reshape(-1, 128).T

    cpack = np.zeros((128, 52), f32)
    cpack[:, 0:8] = col8(inp["norm_ffn1"][0])
    cpack[:, 8:16] = col8(inp["norm_mix"][0])
    cpack[:, 16:24] = col8(inp["norm_ffn2"][0])
    cpack[:, 24:32] = col8(inp["norm_ple"][0])
    cpack[:, 32:40] = col8(inp["norm_final"])
    cpack[:, 40:44] = col8(inp["gn_gla"][0])
    cpack[:, 44:48] = col8(inp["gn_ret"][0])
    cpack[:, 48:52] = ct["decr"]
    xp, xs = inp["x_prompt"], inp["x_sample"]
    pp, psm = inp["p_prompt"][0], inp["p_sample"][0]
    in_maps = []
    for c in range(NCORES):
        xc = np.concatenate([xp[c], xs[16 * c:16 * c + 16].reshape(NSAMP, D)], axis=0)
        pc = np.concatenate([pp[c], psm[16 * c:16 * c + 16].reshape(NSAMP, 256)], axis=0)
        in_maps.append({
            "xT": np.ascontiguousarray(xc.T.astype(f32)),
            "pT": np.ascontiguousarray(pc.T.astype(f32)),
            "wflat": wflat,
            "cpack": cpack,
            "cmat": ct["cmat"], "masks": ct["masks"], "smask": ct["smask"], "maskq": ct["maskq"],
            "wup": np.ascontiguousarray(inp["w_alpha_up"][0].astype(f32)),
            "balpha": np.ascontiguousarray(inp["b_alpha"][0].reshape(1, 256).astype(f32)),
            "rtab": ct["rtab"], "ttab": ct["ttab"],
            "sg_in": np.ascontiguousarray(inp["state_gla"][0, 16 * c:16 * c + 16].astype(f32)),
            "sr_in": np.ascontiguousarray(inp["state_ret"][0, 16 * c:16 * c + 16].astype(f32)),
        })
    return in_maps


def _run(inp, stage=5):
    in_maps = _prep_inputs(inp)
    nc = build_program(stage)
    res = run_bass_kernel_spmd(nc, in_maps, core_ids=list(range(NCORES)))
    return res.results


def kernel(**inputs):
    results = _run(inputs, 5)
    f32 = np.float32
    y_prompt = np.zeros((8, TP, D), f32)
    y_sample = np.zeros((128, 4, D), f32)
    gp = np.zeros((1, 8, 4, 64, 128), f32); rp = np.zeros((1, 8, 4, 64, 128), f32)
    gs = np.zeros((1, 128, 4, 64, 128), f32); rs = np.zeros((1, 128, 4, 64, 128), f32)
    for c, r in enumerate(results):
        yc = np.asarray(r["yT"]).T
        y_prompt[c] = yc[:TP]
        y_sample[16 * c:16 * c + 16] = yc[TP:].reshape(16, 4, D)
        gp[0, c] = np.asarray(r["sgp"]).reshape(4, 64, 128)
        rp[0, c] = np.asarray(r["srp"]).reshape(4, 64, 128)
        gs[0, 16 * c:16 * c + 16] = np.asarray(r["sgs"])
        rs[0, 16 * c:16 * c + 16] = np.asarray(r["srs"])
    return (y_prompt, y_sample, gp, rp, gs, rs)
```
